# Optimizing a Trainium2 kernel written in Bass

```python
import math
import jax
import jax.numpy as jnp
from jax import lax
import numpy as np

D_MODEL = 4096
BATCH = 8
SEQ = 2048
DEPTH = 1
DEC_BATCH = 32
DEC_SEQ = 64
PAST_LEN = 1024

CHUNK = 64
QBLK = 128
HEAD_DIM = 128
N_HEADS_A = D_MODEL // (2 * HEAD_DIM)
N_KV_A = 4
IDX_HEADS = 16
IDX_DIM = 64
TOPK_MAX = 256
N_HEADS_B = D_MODEL // (2 * HEAD_DIM)
DK_B = HEAD_DIM
DV_B = HEAD_DIM
CONV_B = 4
FFN_CONV = 3
D_FF = 11008
N_BUCKETS = 32
MAX_DIST = 128
EPS = 1e-5
ALPHA = (2 * DEPTH) ** 0.25
OUT_INIT = (8 * DEPTH) ** -0.25

W_A = N_HEADS_A * HEAD_DIM
W_B = N_HEADS_B * DV_B
KV_W = N_KV_A * HEAD_DIM
IQ_W = IDX_HEADS * IDX_DIM
C_QKV = 2 * N_HEADS_B * DK_B + N_HEADS_B * DV_B
COL_SIZES = (W_A, KV_W, KV_W, IQ_W, IDX_DIM, IDX_HEADS, C_QKV, W_B, N_HEADS_B, N_HEADS_B)
N_IN = sum(COL_SIZES)
IDX_SCALE = (IDX_HEADS ** -0.5) * (IDX_DIM ** -0.5)

kernel_name = "hybrid_dsa_gdn_stream_step"


def split_cols(h, sizes):
    idx = np.cumsum(np.array(sizes))[:-1].tolist()
    return jnp.split(h, idx, axis=-1)


def layer_norm(x, g, b):
    xf = x.astype(jnp.float32)
    mu = jnp.mean(xf, axis=-1, keepdims=True)
    xc = xf - mu
    var = jnp.mean(xc * xc, axis=-1, keepdims=True)
    return (xc * lax.rsqrt(var + EPS) * g + b).astype(x.dtype)


def l2norm(x):
    xf = x.astype(jnp.float32)
    return xf * lax.rsqrt(jnp.sum(xf * xf, axis=-1, keepdims=True) + 1e-6)


def causal_dwconv(x, prev, w, bias):
    width = w.shape[0]
    t = x.shape[1]
    xp = jnp.concatenate([prev.astype(x.dtype), x], axis=1)
    y = bias
    for j in range(width):
        y = y + xp[:, j:j + t] * w[j]
    return y, xp[:, xp.shape[1] - (width - 1):]


def t5_bucket(rel):
    half = N_BUCKETS // 2
    max_exact = half // 2
    side = jnp.where(rel > 0, half, 0)
    n = jnp.abs(rel)
    nf = jnp.maximum(n, 1).astype(jnp.float32)
    large = max_exact + (jnp.log(nf / max_exact) / math.log(MAX_DIST / max_exact)
                         * (half - max_exact)).astype(jnp.int32)
    large = jnp.minimum(large, half - 1)
    return side + jnp.where(n < max_exact, n, large)


def dsa_block(q, iq, iw, q_pos, k_all, v_all, ik_all, top_k, rel_bias):
    b, t, h, hd = q.shape
    n_len = k_all.shape[1]
    k_pos = jnp.arange(n_len, dtype=jnp.int32)
    s_idx = jnp.einsum("bthd,bsd->bths", iq, ik_all).astype(jnp.float32)
    index = jnp.einsum("bths,bth->bts", jax.nn.relu(s_idx), iw.astype(jnp.float32) * IDX_SCALE)
    limit = (q_pos // CHUNK + 1) * CHUNK
    admissible = k_pos[None, :] < limit[:, None]
    index = jnp.where(admissible[None], index, -jnp.inf)
    _, sel = lax.top_k(index, top_k)
    valid = sel < limit[None, :, None]
    gather = jax.vmap(lambda kv, ii: kv[ii])
    k_sel = gather(k_all, sel)
    v_sel = gather(v_all, sel)
    qg = q.reshape(b, t, N_KV_A, h // N_KV_A, hd)
    logits = jnp.einsum("btngd,btsnd->btngs", qg, k_sel).astype(jnp.float32) * (hd ** -0.5)
    bias = rel_bias[t5_bucket(sel - q_pos[None, :, None])].astype(jnp.float32)
    bias = jnp.transpose(bias.reshape(b, t, top_k, N_KV_A, h // N_KV_A), (0, 1, 3, 4, 2))
    logits = jnp.where(valid[:, :, None, None, :], logits + bias, -jnp.inf)
    p = jax.nn.softmax(logits, axis=-1).astype(v_sel.dtype)
    out = jnp.einsum("btngs,btsnd->btngd", p, v_sel)
    return out.reshape(b, t, h * hd)


def gated_delta_chunked(q, k, v, g, beta, s0, chunk):
    b, t, h, dk = q.shape
    dv = v.shape[-1]
    n = t // chunk

    def blk(a):
        return jnp.swapaxes(a.reshape(b, n, chunk, h, *a.shape[3:]), 2, 3)

    q, k, v = blk(q), blk(k), blk(v.astype(jnp.float32))
    g, beta = blk(g), blk(beta)
    G = jnp.cumsum(g, axis=-1)
    tri = jnp.tril(jnp.ones((chunk, chunk), bool))
    strict = jnp.tril(jnp.ones((chunk, chunk), bool), -1)
    dmask = jnp.exp(jnp.where(tri, G[..., :, None] - G[..., None, :], -jnp.inf))
    kb = k * beta[..., None]
    lmat = jnp.where(strict, jnp.einsum("bnhid,bnhjd->bnhij", kb, k) * dmask, 0.0)
    a_mat = lmat + jnp.eye(chunk, dtype=jnp.float32)
    rhs = jnp.concatenate([v * beta[..., None], kb * jnp.exp(G)[..., None]], axis=-1)
    sol = lax.linalg.triangular_solve(a_mat, rhs, left_side=True, lower=True, unit_diagonal=True)
    u, w = sol[..., :dv], sol[..., dv:]
    qk = jnp.where(tri, jnp.einsum("bnhid,bnhjd->bnhij", q, k) * dmask, 0.0)
    q_dec = q * jnp.exp(G)[..., None]
    k_dec = k * jnp.exp(G[..., -1:] - G)[..., None]
    g_last = jnp.exp(G[..., -1])

    def step(S, xs):
        u_c, w_c, qk_c, qd_c, kd_c, gl_c = xs
        v_new = u_c - jnp.einsum("bhcd,bhde->bhce", w_c, S)
        o = jnp.einsum("bhcd,bhde->bhce", qd_c, S) + jnp.einsum("bhij,bhje->bhie", qk_c, v_new)
        S = S * gl_c[..., None, None] + jnp.einsum("bhcd,bhce->bhde", kd_c, v_new)
        return S, o

    xs = tuple(jnp.moveaxis(a, 1, 0) for a in (u, w, qk, q_dec, k_dec, g_last))
    s_fin, o = lax.scan(step, s0.astype(jnp.float32), xs)
    o = jnp.transpose(o, (1, 0, 3, 2, 4)).reshape(b, t, h, dv)
    return o, s_fin


def encoder_layer(x, cache_k, cache_v, cache_ik, s_delta, conv_prev, ffn_prev,
                  w_in, conv_qkv_w, conv_qkv_b, a_log, dt_bias, delta_norm_g, rel_bias, w_o,
                  ln1_g, ln1_b, w_ffn_up, ffn_conv_w, ffn_conv_b, w_ffn_down, ln2_g, ln2_b):
    b, t, _ = x.shape
    past = cache_k.shape[1]
    h = x @ w_in
    q_a, k_a, v_a, iq, ik, iw, qkv_b, z_b, beta_in, a_in = split_cols(h, COL_SIZES)

    q_a = q_a.reshape(b, t, N_HEADS_A, HEAD_DIM)
    k_new = k_a.reshape(b, t, N_KV_A, HEAD_DIM)
    v_new = v_a.reshape(b, t, N_KV_A, HEAD_DIM)
    iq = iq.reshape(b, t, IDX_HEADS, IDX_DIM)
    k_all = jnp.concatenate([cache_k.astype(x.dtype), k_new], axis=1)
    v_all = jnp.concatenate([cache_v.astype(x.dtype), v_new], axis=1)
    ik_all = jnp.concatenate([cache_ik.astype(x.dtype), ik], axis=1)
    top_k = min(TOPK_MAX, (past + t) // 4)
    q_pos = past + jnp.arange(t, dtype=jnp.int32)
    if t > QBLK and t % QBLK == 0:
        nb = t // QBLK

        def to_blocks(a):
            return jnp.moveaxis(a.reshape(b, nb, QBLK, *a.shape[2:]), 1, 0)

        blocks = (to_blocks(q_a), to_blocks(iq), to_blocks(iw), q_pos.reshape(nb, QBLK))
        out_a = lax.map(lambda args: dsa_block(args[0], args[1], args[2], args[3],
                                               k_all, v_all, ik_all, top_k, rel_bias), blocks)
        out_a = jnp.moveaxis(out_a, 0, 1).reshape(b, t, W_A)
    else:
        out_a = dsa_block(q_a, iq, iw, q_pos, k_all, v_all, ik_all, top_k, rel_bias)

    conv_out, conv_state = causal_dwconv(qkv_b, conv_prev, conv_qkv_w, conv_qkv_b)
    conv_out = jax.nn.silu(conv_out)
    q_b, k_b, v_b = split_cols(conv_out, (N_HEADS_B * DK_B, N_HEADS_B * DK_B, N_HEADS_B * DV_B))
    q_b = l2norm(q_b.reshape(b, t, N_HEADS_B, DK_B)) * (DK_B ** -0.5)
    k_b = l2norm(k_b.reshape(b, t, N_HEADS_B, DK_B))
    v_b = v_b.reshape(b, t, N_HEADS_B, DV_B)
    beta = jax.nn.sigmoid(beta_in.astype(jnp.float32))
    g = -jnp.exp(a_log.astype(jnp.float32)) * jax.nn.softplus(a_in.astype(jnp.float32) + dt_bias)
    chunk = CHUNK if t % CHUNK == 0 else t
    o_b, s_new = gated_delta_chunked(q_b, k_b, v_b, g, beta, s_delta, chunk)
    o_b = o_b * lax.rsqrt(jnp.mean(o_b * o_b, axis=-1, keepdims=True) + EPS) * delta_norm_g
    o_b = o_b * jax.nn.silu(z_b.astype(jnp.float32).reshape(b, t, N_HEADS_B, DV_B))
    o_b = o_b.reshape(b, t, W_B).astype(x.dtype)

    mix = jnp.concatenate([out_a, o_b], axis=-1) @ w_o
    x1 = layer_norm(ALPHA * x + mix, ln1_g, ln1_b)

    up = x1 @ w_ffn_up
    up_c, ffn_state = causal_dwconv(up, ffn_prev, ffn_conv_w, ffn_conv_b)
    gate, val = jnp.split(up_c, 2, axis=-1)
    ffn = (jax.nn.silu(gate) * val) @ w_ffn_down
    y = layer_norm(ALPHA * x1 + ffn, ln2_g, ln2_b)
    return y, (k_new, v_new, ik, s_new.astype(s_delta.dtype), conv_state, ffn_state)


def setup_inputs(seed: int = 0) -> dict:
    key = jax.random.key(seed)
    ks = jax.random.split(key, 26)
    f32 = jnp.float32

    def nrm(k, shape, scale):
        return jax.random.normal(k, shape, f32) * scale

    dt = jnp.exp(jax.random.uniform(ks[13], (DEPTH, N_HEADS_B), f32,
                                    math.log(1e-3), math.log(1e-1)))
    return {
        "x_prompt": nrm(ks[0], (BATCH, SEQ, D_MODEL), 1.0),
        "x_sample": nrm(ks[1], (DEC_BATCH, DEC_SEQ, D_MODEL), 1.0),
        "cache_attn_k": nrm(ks[2], (DEPTH, DEC_BATCH, PAST_LEN, N_KV_A, HEAD_DIM), 1.0),
        "cache_attn_v": nrm(ks[3], (DEPTH, DEC_BATCH, PAST_LEN, N_KV_A, HEAD_DIM), 1.0),
        "cache_idx_k": nrm(ks[4], (DEPTH, DEC_BATCH, PAST_LEN, IDX_DIM), 1.0),
        "state_delta": nrm(ks[5], (DEPTH, DEC_BATCH, N_HEADS_B, DK_B, DV_B), 0.1),
        "state_conv_qkv": nrm(ks[6], (DEPTH, DEC_BATCH, CONV_B - 1, C_QKV), 1.0),
        "state_ffn_conv": nrm(ks[7], (DEPTH, DEC_BATCH, FFN_CONV - 1, 2 * D_FF), 1.0),
        "ln_in_g": 1.0 + nrm(ks[8], (D_MODEL,), 0.01),
        "ln_in_b": nrm(ks[9], (D_MODEL,), 0.01),
        "w_in": nrm(ks[10], (DEPTH, D_MODEL, N_IN), D_MODEL ** -0.5),
        "conv_qkv_w": nrm(ks[11], (DEPTH, CONV_B, C_QKV), CONV_B ** -0.5),
        "conv_qkv_b": nrm(ks[12], (DEPTH, C_QKV), 0.01),
        "a_log": jnp.log(jax.random.uniform(ks[14], (DEPTH, N_HEADS_B), f32, 1.0, 16.0)),
        "dt_bias": dt + jnp.log(-jnp.expm1(-dt)),
        "delta_norm_g": 1.0 + nrm(ks[15], (DEPTH, DV_B), 0.01),
        "rel_bias": nrm(ks[16], (N_BUCKETS, N_HEADS_A), 0.1),
        "w_o": nrm(ks[17], (DEPTH, W_A + W_B, D_MODEL), (W_A + W_B) ** -0.5 * OUT_INIT),
        "ln1_g": 1.0 + nrm(ks[18], (DEPTH, D_MODEL), 0.01),
        "ln1_b": nrm(ks[19], (DEPTH, D_MODEL), 0.01),
        "w_ffn_up": nrm(ks[20], (DEPTH, D_MODEL, 2 * D_FF), D_MODEL ** -0.5),
        "ffn_conv_w": nrm(ks[21], (DEPTH, FFN_CONV, 2 * D_FF), FFN_CONV ** -0.5),
        "ffn_conv_b": nrm(ks[22], (DEPTH, 2 * D_FF), 0.01),
        "w_ffn_down": nrm(ks[23], (DEPTH, D_FF, D_MODEL), D_FF ** -0.5 * OUT_INIT),
        "ln2_g": 1.0 + nrm(ks[24], (DEPTH, D_MODEL), 0.01),
        "ln2_b": nrm(ks[25], (DEPTH, D_MODEL), 0.01),
    }


def reference(x_prompt, x_sample, cache_attn_k, cache_attn_v, cache_idx_k, state_delta,
              state_conv_qkv, state_ffn_conv, ln_in_g, ln_in_b, w_in, conv_qkv_w, conv_qkv_b,
              a_log, dt_bias, delta_norm_g, rel_bias, w_o, ln1_g, ln1_b, w_ffn_up, ffn_conv_w,
              ffn_conv_b, w_ffn_down, ln2_g, ln2_b):
    xp = layer_norm(x_prompt, ln_in_g, ln_in_b)
    xs = layer_norm(x_sample, ln_in_g, ln_in_b)
    b = x_prompt.shape[0]
    dt_ = x_prompt.dtype
    p_states = []
    s_states = []
    for l in range(DEPTH):
        weights = (w_in[l], conv_qkv_w[l], conv_qkv_b[l], a_log[l], dt_bias[l], delta_norm_g[l],
                   rel_bias, w_o[l], ln1_g[l], ln1_b[l], w_ffn_up[l], ffn_conv_w[l],
                   ffn_conv_b[l], w_ffn_down[l], ln2_g[l], ln2_b[l])
        xp, st_p = encoder_layer(
            xp,
            jnp.zeros((b, 0, N_KV_A, HEAD_DIM), dt_),
            jnp.zeros((b, 0, N_KV_A, HEAD_DIM), dt_),
            jnp.zeros((b, 0, IDX_DIM), dt_),
            jnp.zeros((b, N_HEADS_B, DK_B, DV_B), jnp.float32),
            jnp.zeros((b, CONV_B - 1, C_QKV), dt_),
            jnp.zeros((b, FFN_CONV - 1, 2 * D_FF), dt_),
            *weights)
        xs, st_s = encoder_layer(xs, cache_attn_k[l], cache_attn_v[l], cache_idx_k[l],
                                 state_delta[l], state_conv_qkv[l], state_ffn_conv[l], *weights)
        p_states.append(st_p)
        s_states.append(st_s)
    p_k, p_v, p_ik, p_delta, p_conv, p_ffn = [jnp.stack(z) for z in zip(*p_states)]
    s_k, s_v, s_ik, s_delta, s_conv, s_ffn = [jnp.stack(z) for z in zip(*s_states)]
    return (xp, xs, p_k, p_v, p_ik, p_delta, p_conv, p_ffn,
            s_k, s_v, s_ik, s_delta, s_conv, s_ffn)
```

```python
import math
from contextlib import ExitStack
import numpy as np
import ml_dtypes
import concourse.bass as bass
import concourse.mybir as mybir
from concourse.bass_utils import run_bass_kernel_spmd

F32 = mybir.dt.float32
BF16 = mybir.dt.bfloat16
AF = mybir.ActivationFunctionType
ALU = mybir.AluOpType

D = 4096
KC = 32
NIN = 12400
HD = 128
PAST = 1024
DEC = 64
EPS = 1e-5
ALPHA = 2.0 ** 0.25
IDX_SCALE = (16 ** -0.5) * (64 ** -0.5)
O_QA, O_KA, O_VA, O_IQ, O_IK, O_IW, O_QKV, O_Z, O_BETA, O_A = 0, 2048, 2560, 3072, 4096, 4160, 4176, 10320, 12368, 12384
NEG = -1.0e30
WIN_PIECES = [(0, 0, 4096), (4096, 4096, 64), (4160, 4096, 64), (4224, 4176, 8192), (12416, 4096, 80), (12496, 12368, 32)]
WIN_PACKED = 12544
P_QA, P_KA, P_VA, P_IQ, P_IK2, P_QKV, P_Z, P_SM1, P_SM2 = 0, 2048, 2560, 3072, 4096, 4224, 10368, 12416, 12496
DBG = False
STOP_C = 99


MUTE = [False]


def stop_at(n):
    if STOP_C <= n:
        MUTE[0] = True


class Reg:
    __slots__ = ("w", "r", "n")

    def __init__(s, n=""):
        s.w = {}
        s.r = {}
        s.n = n


class Tile:
    def __init__(s, t, n):
        s.t = t
        s.reg = Reg(n)

    def __getitem__(s, i):
        return s.t[i]


class Eng:
    def __init__(s, name, h):
        s.name = name
        s.h = h
        s.semidx = None
        s.cnt = 0
        s.known = {}
        s.pending = False
        s.dsems = []
        s.dnext = 0


class K:
    SEM_LIMIT = 30000

    def __init__(s, nc, es):
        s.nc = nc
        s.es = es
        s.sems = []
        s.semmax = []
        s.E = {}
        for n, h in (("pe", nc.tensor), ("act", nc.scalar), ("dve", nc.vector), ("pool", nc.gpsimd), ("sp", nc.sync)):
            e = Eng(n, h)
            s.E[n] = e
            if n != "sp":
                e.semidx = s.newsem()
        for n, cnt in (("sp", 12), ("act", 4), ("pool", 8)):
            s.E[n].dsems = [s.newsem() for _ in range(cnt)]
        s.uid = 0

    def newsem(s):
        h = s.es.enter_context(s.nc.semaphore("sem%d" % len(s.sems)))
        s.sems.append(h)
        s.semmax.append(0)
        return len(s.sems) - 1

    def sb(s, st, name, shape, dt):
        s.uid += 1
        nm = "%s_%d" % (name, s.uid)
        return Tile(st.enter_context(s.nc.sbuf_tensor(nm, list(shape), dt)), nm)

    def ps(s, st, name, shape, dt=F32):
        s.uid += 1
        nm = "%s_%d" % (name, s.uid)
        return Tile(st.enter_context(s.nc.psum_tensor(nm, list(shape), dt)), nm)

    def dram(s, name, shape, dt, kind="Internal"):
        if DBG and kind == "Internal":
            kind = "ExternalOutput"
        t = s.nc.dram_tensor(name, list(shape), dt, kind=kind)
        tl = Tile(t.ap(), name)
        return tl

    def _deps(s, R, W, Wp):
        deps = {}
        for r in R:
            for k, v in r.w.items():
                if deps.get(k, 0) < v:
                    deps[k] = v
        for w in list(W) + list(Wp):
            for k, v in w.w.items():
                if deps.get(k, 0) < v:
                    deps[k] = v
            for k, v in w.r.items():
                if deps.get(k, 0) < v:
                    deps[k] = v
        return deps

    def _waits(s, eng, deps, ename):
        for k, v in deps.items():
            if k == eng.semidx and ename == "pe":
                continue
            if eng.known.get(k, 0) >= v:
                continue
            eng.h.wait_ge(s.sems[k], v)
            eng.known[k] = v

    def _mark(s, t, R, W, Wp):
        k, v = t
        for r in R:
            if r.r.get(k, 0) < v:
                r.r[k] = v
        for w in W:
            w.w = {k: v}
            w.r = {}
        for w in Wp:
            if w.w.get(k, 0) < v:
                w.w[k] = v

    def op(s, e, fn, R=(), W=(), Wp=(), sig=True, Wa=()):
        if MUTE[0]:
            return None
        R = [x.reg if isinstance(x, Tile) else x for x in R]
        Wa = [x.reg if isinstance(x, Tile) else x for x in Wa]
        W = [x.reg if isinstance(x, Tile) else x for x in W]
        Wp = [x.reg if isinstance(x, Tile) else x for x in Wp]
        eng = s.E[e]
        if eng.cnt >= s.SEM_LIMIT and not eng.pending:
            eng.semidx = s.newsem()
            eng.cnt = 0
        s._waits(eng, s._deps(R, W, list(Wp) + list(Wa)), e)
        ins = fn()
        if sig:
            eng.cnt += 1
            ins.then_inc(s.sems[eng.semidx], 1)
            s.semmax[eng.semidx] = eng.cnt
            eng.pending = False
            t = (eng.semidx, eng.cnt)
        else:
            eng.pending = True
            t = (eng.semidx, eng.cnt + 1)
        s._mark(t, R, W, Wp)
        return ins

    def dma(s, q, out, in_, R=(), W=(), Wp=(), **kw):
        if MUTE[0]:
            return None
        R = [x.reg if isinstance(x, Tile) else x for x in R]
        W = [x.reg if isinstance(x, Tile) else x for x in W]
        Wp = [x.reg if isinstance(x, Tile) else x for x in Wp]
        eng = s.E[q]
        si = eng.dsems[eng.dnext % len(eng.dsems)]
        eng.dnext += 1
        deps = s._deps(R, W, Wp)
        cur = s.semmax[si]
        if cur > 0 and deps.get(si, 0) < cur:
            deps[si] = cur
        s._waits(eng, deps, q)
        ins = eng.h.dma_start(out=out, in_=in_, **kw)
        ins.then_inc(s.sems[si], 16)
        s.semmax[si] = cur + 16
        s._mark((si, cur + 16), R, W, Wp)
        return ins

    def barrier(s, engines=("pe", "act", "dve", "pool", "sp")):
        for n in engines:
            eng = s.E[n]
            assert not eng.pending
            for k, v in enumerate(s.semmax):
                if v > 0 and eng.known.get(k, 0) < v and k != eng.semidx:
                    eng.h.wait_ge(s.sems[k], v)
                    eng.known[k] = v


class Cfg:
    def __init__(s, SEQ, NS, DFF):
        s.SEQ, s.NS, s.DFF = SEQ, NS, DFF
        s.FC = DFF // 128
        s.NT = SEQ + NS * DEC
        assert s.NT % 128 == 0 and SEQ % 128 == 0 and DFF % 128 == 0
        s.NSEQ = 1 + NS
        s.seqs = [dict(t0=0, T=SEQ, past=0, si=-1)] + [dict(t0=SEQ + DEC * i, T=DEC, past=PAST, si=i) for i in range(NS)]
        s.TOPK_P = min(256, SEQ // 4)
        s.TOPK_S = min(256, (PAST + DEC) // 4)

    def groups(s, gmax):
        n = -(-s.NT // gmax)
        per = -(-(s.NT // 128) // n) * 128
        out = []
        t = 0
        while t < s.NT:
            g = min(per, s.NT - t)
            out.append((t, g))
            t += g
        return out


def blocks(n, b=512):
    return [(i, min(b, n - i)) for i in range(0, n, b)]


def t5_bucket_np(rel):
    rel = np.asarray(rel, np.int64)
    half, max_exact = 16, 8
    side = np.where(rel > 0, half, 0)
    n = np.abs(rel)
    nf = np.maximum(n, 1).astype(np.float32)
    large = max_exact + (np.log(nf / np.float32(max_exact)) / np.float32(math.log(128 / max_exact))
                         * np.float32(half - max_exact)).astype(np.int32)
    large = np.minimum(large, half - 1)
    return side + np.where(n < max_exact, n, large)


def host_consts():
    c = {}
    c["c_ident"] = np.eye(128, dtype=np.float32)
    c["c_anti"] = np.eye(128, dtype=np.float32)[::-1].copy()
    i = np.arange(128)
    same = (i[:, None] // 64) == (i[None, :] // 64)
    c["c_cum"] = (same & (i[:, None] <= i[None, :])).astype(np.float32)
    c["c_blk"] = same.astype(np.float32)
    c["c_nmL"] = np.where(same & (i[:, None] > i[None, :]), 0.0, -1e4).astype(np.float32)
    c["c_nmT"] = np.where(same & (i[None, :] >= i[:, None]), 0.0, -1e4).astype(np.float32)
    c["c_strict"] = (same & (i[:, None] > i[None, :])).astype(np.float32)
    rel = np.arange(384) - 255
    bk = t5_bucket_np(rel)
    oh = np.zeros((32, 384), np.float32)
    oh[bk, np.arange(384)] = 1.0
    oh[15, :] -= 1.0
    c["c_oh"] = oh
    return c


CONST_SHAPES = {"c_ident": [128, 128], "c_anti": [128, 128], "c_cum": [128, 128], "c_blk": [128, 128],
                "c_nmL": [128, 128], "c_nmT": [128, 128], "c_strict": [128, 128], "c_oh": [32, 384]}


def build(cfg, phases="ABCDEF"):
    nc = bass.Bass("TRN2", target_bir_lowering=False)
    es = ExitStack()
    with es:
        k = K(nc, es)
        _program(k, cfg, phases)
    return nc


def _program(k, cfg, phases):
    nc = k.nc
    NT, NS, DFF, FC, NSEQ = cfg.NT, cfg.NS, cfg.DFF, cfg.FC, cfg.NSEQ
    I = {}

    def din(name, shape):
        I[name] = k.dram(name, shape, F32, kind="ExternalInput")
        return I[name]

    def dout(name, shape):
        I[name] = k.dram(name, shape, F32, kind="ExternalOutput")
        return I[name]

    din("x", [NT, D])
    din("ck", [NS * PAST, 512]); din("cv", [NS * PAST, 512]); din("cik", [NS * PAST, 64])
    din("sdel", [NS * 16 * 128, 128]); din("sconv", [NS * 3, 6144]); din("sffn", [NS * 2, 2 * DFF])
    din("lnin", [128, 64]); din("ln1", [128, 64]); din("ln2", [128, 64])
    din("w_in", [D, NIN]); din("w_o", [D, D]); din("w_up", [D, 2 * DFF]); din("w_down", [DFF, D])
    din("convw", [128, 48 * 5]); din("ffnw", [128, 2 * FC * 4])
    din("alog", [128, 16]); din("dtb", [128, 16]); din("dng", [128, 1]); din("relb", [32, 16])
    for n, sh in CONST_SHAPES.items():
        din(n, sh)
    dout("y", [NT, D]); dout("ko", [NT, 512]); dout("vo", [NT, 512]); dout("iko", [NT, 64])
    dout("so", [NSEQ * 16 * 128, 128]); dout("convo", [NSEQ * 3, 6144]); dout("ffno", [NSEQ * 2, 2 * DFF])
    S = {}
    S["xnT"] = k.dram("s_xnT", [D, NT], F32)
    S["qaT"] = k.dram("s_qaT", [2048, NT], BF16)
    S["kaT"] = k.dram("s_kaT", [512, NT], BF16)
    S["vbf"] = k.dram("s_vbf", [NT, 512], BF16)
    S["iqT"] = k.dram("s_iqT", [1024, NT], BF16)
    S["ikT2"] = k.dram("s_ikT2", [128, NT], BF16)
    S["iw"] = k.dram("s_iw", [NT, 16], F32)
    S["ba"] = k.dram("s_ba", [NT, 32], F32)
    S["qkvT"] = k.dram("s_qkvT", [6144, NT], F32)
    S["zT"] = k.dram("s_zT", [2048, NT], F32)
    S["aT"] = k.dram("s_aT", [D, NT], BF16)
    S["x1T"] = k.dram("s_x1T", [D, NT], F32)
    S["x1b"] = k.dram("s_x1b", [D, NT], BF16)
    S["actT"] = k.dram("s_actT", [DFF, NT], BF16)
    S["Fd"] = k.dram("s_Fd", [16, 384 + 128], F32)
    S["b_w_in"] = k.dram("s_bwin", [WIN_PACKED // 256, 1, 128, 32, 256], BF16)
    S["b_w_o"] = k.dram("s_bwo", [D // 256, 1, 128, 32, 256], BF16)
    S["b_w_up"] = k.dram("s_bwup", [2 * DFF // 256, 1, 128, 32, 256], BF16)
    S["b_w_down"] = k.dram("s_bwdn", [D // 256, len(kgroups(FC)), 128, 32, 256], BF16)

    with ExitStack() as gs:
        C = {}
        for n, sh in CONST_SHAPES.items():
            C[n] = k.sb(gs, n, sh, F32)
            k.dma("sp", C[n][:], I[n][:], W=[C[n]])
        ident_b = k.sb(gs, "identb", [128, 128], BF16)
        k.op("pool", lambda: nc.gpsimd.tensor_copy(out=ident_b[:], in_=C["c_ident"][:]), R=[C["c_ident"]], W=[ident_b])
        ones_f = k.sb(gs, "onesf", [128, 128], F32)
        k.op("pool", lambda: nc.gpsimd.memset(ones_f[:], 1.0), W=[ones_f])
        ones_b = k.sb(gs, "onesb", [128, 128], BF16)
        k.op("pool", lambda: nc.gpsimd.memset(ones_b[:], 1.0), W=[ones_b])
        G = dict(C=C, ident_b=ident_b, ones_f=ones_f, ones_b=ones_b, I=I, S=S)
        psum = [k.ps(gs, "bank%d" % i, [128, 512]) for i in range(8)]
        G["psum"] = psum
        G["pn"] = 0

        def bank():
            b = psum[G["pn"] % 8]
            G["pn"] += 1
            return b
        G["bank"] = bank
        G["alt"] = 0

        phase_W(k, cfg, G)
        k.barrier()
        if "A" in phases:
            phase_A(k, cfg, G)
            k.barrier()
        if "B" in phases:
            phase_B(k, cfg, G)
            k.barrier()
        if "C" in phases:
            phase_C(k, cfg, G)
            MUTE[0] = False
            k.barrier()
        if "D" in phases:
            phase_D(k, cfg, G)
            k.barrier()
        if "E" in phases:
            phase_E(k, cfg, G)
            k.barrier()
        if "F" in phases:
            phase_F(k, cfg, G)
        k.barrier()


def evac_engine(G):
    G["alt"] += 1
    return "act" if G["alt"] % 2 else "dve"


class WTiles:
    def __init__(s, k, st, nslot=3):
        s.k = k
        s.slots = [k.sb(st, "wt", [128, 32, 256], BF16) for _ in range(nslot)]
        s.tags = [None] * nslot
        s.n = 0

    def get(s, scr, tile, kg=0, kcn=32):
        tag = (scr.reg.n, tile, kg)
        for i, t in enumerate(s.tags):
            if t == tag:
                return s.slots[i]
        i = s.n % len(s.slots)
        s.n += 1
        s.tags[i] = tag
        wb = s.slots[i]
        s.k.dma("sp", wb[:, 0:kcn, :], scr[tile, kg, :, 0:kcn, :], R=[scr], W=[wb])
        return wb


def kgroups(kctot):
    return [(i, min(32, kctot - i)) for i in range(0, kctot, 32)]


def w_units(k, cfg, G, names, st, kstep=4, gcols=512):
    nc = k.nc
    I, S = G["I"], G["S"]
    allspecs = {"w_in": (I["w_in"], WIN_PIECES, WIN_PACKED, KC), "w_o": (I["w_o"], [(0, 0, D)], D, KC),
                "w_up": (I["w_up"], [(0, 0, 2 * cfg.DFF)], 2 * cfg.DFF, KC), "w_down": (I["w_down"], [(0, 0, D)], D, cfg.FC)}
    nt = gcols // 256
    stg = [k.sb(st, "wstg", [128, kstep, gcols], F32) for _ in range(2)]
    sbf = [k.sb(st, "wsbf", [128, nt, kstep, 256], BF16) for _ in range(2)]
    units = []
    for name in names:
        src, pieces, ncol, kctot = allspecs[name]
        dst = S["b_" + name]
        for g0 in range(0, ncol, gcols):
            gw = min(gcols, ncol - g0)
            for kgi, (kg0, kgn) in enumerate(kgroups(kctot)):
                for k8 in range(0, kgn, kstep):
                    units.append((src, pieces, dst, g0, gw, kgi, kg0, k8, min(kstep, kgn - k8)))

    def load(n):
        src, pieces, dst, g0, gw, kgi, kg0, k8, kn = units[n]
        r0 = (kg0 + k8) * 128
        st_ = stg[n % 2]
        first = True
        for (d0, s0, pn) in pieces:
            lo, hi = max(d0, g0), min(d0 + pn, g0 + gw)
            if lo < hi:
                srcap = src[r0:r0 + kn * 128, s0 + lo - d0:s0 + hi - d0].rearrange("(kc p) n -> p kc n", p=128)
                if first:
                    k.dma("sp", st_[:, 0:kn, lo - g0:hi - g0], srcap, W=[st_])
                else:
                    k.dma("sp", st_[:, 0:kn, lo - g0:hi - g0], srcap, Wp=[st_])
                first = False

    def finish(n):
        src, pieces, dst, g0, gw, kgi, kg0, k8, kn = units[n]
        st_ = stg[n % 2]
        sb_ = sbf[n % 2]
        nt4 = gw // 256
        iv = st_[:, 0:kn, 0:gw].rearrange("p k (t c) -> p t k c", c=256)
        ov = sb_[:, 0:nt4, 0:kn, :]
        e = G.get("wcast", ("act", "dve", "pool"))
        e = e[n % len(e)]
        if e == "act":
            k.op("act", lambda: nc.scalar.copy(out=ov, in_=iv), R=[st_], W=[sb_])
        elif e == "dve":
            k.op("dve", lambda: nc.vector.tensor_copy(out=ov, in_=iv), R=[st_], W=[sb_])
        else:
            k.op("pool", lambda: nc.gpsimd.tensor_copy(out=ov, in_=iv), R=[st_], W=[sb_])
        t0 = g0 // 256
        k.dma("pool", dst[t0:t0 + nt4, kgi, :, k8:k8 + kn, :].rearrange("t p k c -> p t k c"), ov, R=[sb_], Wp=[dst])

    for n in range(len(units)):
        load(n)
        if n >= 1:
            finish(n - 1)
        yield n
    finish(len(units) - 1)
    yield len(units)


def phase_W(k, cfg, G):
    with ExitStack() as st:
        G["wcast"] = ("act", "dve", "pool")
        for _ in w_units(k, cfg, G, ["w_in"], st, kstep=8, gcols=1024):
            pass


def phase_A(k, cfg, G):
    nc = k.nc
    I, S, C = G["I"], G["S"], G["C"]
    NT = cfg.NT
    bank = G["bank"]
    for (g0, gn) in cfg.groups(1152):
        with ExitStack() as st:
            xnT = k.sb(st, "xnT", [128, KC, gn], BF16)
            gb = k.sb(st, "lnin", [128, 64], F32)
            k.dma("sp", gb[:], I["lnin"][:], W=[gb])
            with ExitStack() as s1:
                xs = [k.sb(s1, "xs", [128, D], F32) for _ in range(2)]
                xf = [k.sb(s1, "xf", [128, KC, 128], F32) for _ in range(2)]
                stt = [k.sb(s1, "stt", [128, 8, 6], F32) for _ in range(2)]
                mv = [k.sb(s1, "mv", [128, 4], F32) for _ in range(2)]
                for ti in range(gn // 128):
                    t0 = g0 + ti * 128
                    x_, f_, st_, mv_ = xs[ti % 2], xf[ti % 2], stt[ti % 2], mv[ti % 2]
                    k.dma("sp", x_[:], I["x"][t0:t0 + 128, :], W=[x_])
                    for j in range(8):
                        k.op("dve", lambda j=j: nc.vector.bn_stats(out=st_[:, j, :], in_=x_[:, j * 512:(j + 1) * 512]),
                             R=[x_], Wp=[st_] if j else (), W=() if j else [st_])
                    k.op("dve", lambda: nc.vector.bn_aggr(out=mv_[:, 0:2], in_=st_[:].rearrange("p a b -> p (a b)")), R=[st_], W=[mv_])
                    k.op("act", lambda: nc.scalar.activation(out=mv_[:, 2:3], in_=mv_[:, 1:2], func=AF.Sqrt, bias=EPS, scale=1.0), R=[mv_], Wp=[mv_])
                    k.op("dve", lambda: nc.vector.reciprocal(out=mv_[:, 3:4], in_=mv_[:, 2:3]), R=[mv_], Wp=[mv_])
                    k.op("dve", lambda: nc.vector.tensor_scalar(out=x_[:], in0=x_[:], scalar1=mv_[:, 0:1], scalar2=mv_[:, 3:4],
                                                                op0=ALU.subtract, op1=ALU.mult), R=[mv_, x_], W=[x_])
                    for q in range(8):
                        b = bank()
                        for j in range(4):
                            kc = q * 4 + j
                            k.op("pe", lambda kc=kc, j=j: nc.tensor.transpose(b[:, j * 128:(j + 1) * 128], x_[:, kc * 128:(kc + 1) * 128], C["c_ident"][:]),
                                 R=[x_, C["c_ident"]], W=[b], sig=(j == 3))
                        for j in range(4):
                            kc = q * 4 + j
                            if (kc % 2) == 0:
                                k.op("act", lambda kc=kc, j=j: nc.scalar.activation(out=f_[:, kc, :], in_=b[:, j * 128:(j + 1) * 128], func=AF.Identity,
                                                                                   scale=gb[:, kc:kc + 1], bias=gb[:, 32 + kc:33 + kc]),
                                     R=[b, gb], Wp=[f_])
                            else:
                                k.op("dve", lambda kc=kc, j=j: nc.vector.tensor_scalar(out=f_[:, kc, :], in0=b[:, j * 128:(j + 1) * 128],
                                                                                      scalar1=gb[:, kc:kc + 1], scalar2=gb[:, 32 + kc:33 + kc],
                                                                                      op0=ALU.mult, op1=ALU.add),
                                     R=[b, gb], Wp=[f_])
                    k.op("pool", lambda: nc.gpsimd.tensor_copy(out=xnT[:, :, ti * 128:(ti + 1) * 128], in_=f_[:]), R=[f_], Wp=[xnT])
                    k.dma("pool", S["xnT"][:].rearrange("(kc p) t -> p kc t", p=128)[:, :, t0:t0 + 128], f_[:], R=[f_], Wp=[S["xnT"]])
            k.barrier()
            with ExitStack() as s2:
                ws = WTiles(k, s2, nslot=3)
                osf = [k.sb(s2, "osf", [128, gn], F32) for _ in range(2)]
                osb = [k.sb(s2, "osb", [128, gn], BF16) for _ in range(2)]
                otk = [k.sb(s2, "otk", [128, gn // 128, 128], F32) for _ in range(2)]
                otb = [k.sb(s2, "otb", [128, gn // 128, 128], BF16) for _ in range(2)]
                cnt = {"f": 0, "b": 0, "t": 0}

                def fm_job(pc, m, dst, drow, mode):
                    wb = ws.get(S["b_w_in"], pc // 256)
                    sub = pc % 256
                    if mode == "f32" or mode == "silu":
                        o = osf[cnt["f"] % 2]; cnt["f"] += 1
                    else:
                        o = osb[cnt["b"] % 2]; cnt["b"] += 1
                    for (b0, bn) in blocks(gn):
                        b = bank()
                        for kc in range(KC):
                            k.op("pe", lambda kc=kc: nc.tensor.matmul(b[0:m, 0:bn], lhsT=wb[:, kc, sub:sub + m], rhs=xnT[:, kc, b0:b0 + bn],
                                                                       start=(kc == 0), stop=(kc == KC - 1)),
                                 R=[wb, xnT], W=[b], sig=(kc == KC - 1))
                        e = evac_engine(G)
                        if mode == "silu":
                            k.op("act", lambda: nc.scalar.activation(out=o[0:m, b0:b0 + bn], in_=b[0:m, 0:bn], func=AF.Silu), R=[b], Wp=[o])
                        elif mode == "qs":
                            k.op("act", lambda: nc.scalar.mul(o[0:m, b0:b0 + bn], b[0:m, 0:bn], HD ** -0.5), R=[b], Wp=[o])
                        elif e == "act":
                            k.op("act", lambda: nc.scalar.copy(out=o[0:m, b0:b0 + bn], in_=b[0:m, 0:bn]), R=[b], Wp=[o])
                        else:
                            k.op("dve", lambda: nc.vector.tensor_copy(out=o[0:m, b0:b0 + bn], in_=b[0:m, 0:bn]), R=[b], Wp=[o])
                    k.dma("pool", dst[drow:drow + m, g0:g0 + gn], o[0:m, :], R=[o], Wp=[dst])

                def tm_job(pc, ncols, outs):
                    wb = ws.get(S["b_w_in"], pc // 256)
                    sub = pc % 256
                    o = otk[cnt["t"] % 2]
                    ob = otb[cnt["t"] % 2]
                    cnt["t"] += 1
                    for ti in range(gn // 128):
                        b = bank()
                        for kc in range(KC):
                            k.op("pe", lambda kc=kc: nc.tensor.matmul(b[:, 0:ncols], lhsT=xnT[:, kc, ti * 128:(ti + 1) * 128], rhs=wb[:, kc, sub:sub + ncols],
                                                                       start=(kc == 0), stop=(kc == KC - 1)),
                                 R=[wb, xnT], W=[b], sig=(kc == KC - 1))
                        k.op("dve", lambda: nc.vector.tensor_copy(out=o[:, ti, 0:ncols], in_=b[:, 0:ncols]), R=[b], Wp=[o])
                    for (dst, dc0, sc0, n, dt) in outs:
                        dview = dst[g0:g0 + gn, dc0:dc0 + n].rearrange("(ti p) n -> p ti n", p=128)
                        if dt == "bf16":
                            k.op("pool", lambda: nc.gpsimd.tensor_copy(out=ob[:, :, sc0:sc0 + n], in_=o[:, :, sc0:sc0 + n]), R=[o], W=[ob])
                            k.dma("pool", dview, ob[:, :, sc0:sc0 + n], R=[ob], Wp=[dst])
                        else:
                            k.dma("pool", dview, o[:, :, sc0:sc0 + n], R=[o], Wp=[dst])

                for c in range(16):
                    fm_job(P_QA + c * 128, 128, S["qaT"], c * 128, "qs")
                for c in range(4):
                    fm_job(P_KA + c * 128, 128, S["kaT"], c * 128, "bf16")
                for c in range(4):
                    tm_job(P_KA + c * 128, 128, [(I["ko"], c * 128, 0, 128, "f32")])
                for c in range(4):
                    tm_job(P_VA + c * 128, 128, [(I["vo"], c * 128, 0, 128, "f32"), (S["vbf"], c * 128, 0, 128, "bf16")])
                for c in range(8):
                    fm_job(P_IQ + c * 128, 128, S["iqT"], c * 128, "bf16")
                fm_job(P_IK2, 128, S["ikT2"], 0, "bf16")
                for c in range(48):
                    fm_job(P_QKV + c * 128, 128, S["qkvT"], c * 128, "f32")
                for c in range(16):
                    fm_job(P_Z + c * 128, 128, S["zT"], c * 128, "silu")
                tm_job(P_SM1, 80, [(I["iko"], 0, 0, 64, "f32"), (S["iw"], 0, 64, 16, "f32")])
                tm_job(P_SM2, 32, [(S["ba"], 0, 0, 32, "f32")])
            k.barrier()


def phase_B(k, cfg, G):
    nc = k.nc
    I, S, C = G["I"], G["S"], G["C"]
    bank = G["bank"]
    ident, ident_b, ones_b = C["c_ident"], G["ident_b"], G["ones_b"]
    NS = cfg.NS
    SKMAX = max(cfg.SEQ, PAST + 128)
    KTMAX = SKMAX // 128
    with ExitStack() as pst:
        sb = lambda n, sh, dt=F32: k.sb(pst, n, sh, dt)
        biasT = sb("biasT", [128, 2, 16, 128], BF16)
        with ExitStack() as s0:
            relb = k.sb(s0, "relb", [32, 16], F32)
            Fs = k.sb(s0, "Fs", [16, 512], F32)
            XT = k.sb(s0, "XT", [128, 2, 16, 128], F32)
            k.dma("sp", relb[:], I["relb"][:], W=[relb])
            k.op("pool", lambda: nc.gpsimd.memset(Fs[:], 0.0), W=[Fs])
            b = bank()
            k.op("pe", lambda: nc.tensor.matmul(b[0:16, 0:384], lhsT=relb[:, :], rhs=C["c_oh"][:, :], start=True, stop=True), R=[relb, C["c_oh"]], W=[b])
            k.op("dve", lambda: nc.vector.tensor_copy(out=Fs[:, 0:384], in_=b[0:16, 0:384]), R=[b], Wp=[Fs])
            k.dma("sp", S["Fd"][:], Fs[:], R=[Fs], W=[S["Fd"]])
            fd_t = S["Fd"][:].tensor
            for w, off in ((0, 128), (1, 0)):
                src = bass.AP(tensor=fd_t, offset=off, ap=[[1, 128], [512, 16], [1, 128]])
                k.dma("sp", XT[:, w, :, :], src, R=[S["Fd"]], Wp=[XT])
            for w in range(2):
                for h4 in range(0, 16, 4):
                    b = bank()
                    for j in range(4):
                        k.op("pe", lambda: nc.tensor.matmul(b[:, j * 128:(j + 1) * 128], lhsT=XT[:, w, h4 + j, :], rhs=C["c_anti"][:], start=True, stop=True),
                             R=[XT, C["c_anti"]], W=[b], sig=(j == 3))
                    k.op("dve", lambda: nc.vector.tensor_copy(out=biasT[:, w, h4:h4 + 4, :], in_=b[:, :].rearrange("p (a b) -> p a b", b=128)), R=[b], Wp=[biasT])
        k.barrier()
        kT = sb("kT", [128, 4, SKMAX], BF16)
        vv = sb("vv", [128, KTMAX, 4, 128], BF16)
        ik2 = sb("ik2", [128, SKMAX], BF16)
        cst = sb("cst", [128, 8, 512], F32)
        cikst = sb("cikst", [128, 8, 128], F32)
        qT = [sb("qT", [128, 16, 128], BF16) for _ in range(2)]
        iq = [sb("iq", [128, 8, 128], BF16) for _ in range(2)]
        iw = [sb("iw", [128, 16], F32) for _ in range(2)]
        index = sb("index", [128, SKMAX]); work = sb("work", [128, SKMAX]); mask01 = sb("mask01", [128, SKMAX])
        rr_ = [sb("relu", [128, 512]) for _ in range(2)]
        m8 = sb("m8", [128, 8]); thr = sb("thr", [128, 1])
        maskT = sb("maskT", [128, KTMAX, 128], BF16)
        pt = [sb("pt", [128, 512], BF16) for _ in range(3)]
        rcp = sb("rcp", [128, 512])
        oa = [sb("oa", [128, 16, 128], BF16) for _ in range(2)]
        G["wcast"] = ("pool", "act")
        wgen = w_units(k, cfg, G, ["w_o", "w_up", "w_down"], pst, kstep=4, gcols=512)
        n_units = 0
        for nm_, kct_ in (("w_o", KC), ("w_up", KC), ("w_down", cfg.FC)):
            ncol_ = {"w_o": D, "w_up": 2 * cfg.DFF, "w_down": D}[nm_]
            n_units += (-(-ncol_ // 512)) * sum(-(-kn_ // 4) for (_, kn_) in kgroups(kct_))
        n_iter = sum(4 * (-(-sq_["T"] // 128)) for sq_ in cfg.seqs)
        per_iter = -(-n_units // n_iter)

        def wstep(cnt):
            for _ in range(cnt):
                try:
                    next(wgen)
                except StopIteration:
                    return
        qn = 0
        for qi, sq in enumerate(cfg.seqs):
            t0, T, si, past = sq["t0"], sq["T"], sq["si"], sq["past"]
            SK = past + T
            if si < 0:
                k.dma("sp", kT[:, :, 0:T], S["kaT"][:, t0:t0 + T].rearrange("(g p) t -> p g t", p=128), R=[S["kaT"]], W=[kT])
                k.dma("sp", vv[:, 0:T // 128, :, :], S["vbf"][t0:t0 + T, :].rearrange("(kt p) (g d) -> p kt g d", p=128, d=128), R=[S["vbf"]], W=[vv])
                k.dma("sp", ik2[:, 0:T], S["ikT2"][:, t0:t0 + T], R=[S["ikT2"]], W=[ik2])
            else:
                k.dma("sp", cst[:], I["ck"][si * PAST:(si + 1) * PAST, :].rearrange("(kt p) n -> p kt n", p=128), W=[cst])
                for kt in range(8):
                    b = bank()
                    for g in range(4):
                        k.op("pe", lambda: nc.tensor.transpose(b[:, g * 128:(g + 1) * 128], cst[:, kt, g * 128:(g + 1) * 128], ident[:]), R=[cst, ident], W=[b], sig=(g == 3))
                    k.op("act", lambda: nc.scalar.copy(out=kT[:, :, kt * 128:(kt + 1) * 128], in_=b[:, :].rearrange("p (a b) -> p a b", b=128)), R=[b], Wp=[kT])
                k.dma("sp", cst[:], I["cv"][si * PAST:(si + 1) * PAST, :].rearrange("(kt p) n -> p kt n", p=128), W=[cst])
                k.op("pool", lambda: nc.gpsimd.tensor_copy(out=vv[:, 0:8, :, :].rearrange("p a g d -> p a (g d)"), in_=cst[:]), R=[cst], Wp=[vv])
                ciksrc = I["cik"][si * PAST:(si + 1) * PAST, :].rearrange("(kt p) n -> p kt n", p=128)
                k.dma("sp", cikst[:, :, 0:64], ciksrc, W=[cikst])
                k.dma("sp", cikst[:, :, 64:128], ciksrc, Wp=[cikst])
                for k4 in range(0, 8, 4):
                    b = bank()
                    for j in range(4):
                        k.op("pe", lambda: nc.tensor.transpose(b[:, j * 128:(j + 1) * 128], cikst[:, k4 + j, :], ident[:]), R=[cikst, ident], W=[b], sig=(j == 3))
                    k.op("dve", lambda: nc.vector.tensor_copy(out=ik2[:, k4 * 128:(k4 + 4) * 128], in_=b[:, :]), R=[b], Wp=[ik2])
                k.dma("sp", kT[:, :, PAST:PAST + T], S["kaT"][:, t0:t0 + T].rearrange("(g p) t -> p g t", p=128), R=[S["kaT"]], Wp=[kT])
                k.dma("sp", vv[0:T, 8, :, :], S["vbf"][t0:t0 + T, :].rearrange("p (g d) -> p g d", d=128), R=[S["vbf"]], Wp=[vv])
                k.dma("sp", ik2[:, PAST:PAST + T], S["ikT2"][:, t0:t0 + T], R=[S["ikT2"]], Wp=[ik2])
            topk = cfg.TOPK_P if si < 0 else cfg.TOPK_S
            for qt in range(-(-T // 128)):
                nq = min(128, T - qt * 128)
                ta = t0 + qt * 128
                SKq = past + qt * 128 + nq if si < 0 else SK
                KTq = -(-SKq // 128)
                q_, iq_, iw_, oa_ = qT[qn % 2], iq[qn % 2], iw[qn % 2], oa[qn % 2]
                qn += 1
                k.dma("sp", q_[:, :, 0:nq], S["qaT"][:, ta:ta + nq].rearrange("(h p) t -> p h t", p=128), R=[S["qaT"]], W=[q_])
                k.dma("sp", iq_[:, :, 0:nq], S["iqT"][:, ta:ta + nq].rearrange("(h p) t -> p h t", p=128), R=[S["iqT"]], W=[iq_])
                k.dma("sp", iw_[0:nq, :], S["iw"][ta:ta + nq, :], R=[S["iw"]], W=[iw_])
                k.op("pool", lambda: nc.gpsimd.tensor_scalar(out=iw_[0:nq, :], in0=iw_[0:nq, :], scalar1=IDX_SCALE, scalar2=None, op0=ALU.mult), R=[iw_], W=[iw_])
                rn = 0
                for (c0, cn) in blocks(SKq):
                    for hp in range(8):
                        for half in range(2):
                            h = hp * 2 + half
                            pr = slice(half * 64, half * 64 + 64)
                            b = bank()
                            k.op("pe", lambda: nc.tensor.matmul(b[0:nq, 0:cn], lhsT=iq_[pr, hp, 0:nq], rhs=ik2[pr, c0:c0 + cn], start=True, stop=True), R=[iq_, ik2], W=[b])
                            r_ = rr_[rn % 2]
                            rn += 1
                            k.op("act", lambda: nc.scalar.activation(out=r_[0:nq, 0:cn], in_=b[0:nq, 0:cn], func=AF.Relu), R=[b], W=[r_])
                            if h == 0:
                                k.op("dve", lambda: nc.vector.tensor_scalar(out=index[0:nq, c0:c0 + cn], in0=r_[0:nq, 0:cn], scalar1=iw_[0:nq, 0:1], scalar2=None, op0=ALU.mult),
                                     R=[r_, iw_], Wp=[index])
                            else:
                                k.op("dve", lambda: nc.vector.scalar_tensor_tensor(out=index[0:nq, c0:c0 + cn], in0=r_[0:nq, 0:cn], scalar=iw_[0:nq, h:h + 1],
                                                                                   in1=index[0:nq, c0:c0 + cn], op0=ALU.mult, op1=ALU.add), R=[r_, iw_, index], Wp=[index])
                if si < 0:
                    k.op("dve", lambda: nc.vector.memset(index[0:64, SKq - 64:SKq], NEG), R=[index], Wp=[index])
                if SKq > topk:
                    nr = topk // 8
                    for rd in range(nr):
                        srcw = index if rd == 0 else work
                        k.op("dve", lambda: nc.vector.max(out=m8[0:nq, :], in_=srcw[0:nq, 0:SKq]), R=[srcw], W=[m8])
                        if rd < nr - 1:
                            k.op("dve", lambda: nc.vector.match_replace(out=work[0:nq, 0:SKq], in_to_replace=m8[0:nq, :], in_values=srcw[0:nq, 0:SKq], imm_value=NEG),
                                 R=[srcw, m8], W=[work])
                    k.op("dve", lambda: nc.vector.tensor_scalar(out=thr[0:nq, :], in0=m8[0:nq, 7:8], scalar1=-1.0e29, scalar2=None, op0=ALU.max), R=[m8], W=[thr])
                    k.op("dve", lambda: nc.vector.tensor_scalar(out=mask01[0:nq, 0:SKq], in0=index[0:nq, 0:SKq], scalar1=thr[0:nq, 0:1], scalar2=None, op0=ALU.is_ge),
                         R=[index, thr], W=[mask01])
                else:
                    k.op("dve", lambda: nc.vector.tensor_scalar(out=mask01[0:nq, 0:SKq], in0=index[0:nq, 0:SKq], scalar1=-1.0e29, scalar2=None, op0=ALU.is_ge),
                         R=[index], W=[mask01])
                for k4 in range(0, KTq, 4):
                    b = bank()
                    n4 = min(4, KTq - k4)
                    for j in range(n4):
                        kt = k4 + j
                        ks = min(128, SKq - kt * 128)
                        k.op("pe", lambda: nc.tensor.transpose(b[0:ks, j * 128:j * 128 + nq], mask01[0:nq, kt * 128:kt * 128 + ks], ident[0:nq, 0:nq]), R=[mask01, ident], W=[b], sig=(j == n4 - 1))
                    for j in range(n4):
                        kt = k4 + j
                        ks = min(128, SKq - kt * 128)
                        k.op("act", lambda: nc.scalar.copy(out=maskT[0:ks, kt, 0:nq], in_=b[0:ks, j * 128:j * 128 + nq]), R=[b], Wp=[maskT])
                pn = 0
                for g in range(4):
                    bO, bR = (G["psum"][4], G["psum"][5]) if g % 2 == 0 else (G["psum"][6], G["psum"][7])
                    for kt in range(KTq):
                        ks = min(128, SKq - kt * 128)
                        near = kt >= KTq - 2
                        w = 0 if kt == KTq - 1 else 1
                        wstep(1)
                        bl = G["psum"][pn % 4]
                        k.op("pe", lambda: nc.tensor.matmul(bl[0:ks, 0:4 * nq], lhsT=kT[:, g, kt * 128:kt * 128 + ks], rhs=q_[:, 4 * g:4 * g + 4, 0:nq], start=True, stop=not near),
                             R=[kT, q_], W=[bl], sig=not near)
                        if near:
                            k.op("pe", lambda: nc.tensor.matmul(bl[0:ks, 0:4 * nq], lhsT=ident_b[:, 0:ks], rhs=biasT[:, w, 4 * g:4 * g + 4, 0:nq], start=False, stop=True),
                                 R=[ident_b, biasT], W=[bl])
                        p_ = pt[pn % 3]
                        pn += 1
                        k.op("act", lambda: nc.scalar.activation(out=p_[0:ks, 0:4 * nq], in_=bl[0:ks, 0:4 * nq], func=AF.Exp), R=[bl], W=[p_])
                        pv = p_[0:ks, 0:4 * nq].rearrange("p (a b) -> p a b", b=nq)
                        k.op("pool", lambda: nc.gpsimd.tensor_tensor(out=pv, in0=pv, in1=maskT[0:ks, kt, 0:nq].unsqueeze(1).to_broadcast([ks, 4, nq]), op=ALU.mult),
                             R=[p_, maskT], W=[p_])
                        k.op("pe", lambda: nc.tensor.matmul(bO[:, 0:4 * nq], lhsT=vv[0:ks, kt, g, :], rhs=p_[0:ks, 0:4 * nq], start=(kt == 0), stop=(kt == KTq - 1)),
                             R=[vv, p_], W=[bO], sig=False)
                        k.op("pe", lambda: nc.tensor.matmul(bR[:, 0:4 * nq], lhsT=ones_b[0:ks, :], rhs=p_[0:ks, 0:4 * nq], start=(kt == 0), stop=(kt == KTq - 1)),
                             R=[ones_b, p_], W=[bR], sig=True)
                    k.op("dve", lambda: nc.vector.reciprocal(out=rcp[:, 0:4 * nq], in_=bR[:, 0:4 * nq]), R=[bR], W=[rcp])
                    k.op("dve", lambda: nc.vector.tensor_tensor(out=oa_[:, 4 * g:4 * g + 4, 0:nq], in0=bO[:, 0:4 * nq].rearrange("p (a b) -> p a b", b=nq),
                                                                in1=rcp[:, 0:4 * nq].rearrange("p (a b) -> p a b", b=nq), op=ALU.mult), R=[bO, rcp], Wp=[oa_])
                k.dma("pool", S["aT"][0:2048, ta:ta + nq].rearrange("(h p) t -> p h t", p=128), oa_[:, :, 0:nq], R=[oa_], Wp=[S["aT"]])
        wstep(10 ** 9)


def phase_C(k, cfg, G):
    nc = k.nc
    I, S, C = G["I"], G["S"], G["C"]
    bank = G["bank"]
    ones_f = G["ones_f"]
    ident = C["c_ident"]
    NS, NSEQ = cfg.NS, cfg.NSEQ
    HG = 4
    with ExitStack() as pst:
        sb = lambda n, sh, dt=F32: k.sb(pst, n, sh, dt)
        cw = sb("convw", [128, 48, 5])
        k.dma("sp", cw[:].rearrange("p a b -> p (a b)"), I["convw"][:], W=[cw])
        nea = sb("nea", [128, 16]); dtb = sb("dtb", [128, 16]); dng = sb("dng", [128, 1])
        k.dma("sp", nea[:], I["alog"][:], W=[nea])
        k.dma("sp", dtb[:], I["dtb"][:], W=[dtb])
        k.dma("sp", dng[:], I["dng"][:], W=[dng])
        k.op("act", lambda: nc.scalar.activation(out=nea[:], in_=nea[:], func=AF.Exp), R=[nea], W=[nea])
        k.op("pool", lambda: nc.gpsimd.tensor_scalar(out=nea[:], in0=nea[:], scalar1=-1.0, scalar2=None, op0=ALU.mult), R=[nea], W=[nea])
        cH = sb("cH", [128, 48, max(NS, 1) * 3])
        lst = sb("lst", [128, 48, NSEQ * 3])
        s0 = ExitStack()
        orow = k.sb(s0, "orow", [NSEQ * 3, 6144], F32)
        if NS > 0:
            srow = k.sb(s0, "srowc", [NS * 3, 6144], F32)
            k.dma("sp", srow[:], I["sconv"][:], W=[srow])
            for c4 in range(0, 48, 4):
                b = bank()
                for j in range(4):
                    k.op("pe", lambda j=j: nc.tensor.transpose(b[:, j * 128:j * 128 + NS * 3], srow[:, (c4 + j) * 128:(c4 + j + 1) * 128],
                                                               ident[0:NS * 3, 0:NS * 3]), R=[srow, ident], W=[b], sig=(j == 3))
                k.op("dve", lambda: nc.vector.tensor_copy(out=cH[:, c4:c4 + 4, :], in_=b[:, :].rearrange("p (a b) -> p a b", b=128)[:, :, 0:NS * 3]),
                     R=[b], Wp=[cH])
        qv = S["qkvT"][:].rearrange("(c p) t -> p c t", p=128)
        for qi, sq in enumerate(cfg.seqs):
            te = sq["t0"] + sq["T"]
            k.dma("sp", lst[:, :, qi * 3:(qi + 1) * 3], qv[:, :, te - 3:te], R=[S["qkvT"]], Wp=[lst])
        for c4 in range(0, 48, 4):
            b = bank()
            for j in range(4):
                k.op("pe", lambda j=j: nc.tensor.transpose(b[0:NSEQ * 3, j * 128:(j + 1) * 128], lst[:, c4 + j, :], ident[:]),
                     R=[lst, ident], W=[b], sig=(j == 3))
            k.op("dve", lambda: nc.vector.tensor_copy(out=orow[:, c4 * 128:(c4 + 4) * 128], in_=b[0:NSEQ * 3, :]), R=[b], Wp=[orow])
        k.dma("pool", I["convo"][:], orow[:], R=[orow], W=[I["convo"]])

        k.barrier()
        s0.close()
        stop_at(1)
        S_g = [sb("S", [128, HG, 128]) for _ in range(16 // HG)]
        ba = sb("ba", [128, 32]); beta = sb("beta", [128, 16]); xx = sb("xx", [128, 16]); t16 = sb("t16", [128, 16])
        g_ = sb("g", [128, 16]); Gs = sb("Gs", [128, 16]); bg = sb("bg", [128, 16]); edec = sb("edec", [128, 16])
        Dm = sb("Dm", [128, 16, 128]); eGbc = sb("eGbc", [128, 16, 128]); dmS = sb("dmS", [128, 16, 128]); dmT = sb("dmT", [128, 16, 128])

        def make_set():
            Bf = {}
            Bf['raw'] = sb("raw", [128, HG, 3, 131]); Bf['cv'] = sb("cv", [128, HG, 3, 128]); Bf['sqb'] = sb("sqb", [128, HG, 2, 128])
            Bf['ctmp'] = sb("ctmp", [128, HG, 3, 128])
            Bf['cvR'] = [[Reg("cvR") for _ in range(3)] for _ in range(HG)]
            Bf['ctR'] = [[Reg("ctR") for _ in range(3)] for _ in range(HG)]
            Bf['rst'] = sb("rst", [128, HG, 2, 128]); Bf['qd'] = sb("qd", [128, HG, 128])
            Bf['kbg'] = sb("kbg", [128, HG, 128]); Bf['kdec'] = sb("kdec", [128, HG, 128]); Bf['vb'] = sb("vb", [128, HG, 128])
            Bf['L'] = [sb("L", [128, HG, 128]) for _ in range(2)]; Bf['U'] = [sb("U", [128, HG, 128]) for _ in range(2)]
            Bf['P'] = sb("P", [128, HG, 128]); Bf['qkm'] = sb("qkm", [128, HG, 128]); Bf['wT'] = sb("wT", [128, HG, 128]); Bf['u'] = sb("u", [128, HG, 128])
            Bf['vnew'] = sb("vnew", [128, HG, 128]); Bf['oT'] = sb("oT", [128, HG, 128]); Bf['zs'] = sb("zs", [128, HG, 128]); Bf['ob'] = sb("ob", [128, HG, 128], BF16)
            k.op("pool", lambda: nc.gpsimd.memset(Bf['vnew'][:], 0.0), W=[Bf['vnew']])
            return Bf
        BS = [make_set() for _ in range(2)]

        for qi, sq in enumerate(cfg.seqs):
            t0, T, si = sq["t0"], sq["T"], sq["si"]
            for gi_ in range(16 // HG):
                Sg_ = S_g[gi_]
                if si < 0:
                    k.op("pool", lambda: nc.gpsimd.memset(Sg_[:], 0.0), W=[Sg_])
                else:
                    r0_ = si * 2048 + gi_ * HG * 128
                    k.dma("sp", Sg_[:], I["sdel"][r0_:r0_ + HG * 128, :].rearrange("(h d) e -> d h e", d=128), W=[Sg_])
            for tt in range(-(-T // 128)):
                nt = min(128, T - tt * 128)
                ta = t0 + tt * 128
                nch = nt // 64
                k.dma("sp", ba[0:nt, :], S["ba"][ta:ta + nt, :], R=[S["ba"]], W=[ba])
                k.op("act", lambda: nc.scalar.activation(out=beta[0:nt, :], in_=ba[0:nt, 0:16], func=AF.Sigmoid), R=[ba], W=[beta])
                k.op("dve", lambda: nc.vector.tensor_tensor(out=xx[0:nt, :], in0=ba[0:nt, 16:32], in1=dtb[0:nt, :], op=ALU.add), R=[ba, dtb], W=[xx])
                k.op("act", lambda: nc.scalar.activation(out=t16[0:nt, :], in_=xx[0:nt, :], func=AF.Abs), R=[xx], W=[t16])
                k.op("act", lambda: nc.scalar.activation(out=t16[0:nt, :], in_=t16[0:nt, :], func=AF.Exp, scale=-1.0), R=[t16], W=[t16])
                k.op("act", lambda: nc.scalar.activation(out=t16[0:nt, :], in_=t16[0:nt, :], func=AF.Ln, bias=1.0, scale=1.0), R=[t16], W=[t16])
                k.op("dve", lambda: nc.vector.scalar_tensor_tensor(out=g_[0:nt, :], in0=xx[0:nt, :], scalar=0.0, in1=t16[0:nt, :], op0=ALU.max, op1=ALU.add),
                     R=[xx, t16], W=[g_])
                k.op("dve", lambda: nc.vector.tensor_tensor(out=g_[0:nt, :], in0=g_[0:nt, :], in1=nea[0:nt, :], op=ALU.mult), R=[g_, nea], W=[g_])
                stop_at(1.2)
                bG = bank()
                k.op("pe", lambda: nc.tensor.matmul(bG[0:nt, 0:16], lhsT=C["c_cum"][0:nt, 0:nt], rhs=g_[0:nt, :], start=True, stop=True),
                     R=[C["c_cum"], g_], W=[bG], sig=False)
                k.op("pe", lambda: nc.tensor.matmul(bG[0:nt, 16:32], lhsT=C["c_blk"][0:nt, 0:nt], rhs=g_[0:nt, :], start=True, stop=True),
                     R=[C["c_blk"], g_], W=[bG])
                k.op("dve", lambda: nc.vector.tensor_copy(out=Gs[0:nt, :], in_=bG[0:nt, 0:16]), R=[bG], W=[Gs])
                k.op("act", lambda: nc.scalar.activation(out=bg[0:nt, :], in_=Gs[0:nt, :], func=AF.Exp), R=[Gs], W=[bg])
                k.op("dve", lambda: nc.vector.tensor_tensor(out=bg[0:nt, :], in0=bg[0:nt, :], in1=beta[0:nt, :], op=ALU.mult), R=[bg, beta], W=[bg])
                k.op("dve", lambda: nc.vector.tensor_tensor(out=edec[0:nt, :], in0=bG[0:nt, 16:32], in1=Gs[0:nt, :], op=ALU.subtract), R=[bG, Gs], W=[edec])
                k.op("act", lambda: nc.scalar.activation(out=edec[0:nt, :], in_=edec[0:nt, :], func=AF.Exp), R=[edec], W=[edec])
                stop_at(1.4)
                for h in range(16):
                    e = "pool" if h % 2 else "dve"
                    eh = nc.gpsimd if h % 2 else nc.vector
                    k.op(e, lambda: eh.tensor_scalar(out=Dm[0:nt, h, 0:nt], in0=ident[0:nt, 0:nt], scalar1=Gs[0:nt, h:h + 1], scalar2=None, op0=ALU.mult),
                         R=[ident, Gs], Wp=[Dm])
                stop_at(1.6)
                for q4 in range(4):
                    b = bank()
                    k.op("pe", lambda: nc.tensor.matmul(b[:, 0:4 * nt], lhsT=ones_f[0:nt, :], rhs=Dm[0:nt, 4 * q4:4 * q4 + 4, 0:nt], start=True, stop=True),
                         R=[ones_f, Dm], W=[b])
                    stop_at(1.65)
                    bv = b[:, 0:4 * nt].rearrange("p (a b) -> p a b", b=nt)
                    k.op("act", lambda: nc.scalar.activation(out=eGbc[:, 4 * q4:4 * q4 + 4, 0:nt], in_=bv, func=AF.Exp), R=[b], Wp=[eGbc, b])
                    stop_at(1.7)
                    for j in range(4):
                        h = 4 * q4 + j
                        k.op("dve", lambda: nc.vector.scalar_tensor_tensor(out=dmS[0:nt, h, 0:nt], in0=b[0:nt, j * nt:(j + 1) * nt], scalar=Gs[0:nt, h:h + 1],
                                                                           in1=C["c_nmL"][0:nt, 0:nt], op0=ALU.subtract, op1=ALU.subtract),
                             R=[b, Gs, C["c_nmL"]], Wp=[dmS])
                        k.op("dve", lambda: nc.vector.scalar_tensor_tensor(out=dmT[0:nt, h, 0:nt], in0=b[0:nt, j * nt:(j + 1) * nt], scalar=Gs[0:nt, h:h + 1],
                                                                           in1=C["c_nmT"][0:nt, 0:nt], op0=ALU.subtract, op1=ALU.add),
                             R=[b, Gs, C["c_nmT"]], Wp=[dmT])
                stop_at(1.8)
                k.op("act", lambda: nc.scalar.activation(out=dmS[0:nt, :, 0:nt], in_=dmS[0:nt, :, 0:nt], func=AF.Exp, scale=-1.0), R=[dmS], W=[dmS])
                k.op("act", lambda: nc.scalar.activation(out=dmT[0:nt, :, 0:nt], in_=dmT[0:nt, :, 0:nt], func=AF.Exp), R=[dmT], W=[dmT])
                k.op("dve", lambda: nc.vector.tensor_tensor(out=dmS[0:nt, :, 0:nt], in0=dmS[0:nt, :, 0:nt], in1=beta[0:nt, :].unsqueeze(2).to_broadcast([nt, 16, nt]),
                                                            op=ALU.mult), R=[dmS, beta], W=[dmS])
                stop_at(2)
                def hg_gen(hg, Bf, Sg):
                    raw, cv, sqb, rst, qd, kbg, kdec, vb = Bf['raw'], Bf['cv'], Bf['sqb'], Bf['rst'], Bf['qd'], Bf['kbg'], Bf['kdec'], Bf['vb']
                    L, U, P, qkm, wT, u_, vnew, oT, zs, ob = Bf['L'], Bf['U'], Bf['P'], Bf['qkm'], Bf['wT'], Bf['u'], Bf['vnew'], Bf['oT'], Bf['zs'], Bf['ob']
                    for hh in range(HG):
                        h = hg + hh
                        src = S["qkvT"][:].rearrange("(c h p) t -> p c h t", c=3, h=16)[:, :, h, :]
                        if tt > 0:
                            k.dma("sp", raw[:, hh, :, 0:3 + nt], src[:, :, ta - 3:ta + nt], R=[S["qkvT"]], Wp=[raw])
                        else:
                            k.dma("sp", raw[:, hh, :, 3:3 + nt], src[:, :, ta:ta + nt], R=[S["qkvT"]], Wp=[raw])
                            for comp in range(3):
                                if si < 0:
                                    k.op("pool", lambda: nc.gpsimd.memset(raw[:, hh, comp, 0:3], 0.0), Wp=[raw])
                                else:
                                    k.op("pool", lambda: nc.gpsimd.tensor_copy(out=raw[:, hh, comp, 0:3], in_=cH[:, comp * 16 + h, si * 3:(si + 1) * 3]), R=[cH], Wp=[raw])
                    k.dma("sp", zs[:, :, 0:nt], S["zT"][hg * 128:(hg + HG) * 128, ta:ta + nt].rearrange("(h p) t -> p h t", p=128), R=[S["zT"]], W=[zs])
                    cvR, ctmp = Bf['cvR'], Bf['ctmp']
                    for j in range(4):
                        for hh in range(HG):
                            h = hg + hh
                            for comp in range(3):
                                ch = comp * 16 + h
                                rg = cvR[hh][comp]
                                on_pool = (hh * 3 + comp) % 2 == 1
                                o_ = cv[:, hh, comp, 0:nt]
                                i_ = raw[:, hh, comp, j:j + nt]
                                if j == 0:
                                    if on_pool:
                                        k.op("pool", lambda: nc.gpsimd.tensor_scalar(out=o_, in0=i_, scalar1=cw[:, ch, 0:1], scalar2=cw[:, ch, 4:5], op0=ALU.mult, op1=ALU.add),
                                             R=[raw, cw], Wp=[rg], Wa=[cv])
                                    else:
                                        k.op("dve", lambda: nc.vector.tensor_scalar(out=o_, in0=i_, scalar1=cw[:, ch, 0:1], scalar2=cw[:, ch, 4:5], op0=ALU.mult, op1=ALU.add),
                                             R=[raw, cw], Wp=[rg], Wa=[cv])
                                elif on_pool:
                                    t_ = ctmp[:, hh, comp, 0:nt]
                                    tr = Bf['ctR'][hh][comp]
                                    k.op("pool", lambda: nc.gpsimd.tensor_scalar(out=t_, in0=i_, scalar1=cw[:, ch, j:j + 1], scalar2=None, op0=ALU.mult), R=[raw, cw], W=[tr])
                                    k.op("pool", lambda: nc.gpsimd.tensor_tensor(out=o_, in0=o_, in1=t_, op=ALU.add), R=[tr, rg], Wp=[rg])
                                else:
                                    k.op("dve", lambda: nc.vector.scalar_tensor_tensor(out=o_, in0=i_, scalar=cw[:, ch, j:j + 1], in1=o_, op0=ALU.mult, op1=ALU.add),
                                         R=[raw, cw, rg], Wp=[rg])
                    allcv = [cvR[a][b_] for a in range(HG) for b_ in range(3)]
                    yield
                    k.op("act", lambda: nc.scalar.activation(out=cv[:, :, :, 0:nt], in_=cv[:, :, :, 0:nt], func=AF.Silu), R=allcv, W=[cv] + allcv)
                    k.op("pool", lambda: nc.gpsimd.tensor_tensor(out=sqb[:, :, :, 0:nt], in0=cv[:, :, 0:2, 0:nt], in1=cv[:, :, 0:2, 0:nt], op=ALU.mult), R=[cv], W=[sqb])
                    for b4 in range(0, HG, 2):
                        b = bank()
                        for j in range(2):
                            k.op("pe", lambda: nc.tensor.matmul(b[:, j * 2 * nt:(j + 1) * 2 * nt], lhsT=ones_f[:], rhs=sqb[:, b4 + j, :, 0:nt], start=True, stop=True),
                                 R=[ones_f, sqb], W=[b], sig=(j == 1))
                        bv = b[:, 0:4 * nt].rearrange("p (a c b) -> p a c b", a=2, c=2)
                        k.op("act", lambda: nc.scalar.activation(out=rst[:, b4:b4 + 2, :, 0:nt], in_=bv, func=AF.Sqrt, bias=1e-6, scale=1.0), R=[b], Wp=[rst])
                    k.op("dve", lambda: nc.vector.reciprocal(out=rst[:, :, :, 0:nt], in_=rst[:, :, :, 0:nt]), R=[rst], W=[rst])
                    k.op("dve", lambda: nc.vector.scalar_tensor_tensor(out=cv[:, :, 0, 0:nt], in0=cv[:, :, 0, 0:nt], scalar=HD ** -0.5, in1=rst[:, :, 0, 0:nt],
                                                                       op0=ALU.mult, op1=ALU.mult), R=[cv, rst], Wp=[cv])
                    k.op("pool", lambda: nc.gpsimd.tensor_tensor(out=cv[:, :, 1, 0:nt], in0=cv[:, :, 1, 0:nt], in1=rst[:, :, 1, 0:nt], op=ALU.mult), R=[cv, rst], Wp=[cv])
                    k.op("dve", lambda: nc.vector.tensor_tensor(out=qd[:, :, 0:nt], in0=cv[:, :, 0, 0:nt], in1=eGbc[:, hg:hg + HG, 0:nt], op=ALU.mult), R=[cv, eGbc], W=[qd])
                    yield
                    for b4 in range(0, HG, 4):
                        bk_, bv_ = bank(), bank()
                        for j in range(4):
                            k.op("pe", lambda: nc.tensor.transpose(bk_[0:nt, j * 128:(j + 1) * 128], cv[:, b4 + j, 1, 0:nt], ident[:]), R=[cv, ident], W=[bk_], sig=(j == 3))
                        for j in range(4):
                            k.op("pe", lambda: nc.tensor.transpose(bv_[0:nt, j * 128:(j + 1) * 128], cv[:, b4 + j, 2, 0:nt], ident[:]), R=[cv, ident], W=[bv_], sig=(j == 3))
                        hs = slice(hg + b4, hg + b4 + 4)
                        kv3 = bk_[0:nt, :].rearrange("p (a b) -> p a b", b=128)
                        vv3 = bv_[0:nt, :].rearrange("p (a b) -> p a b", b=128)
                        k.op("dve", lambda: nc.vector.tensor_tensor(out=kbg[0:nt, b4:b4 + 4, :], in0=kv3, in1=bg[0:nt, hs].unsqueeze(2).to_broadcast([nt, 4, 128]), op=ALU.mult),
                             R=[bk_, bg], Wp=[kbg])
                        k.op("dve", lambda: nc.vector.tensor_tensor(out=kdec[0:nt, b4:b4 + 4, :], in0=kv3, in1=edec[0:nt, hs].unsqueeze(2).to_broadcast([nt, 4, 128]), op=ALU.mult),
                             R=[bk_, edec], Wp=[kdec])
                        k.op("dve", lambda: nc.vector.tensor_tensor(out=vb[0:nt, b4:b4 + 4, :], in0=vv3, in1=beta[0:nt, hs].unsqueeze(2).to_broadcast([nt, 4, 128]), op=ALU.mult),
                             R=[bv_, beta], Wp=[vb])
                    yield
                    for b4 in range(0, HG, 4):
                        b1, b2 = bank(), bank()
                        for j in range(4):
                            k.op("pe", lambda: nc.tensor.matmul(b1[0:nt, j * nt:(j + 1) * nt], lhsT=cv[:, b4 + j, 1, 0:nt], rhs=cv[:, b4 + j, 1, 0:nt], start=True, stop=True),
                                 R=[cv], W=[b1], sig=(j == 3))
                        for j in range(4):
                            k.op("pe", lambda: nc.tensor.matmul(b2[0:nt, j * nt:(j + 1) * nt], lhsT=cv[:, b4 + j, 1, 0:nt], rhs=cv[:, b4 + j, 0, 0:nt], start=True, stop=True),
                                 R=[cv], W=[b2], sig=(j == 3))
                        hs = slice(hg + b4, hg + b4 + 4)
                        k.op("dve", lambda: nc.vector.tensor_tensor(out=L[0][0:nt, b4:b4 + 4, 0:nt], in0=b1[0:nt, 0:4 * nt].rearrange("p (a b) -> p a b", b=nt),
                                                                    in1=dmS[0:nt, hs, 0:nt], op=ALU.mult), R=[b1, dmS], Wp=[L[0]])
                        k.op("dve", lambda: nc.vector.tensor_tensor(out=qkm[0:nt, b4:b4 + 4, 0:nt], in0=b2[0:nt, 0:4 * nt].rearrange("p (a b) -> p a b", b=nt),
                                                                    in1=dmT[0:nt, hs, 0:nt], op=ALU.mult), R=[b2, dmT], Wp=[qkm])
                    yield
                    for b4 in range(0, HG, 4):
                        b = bank()
                        for j in range(4):
                            k.op("pe", lambda: nc.tensor.transpose(b[0:nt, j * nt:(j + 1) * nt], L[0][0:nt, b4 + j, 0:nt], ident[0:nt, 0:nt]), R=[L[0], ident], W=[b], sig=(j == 3))
                        bv = b[0:nt, 0:4 * nt].rearrange("p (a b) -> p a b", b=nt)
                        k.op("act", lambda: nc.scalar.copy(out=U[0][0:nt, b4:b4 + 4, 0:nt], in_=bv), R=[b], Wp=[U[0]])
                        k.op("pool", lambda: nc.gpsimd.tensor_tensor(out=P[0:nt, b4:b4 + 4, 0:nt], in0=ident[0:nt, 0:nt].unsqueeze(1).to_broadcast([nt, 4, nt]),
                                                                     in1=U[0][0:nt, b4:b4 + 4, 0:nt], op=ALU.subtract), R=[U[0], ident], Wp=[P])
                    cur = 0
                    for step in range(5):
                        nx = 1 - cur
                        for b4 in range(0, HG, 4):
                            b1 = bank()
                            for j in range(4):
                                k.op("pe", lambda: nc.tensor.matmul(b1[0:nt, j * nt:(j + 1) * nt], lhsT=U[cur][0:nt, b4 + j, 0:nt], rhs=L[cur][0:nt, b4 + j, 0:nt], start=True, stop=True),
                                     R=[U[cur], L[cur]], W=[b1], sig=(j == 3))
                            k.op("act", lambda: nc.scalar.copy(out=L[nx][0:nt, b4:b4 + 4, 0:nt], in_=b1[0:nt, 0:4 * nt].rearrange("p (a b) -> p a b", b=nt)), R=[b1], Wp=[L[nx]])
                            if step < 4:
                                b2 = bank()
                                for j in range(4):
                                    k.op("pe", lambda: nc.tensor.matmul(b2[0:nt, j * nt:(j + 1) * nt], lhsT=L[cur][0:nt, b4 + j, 0:nt], rhs=U[cur][0:nt, b4 + j, 0:nt], start=True, stop=True),
                                         R=[U[cur], L[cur]], W=[b2], sig=(j == 3))
                                k.op("dve", lambda: nc.vector.tensor_copy(out=U[nx][0:nt, b4:b4 + 4, 0:nt], in_=b2[0:nt, 0:4 * nt].rearrange("p (a b) -> p a b", b=nt)), R=[b2], Wp=[U[nx]])
                        for b4 in range(0, HG, 4):
                            b3 = bank()
                            for j in range(4):
                                k.op("pe", lambda: nc.tensor.matmul(b3[0:nt, j * nt:(j + 1) * nt], lhsT=L[nx][0:nt, b4 + j, 0:nt], rhs=P[0:nt, b4 + j, 0:nt], start=True, stop=True),
                                     R=[L[nx], P], W=[b3], sig=(j == 3))
                            k.op("dve", lambda: nc.vector.tensor_tensor(out=P[0:nt, b4:b4 + 4, 0:nt], in0=P[0:nt, b4:b4 + 4, 0:nt],
                                                                        in1=b3[0:nt, 0:4 * nt].rearrange("p (a b) -> p a b", b=nt), op=ALU.add), R=[b3, P], Wp=[P])
                        cur = nx
                        yield
                    yield
                    for b4 in range(0, HG, 4):
                        b1, b2 = bank(), bank()
                        for j in range(4):
                            k.op("pe", lambda: nc.tensor.matmul(b1[:, j * nt:(j + 1) * nt], lhsT=kbg[0:nt, b4 + j, :], rhs=P[0:nt, b4 + j, 0:nt], start=True, stop=True),
                                 R=[kbg, P], W=[b1], sig=(j == 3))
                        for j in range(4):
                            k.op("pe", lambda: nc.tensor.matmul(b2[0:nt, j * 128:(j + 1) * 128], lhsT=P[0:nt, b4 + j, 0:nt], rhs=vb[0:nt, b4 + j, :], start=True, stop=True),
                                 R=[vb, P], W=[b2], sig=(j == 3))
                        k.op("act", lambda: nc.scalar.copy(out=wT[:, b4:b4 + 4, 0:nt], in_=b1[:, 0:4 * nt].rearrange("p (a b) -> p a b", b=nt)), R=[b1], Wp=[wT])
                        k.op("dve", lambda: nc.vector.tensor_copy(out=u_[0:nt, b4:b4 + 4, :], in_=b2[0:nt, :].rearrange("p (a b) -> p a b", b=128)), R=[b2], Wp=[u_])
                    yield
                    for ci in range(nch):
                        r = slice(ci * 64, ci * 64 + 64)
                        for b4 in range(0, HG, 4):
                            b1 = bank()
                            for j in range(4):
                                k.op("pe", lambda: nc.tensor.matmul(b1[r, j * 128:(j + 1) * 128], lhsT=wT[:, b4 + j, r], rhs=Sg[:, b4 + j, :], start=True, stop=True),
                                     R=[wT, Sg], W=[b1], sig=(j == 3))
                            k.op("dve", lambda: nc.vector.tensor_tensor(out=vnew[r, b4:b4 + 4, :], in0=u_[r, b4:b4 + 4, :], in1=b1[r, :].rearrange("p (a b) -> p a b", b=128),
                                                                        op=ALU.subtract), R=[u_, b1], Wp=[vnew])
                        bo = bank()
                        for hh in range(HG):
                            k.op("pe", lambda: nc.tensor.matmul(bo[:, hh * 64:(hh + 1) * 64], lhsT=Sg[:, hh, :], rhs=qd[:, hh, r], start=True, stop=False),
                                 R=[Sg, qd], W=[bo], sig=False)
                            k.op("pe", lambda: nc.tensor.matmul(bo[:, hh * 64:(hh + 1) * 64], lhsT=vnew[0:nt, hh, :], rhs=qkm[0:nt, hh, r], start=False, stop=True),
                                 R=[vnew, qkm], W=[bo], sig=(hh == HG - 1))
                        k.op("act", lambda: nc.scalar.copy(out=oT[:, :, r], in_=bo[:, 0:HG * 64].rearrange("p (a b) -> p a b", b=64)), R=[bo], Wp=[oT])
                        for b4 in range(0, HG, 4):
                            b2 = bank()
                            for j in range(4):
                                k.op("pe", lambda: nc.tensor.matmul(b2[:, j * 128:(j + 1) * 128], lhsT=kdec[r, b4 + j, :], rhs=vnew[r, b4 + j, :], start=True, stop=True),
                                     R=[kdec, vnew], W=[b2], sig=(j == 3))
                            hs = slice(hg + b4, hg + b4 + 4)
                            col = ci * 64 + 63
                            k.op("pool", lambda: nc.gpsimd.tensor_tensor(out=Sg[:, b4:b4 + 4, :], in0=Sg[:, b4:b4 + 4, :], in1=eGbc[:, hs, col:col + 1].to_broadcast([128, 4, 128]), op=ALU.mult),
                                 R=[Sg, eGbc], Wp=[Sg])
                            k.op("dve", lambda: nc.vector.tensor_tensor(out=Sg[:, b4:b4 + 4, :], in0=Sg[:, b4:b4 + 4, :], in1=b2[:, :].rearrange("p (a b) -> p a b", b=128), op=ALU.add),
                                 R=[Sg, b2], Wp=[Sg])
                        yield
                    yield
                    k.op("pool", lambda: nc.gpsimd.tensor_tensor(out=sqb[:, :, 0, 0:nt], in0=oT[:, :, 0:nt], in1=oT[:, :, 0:nt], op=ALU.mult), R=[oT], Wp=[sqb])
                    for b4 in range(0, HG, 4):
                        b = bank()
                        for j in range(4):
                            k.op("pe", lambda: nc.tensor.matmul(b[:, j * nt:(j + 1) * nt], lhsT=ones_f[:], rhs=sqb[:, b4 + j, 0, 0:nt], start=True, stop=True),
                                 R=[ones_f, sqb], W=[b], sig=(j == 3))
                        k.op("act", lambda: nc.scalar.activation(out=rst[:, b4:b4 + 4, 0, 0:nt], in_=b[:, 0:4 * nt].rearrange("p (a b) -> p a b", b=nt), func=AF.Sqrt,
                                                                 bias=EPS, scale=1.0 / 128), R=[b], Wp=[rst])
                    k.op("dve", lambda: nc.vector.reciprocal(out=rst[:, :, 0, 0:nt], in_=rst[:, :, 0, 0:nt]), R=[rst], Wp=[rst])
                    k.op("dve", lambda: nc.vector.tensor_tensor(out=oT[:, :, 0:nt], in0=oT[:, :, 0:nt], in1=rst[:, :, 0, 0:nt], op=ALU.mult), R=[oT, rst], W=[oT])
                    k.op("dve", lambda: nc.vector.scalar_tensor_tensor(out=ob[:, :, 0:nt], in0=oT[:, :, 0:nt], scalar=dng[:, 0:1], in1=zs[:, :, 0:nt], op0=ALU.mult, op1=ALU.mult),
                         R=[oT, dng, zs], W=[ob])
                    k.dma("pool", S["aT"][2048 + hg * 128:2048 + (hg + HG) * 128, ta:ta + nt].rearrange("(h p) t -> p h t", p=128), ob[:, :, 0:nt], R=[ob], Wp=[S["aT"]])

                gens = [hg_gen(hg, BS[i % 2], S_g[hg // HG]) for i, hg in enumerate(range(0, 16, HG))]
                active = []
                while gens or active:
                    while len(active) < 2 and gens:
                        active.append(gens.pop(0))
                    for gen_ in list(active):
                        try:
                            next(gen_)
                        except StopIteration:
                            active.remove(gen_)
            for gi_ in range(16 // HG):
                r0_ = qi * 2048 + gi_ * HG * 128
                k.dma("pool", I["so"][r0_:r0_ + HG * 128, :].rearrange("(h d) e -> d h e", d=128), S_g[gi_][:], R=[S_g[gi_]], Wp=[I["so"]])


def ln_stats(k, G, st, s1, gn):
    nc = k.nc
    bank = G["bank"]
    ones_f = G["ones_f"]
    sq = [k.sb(st, "lnsq", [128, gn], F32) for _ in range(2)]
    mt = k.sb(st, "lnm", [128, gn], F32)
    t1 = k.sb(st, "lnt", [128, gn], F32)
    rs = k.sb(st, "lnrs", [128, gn], F32)
    nm = k.sb(st, "lnnm", [128, gn], F32)
    bs, bq = bank(), bank()
    for m in range(KC):
        q_ = sq[m % 2]
        k.op("act", lambda: nc.scalar.activation(out=q_[:], in_=s1[:, m, :], func=AF.Square), R=[s1], W=[q_])
        k.op("pe", lambda: nc.tensor.matmul(bs[:, 0:gn], lhsT=ones_f[:], rhs=s1[:, m, :], start=(m == 0), stop=(m == KC - 1)),
             R=[s1, ones_f], W=[bs], sig=(m == KC - 1))
        k.op("pe", lambda: nc.tensor.matmul(bq[:, 0:gn], lhsT=ones_f[:], rhs=q_[:], start=(m == 0), stop=(m == KC - 1)),
             R=[q_, ones_f], W=[bq], sig=True)
    k.op("dve", lambda: nc.vector.tensor_scalar(out=mt[:], in0=bs[:, 0:gn], scalar1=1.0 / D, scalar2=None, op0=ALU.mult), R=[bs], W=[mt])
    k.op("dve", lambda: nc.vector.tensor_tensor(out=t1[:], in0=mt[:], in1=mt[:], op=ALU.mult), R=[mt], W=[t1])
    k.op("dve", lambda: nc.vector.scalar_tensor_tensor(out=t1[:], in0=bq[:, 0:gn], scalar=1.0 / D, in1=t1[:], op0=ALU.mult, op1=ALU.subtract),
         R=[bq, t1], W=[t1])
    k.op("act", lambda: nc.scalar.activation(out=t1[:], in_=t1[:], func=AF.Sqrt, bias=EPS, scale=1.0), R=[t1], W=[t1])
    k.op("dve", lambda: nc.vector.reciprocal(out=rs[:], in_=t1[:]), R=[t1], W=[rs])
    k.op("dve", lambda: nc.vector.scalar_tensor_tensor(out=nm[:], in0=mt[:], scalar=-1.0, in1=rs[:], op0=ALU.mult, op1=ALU.mult),
         R=[mt, rs], W=[nm])
    return rs, nm


def phase_D(k, cfg, G):
    nc = k.nc
    I, S = G["I"], G["S"]
    bank = G["bank"]
    for (g0, gn) in cfg.groups(384):
        with ExitStack() as st:
            aT = k.sb(st, "aT", [128, KC, gn], BF16)
            s1 = k.sb(st, "s1", [128, KC, gn], F32)
            gb = k.sb(st, "ln1", [128, 64], F32)
            k.dma("sp", gb[:], I["ln1"][:], W=[gb])
            k.dma("sp", aT[:], S["aT"][:].rearrange("(kc p) t -> p kc t", p=128)[:, :, g0:g0 + gn], R=[S["aT"]], W=[aT])
            ws = WTiles(k, st, nslot=3)
            xr = [k.sb(st, "xr", [128, gn], F32) for _ in range(3)]
            for m in range(KC):
                wb = ws.get(S["b_w_o"], m // 2)
                sub = (m % 2) * 128
                x_ = xr[m % 3]
                k.dma("sp", x_[:], S["xnT"][m * 128:(m + 1) * 128, g0:g0 + gn], R=[S["xnT"]], W=[x_])
                b = bank()
                for kc in range(KC):
                    k.op("pe", lambda kc=kc: nc.tensor.matmul(b[:, 0:gn], lhsT=wb[:, kc, sub:sub + 128], rhs=aT[:, kc, :], start=(kc == 0), stop=(kc == KC - 1)),
                         R=[wb, aT], W=[b], sig=(kc == KC - 1))
                k.op("dve", lambda: nc.vector.scalar_tensor_tensor(out=s1[:, m, :], in0=x_[:], scalar=ALPHA, in1=b[:, 0:gn], op0=ALU.mult, op1=ALU.add),
                     R=[x_, b], Wp=[s1])
            rs, nm = ln_stats(k, G, st, s1, gn)
            of = [k.sb(st, "of", [128, gn], F32) for _ in range(2)]
            ob = [k.sb(st, "ob", [128, gn], BF16) for _ in range(2)]
            for m in range(KC):
                o_, b_ = of[m % 2], ob[m % 2]
                k.op("pool", lambda: nc.gpsimd.tensor_tensor(out=o_[:], in0=s1[:, m, :], in1=rs[:], op=ALU.mult), R=[s1, rs], W=[o_])
                k.op("dve", lambda: nc.vector.tensor_tensor(out=o_[:], in0=o_[:], in1=nm[:], op=ALU.add), R=[o_, nm], W=[o_])
                k.op("act", lambda: nc.scalar.activation(out=o_[:], in_=o_[:], func=AF.Identity, scale=gb[:, m:m + 1], bias=gb[:, 32 + m:33 + m]),
                     R=[o_, gb], W=[o_])
                k.op("pool", lambda: nc.gpsimd.tensor_copy(out=b_[:], in_=o_[:]), R=[o_], W=[b_])
                k.dma("pool", S["x1T"][m * 128:(m + 1) * 128, g0:g0 + gn], o_[:], R=[o_], Wp=[S["x1T"]])
                k.dma("pool", S["x1b"][m * 128:(m + 1) * 128, g0:g0 + gn], b_[:], R=[b_], Wp=[S["x1b"]])
        k.barrier()


def seg_pieces(cfg, g0, gn):
    out = []
    for qi, sq in enumerate(cfg.seqs):
        a = max(sq["t0"], g0)
        b = min(sq["t0"] + sq["T"], g0 + gn)
        if a < b:
            out.append((qi, a - g0, b - a, a == sq["t0"], b == sq["t0"] + sq["T"]))
    return out


def phase_E(k, cfg, G):
    nc = k.nc
    I, S, C = G["I"], G["S"], G["C"]
    bank = G["bank"]
    FC, NS, NSEQ, DFF = cfg.FC, cfg.NS, cfg.NSEQ, cfg.DFF
    with ExitStack() as pst:
        fw = k.sb(pst, "ffnw", [128, 2 * FC, 4], F32)
        k.dma("sp", fw[:].rearrange("p a b -> p (a b)"), I["ffnw"][:], W=[fw])
        hsave = k.sb(pst, "hsave", [128, 2 * FC, 2], F32)
        k.op("pool", lambda: nc.gpsimd.memset(hsave[:], 0.0), W=[hsave])
        fst = k.sb(pst, "fst", [128, 2 * FC, NSEQ * 2], F32)
        fH = k.sb(pst, "fH", [128, 2 * FC, max(NS, 1) * 2], F32)
        if NS > 0:
            with ExitStack() as s0:
                srow = [k.sb(s0, "srow", [NS * 2, 512], F32) for _ in range(2)]
                n = 0
                for c4 in range(0, 2 * FC, 4):
                    b = bank()
                    n4 = min(4, 2 * FC - c4)
                    sr = srow[n % 2]
                    n += 1
                    k.dma("sp", sr[:, 0:n4 * 128], I["sffn"][:, c4 * 128:(c4 + n4) * 128], W=[sr])
                    for j in range(n4):
                        k.op("pe", lambda j=j: nc.tensor.transpose(b[:, j * 128:j * 128 + NS * 2], sr[:, j * 128:(j + 1) * 128],
                                                                   C["c_ident"][0:NS * 2, 0:NS * 2]),
                             R=[sr, C["c_ident"]], W=[b], sig=(j == n4 - 1))
                    k.op("dve", lambda: nc.vector.tensor_copy(out=fH[:, c4:c4 + n4, :],
                                                              in_=b[:, 0:n4 * 128].rearrange("p (a b) -> p a b", b=128)[:, :, 0:NS * 2]),
                         R=[b], Wp=[fH])
            k.barrier()
        for (g0, gn) in cfg.groups(768):
            pcs = seg_pieces(cfg, g0, gn)
            offs = []
            o = 0
            for p in pcs:
                offs.append(o)
                o += p[2] + 2
            RW = o
            with ExitStack() as st:
                x1 = k.sb(st, "x1b", [128, KC, gn], BF16)
                k.dma("sp", x1[:], S["x1b"][:].rearrange("(kc p) t -> p kc t", p=128)[:, :, g0:g0 + gn], R=[S["x1b"]], W=[x1])
                ws = WTiles(k, st, nslot=4)
                raw = [[k.sb(st, "raw", [128, RW], F32) for _ in range(2)] for _ in range(2)]
                cv = [[k.sb(st, "cv", [128, gn], F32) for _ in range(2)] for _ in range(2)]
                ao = [k.sb(st, "ao", [128, gn], BF16) for _ in range(2)]
                for c in range(FC):
                    par = c % 2
                    for half in range(2):
                        ch = half * FC + c
                        r_ = raw[half][par]
                        wb = ws.get(S["b_w_up"], ch // 2)
                        sub = (ch % 2) * 128
                        for pi, (qi, a, ln, s_st, s_en) in enumerate(pcs):
                            o_ = offs[pi]
                            if not s_st:
                                k.op("pool", lambda o_=o_: nc.gpsimd.tensor_copy(out=r_[:, o_:o_ + 2], in_=hsave[:, ch, :]), R=[hsave], Wp=[r_])
                            elif qi == 0:
                                k.op("pool", lambda o_=o_: nc.gpsimd.memset(r_[:, o_:o_ + 2], 0.0), Wp=[r_])
                            else:
                                k.op("pool", lambda o_=o_, qi=qi: nc.gpsimd.tensor_copy(out=r_[:, o_:o_ + 2], in_=fH[:, ch, (qi - 1) * 2:qi * 2]), R=[fH], Wp=[r_])
                        for (b0, bn) in blocks(gn):
                            b = bank()
                            for kc in range(KC):
                                k.op("pe", lambda kc=kc: nc.tensor.matmul(b[:, 0:bn], lhsT=wb[:, kc, sub:sub + 128], rhs=x1[:, kc, b0:b0 + bn], start=(kc == 0), stop=(kc == KC - 1)),
                                     R=[wb, x1], W=[b], sig=(kc == KC - 1))
                            for pi, (qi, a, ln, s_st, s_en) in enumerate(pcs):
                                lo, hi = max(a, b0), min(a + ln, b0 + bn)
                                if lo < hi:
                                    d0 = offs[pi] + 2 + (lo - a)
                                    k.op("act", lambda lo=lo, hi=hi, d0=d0: nc.scalar.copy(out=r_[:, d0:d0 + hi - lo], in_=b[:, lo - b0:hi - b0]), R=[b], Wp=[r_])
                        c_ = cv[half][par]
                        for pi, (qi, a, ln, s_st, s_en) in enumerate(pcs):
                            o_ = offs[pi]
                            k.op("dve", lambda o_=o_, a=a, ln=ln: nc.vector.tensor_scalar(out=c_[:, a:a + ln], in0=r_[:, o_:o_ + ln], scalar1=fw[:, ch, 0:1], scalar2=fw[:, ch, 3:4],
                                                                                       op0=ALU.mult, op1=ALU.add), R=[r_, fw], Wp=[c_])
                            for j in (1, 2):
                                k.op("dve", lambda o_=o_, a=a, ln=ln, j=j: nc.vector.scalar_tensor_tensor(out=c_[:, a:a + ln], in0=r_[:, o_ + j:o_ + j + ln], scalar=fw[:, ch, j:j + 1],
                                                                                                         in1=c_[:, a:a + ln], op0=ALU.mult, op1=ALU.add), R=[r_, fw, c_], Wp=[c_])
                            if s_en:
                                k.op("pool", lambda o_=o_, ln=ln, qi=qi: nc.gpsimd.tensor_copy(out=fst[:, ch, qi * 2:qi * 2 + 2], in_=r_[:, o_ + ln:o_ + ln + 2]), R=[r_], Wp=[fst])
                            else:
                                k.op("pool", lambda o_=o_, ln=ln: nc.gpsimd.tensor_copy(out=hsave[:, ch, :], in_=r_[:, o_ + ln:o_ + ln + 2]), R=[r_], Wp=[hsave])
                    gt, vl, a_ = cv[0][par], cv[1][par], ao[par]
                    k.op("act", lambda: nc.scalar.activation(out=gt[:], in_=gt[:], func=AF.Silu), R=[gt], W=[gt])
                    k.op("pool", lambda: nc.gpsimd.tensor_tensor(out=a_[:], in0=gt[:], in1=vl[:], op=ALU.mult), R=[gt, vl], W=[a_])
                    k.dma("pool", S["actT"][c * 128:(c + 1) * 128, g0:g0 + gn], a_[:], R=[a_], Wp=[S["actT"]])
            k.barrier()
        with ExitStack() as st:
            orow = [k.sb(st, "orow", [NSEQ * 2, 512], F32) for _ in range(2)]
            n = 0
            for c4 in range(0, 2 * FC, 4):
                n4 = min(4, 2 * FC - c4)
                b = bank()
                for j in range(n4):
                    k.op("pe", lambda j=j: nc.tensor.transpose(b[0:NSEQ * 2, j * 128:(j + 1) * 128], fst[:, c4 + j, :], C["c_ident"][:]),
                         R=[fst, C["c_ident"]], W=[b], sig=(j == n4 - 1))
                o_ = orow[n % 2]
                n += 1
                k.op("dve", lambda: nc.vector.tensor_copy(out=o_[:, 0:n4 * 128], in_=b[0:NSEQ * 2, 0:n4 * 128]), R=[b], W=[o_])
                k.dma("pool", I["ffno"][:, c4 * 128:(c4 + n4) * 128], o_[:, 0:n4 * 128], R=[o_], Wp=[I["ffno"]])
        k.barrier()


def phase_F(k, cfg, G):
    nc = k.nc
    I, S, C = G["I"], G["S"], G["C"]
    bank = G["bank"]
    FC = cfg.FC
    kgs = [(i, min(32, FC - i)) for i in range(0, FC, 32)]
    for (g0, gn) in cfg.groups(256):
        with ExitStack() as st:
            aT = k.sb(st, "actT", [128, FC, gn], BF16)
            s1 = k.sb(st, "s2", [128, KC, gn], F32)
            gb = k.sb(st, "ln2", [128, 64], F32)
            k.dma("sp", gb[:], I["ln2"][:], W=[gb])
            k.dma("sp", aT[:], S["actT"][:].rearrange("(kc p) t -> p kc t", p=128)[:, :, g0:g0 + gn], R=[S["actT"]], W=[aT])
            ws = WTiles(k, st, nslot=4)
            xr = [k.sb(st, "xr", [128, gn], F32) for _ in range(3)]
            for m in range(KC):
                x_ = xr[m % 3]
                sub = (m % 2) * 128
                k.dma("sp", x_[:], S["x1T"][m * 128:(m + 1) * 128, g0:g0 + gn], R=[S["x1T"]], W=[x_])
                b = bank()
                for gi, (k0, kn) in enumerate(kgs):
                    wb = ws.get(S["b_w_down"], m // 2, kg=gi, kcn=kn)
                    for kc in range(kn):
                        first = (gi == 0 and kc == 0)
                        last = (gi == len(kgs) - 1 and kc == kn - 1)
                        k.op("pe", lambda kc=kc: nc.tensor.matmul(b[:, 0:gn], lhsT=wb[:, kc, sub:sub + 128], rhs=aT[:, k0 + kc, :], start=first, stop=last),
                             R=[wb, aT], W=[b], sig=(kc == kn - 1))
                k.op("dve", lambda: nc.vector.scalar_tensor_tensor(out=s1[:, m, :], in0=x_[:], scalar=ALPHA, in1=b[:, 0:gn], op0=ALU.mult, op1=ALU.add),
                     R=[x_, b], Wp=[s1])
            rs, nm = ln_stats(k, G, st, s1, gn)
            for m in range(KC):
                k.op("pool", lambda: nc.gpsimd.tensor_tensor(out=s1[:, m, :], in0=s1[:, m, :], in1=rs[:], op=ALU.mult), R=[s1, rs], Wp=[s1])
                k.op("dve", lambda: nc.vector.tensor_tensor(out=s1[:, m, :], in0=s1[:, m, :], in1=nm[:], op=ALU.add), R=[s1, nm], Wp=[s1])
                k.op("act", lambda: nc.scalar.activation(out=s1[:, m, :], in_=s1[:, m, :], func=AF.Identity, scale=gb[:, m:m + 1], bias=gb[:, 32 + m:33 + m]),
                     R=[s1, gb], Wp=[s1])
            yt = [k.sb(st, "yt", [128, 2048], F32) for _ in range(2)]
            yn = 0
            for ti in range(gn // 128):
                for hf in range(2):
                    y_ = yt[yn % 2]
                    yn += 1
                    for q in range(4):
                        b = bank()
                        for j in range(4):
                            m = hf * 16 + q * 4 + j
                            k.op("pe", lambda m=m, j=j: nc.tensor.transpose(b[:, j * 128:(j + 1) * 128], s1[:, m, ti * 128:(ti + 1) * 128], C["c_ident"][:]),
                                 R=[s1, C["c_ident"]], W=[b], sig=(j == 3))
                        if q % 2:
                            k.op("act", lambda: nc.scalar.copy(out=y_[:, q * 512:(q + 1) * 512], in_=b[:, :]), R=[b], Wp=[y_])
                        else:
                            k.op("dve", lambda: nc.vector.tensor_copy(out=y_[:, q * 512:(q + 1) * 512], in_=b[:, :]), R=[b], Wp=[y_])
                    k.dma("pool", I["y"][g0 + ti * 128:g0 + (ti + 1) * 128, hf * 2048:(hf + 1) * 2048], y_[:], R=[y_], Wp=[I["y"]])
        k.barrier()


_CACHE = {}


def _pp(v):
    return np.ascontiguousarray(np.asarray(v, np.float32).reshape(32, 128).T)


def make_in_maps(cfg, n_cores, inp):
    f = lambda a: np.ascontiguousarray(np.asarray(a, dtype=np.float32))
    NS, DFF, FC = cfg.NS, cfg.DFF, cfg.FC
    shared = {}
    shared["lnin"] = np.concatenate([_pp(inp["ln_in_g"]), _pp(inp["ln_in_b"])], axis=1)
    shared["ln1"] = np.concatenate([_pp(inp["ln1_g"][0]), _pp(inp["ln1_b"][0])], axis=1)
    shared["ln2"] = np.concatenate([_pp(inp["ln2_g"][0]), _pp(inp["ln2_b"][0])], axis=1)
    shared["w_in"] = f(inp["w_in"][0]); shared["w_o"] = f(inp["w_o"][0])
    shared["w_up"] = f(inp["w_ffn_up"][0]); shared["w_down"] = f(inp["w_ffn_down"][0])
    cw = np.concatenate([f(inp["conv_qkv_w"][0]), f(inp["conv_qkv_b"])], axis=0)
    shared["convw"] = np.ascontiguousarray(cw.reshape(5, 48, 128).transpose(2, 1, 0).reshape(128, 240))
    fw = np.concatenate([f(inp["ffn_conv_w"][0]), f(inp["ffn_conv_b"])], axis=0)
    shared["ffnw"] = np.ascontiguousarray(fw.reshape(4, 2 * FC, 128).transpose(2, 1, 0).reshape(128, 2 * FC * 4))
    shared["alog"] = np.ascontiguousarray(np.broadcast_to(f(inp["a_log"][0])[None, :], (128, 16)))
    shared["dtb"] = np.ascontiguousarray(np.broadcast_to(f(inp["dt_bias"][0])[None, :], (128, 16)))
    shared["dng"] = f(inp["delta_norm_g"][0]).reshape(128, 1)
    shared["relb"] = f(inp["rel_bias"])
    shared.update(host_consts())
    maps = []
    for c in range(n_cores):
        m = dict(shared)
        sl = slice(c * NS, (c + 1) * NS)
        m["x"] = np.concatenate([f(inp["x_prompt"][c]), f(inp["x_sample"][sl]).reshape(NS * DEC, D)], axis=0)
        m["ck"] = f(inp["cache_attn_k"][0, sl]).reshape(NS * PAST, 512)
        m["cv"] = f(inp["cache_attn_v"][0, sl]).reshape(NS * PAST, 512)
        m["cik"] = f(inp["cache_idx_k"][0, sl]).reshape(NS * PAST, 64)
        m["sdel"] = f(inp["state_delta"][0, sl]).reshape(NS * 16 * 128, 128)
        m["sconv"] = f(inp["state_conv_qkv"][0, sl]).reshape(NS * 3, 6144)
        m["sffn"] = f(inp["state_ffn_conv"][0, sl]).reshape(NS * 2, 2 * DFF)
        maps.append(m)
    return maps


def kernel(**inp):
    B, SEQ = inp["x_prompt"].shape[0], inp["x_prompt"].shape[1]
    DB = inp["x_sample"].shape[0]
    DFF = inp["w_ffn_down"].shape[1]
    n_cores = B
    NS = DB // n_cores
    cfg = Cfg(SEQ, NS, DFF)
    key = (SEQ, NS, DFF)
    if key not in _CACHE:
        _CACHE[key] = build(cfg)
    nc = _CACHE[key]
    maps = make_in_maps(cfg, n_cores, inp)
    res = run_bass_kernel_spmd(nc, maps, core_ids=list(range(n_cores)))
    R = res.results
    return assemble(cfg, n_cores, R)


def assemble(cfg, n_cores, R):
    NS, SEQ, DFF, NSEQ = cfg.NS, cfg.SEQ, cfg.DFF, cfg.NSEQ
    g = lambda n: [np.asarray(R[c][n], dtype=np.float32) for c in range(n_cores)]
    y, ko, vo, iko, so, co, fo = g("y"), g("ko"), g("vo"), g("iko"), g("so"), g("convo"), g("ffno")
    yp = np.stack([a[:SEQ] for a in y])
    ys = np.concatenate([a[SEQ:].reshape(NS, DEC, D) for a in y])
    pk = np.stack([a[:SEQ].reshape(SEQ, 4, 128) for a in ko])[None]
    pv = np.stack([a[:SEQ].reshape(SEQ, 4, 128) for a in vo])[None]
    pik = np.stack([a[:SEQ] for a in iko])[None]
    sk = np.concatenate([a[SEQ:].reshape(NS, DEC, 4, 128) for a in ko])[None]
    sv = np.concatenate([a[SEQ:].reshape(NS, DEC, 4, 128) for a in vo])[None]
    sik = np.concatenate([a[SEQ:].reshape(NS, DEC, 64) for a in iko])[None]
    pd = np.stack([a.reshape(NSEQ, 16, 128, 128)[0] for a in so])[None]
    sd = np.concatenate([a.reshape(NSEQ, 16, 128, 128)[1:] for a in so])[None]
    pc = np.stack([a.reshape(NSEQ, 3, 6144)[0] for a in co])[None]
    sc = np.concatenate([a.reshape(NSEQ, 3, 6144)[1:] for a in co])[None]
    pf = np.stack([a.reshape(NSEQ, 2, 2 * DFF)[0] for a in fo])[None]
    sf = np.concatenate([a.reshape(NSEQ, 2, 2 * DFF)[1:] for a in fo])[None]
    return (yp, ys, pk, pv, pik, pd, pc, pf, sk, sv, sik, sd, sc, sf)
```

```python
import math
from contextlib import ExitStack
import numpy as np
import ml_dtypes
import concourse.bass as bass
import concourse.mybir as mybir
from concourse.bass_utils import run_bass_kernel_spmd

F32 = mybir.dt.float32
BF16 = mybir.dt.bfloat16
AF = mybir.ActivationFunctionType
ALU = mybir.AluOpType

D = 4096
KC = 32
NIN = 12400
HD = 128
PAST = 1024
DEC = 64
EPS = 1e-5
ALPHA = 2.0 ** 0.25
IDX_SCALE = (16 ** -0.5) * (64 ** -0.5)
O_QA, O_KA, O_VA, O_IQ, O_IK, O_IW, O_QKV, O_Z, O_BETA, O_A = 0, 2048, 2560, 3072, 4096, 4160, 4176, 10320, 12368, 12384
NEG = -1.0e30
WIN_PIECES = [(0, 0, 4096), (4096, 4096, 64), (4160, 4096, 64), (4224, 4176, 8192), (12416, 4096, 80), (12496, 12368, 32)]
WIN_PACKED = 12544
P_QA, P_KA, P_VA, P_IQ, P_IK2, P_QKV, P_Z, P_SM1, P_SM2 = 0, 2048, 2560, 3072, 4096, 4224, 10368, 12416, 12496
DBG = False
STOP_C = 99


MUTE = [False]


def stop_at(n):
    if STOP_C <= n:
        MUTE[0] = True


class Reg:
    __slots__ = ("w", "r", "n")

    def __init__(s, n=""):
        s.w = {}
        s.r = {}
        s.n = n


class Tile:
    def __init__(s, t, n):
        s.t = t
        s.reg = Reg(n)

    def __getitem__(s, i):
        return s.t[i]


class Eng:
    def __init__(s, name, h):
        s.name = name
        s.h = h
        s.semidx = None
        s.cnt = 0
        s.known = {}
        s.pending = False
        s.dsems = []
        s.dnext = 0


class K:
    SEM_LIMIT = 30000

    def __init__(s, nc, es):
        s.nc = nc
        s.es = es
        s.sems = []
        s.semmax = []
        s.E = {}
        for n, h in (("pe", nc.tensor), ("act", nc.scalar), ("dve", nc.vector), ("pool", nc.gpsimd), ("sp", nc.sync)):
            e = Eng(n, h)
            s.E[n] = e
            if n != "sp":
                e.semidx = s.newsem()
        for n, cnt in (("sp", 12), ("act", 4), ("pool", 8)):
            s.E[n].dsems = [s.newsem() for _ in range(cnt)]
        s.uid = 0

    def newsem(s):
        h = s.es.enter_context(s.nc.semaphore("sem%d" % len(s.sems)))
        s.sems.append(h)
        s.semmax.append(0)
        return len(s.sems) - 1

    def sb(s, st, name, shape, dt):
        s.uid += 1
        nm = "%s_%d" % (name, s.uid)
        return Tile(st.enter_context(s.nc.sbuf_tensor(nm, list(shape), dt)), nm)

    def ps(s, st, name, shape, dt=F32):
        s.uid += 1
        nm = "%s_%d" % (name, s.uid)
        return Tile(st.enter_context(s.nc.psum_tensor(nm, list(shape), dt)), nm)

    def dram(s, name, shape, dt, kind="Internal"):
        if DBG and kind == "Internal":
            kind = "ExternalOutput"
        t = s.nc.dram_tensor(name, list(shape), dt, kind=kind)
        tl = Tile(t.ap(), name)
        return tl

    def _deps(s, R, W, Wp):
        deps = {}
        for r in R:
            for k, v in r.w.items():
                if deps.get(k, 0) < v:
                    deps[k] = v
        for w in list(W) + list(Wp):
            for k, v in w.w.items():
                if deps.get(k, 0) < v:
                    deps[k] = v
            for k, v in w.r.items():
                if deps.get(k, 0) < v:
                    deps[k] = v
        return deps

    def _waits(s, eng, deps, ename):
        for k, v in deps.items():
            if k == eng.semidx and ename == "pe":
                continue
            if eng.known.get(k, 0) >= v:
                continue
            eng.h.wait_ge(s.sems[k], v)
            eng.known[k] = v

    def _mark(s, t, R, W, Wp):
        k, v = t
        for r in R:
            if r.r.get(k, 0) < v:
                r.r[k] = v
        for w in W:
            w.w = {k: v}
            w.r = {}
        for w in Wp:
            if w.w.get(k, 0) < v:
                w.w[k] = v

    def op(s, e, fn, R=(), W=(), Wp=(), sig=True, Wa=()):
        if MUTE[0]:
            return None
        R = [x.reg if isinstance(x, Tile) else x for x in R]
        Wa = [x.reg if isinstance(x, Tile) else x for x in Wa]
        W = [x.reg if isinstance(x, Tile) else x for x in W]
        Wp = [x.reg if isinstance(x, Tile) else x for x in Wp]
        eng = s.E[e]
        if eng.cnt >= s.SEM_LIMIT and not eng.pending:
            eng.semidx = s.newsem()
            eng.cnt = 0
        s._waits(eng, s._deps(R, W, list(Wp) + list(Wa)), e)
        ins = fn()
        if sig:
            eng.cnt += 1
            ins.then_inc(s.sems[eng.semidx], 1)
            s.semmax[eng.semidx] = eng.cnt
            eng.pending = False
            t = (eng.semidx, eng.cnt)
        else:
            eng.pending = True
            t = (eng.semidx, eng.cnt + 1)
        s._mark(t, R, W, Wp)
        return ins

    def dma(s, q, out, in_, R=(), W=(), Wp=(), **kw):
        if MUTE[0]:
            return None
        R = [x.reg if isinstance(x, Tile) else x for x in R]
        W = [x.reg if isinstance(x, Tile) else x for x in W]
        Wp = [x.reg if isinstance(x, Tile) else x for x in Wp]
        eng = s.E[q]
        si = eng.dsems[eng.dnext % len(eng.dsems)]
        eng.dnext += 1
        deps = s._deps(R, W, Wp)
        cur = s.semmax[si]
        if cur > 0 and deps.get(si, 0) < cur:
            deps[si] = cur
        s._waits(eng, deps, q)
        ins = eng.h.dma_start(out=out, in_=in_, **kw)
        ins.then_inc(s.sems[si], 16)
        s.semmax[si] = cur + 16
        s._mark((si, cur + 16), R, W, Wp)
        return ins

    def barrier(s, engines=("pe", "act", "dve", "pool", "sp")):
        for n in engines:
            eng = s.E[n]
            assert not eng.pending
            for k, v in enumerate(s.semmax):
                if v > 0 and eng.known.get(k, 0) < v and k != eng.semidx:
                    eng.h.wait_ge(s.sems[k], v)
                    eng.known[k] = v


class Cfg:
    def __init__(s, SEQ, NS, DFF):
        s.SEQ, s.NS, s.DFF = SEQ, NS, DFF
        s.FC = DFF // 128
        s.NT = SEQ + NS * DEC
        assert s.NT % 128 == 0 and SEQ % 128 == 0 and DFF % 128 == 0
        s.NSEQ = 1 + NS
        s.seqs = [dict(t0=0, T=SEQ, past=0, si=-1)] + [dict(t0=SEQ + DEC * i, T=DEC, past=PAST, si=i) for i in range(NS)]
        s.TOPK_P = min(256, SEQ // 4)
        s.TOPK_S = min(256, (PAST + DEC) // 4)

    def groups(s, gmax):
        n = -(-s.NT // gmax)
        per = -(-(s.NT // 128) // n) * 128
        out = []
        t = 0
        while t < s.NT:
            g = min(per, s.NT - t)
            out.append((t, g))
            t += g
        return out


def blocks(n, b=512):
    return [(i, min(b, n - i)) for i in range(0, n, b)]


def t5_bucket_np(rel):
    rel = np.asarray(rel, np.int64)
    half, max_exact = 16, 8
    side = np.where(rel > 0, half, 0)
    n = np.abs(rel)
    nf = np.maximum(n, 1).astype(np.float32)
    large = max_exact + (np.log(nf / np.float32(max_exact)) / np.float32(math.log(128 / max_exact))
                         * np.float32(half - max_exact)).astype(np.int32)
    large = np.minimum(large, half - 1)
    return side + np.where(n < max_exact, n, large)


def host_consts():
    c = {}
    c["c_ident"] = np.eye(128, dtype=np.float32)
    c["c_anti"] = np.eye(128, dtype=np.float32)[::-1].copy()
    i = np.arange(128)
    same = (i[:, None] // 64) == (i[None, :] // 64)
    c["c_cum"] = (same & (i[:, None] <= i[None, :])).astype(np.float32)
    c["c_blk"] = same.astype(np.float32)
    c["c_nmL"] = np.where(same & (i[:, None] > i[None, :]), 0.0, -1e4).astype(np.float32)
    c["c_nmT"] = np.where(same & (i[None, :] >= i[:, None]), 0.0, -1e4).astype(np.float32)
    c["c_strict"] = (same & (i[:, None] > i[None, :])).astype(np.float32)
    rel = np.arange(384) - 255
    bk = t5_bucket_np(rel)
    oh = np.zeros((32, 384), np.float32)
    oh[bk, np.arange(384)] = 1.0
    oh[15, :] -= 1.0
    c["c_oh"] = oh
    return c


CONST_SHAPES = {"c_ident": [128, 128], "c_anti": [128, 128], "c_cum": [128, 128], "c_blk": [128, 128],
                "c_nmL": [128, 128], "c_nmT": [128, 128], "c_strict": [128, 128], "c_oh": [32, 384]}


def build(cfg, phases="ABCDEF"):
    nc = bass.Bass("TRN2", target_bir_lowering=False)
    es = ExitStack()
    with es:
        k = K(nc, es)
        _program(k, cfg, phases)
    return nc


def _program(k, cfg, phases):
    nc = k.nc
    NT, NS, DFF, FC, NSEQ = cfg.NT, cfg.NS, cfg.DFF, cfg.FC, cfg.NSEQ
    I = {}

    def din(name, shape):
        I[name] = k.dram(name, shape, F32, kind="ExternalInput")
        return I[name]

    def dout(name, shape):
        I[name] = k.dram(name, shape, F32, kind="ExternalOutput")
        return I[name]

    din("x", [NT, D])
    din("ck", [NS * PAST, 512]); din("cv", [NS * PAST, 512]); din("cik", [NS * PAST, 64])
    din("sdel", [NS * 16 * 128, 128]); din("sconv", [NS * 3, 6144]); din("sffn", [NS * 2, 2 * DFF])
    din("lnin", [128, 64]); din("ln1", [128, 64]); din("ln2", [128, 64])
    din("w_in", [D, NIN]); din("w_o", [D, D]); din("w_up", [D, 2 * DFF]); din("w_down", [DFF, D])
    din("convw", [128, 48 * 5]); din("ffnw", [128, 2 * FC * 4])
    din("alog", [128, 16]); din("dtb", [128, 16]); din("dng", [128, 1]); din("relb", [32, 16])
    for n, sh in CONST_SHAPES.items():
        din(n, sh)
    dout("y", [NT, D]); dout("ko", [NT, 512]); dout("vo", [NT, 512]); dout("iko", [NT, 64])
    dout("so", [NSEQ * 16 * 128, 128]); dout("convo", [NSEQ * 3, 6144]); dout("ffno", [NSEQ * 2, 2 * DFF])
    S = {}
    S["xnT"] = k.dram("s_xnT", [D, NT], F32)
    S["qaT"] = k.dram("s_qaT", [2048, NT], BF16)
    S["kaT"] = k.dram("s_kaT", [512, NT], BF16)
    S["vbf"] = k.dram("s_vbf", [NT, 512], BF16)
    S["iqT"] = k.dram("s_iqT", [1024, NT], BF16)
    S["ikT2"] = k.dram("s_ikT2", [128, NT], BF16)
    S["iw"] = k.dram("s_iw", [NT, 16], F32)
    S["ba"] = k.dram("s_ba", [NT, 32], F32)
    S["qkvT"] = k.dram("s_qkvT", [6144, NT], F32)
    S["zT"] = k.dram("s_zT", [2048, NT], F32)
    S["aT"] = k.dram("s_aT", [D, NT], BF16)
    S["x1T"] = k.dram("s_x1T", [D, NT], F32)
    S["x1b"] = k.dram("s_x1b", [D, NT], BF16)
    S["actT"] = k.dram("s_actT", [DFF, NT], BF16)
    S["Fd"] = k.dram("s_Fd", [16, 384 + 128], F32)
    S["b_w_in"] = k.dram("s_bwin", [WIN_PACKED // 256, 1, 128, 32, 256], BF16)
    S["b_w_o"] = k.dram("s_bwo", [D // 256, 1, 128, 32, 256], BF16)
    S["b_w_up"] = k.dram("s_bwup", [2 * DFF // 256, 1, 128, 32, 256], BF16)
    S["b_w_down"] = k.dram("s_bwdn", [D // 256, len(kgroups(FC)), 128, 32, 256], BF16)

    with ExitStack() as gs:
        C = {}
        for n, sh in CONST_SHAPES.items():
            C[n] = k.sb(gs, n, sh, F32)
            k.dma("sp", C[n][:], I[n][:], W=[C[n]])
        ident_b = k.sb(gs, "identb", [128, 128], BF16)
        k.op("pool", lambda: nc.gpsimd.tensor_copy(out=ident_b[:], in_=C["c_ident"][:]), R=[C["c_ident"]], W=[ident_b])
        ones_f = k.sb(gs, "onesf", [128, 128], F32)
        k.op("pool", lambda: nc.gpsimd.memset(ones_f[:], 1.0), W=[ones_f])
        ones_b = k.sb(gs, "onesb", [128, 128], BF16)
        k.op("pool", lambda: nc.gpsimd.memset(ones_b[:], 1.0), W=[ones_b])
        G = dict(C=C, ident_b=ident_b, ones_f=ones_f, ones_b=ones_b, I=I, S=S)
        psum = [k.ps(gs, "bank%d" % i, [128, 512]) for i in range(8)]
        G["psum"] = psum
        G["pn"] = 0

        def bank():
            b = psum[G["pn"] % 8]
            G["pn"] += 1
            return b
        G["bank"] = bank
        G["alt"] = 0

        phase_W(k, cfg, G)
        k.barrier()
        if "A" in phases:
            phase_A(k, cfg, G)
            k.barrier()
        if "B" in phases:
            phase_B(k, cfg, G)
            k.barrier()
        if "C" in phases:
            phase_C(k, cfg, G)
            MUTE[0] = False
            k.barrier()
        if "D" in phases:
            phase_D(k, cfg, G)
            k.barrier()
        if "E" in phases:
            phase_E(k, cfg, G)
            k.barrier()
        if "F" in phases:
            phase_F(k, cfg, G)
        k.barrier()


def evac_engine(G):
    G["alt"] += 1
    return "act" if G["alt"] % 2 else "dve"


class WTiles:
    def __init__(s, k, st, nslot=3):
        s.k = k
        s.slots = [k.sb(st, "wt", [128, 32, 256], BF16) for _ in range(nslot)]
        s.tags = [None] * nslot
        s.n = 0

    def get(s, scr, tile, kg=0, kcn=32):
        tag = (scr.reg.n, tile, kg)
        for i, t in enumerate(s.tags):
            if t == tag:
                return s.slots[i]
        i = s.n % len(s.slots)
        s.n += 1
        s.tags[i] = tag
        wb = s.slots[i]
        s.k.dma("sp", wb[:, 0:kcn, :], scr[tile, kg, :, 0:kcn, :], R=[scr], W=[wb])
        return wb


def kgroups(kctot):
    return [(i, min(32, kctot - i)) for i in range(0, kctot, 32)]


def w_units(k, cfg, G, names, st, kstep=4, gcols=512):
    nc = k.nc
    I, S = G["I"], G["S"]
    allspecs = {"w_in": (I["w_in"], WIN_PIECES, WIN_PACKED, KC), "w_o": (I["w_o"], [(0, 0, D)], D, KC),
                "w_up": (I["w_up"], [(0, 0, 2 * cfg.DFF)], 2 * cfg.DFF, KC), "w_down": (I["w_down"], [(0, 0, D)], D, cfg.FC)}
    nt = gcols // 256
    stg = [k.sb(st, "wstg", [128, kstep, gcols], F32) for _ in range(2)]
    sbf = [k.sb(st, "wsbf", [128, nt, kstep, 256], BF16) for _ in range(2)]
    units = []
    for name in names:
        src, pieces, ncol, kctot = allspecs[name]
        dst = S["b_" + name]
        for g0 in range(0, ncol, gcols):
            gw = min(gcols, ncol - g0)
            for kgi, (kg0, kgn) in enumerate(kgroups(kctot)):
                for k8 in range(0, kgn, kstep):
                    units.append((src, pieces, dst, g0, gw, kgi, kg0, k8, min(kstep, kgn - k8)))

    def load(n):
        src, pieces, dst, g0, gw, kgi, kg0, k8, kn = units[n]
        r0 = (kg0 + k8) * 128
        st_ = stg[n % 2]
        first = True
        for (d0, s0, pn) in pieces:
            lo, hi = max(d0, g0), min(d0 + pn, g0 + gw)
            if lo < hi:
                srcap = src[r0:r0 + kn * 128, s0 + lo - d0:s0 + hi - d0].rearrange("(kc p) n -> p kc n", p=128)
                if first:
                    k.dma("sp", st_[:, 0:kn, lo - g0:hi - g0], srcap, W=[st_])
                else:
                    k.dma("sp", st_[:, 0:kn, lo - g0:hi - g0], srcap, Wp=[st_])
                first = False

    def finish(n):
        src, pieces, dst, g0, gw, kgi, kg0, k8, kn = units[n]
        st_ = stg[n % 2]
        sb_ = sbf[n % 2]
        nt4 = gw // 256
        iv = st_[:, 0:kn, 0:gw].rearrange("p k (t c) -> p t k c", c=256)
        ov = sb_[:, 0:nt4, 0:kn, :]
        e = G.get("wcast", ("act", "dve", "pool"))
        e = e[n % len(e)]
        if e == "act":
            k.op("act", lambda: nc.scalar.copy(out=ov, in_=iv), R=[st_], W=[sb_])
        elif e == "dve":
            k.op("dve", lambda: nc.vector.tensor_copy(out=ov, in_=iv), R=[st_], W=[sb_])
        else:
            k.op("pool", lambda: nc.gpsimd.tensor_copy(out=ov, in_=iv), R=[st_], W=[sb_])
        t0 = g0 // 256
        k.dma("pool", dst[t0:t0 + nt4, kgi, :, k8:k8 + kn, :].rearrange("t p k c -> p t k c"), ov, R=[sb_], Wp=[dst])

    for n in range(len(units)):
        load(n)
        if n >= 1:
            finish(n - 1)
        yield n
    finish(len(units) - 1)
    yield len(units)


def phase_W(k, cfg, G):
    with ExitStack() as st:
        G["wcast"] = ("act", "dve")
        for _ in w_units(k, cfg, G, ["w_in"], st, kstep=8, gcols=1024):
            pass


def phase_A(k, cfg, G):
    nc = k.nc
    I, S, C = G["I"], G["S"], G["C"]
    NT = cfg.NT
    bank = G["bank"]
    for (g0, gn) in cfg.groups(1152):
        with ExitStack() as st:
            xnT = k.sb(st, "xnT", [128, KC, gn], BF16)
            gb = k.sb(st, "lnin", [128, 64], F32)
            k.dma("sp", gb[:], I["lnin"][:], W=[gb])
            with ExitStack() as s1:
                xs = [k.sb(s1, "xs", [128, D], F32) for _ in range(2)]
                xf = [k.sb(s1, "xf", [128, KC, 128], F32) for _ in range(2)]
                stt = [k.sb(s1, "stt", [128, 8, 6], F32) for _ in range(2)]
                mv = [k.sb(s1, "mv", [128, 4], F32) for _ in range(2)]
                for ti in range(gn // 128):
                    t0 = g0 + ti * 128
                    x_, f_, st_, mv_ = xs[ti % 2], xf[ti % 2], stt[ti % 2], mv[ti % 2]
                    k.dma("sp", x_[:], I["x"][t0:t0 + 128, :], W=[x_])
                    for j in range(8):
                        k.op("dve", lambda j=j: nc.vector.bn_stats(out=st_[:, j, :], in_=x_[:, j * 512:(j + 1) * 512]),
                             R=[x_], Wp=[st_] if j else (), W=() if j else [st_])
                    k.op("dve", lambda: nc.vector.bn_aggr(out=mv_[:, 0:2], in_=st_[:].rearrange("p a b -> p (a b)")), R=[st_], W=[mv_])
                    k.op("act", lambda: nc.scalar.activation(out=mv_[:, 2:3], in_=mv_[:, 1:2], func=AF.Sqrt, bias=EPS, scale=1.0), R=[mv_], Wp=[mv_])
                    k.op("dve", lambda: nc.vector.reciprocal(out=mv_[:, 3:4], in_=mv_[:, 2:3]), R=[mv_], Wp=[mv_])
                    k.op("dve", lambda: nc.vector.tensor_scalar(out=x_[:], in0=x_[:], scalar1=mv_[:, 0:1], scalar2=mv_[:, 3:4],
                                                                op0=ALU.subtract, op1=ALU.mult), R=[mv_, x_], W=[x_])
                    for q in range(8):
                        b = bank()
                        for j in range(4):
                            kc = q * 4 + j
                            k.op("pe", lambda kc=kc, j=j: nc.tensor.transpose(b[:, j * 128:(j + 1) * 128], x_[:, kc * 128:(kc + 1) * 128], C["c_ident"][:]),
                                 R=[x_, C["c_ident"]], W=[b], sig=(j == 3))
                        for j in range(4):
                            kc = q * 4 + j
                            if (kc % 2) == 0:
                                k.op("act", lambda kc=kc, j=j: nc.scalar.activation(out=f_[:, kc, :], in_=b[:, j * 128:(j + 1) * 128], func=AF.Identity,
                                                                                   scale=gb[:, kc:kc + 1], bias=gb[:, 32 + kc:33 + kc]),
                                     R=[b, gb], Wp=[f_])
                            else:
                                k.op("dve", lambda kc=kc, j=j: nc.vector.tensor_scalar(out=f_[:, kc, :], in0=b[:, j * 128:(j + 1) * 128],
                                                                                      scalar1=gb[:, kc:kc + 1], scalar2=gb[:, 32 + kc:33 + kc],
                                                                                      op0=ALU.mult, op1=ALU.add),
                                     R=[b, gb], Wp=[f_])
                    k.op("pool", lambda: nc.gpsimd.tensor_copy(out=xnT[:, :, ti * 128:(ti + 1) * 128], in_=f_[:]), R=[f_], Wp=[xnT])
                    k.dma("pool", S["xnT"][:].rearrange("(kc p) t -> p kc t", p=128)[:, :, t0:t0 + 128], f_[:], R=[f_], Wp=[S["xnT"]])
            k.barrier()
            with ExitStack() as s2:
                ws = WTiles(k, s2, nslot=3)
                osf = [k.sb(s2, "osf", [128, gn], F32) for _ in range(2)]
                osb = [k.sb(s2, "osb", [128, gn], BF16) for _ in range(2)]
                otk = [k.sb(s2, "otk", [128, gn // 128, 128], F32) for _ in range(2)]
                otb = [k.sb(s2, "otb", [128, gn // 128, 128], BF16) for _ in range(2)]
                cnt = {"f": 0, "b": 0, "t": 0}

                def fm_job(pc, m, dst, drow, mode):
                    wb = ws.get(S["b_w_in"], pc // 256)
                    sub = pc % 256
                    if mode == "f32" or mode == "silu":
                        o = osf[cnt["f"] % 2]; cnt["f"] += 1
                    else:
                        o = osb[cnt["b"] % 2]; cnt["b"] += 1
                    for (b0, bn) in blocks(gn):
                        b = bank()
                        for kc in range(KC):
                            k.op("pe", lambda kc=kc: nc.tensor.matmul(b[0:m, 0:bn], lhsT=wb[:, kc, sub:sub + m], rhs=xnT[:, kc, b0:b0 + bn],
                                                                       start=(kc == 0), stop=(kc == KC - 1)),
                                 R=[wb, xnT], W=[b], sig=(kc == KC - 1))
                        e = evac_engine(G)
                        if mode == "silu":
                            k.op("act", lambda: nc.scalar.activation(out=o[0:m, b0:b0 + bn], in_=b[0:m, 0:bn], func=AF.Silu), R=[b], Wp=[o])
                        elif mode == "qs":
                            k.op("act", lambda: nc.scalar.mul(o[0:m, b0:b0 + bn], b[0:m, 0:bn], HD ** -0.5), R=[b], Wp=[o])
                        elif e == "act":
                            k.op("act", lambda: nc.scalar.copy(out=o[0:m, b0:b0 + bn], in_=b[0:m, 0:bn]), R=[b], Wp=[o])
                        else:
                            k.op("dve", lambda: nc.vector.tensor_copy(out=o[0:m, b0:b0 + bn], in_=b[0:m, 0:bn]), R=[b], Wp=[o])
                    k.dma("pool", dst[drow:drow + m, g0:g0 + gn], o[0:m, :], R=[o], Wp=[dst])

                def tm_job(pc, ncols, outs):
                    wb = ws.get(S["b_w_in"], pc // 256)
                    sub = pc % 256
                    o = otk[cnt["t"] % 2]
                    ob = otb[cnt["t"] % 2]
                    cnt["t"] += 1
                    for ti in range(gn // 128):
                        b = bank()
                        for kc in range(KC):
                            k.op("pe", lambda kc=kc: nc.tensor.matmul(b[:, 0:ncols], lhsT=xnT[:, kc, ti * 128:(ti + 1) * 128], rhs=wb[:, kc, sub:sub + ncols],
                                                                       start=(kc == 0), stop=(kc == KC - 1)),
                                 R=[wb, xnT], W=[b], sig=(kc == KC - 1))
                        k.op("dve", lambda: nc.vector.tensor_copy(out=o[:, ti, 0:ncols], in_=b[:, 0:ncols]), R=[b], Wp=[o])
                    for (dst, dc0, sc0, n, dt) in outs:
                        dview = dst[g0:g0 + gn, dc0:dc0 + n].rearrange("(ti p) n -> p ti n", p=128)
                        if dt == "bf16":
                            k.op("pool", lambda: nc.gpsimd.tensor_copy(out=ob[:, :, sc0:sc0 + n], in_=o[:, :, sc0:sc0 + n]), R=[o], W=[ob])
                            k.dma("pool", dview, ob[:, :, sc0:sc0 + n], R=[ob], Wp=[dst])
                        else:
                            k.dma("pool", dview, o[:, :, sc0:sc0 + n], R=[o], Wp=[dst])

                for c in range(16):
                    fm_job(P_QA + c * 128, 128, S["qaT"], c * 128, "qs")
                for c in range(4):
                    fm_job(P_KA + c * 128, 128, S["kaT"], c * 128, "bf16")
                for c in range(4):
                    tm_job(P_KA + c * 128, 128, [(I["ko"], c * 128, 0, 128, "f32")])
                for c in range(4):
                    tm_job(P_VA + c * 128, 128, [(I["vo"], c * 128, 0, 128, "f32"), (S["vbf"], c * 128, 0, 128, "bf16")])
                for c in range(8):
                    fm_job(P_IQ + c * 128, 128, S["iqT"], c * 128, "bf16")
                fm_job(P_IK2, 128, S["ikT2"], 0, "bf16")
                for c in range(48):
                    fm_job(P_QKV + c * 128, 128, S["qkvT"], c * 128, "f32")
                for c in range(16):
                    fm_job(P_Z + c * 128, 128, S["zT"], c * 128, "silu")
                tm_job(P_SM1, 80, [(I["iko"], 0, 0, 64, "f32"), (S["iw"], 0, 64, 16, "f32")])
                tm_job(P_SM2, 32, [(S["ba"], 0, 0, 32, "f32")])
            k.barrier()


def phase_B(k, cfg, G):
    nc = k.nc
    I, S, C = G["I"], G["S"], G["C"]
    bank = G["bank"]
    ident, ident_b, ones_b = C["c_ident"], G["ident_b"], G["ones_b"]
    NS = cfg.NS
    SKMAX = max(cfg.SEQ, PAST + 128)
    KTMAX = SKMAX // 128
    with ExitStack() as pst:
        sb = lambda n, sh, dt=F32: k.sb(pst, n, sh, dt)
        biasT = sb("biasT", [128, 2, 16, 128], BF16)
        with ExitStack() as s0:
            relb = k.sb(s0, "relb", [32, 16], F32)
            Fs = k.sb(s0, "Fs", [16, 512], F32)
            XT = k.sb(s0, "XT", [128, 2, 16, 128], F32)
            k.dma("sp", relb[:], I["relb"][:], W=[relb])
            k.op("pool", lambda: nc.gpsimd.memset(Fs[:], 0.0), W=[Fs])
            b = bank()
            k.op("pe", lambda: nc.tensor.matmul(b[0:16, 0:384], lhsT=relb[:, :], rhs=C["c_oh"][:, :], start=True, stop=True), R=[relb, C["c_oh"]], W=[b])
            k.op("dve", lambda: nc.vector.tensor_copy(out=Fs[:, 0:384], in_=b[0:16, 0:384]), R=[b], Wp=[Fs])
            k.dma("sp", S["Fd"][:], Fs[:], R=[Fs], W=[S["Fd"]])
            fd_t = S["Fd"][:].tensor
            for w, off in ((0, 128), (1, 0)):
                src = bass.AP(tensor=fd_t, offset=off, ap=[[1, 128], [512, 16], [1, 128]])
                k.dma("sp", XT[:, w, :, :], src, R=[S["Fd"]], Wp=[XT])
            for w in range(2):
                for h4 in range(0, 16, 4):
                    b = bank()
                    for j in range(4):
                        k.op("pe", lambda: nc.tensor.matmul(b[:, j * 128:(j + 1) * 128], lhsT=XT[:, w, h4 + j, :], rhs=C["c_anti"][:], start=True, stop=True),
                             R=[XT, C["c_anti"]], W=[b], sig=(j == 3))
                    k.op("dve", lambda: nc.vector.tensor_copy(out=biasT[:, w, h4:h4 + 4, :], in_=b[:, :].rearrange("p (a b) -> p a b", b=128)), R=[b], Wp=[biasT])
        k.barrier()
        kT = sb("kT", [128, 4, SKMAX], BF16)
        vv = sb("vv", [128, KTMAX, 4, 128], BF16)
        ik2 = sb("ik2", [128, SKMAX], BF16)
        cst = sb("cst", [128, 8, 512], F32)
        cikst = sb("cikst", [128, 8, 128], F32)
        qT = [sb("qT", [128, 16, 128], BF16) for _ in range(2)]
        iq = [sb("iq", [128, 8, 128], BF16) for _ in range(2)]
        iw = [sb("iw", [128, 16], F32) for _ in range(2)]
        index = sb("index", [128, SKMAX]); work = sb("work", [128, SKMAX]); mask01 = sb("mask01", [128, SKMAX])
        rr_ = [sb("relu", [128, 512]) for _ in range(2)]
        m8 = sb("m8", [128, 8]); thr = sb("thr", [128, 1])
        maskT = sb("maskT", [128, KTMAX, 128], BF16)
        pt = [sb("pt", [128, 512], BF16) for _ in range(3)]
        rcp = sb("rcp", [128, 512])
        oa = [sb("oa", [128, 16, 128], BF16) for _ in range(2)]
        G["wcast"] = ("act",)
        wgen = w_units(k, cfg, G, ["w_o", "w_up", "w_down"], pst, kstep=4, gcols=512)
        n_units = 0
        for nm_, kct_ in (("w_o", KC), ("w_up", KC), ("w_down", cfg.FC)):
            ncol_ = {"w_o": D, "w_up": 2 * cfg.DFF, "w_down": D}[nm_]
            n_units += (-(-ncol_ // 512)) * sum(-(-kn_ // 4) for (_, kn_) in kgroups(kct_))
        n_iter = sum(4 * (-(-sq_["T"] // 128)) for sq_ in cfg.seqs)
        per_iter = -(-n_units // n_iter)

        def wstep(cnt):
            for _ in range(cnt):
                try:
                    next(wgen)
                except StopIteration:
                    return
        qn = 0
        for qi, sq in enumerate(cfg.seqs):
            t0, T, si, past = sq["t0"], sq["T"], sq["si"], sq["past"]
            SK = past + T
            if si < 0:
                k.dma("sp", kT[:, :, 0:T], S["kaT"][:, t0:t0 + T].rearrange("(g p) t -> p g t", p=128), R=[S["kaT"]], W=[kT])
                k.dma("sp", vv[:, 0:T // 128, :, :], S["vbf"][t0:t0 + T, :].rearrange("(kt p) (g d) -> p kt g d", p=128, d=128), R=[S["vbf"]], W=[vv])
                k.dma("sp", ik2[:, 0:T], S["ikT2"][:, t0:t0 + T], R=[S["ikT2"]], W=[ik2])
            else:
                k.dma("sp", cst[:], I["ck"][si * PAST:(si + 1) * PAST, :].rearrange("(kt p) n -> p kt n", p=128), W=[cst])
                for kt in range(8):
                    b = bank()
                    for g in range(4):
                        k.op("pe", lambda: nc.tensor.transpose(b[:, g * 128:(g + 1) * 128], cst[:, kt, g * 128:(g + 1) * 128], ident[:]), R=[cst, ident], W=[b], sig=(g == 3))
                    k.op("act", lambda: nc.scalar.copy(out=kT[:, :, kt * 128:(kt + 1) * 128], in_=b[:, :].rearrange("p (a b) -> p a b", b=128)), R=[b], Wp=[kT])
                k.dma("sp", cst[:], I["cv"][si * PAST:(si + 1) * PAST, :].rearrange("(kt p) n -> p kt n", p=128), W=[cst])
                k.op("pool", lambda: nc.gpsimd.tensor_copy(out=vv[:, 0:8, :, :].rearrange("p a g d -> p a (g d)"), in_=cst[:]), R=[cst], Wp=[vv])
                ciksrc = I["cik"][si * PAST:(si + 1) * PAST, :].rearrange("(kt p) n -> p kt n", p=128)
                k.dma("sp", cikst[:, :, 0:64], ciksrc, W=[cikst])
                k.dma("sp", cikst[:, :, 64:128], ciksrc, Wp=[cikst])
                for k4 in range(0, 8, 4):
                    b = bank()
                    for j in range(4):
                        k.op("pe", lambda: nc.tensor.transpose(b[:, j * 128:(j + 1) * 128], cikst[:, k4 + j, :], ident[:]), R=[cikst, ident], W=[b], sig=(j == 3))
                    k.op("dve", lambda: nc.vector.tensor_copy(out=ik2[:, k4 * 128:(k4 + 4) * 128], in_=b[:, :]), R=[b], Wp=[ik2])
                k.dma("sp", kT[:, :, PAST:PAST + T], S["kaT"][:, t0:t0 + T].rearrange("(g p) t -> p g t", p=128), R=[S["kaT"]], Wp=[kT])
                k.dma("sp", vv[0:T, 8, :, :], S["vbf"][t0:t0 + T, :].rearrange("p (g d) -> p g d", d=128), R=[S["vbf"]], Wp=[vv])
                k.dma("sp", ik2[:, PAST:PAST + T], S["ikT2"][:, t0:t0 + T], R=[S["ikT2"]], Wp=[ik2])
            topk = cfg.TOPK_P if si < 0 else cfg.TOPK_S
            for qt in range(-(-T // 128)):
                nq = min(128, T - qt * 128)
                ta = t0 + qt * 128
                SKq = past + qt * 128 + nq if si < 0 else SK
                KTq = -(-SKq // 128)
                q_, iq_, iw_, oa_ = qT[qn % 2], iq[qn % 2], iw[qn % 2], oa[qn % 2]
                qn += 1
                k.dma("sp", q_[:, :, 0:nq], S["qaT"][:, ta:ta + nq].rearrange("(h p) t -> p h t", p=128), R=[S["qaT"]], W=[q_])
                k.dma("sp", iq_[:, :, 0:nq], S["iqT"][:, ta:ta + nq].rearrange("(h p) t -> p h t", p=128), R=[S["iqT"]], W=[iq_])
                k.dma("sp", iw_[0:nq, :], S["iw"][ta:ta + nq, :], R=[S["iw"]], W=[iw_])
                k.op("pool", lambda: nc.gpsimd.tensor_scalar(out=iw_[0:nq, :], in0=iw_[0:nq, :], scalar1=IDX_SCALE, scalar2=None, op0=ALU.mult), R=[iw_], W=[iw_])
                rn = 0
                for (c0, cn) in blocks(SKq):
                    for hp in range(8):
                        for half in range(2):
                            h = hp * 2 + half
                            pr = slice(half * 64, half * 64 + 64)
                            b = bank()
                            k.op("pe", lambda: nc.tensor.matmul(b[0:nq, 0:cn], lhsT=iq_[pr, hp, 0:nq], rhs=ik2[pr, c0:c0 + cn], start=True, stop=True), R=[iq_, ik2], W=[b])
                            r_ = rr_[rn % 2]
                            rn += 1
                            k.op("act", lambda: nc.scalar.activation(out=r_[0:nq, 0:cn], in_=b[0:nq, 0:cn], func=AF.Relu), R=[b], W=[r_])
                            if h == 0:
                                k.op("dve", lambda: nc.vector.tensor_scalar(out=index[0:nq, c0:c0 + cn], in0=r_[0:nq, 0:cn], scalar1=iw_[0:nq, 0:1], scalar2=None, op0=ALU.mult),
                                     R=[r_, iw_], Wp=[index])
                            else:
                                k.op("dve", lambda: nc.vector.scalar_tensor_tensor(out=index[0:nq, c0:c0 + cn], in0=r_[0:nq, 0:cn], scalar=iw_[0:nq, h:h + 1],
                                                                                   in1=index[0:nq, c0:c0 + cn], op0=ALU.mult, op1=ALU.add), R=[r_, iw_, index], Wp=[index])
                if si < 0:
                    k.op("dve", lambda: nc.vector.memset(index[0:64, SKq - 64:SKq], NEG), R=[index], Wp=[index])
                if SKq > topk:
                    nr = topk // 8
                    for rd in range(nr):
                        srcw = index if rd == 0 else work
                        k.op("dve", lambda: nc.vector.max(out=m8[0:nq, :], in_=srcw[0:nq, 0:SKq]), R=[srcw], W=[m8])
                        if rd < nr - 1:
                            k.op("dve", lambda: nc.vector.match_replace(out=work[0:nq, 0:SKq], in_to_replace=m8[0:nq, :], in_values=srcw[0:nq, 0:SKq], imm_value=NEG),
                                 R=[srcw, m8], W=[work])
                    k.op("dve", lambda: nc.vector.tensor_scalar(out=thr[0:nq, :], in0=m8[0:nq, 7:8], scalar1=-1.0e29, scalar2=None, op0=ALU.max), R=[m8], W=[thr])
                    k.op("dve", lambda: nc.vector.tensor_scalar(out=mask01[0:nq, 0:SKq], in0=index[0:nq, 0:SKq], scalar1=thr[0:nq, 0:1], scalar2=None, op0=ALU.is_ge),
                         R=[index, thr], W=[mask01])
                else:
                    k.op("dve", lambda: nc.vector.tensor_scalar(out=mask01[0:nq, 0:SKq], in0=index[0:nq, 0:SKq], scalar1=-1.0e29, scalar2=None, op0=ALU.is_ge),
                         R=[index], W=[mask01])
                for k4 in range(0, KTq, 4):
                    b = bank()
                    n4 = min(4, KTq - k4)
                    for j in range(n4):
                        kt = k4 + j
                        ks = min(128, SKq - kt * 128)
                        k.op("pe", lambda: nc.tensor.transpose(b[0:ks, j * 128:j * 128 + nq], mask01[0:nq, kt * 128:kt * 128 + ks], ident[0:nq, 0:nq]), R=[mask01, ident], W=[b], sig=(j == n4 - 1))
                    for j in range(n4):
                        kt = k4 + j
                        ks = min(128, SKq - kt * 128)
                        k.op("act", lambda: nc.scalar.copy(out=maskT[0:ks, kt, 0:nq], in_=b[0:ks, j * 128:j * 128 + nq]), R=[b], Wp=[maskT])
                pn = 0
                for g in range(4):
                    bO, bR = (G["psum"][4], G["psum"][5]) if g % 2 == 0 else (G["psum"][6], G["psum"][7])
                    for kt in range(KTq):
                        ks = min(128, SKq - kt * 128)
                        near = kt >= KTq - 2
                        w = 0 if kt == KTq - 1 else 1
                        wstep(1)
                        bl = G["psum"][pn % 4]
                        k.op("pe", lambda: nc.tensor.matmul(bl[0:ks, 0:4 * nq], lhsT=kT[:, g, kt * 128:kt * 128 + ks], rhs=q_[:, 4 * g:4 * g + 4, 0:nq], start=True, stop=not near),
                             R=[kT, q_], W=[bl], sig=not near)
                        if near:
                            k.op("pe", lambda: nc.tensor.matmul(bl[0:ks, 0:4 * nq], lhsT=ident_b[:, 0:ks], rhs=biasT[:, w, 4 * g:4 * g + 4, 0:nq], start=False, stop=True),
                                 R=[ident_b, biasT], W=[bl])
                        p_ = pt[pn % 3]
                        pn += 1
                        k.op("act", lambda: nc.scalar.activation(out=p_[0:ks, 0:4 * nq], in_=bl[0:ks, 0:4 * nq], func=AF.Exp), R=[bl], W=[p_])
                        pv = p_[0:ks, 0:4 * nq].rearrange("p (a b) -> p a b", b=nq)
                        k.op("pool", lambda: nc.gpsimd.tensor_tensor(out=pv, in0=pv, in1=maskT[0:ks, kt, 0:nq].unsqueeze(1).to_broadcast([ks, 4, nq]), op=ALU.mult),
                             R=[p_, maskT], W=[p_])
                        k.op("pe", lambda: nc.tensor.matmul(bO[:, 0:4 * nq], lhsT=vv[0:ks, kt, g, :], rhs=p_[0:ks, 0:4 * nq], start=(kt == 0), stop=(kt == KTq - 1)),
                             R=[vv, p_], W=[bO], sig=False)
                        k.op("pe", lambda: nc.tensor.matmul(bR[:, 0:4 * nq], lhsT=ones_b[0:ks, :], rhs=p_[0:ks, 0:4 * nq], start=(kt == 0), stop=(kt == KTq - 1)),
                             R=[ones_b, p_], W=[bR], sig=True)
                    k.op("dve", lambda: nc.vector.reciprocal(out=rcp[:, 0:4 * nq], in_=bR[:, 0:4 * nq]), R=[bR], W=[rcp])
                    k.op("dve", lambda: nc.vector.tensor_tensor(out=oa_[:, 4 * g:4 * g + 4, 0:nq], in0=bO[:, 0:4 * nq].rearrange("p (a b) -> p a b", b=nq),
                                                                in1=rcp[:, 0:4 * nq].rearrange("p (a b) -> p a b", b=nq), op=ALU.mult), R=[bO, rcp], Wp=[oa_])
                k.dma("pool", S["aT"][0:2048, ta:ta + nq].rearrange("(h p) t -> p h t", p=128), oa_[:, :, 0:nq], R=[oa_], Wp=[S["aT"]])
        wstep(10 ** 9)


def phase_C(k, cfg, G):
    nc = k.nc
    I, S, C = G["I"], G["S"], G["C"]
    bank = G["bank"]
    ones_f = G["ones_f"]
    ident = C["c_ident"]
    NS, NSEQ = cfg.NS, cfg.NSEQ
    HG = 4
    with ExitStack() as pst:
        sb = lambda n, sh, dt=F32: k.sb(pst, n, sh, dt)
        cw = sb("convw", [128, 48, 5])
        k.dma("sp", cw[:].rearrange("p a b -> p (a b)"), I["convw"][:], W=[cw])
        nea = sb("nea", [128, 16]); dtb = sb("dtb", [128, 16]); dng = sb("dng", [128, 1])
        k.dma("sp", nea[:], I["alog"][:], W=[nea])
        k.dma("sp", dtb[:], I["dtb"][:], W=[dtb])
        k.dma("sp", dng[:], I["dng"][:], W=[dng])
        k.op("act", lambda: nc.scalar.activation(out=nea[:], in_=nea[:], func=AF.Exp), R=[nea], W=[nea])
        k.op("pool", lambda: nc.gpsimd.tensor_scalar(out=nea[:], in0=nea[:], scalar1=-1.0, scalar2=None, op0=ALU.mult), R=[nea], W=[nea])
        cH = sb("cH", [128, 48, max(NS, 1) * 3])
        lst = sb("lst", [128, 48, NSEQ * 3])
        s0 = ExitStack()
        orow = k.sb(s0, "orow", [NSEQ * 3, 6144], F32)
        if NS > 0:
            srow = k.sb(s0, "srowc", [NS * 3, 6144], F32)
            k.dma("sp", srow[:], I["sconv"][:], W=[srow])
            for c4 in range(0, 48, 4):
                b = bank()
                for j in range(4):
                    k.op("pe", lambda j=j: nc.tensor.transpose(b[:, j * 128:j * 128 + NS * 3], srow[:, (c4 + j) * 128:(c4 + j + 1) * 128],
                                                               ident[0:NS * 3, 0:NS * 3]), R=[srow, ident], W=[b], sig=(j == 3))
                k.op("dve", lambda: nc.vector.tensor_copy(out=cH[:, c4:c4 + 4, :], in_=b[:, :].rearrange("p (a b) -> p a b", b=128)[:, :, 0:NS * 3]),
                     R=[b], Wp=[cH])
        qv = S["qkvT"][:].rearrange("(c p) t -> p c t", p=128)
        for qi, sq in enumerate(cfg.seqs):
            te = sq["t0"] + sq["T"]
            k.dma("sp", lst[:, :, qi * 3:(qi + 1) * 3], qv[:, :, te - 3:te], R=[S["qkvT"]], Wp=[lst])
        for c4 in range(0, 48, 4):
            b = bank()
            for j in range(4):
                k.op("pe", lambda j=j: nc.tensor.transpose(b[0:NSEQ * 3, j * 128:(j + 1) * 128], lst[:, c4 + j, :], ident[:]),
                     R=[lst, ident], W=[b], sig=(j == 3))
            k.op("dve", lambda: nc.vector.tensor_copy(out=orow[:, c4 * 128:(c4 + 4) * 128], in_=b[0:NSEQ * 3, :]), R=[b], Wp=[orow])
        k.dma("pool", I["convo"][:], orow[:], R=[orow], W=[I["convo"]])

        k.barrier()
        s0.close()
        stop_at(1)
        S_g = [sb("S", [128, HG, 128]) for _ in range(16 // HG)]
        ba = sb("ba", [128, 32]); beta = sb("beta", [128, 16]); xx = sb("xx", [128, 16]); t16 = sb("t16", [128, 16])
        g_ = sb("g", [128, 16]); Gs = sb("Gs", [128, 16]); bg = sb("bg", [128, 16]); edec = sb("edec", [128, 16])
        Dm = sb("Dm", [128, 16, 128]); eGbc = sb("eGbc", [128, 16, 128]); dmS = sb("dmS", [128, 16, 128]); dmT = sb("dmT", [128, 16, 128])

        def make_set():
            Bf = {}
            Bf['raw'] = sb("raw", [128, HG, 3, 131]); Bf['cv'] = sb("cv", [128, HG, 3, 128]); Bf['sqb'] = sb("sqb", [128, HG, 2, 128])
            Bf['ctmp'] = sb("ctmp", [128, HG, 3, 128])
            Bf['cvR'] = [[Reg("cvR") for _ in range(3)] for _ in range(HG)]
            Bf['ctR'] = [[Reg("ctR") for _ in range(3)] for _ in range(HG)]
            Bf['rst'] = sb("rst", [128, HG, 2, 128]); Bf['qd'] = sb("qd", [128, HG, 128])
            Bf['kbg'] = sb("kbg", [128, HG, 128]); Bf['kdec'] = sb("kdec", [128, HG, 128]); Bf['vb'] = sb("vb", [128, HG, 128])
            Bf['L'] = [sb("L", [128, HG, 128]) for _ in range(2)]; Bf['U'] = [sb("U", [128, HG, 128]) for _ in range(2)]
            Bf['P'] = sb("P", [128, HG, 128]); Bf['qkm'] = sb("qkm", [128, HG, 128]); Bf['wT'] = sb("wT", [128, HG, 128]); Bf['u'] = sb("u", [128, HG, 128])
            Bf['vnew'] = sb("vnew", [128, HG, 128]); Bf['oT'] = sb("oT", [128, HG, 128]); Bf['zs'] = sb("zs", [128, HG, 128]); Bf['ob'] = sb("ob", [128, HG, 128], BF16)
            k.op("pool", lambda: nc.gpsimd.memset(Bf['vnew'][:], 0.0), W=[Bf['vnew']])
            return Bf
        BS = [make_set() for _ in range(2)]

        for qi, sq in enumerate(cfg.seqs):
            t0, T, si = sq["t0"], sq["T"], sq["si"]
            for gi_ in range(16 // HG):
                Sg_ = S_g[gi_]
                if si < 0:
                    k.op("pool", lambda: nc.gpsimd.memset(Sg_[:], 0.0), W=[Sg_])
                else:
                    r0_ = si * 2048 + gi_ * HG * 128
                    k.dma("sp", Sg_[:], I["sdel"][r0_:r0_ + HG * 128, :].rearrange("(h d) e -> d h e", d=128), W=[Sg_])
            for tt in range(-(-T // 128)):
                nt = min(128, T - tt * 128)
                ta = t0 + tt * 128
                nch = nt // 64
                k.dma("sp", ba[0:nt, :], S["ba"][ta:ta + nt, :], R=[S["ba"]], W=[ba])
                k.op("act", lambda: nc.scalar.activation(out=beta[0:nt, :], in_=ba[0:nt, 0:16], func=AF.Sigmoid), R=[ba], W=[beta])
                k.op("dve", lambda: nc.vector.tensor_tensor(out=xx[0:nt, :], in0=ba[0:nt, 16:32], in1=dtb[0:nt, :], op=ALU.add), R=[ba, dtb], W=[xx])
                k.op("act", lambda: nc.scalar.activation(out=t16[0:nt, :], in_=xx[0:nt, :], func=AF.Abs), R=[xx], W=[t16])
                k.op("act", lambda: nc.scalar.activation(out=t16[0:nt, :], in_=t16[0:nt, :], func=AF.Exp, scale=-1.0), R=[t16], W=[t16])
                k.op("act", lambda: nc.scalar.activation(out=t16[0:nt, :], in_=t16[0:nt, :], func=AF.Ln, bias=1.0, scale=1.0), R=[t16], W=[t16])
                k.op("dve", lambda: nc.vector.scalar_tensor_tensor(out=g_[0:nt, :], in0=xx[0:nt, :], scalar=0.0, in1=t16[0:nt, :], op0=ALU.max, op1=ALU.add),
                     R=[xx, t16], W=[g_])
                k.op("dve", lambda: nc.vector.tensor_tensor(out=g_[0:nt, :], in0=g_[0:nt, :], in1=nea[0:nt, :], op=ALU.mult), R=[g_, nea], W=[g_])
                stop_at(1.2)
                bG = bank()
                k.op("pe", lambda: nc.tensor.matmul(bG[0:nt, 0:16], lhsT=C["c_cum"][0:nt, 0:nt], rhs=g_[0:nt, :], start=True, stop=True),
                     R=[C["c_cum"], g_], W=[bG], sig=False)
                k.op("pe", lambda: nc.tensor.matmul(bG[0:nt, 16:32], lhsT=C["c_blk"][0:nt, 0:nt], rhs=g_[0:nt, :], start=True, stop=True),
                     R=[C["c_blk"], g_], W=[bG])
                k.op("dve", lambda: nc.vector.tensor_copy(out=Gs[0:nt, :], in_=bG[0:nt, 0:16]), R=[bG], W=[Gs])
                k.op("act", lambda: nc.scalar.activation(out=bg[0:nt, :], in_=Gs[0:nt, :], func=AF.Exp), R=[Gs], W=[bg])
                k.op("dve", lambda: nc.vector.tensor_tensor(out=bg[0:nt, :], in0=bg[0:nt, :], in1=beta[0:nt, :], op=ALU.mult), R=[bg, beta], W=[bg])
                k.op("dve", lambda: nc.vector.tensor_tensor(out=edec[0:nt, :], in0=bG[0:nt, 16:32], in1=Gs[0:nt, :], op=ALU.subtract), R=[bG, Gs], W=[edec])
                k.op("act", lambda: nc.scalar.activation(out=edec[0:nt, :], in_=edec[0:nt, :], func=AF.Exp), R=[edec], W=[edec])
                stop_at(1.4)
                for h in range(16):
                    e = "pool" if h % 2 else "dve"
                    eh = nc.gpsimd if h % 2 else nc.vector
                    k.op(e, lambda: eh.tensor_scalar(out=Dm[0:nt, h, 0:nt], in0=ident[0:nt, 0:nt], scalar1=Gs[0:nt, h:h + 1], scalar2=0.0, op0=ALU.mult, op1=ALU.add),
                         R=[ident, Gs], Wp=[Dm])
                stop_at(1.6)
                for q4 in range(4):
                    b = bank()
                    k.op("pe", lambda: nc.tensor.matmul(b[:, 0:4 * nt], lhsT=ones_f[0:nt, :], rhs=Dm[0:nt, 4 * q4:4 * q4 + 4, 0:nt], start=True, stop=True),
                         R=[ones_f, Dm], W=[b])
                    stop_at(1.65)
                    bv = b[:, 0:4 * nt].rearrange("p (a b) -> p a b", b=nt)
                    k.op("act", lambda: nc.scalar.activation(out=eGbc[:, 4 * q4:4 * q4 + 4, 0:nt], in_=bv, func=AF.Exp), R=[b], Wp=[eGbc, b])
                    stop_at(1.7)
                    for j in range(4):
                        h = 4 * q4 + j
                        k.op("dve", lambda: nc.vector.scalar_tensor_tensor(out=dmS[0:nt, h, 0:nt], in0=b[0:nt, j * nt:(j + 1) * nt], scalar=Gs[0:nt, h:h + 1],
                                                                           in1=C["c_nmL"][0:nt, 0:nt], op0=ALU.subtract, op1=ALU.subtract),
                             R=[b, Gs, C["c_nmL"]], Wp=[dmS])
                        k.op("dve", lambda: nc.vector.scalar_tensor_tensor(out=dmT[0:nt, h, 0:nt], in0=b[0:nt, j * nt:(j + 1) * nt], scalar=Gs[0:nt, h:h + 1],
                                                                           in1=C["c_nmT"][0:nt, 0:nt], op0=ALU.subtract, op1=ALU.add),
                             R=[b, Gs, C["c_nmT"]], Wp=[dmT])
                stop_at(1.8)
                k.op("act", lambda: nc.scalar.activation(out=dmS[0:nt, :, 0:nt], in_=dmS[0:nt, :, 0:nt], func=AF.Exp, scale=-1.0), R=[dmS], W=[dmS])
                k.op("act", lambda: nc.scalar.activation(out=dmT[0:nt, :, 0:nt], in_=dmT[0:nt, :, 0:nt], func=AF.Exp), R=[dmT], W=[dmT])
                k.op("dve", lambda: nc.vector.tensor_tensor(out=dmS[0:nt, :, 0:nt], in0=dmS[0:nt, :, 0:nt], in1=beta[0:nt, :].unsqueeze(2).to_broadcast([nt, 16, nt]),
                                                            op=ALU.mult), R=[dmS, beta], W=[dmS])
                stop_at(2)
                def hg_gen(hg, Bf, Sg):
                    raw, cv, sqb, rst, qd, kbg, kdec, vb = Bf['raw'], Bf['cv'], Bf['sqb'], Bf['rst'], Bf['qd'], Bf['kbg'], Bf['kdec'], Bf['vb']
                    L, U, P, qkm, wT, u_, vnew, oT, zs, ob = Bf['L'], Bf['U'], Bf['P'], Bf['qkm'], Bf['wT'], Bf['u'], Bf['vnew'], Bf['oT'], Bf['zs'], Bf['ob']
                    for hh in range(HG):
                        h = hg + hh
                        src = S["qkvT"][:].rearrange("(c h p) t -> p c h t", c=3, h=16)[:, :, h, :]
                        if tt > 0:
                            k.dma("sp", raw[:, hh, :, 0:3 + nt], src[:, :, ta - 3:ta + nt], R=[S["qkvT"]], Wp=[raw])
                        else:
                            k.dma("sp", raw[:, hh, :, 3:3 + nt], src[:, :, ta:ta + nt], R=[S["qkvT"]], Wp=[raw])
                            for comp in range(3):
                                if si < 0:
                                    k.op("pool", lambda: nc.gpsimd.memset(raw[:, hh, comp, 0:3], 0.0), Wp=[raw])
                                else:
                                    k.op("pool", lambda: nc.gpsimd.tensor_copy(out=raw[:, hh, comp, 0:3], in_=cH[:, comp * 16 + h, si * 3:(si + 1) * 3]), R=[cH], Wp=[raw])
                    k.dma("sp", zs[:, :, 0:nt], S["zT"][hg * 128:(hg + HG) * 128, ta:ta + nt].rearrange("(h p) t -> p h t", p=128), R=[S["zT"]], W=[zs])
                    cvR, ctmp = Bf['cvR'], Bf['ctmp']
                    for j in range(4):
                        for hh in range(HG):
                            h = hg + hh
                            for comp in range(3):
                                ch = comp * 16 + h
                                rg = cvR[hh][comp]
                                on_pool = comp == 2
                                o_ = cv[:, hh, comp, 0:nt]
                                i_ = raw[:, hh, comp, j:j + nt]
                                if j == 0:
                                    if on_pool:
                                        k.op("pool", lambda: nc.gpsimd.tensor_scalar(out=o_, in0=i_, scalar1=cw[:, ch, 0:1], scalar2=cw[:, ch, 4:5], op0=ALU.mult, op1=ALU.add),
                                             R=[raw, cw], Wp=[rg], Wa=[cv])
                                    else:
                                        k.op("dve", lambda: nc.vector.tensor_scalar(out=o_, in0=i_, scalar1=cw[:, ch, 0:1], scalar2=cw[:, ch, 4:5], op0=ALU.mult, op1=ALU.add),
                                             R=[raw, cw], Wp=[rg], Wa=[cv])
                                elif on_pool:
                                    t_ = ctmp[:, hh, comp, 0:nt]
                                    tr = Bf['ctR'][hh][comp]
                                    k.op("pool", lambda: nc.gpsimd.tensor_scalar(out=t_, in0=i_, scalar1=cw[:, ch, j:j + 1], scalar2=0.0, op0=ALU.mult, op1=ALU.add), R=[raw, cw], W=[tr])
                                    k.op("pool", lambda: nc.gpsimd.tensor_tensor(out=o_, in0=o_, in1=t_, op=ALU.add), R=[tr, rg], Wp=[rg])
                                else:
                                    k.op("dve", lambda: nc.vector.scalar_tensor_tensor(out=o_, in0=i_, scalar=cw[:, ch, j:j + 1], in1=o_, op0=ALU.mult, op1=ALU.add),
                                         R=[raw, cw, rg], Wp=[rg])
                    allcv = [cvR[a][b_] for a in range(HG) for b_ in range(3)]
                    yield
                    k.op("act", lambda: nc.scalar.activation(out=cv[:, :, :, 0:nt], in_=cv[:, :, :, 0:nt], func=AF.Silu), R=allcv, W=[cv] + allcv)
                    k.op("pool", lambda: nc.gpsimd.tensor_tensor(out=sqb[:, :, :, 0:nt], in0=cv[:, :, 0:2, 0:nt], in1=cv[:, :, 0:2, 0:nt], op=ALU.mult), R=[cv], W=[sqb])
                    for b4 in range(0, HG, 2):
                        b = bank()
                        for j in range(2):
                            k.op("pe", lambda: nc.tensor.matmul(b[:, j * 2 * nt:(j + 1) * 2 * nt], lhsT=ones_f[:], rhs=sqb[:, b4 + j, :, 0:nt], start=True, stop=True),
                                 R=[ones_f, sqb], W=[b], sig=(j == 1))
                        bv = b[:, 0:4 * nt].rearrange("p (a c b) -> p a c b", a=2, c=2)
                        k.op("act", lambda: nc.scalar.activation(out=rst[:, b4:b4 + 2, :, 0:nt], in_=bv, func=AF.Sqrt, bias=1e-6, scale=1.0), R=[b], Wp=[rst])
                    k.op("dve", lambda: nc.vector.reciprocal(out=rst[:, :, :, 0:nt], in_=rst[:, :, :, 0:nt]), R=[rst], W=[rst])
                    k.op("dve", lambda: nc.vector.scalar_tensor_tensor(out=cv[:, :, 0, 0:nt], in0=cv[:, :, 0, 0:nt], scalar=HD ** -0.5, in1=rst[:, :, 0, 0:nt],
                                                                       op0=ALU.mult, op1=ALU.mult), R=[cv, rst], Wp=[cv])
                    k.op("pool", lambda: nc.gpsimd.tensor_tensor(out=cv[:, :, 1, 0:nt], in0=cv[:, :, 1, 0:nt], in1=rst[:, :, 1, 0:nt], op=ALU.mult), R=[cv, rst], Wp=[cv])
                    k.op("dve", lambda: nc.vector.tensor_tensor(out=qd[:, :, 0:nt], in0=cv[:, :, 0, 0:nt], in1=eGbc[:, hg:hg + HG, 0:nt], op=ALU.mult), R=[cv, eGbc], W=[qd])
                    yield
                    for b4 in range(0, HG, 4):
                        bk_, bv_ = bank(), bank()
                        for j in range(4):
                            k.op("pe", lambda: nc.tensor.transpose(bk_[0:nt, j * 128:(j + 1) * 128], cv[:, b4 + j, 1, 0:nt], ident[:]), R=[cv, ident], W=[bk_], sig=(j == 3))
                        for j in range(4):
                            k.op("pe", lambda: nc.tensor.transpose(bv_[0:nt, j * 128:(j + 1) * 128], cv[:, b4 + j, 2, 0:nt], ident[:]), R=[cv, ident], W=[bv_], sig=(j == 3))
                        hs = slice(hg + b4, hg + b4 + 4)
                        kv3 = bk_[0:nt, :].rearrange("p (a b) -> p a b", b=128)
                        vv3 = bv_[0:nt, :].rearrange("p (a b) -> p a b", b=128)
                        k.op("dve", lambda: nc.vector.tensor_tensor(out=kbg[0:nt, b4:b4 + 4, :], in0=kv3, in1=bg[0:nt, hs].unsqueeze(2).to_broadcast([nt, 4, 128]), op=ALU.mult),
                             R=[bk_, bg], Wp=[kbg])
                        k.op("dve", lambda: nc.vector.tensor_tensor(out=kdec[0:nt, b4:b4 + 4, :], in0=kv3, in1=edec[0:nt, hs].unsqueeze(2).to_broadcast([nt, 4, 128]), op=ALU.mult),
                             R=[bk_, edec], Wp=[kdec])
                        k.op("dve", lambda: nc.vector.tensor_tensor(out=vb[0:nt, b4:b4 + 4, :], in0=vv3, in1=beta[0:nt, hs].unsqueeze(2).to_broadcast([nt, 4, 128]), op=ALU.mult),
                             R=[bv_, beta], Wp=[vb])
                    yield
                    for b4 in range(0, HG, 4):
                        b1, b2 = bank(), bank()
                        for j in range(4):
                            k.op("pe", lambda: nc.tensor.matmul(b1[0:nt, j * nt:(j + 1) * nt], lhsT=cv[:, b4 + j, 1, 0:nt], rhs=cv[:, b4 + j, 1, 0:nt], start=True, stop=True),
                                 R=[cv], W=[b1], sig=(j == 3))
                        for j in range(4):
                            k.op("pe", lambda: nc.tensor.matmul(b2[0:nt, j * nt:(j + 1) * nt], lhsT=cv[:, b4 + j, 1, 0:nt], rhs=cv[:, b4 + j, 0, 0:nt], start=True, stop=True),
                                 R=[cv], W=[b2], sig=(j == 3))
                        hs = slice(hg + b4, hg + b4 + 4)
                        k.op("dve", lambda: nc.vector.tensor_tensor(out=L[0][0:nt, b4:b4 + 4, 0:nt], in0=b1[0:nt, 0:4 * nt].rearrange("p (a b) -> p a b", b=nt),
                                                                    in1=dmS[0:nt, hs, 0:nt], op=ALU.mult), R=[b1, dmS], Wp=[L[0]])
                        k.op("dve", lambda: nc.vector.tensor_tensor(out=qkm[0:nt, b4:b4 + 4, 0:nt], in0=b2[0:nt, 0:4 * nt].rearrange("p (a b) -> p a b", b=nt),
                                                                    in1=dmT[0:nt, hs, 0:nt], op=ALU.mult), R=[b2, dmT], Wp=[qkm])
                    yield
                    for b4 in range(0, HG, 4):
                        b = bank()
                        for j in range(4):
                            k.op("pe", lambda: nc.tensor.transpose(b[0:nt, j * nt:(j + 1) * nt], L[0][0:nt, b4 + j, 0:nt], ident[0:nt, 0:nt]), R=[L[0], ident], W=[b], sig=(j == 3))
                        bv = b[0:nt, 0:4 * nt].rearrange("p (a b) -> p a b", b=nt)
                        k.op("act", lambda: nc.scalar.copy(out=U[0][0:nt, b4:b4 + 4, 0:nt], in_=bv), R=[b], Wp=[U[0]])
                        k.op("pool", lambda: nc.gpsimd.tensor_tensor(out=P[0:nt, b4:b4 + 4, 0:nt], in0=ident[0:nt, 0:nt].unsqueeze(1).to_broadcast([nt, 4, nt]),
                                                                     in1=U[0][0:nt, b4:b4 + 4, 0:nt], op=ALU.subtract), R=[U[0], ident], Wp=[P])
                    cur = 0
                    for step in range(5):
                        nx = 1 - cur
                        for b4 in range(0, HG, 4):
                            b1 = bank()
                            for j in range(4):
                                k.op("pe", lambda: nc.tensor.matmul(b1[0:nt, j * nt:(j + 1) * nt], lhsT=U[cur][0:nt, b4 + j, 0:nt], rhs=L[cur][0:nt, b4 + j, 0:nt], start=True, stop=True),
                                     R=[U[cur], L[cur]], W=[b1], sig=(j == 3))
                            k.op("act", lambda: nc.scalar.copy(out=L[nx][0:nt, b4:b4 + 4, 0:nt], in_=b1[0:nt, 0:4 * nt].rearrange("p (a b) -> p a b", b=nt)), R=[b1], Wp=[L[nx]])
                            if step < 4:
                                b2 = bank()
                                for j in range(4):
                                    k.op("pe", lambda: nc.tensor.matmul(b2[0:nt, j * nt:(j + 1) * nt], lhsT=L[cur][0:nt, b4 + j, 0:nt], rhs=U[cur][0:nt, b4 + j, 0:nt], start=True, stop=True),
                                         R=[U[cur], L[cur]], W=[b2], sig=(j == 3))
                                k.op("dve", lambda: nc.vector.tensor_copy(out=U[nx][0:nt, b4:b4 + 4, 0:nt], in_=b2[0:nt, 0:4 * nt].rearrange("p (a b) -> p a b", b=nt)), R=[b2], Wp=[U[nx]])
                        for b4 in range(0, HG, 4):
                            b3 = bank()
                            for j in range(4):
                                k.op("pe", lambda: nc.tensor.matmul(b3[0:nt, j * nt:(j + 1) * nt], lhsT=L[nx][0:nt, b4 + j, 0:nt], rhs=P[0:nt, b4 + j, 0:nt], start=True, stop=True),
                                     R=[L[nx], P], W=[b3], sig=(j == 3))
                            k.op("dve", lambda: nc.vector.tensor_tensor(out=P[0:nt, b4:b4 + 4, 0:nt], in0=P[0:nt, b4:b4 + 4, 0:nt],
                                                                        in1=b3[0:nt, 0:4 * nt].rearrange("p (a b) -> p a b", b=nt), op=ALU.add), R=[b3, P], Wp=[P])
                        cur = nx
                        yield
                    yield
                    for b4 in range(0, HG, 4):
                        b1, b2 = bank(), bank()
                        for j in range(4):
                            k.op("pe", lambda: nc.tensor.matmul(b1[:, j * nt:(j + 1) * nt], lhsT=kbg[0:nt, b4 + j, :], rhs=P[0:nt, b4 + j, 0:nt], start=True, stop=True),
                                 R=[kbg, P], W=[b1], sig=(j == 3))
                        for j in range(4):
                            k.op("pe", lambda: nc.tensor.matmul(b2[0:nt, j * 128:(j + 1) * 128], lhsT=P[0:nt, b4 + j, 0:nt], rhs=vb[0:nt, b4 + j, :], start=True, stop=True),
                                 R=[vb, P], W=[b2], sig=(j == 3))
                        k.op("act", lambda: nc.scalar.copy(out=wT[:, b4:b4 + 4, 0:nt], in_=b1[:, 0:4 * nt].rearrange("p (a b) -> p a b", b=nt)), R=[b1], Wp=[wT])
                        k.op("dve", lambda: nc.vector.tensor_copy(out=u_[0:nt, b4:b4 + 4, :], in_=b2[0:nt, :].rearrange("p (a b) -> p a b", b=128)), R=[b2], Wp=[u_])
                    yield
                    for ci in range(nch):
                        r = slice(ci * 64, ci * 64 + 64)
                        for b4 in range(0, HG, 4):
                            b1 = bank()
                            for j in range(4):
                                k.op("pe", lambda: nc.tensor.matmul(b1[r, j * 128:(j + 1) * 128], lhsT=wT[:, b4 + j, r], rhs=Sg[:, b4 + j, :], start=True, stop=True),
                                     R=[wT, Sg], W=[b1], sig=(j == 3))
                            k.op("dve", lambda: nc.vector.tensor_tensor(out=vnew[r, b4:b4 + 4, :], in0=u_[r, b4:b4 + 4, :], in1=b1[r, :].rearrange("p (a b) -> p a b", b=128),
                                                                        op=ALU.subtract), R=[u_, b1], Wp=[vnew])
                        bo = bank()
                        for hh in range(HG):
                            k.op("pe", lambda: nc.tensor.matmul(bo[:, hh * 64:(hh + 1) * 64], lhsT=Sg[:, hh, :], rhs=qd[:, hh, r], start=True, stop=False),
                                 R=[Sg, qd], W=[bo], sig=False)
                            k.op("pe", lambda: nc.tensor.matmul(bo[:, hh * 64:(hh + 1) * 64], lhsT=vnew[0:nt, hh, :], rhs=qkm[0:nt, hh, r], start=False, stop=True),
                                 R=[vnew, qkm], W=[bo], sig=(hh == HG - 1))
                        k.op("act", lambda: nc.scalar.copy(out=oT[:, :, r], in_=bo[:, 0:HG * 64].rearrange("p (a b) -> p a b", b=64)), R=[bo], Wp=[oT])
                        for b4 in range(0, HG, 4):
                            b2 = bank()
                            for j in range(4):
                                k.op("pe", lambda: nc.tensor.matmul(b2[:, j * 128:(j + 1) * 128], lhsT=kdec[r, b4 + j, :], rhs=vnew[r, b4 + j, :], start=True, stop=True),
                                     R=[kdec, vnew], W=[b2], sig=(j == 3))
                            hs = slice(hg + b4, hg + b4 + 4)
                            col = ci * 64 + 63
                            k.op("pool", lambda: nc.gpsimd.tensor_tensor(out=Sg[:, b4:b4 + 4, :], in0=Sg[:, b4:b4 + 4, :], in1=eGbc[:, hs, col:col + 1].to_broadcast([128, 4, 128]), op=ALU.mult),
                                 R=[Sg, eGbc], Wp=[Sg])
                            k.op("dve", lambda: nc.vector.tensor_tensor(out=Sg[:, b4:b4 + 4, :], in0=Sg[:, b4:b4 + 4, :], in1=b2[:, :].rearrange("p (a b) -> p a b", b=128), op=ALU.add),
                                 R=[Sg, b2], Wp=[Sg])
                        yield
                    yield
                    k.op("pool", lambda: nc.gpsimd.tensor_tensor(out=sqb[:, :, 0, 0:nt], in0=oT[:, :, 0:nt], in1=oT[:, :, 0:nt], op=ALU.mult), R=[oT], Wp=[sqb])
                    for b4 in range(0, HG, 4):
                        b = bank()
                        for j in range(4):
                            k.op("pe", lambda: nc.tensor.matmul(b[:, j * nt:(j + 1) * nt], lhsT=ones_f[:], rhs=sqb[:, b4 + j, 0, 0:nt], start=True, stop=True),
                                 R=[ones_f, sqb], W=[b], sig=(j == 3))
                        k.op("act", lambda: nc.scalar.activation(out=rst[:, b4:b4 + 4, 0, 0:nt], in_=b[:, 0:4 * nt].rearrange("p (a b) -> p a b", b=nt), func=AF.Sqrt,
                                                                 bias=EPS, scale=1.0 / 128), R=[b], Wp=[rst])
                    k.op("dve", lambda: nc.vector.reciprocal(out=rst[:, :, 0, 0:nt], in_=rst[:, :, 0, 0:nt]), R=[rst], Wp=[rst])
                    k.op("dve", lambda: nc.vector.tensor_tensor(out=oT[:, :, 0:nt], in0=oT[:, :, 0:nt], in1=rst[:, :, 0, 0:nt], op=ALU.mult), R=[oT, rst], W=[oT])
                    k.op("dve", lambda: nc.vector.scalar_tensor_tensor(out=ob[:, :, 0:nt], in0=oT[:, :, 0:nt], scalar=dng[:, 0:1], in1=zs[:, :, 0:nt], op0=ALU.mult, op1=ALU.mult),
                         R=[oT, dng, zs], W=[ob])
                    k.dma("pool", S["aT"][2048 + hg * 128:2048 + (hg + HG) * 128, ta:ta + nt].rearrange("(h p) t -> p h t", p=128), ob[:, :, 0:nt], R=[ob], Wp=[S["aT"]])

                gens = [hg_gen(hg, BS[i % 2], S_g[hg // HG]) for i, hg in enumerate(range(0, 16, HG))]
                active = []
                while gens or active:
                    while len(active) < 2 and gens:
                        active.append(gens.pop(0))
                    for gen_ in list(active):
                        try:
                            next(gen_)
                        except StopIteration:
                            active.remove(gen_)
            for gi_ in range(16 // HG):
                r0_ = qi * 2048 + gi_ * HG * 128
                k.dma("pool", I["so"][r0_:r0_ + HG * 128, :].rearrange("(h d) e -> d h e", d=128), S_g[gi_][:], R=[S_g[gi_]], Wp=[I["so"]])


def ln_stats(k, G, st, s1, gn):
    nc = k.nc
    bank = G["bank"]
    ones_f = G["ones_f"]
    sq = [k.sb(st, "lnsq", [128, gn], F32) for _ in range(2)]
    mt = k.sb(st, "lnm", [128, gn], F32)
    t1 = k.sb(st, "lnt", [128, gn], F32)
    rs = k.sb(st, "lnrs", [128, gn], F32)
    nm = k.sb(st, "lnnm", [128, gn], F32)
    bs, bq = bank(), bank()
    for m in range(KC):
        q_ = sq[m % 2]
        k.op("act", lambda: nc.scalar.activation(out=q_[:], in_=s1[:, m, :], func=AF.Square), R=[s1], W=[q_])
        k.op("pe", lambda: nc.tensor.matmul(bs[:, 0:gn], lhsT=ones_f[:], rhs=s1[:, m, :], start=(m == 0), stop=(m == KC - 1)),
             R=[s1, ones_f], W=[bs], sig=(m == KC - 1))
        k.op("pe", lambda: nc.tensor.matmul(bq[:, 0:gn], lhsT=ones_f[:], rhs=q_[:], start=(m == 0), stop=(m == KC - 1)),
             R=[q_, ones_f], W=[bq], sig=True)
    k.op("dve", lambda: nc.vector.tensor_scalar(out=mt[:], in0=bs[:, 0:gn], scalar1=1.0 / D, scalar2=None, op0=ALU.mult), R=[bs], W=[mt])
    k.op("dve", lambda: nc.vector.tensor_tensor(out=t1[:], in0=mt[:], in1=mt[:], op=ALU.mult), R=[mt], W=[t1])
    k.op("dve", lambda: nc.vector.scalar_tensor_tensor(out=t1[:], in0=bq[:, 0:gn], scalar=1.0 / D, in1=t1[:], op0=ALU.mult, op1=ALU.subtract),
         R=[bq, t1], W=[t1])
    k.op("act", lambda: nc.scalar.activation(out=t1[:], in_=t1[:], func=AF.Sqrt, bias=EPS, scale=1.0), R=[t1], W=[t1])
    k.op("dve", lambda: nc.vector.reciprocal(out=rs[:], in_=t1[:]), R=[t1], W=[rs])
    k.op("dve", lambda: nc.vector.scalar_tensor_tensor(out=nm[:], in0=mt[:], scalar=-1.0, in1=rs[:], op0=ALU.mult, op1=ALU.mult),
         R=[mt, rs], W=[nm])
    return rs, nm


def phase_D(k, cfg, G):
    nc = k.nc
    I, S = G["I"], G["S"]
    bank = G["bank"]
    for (g0, gn) in cfg.groups(384):
        with ExitStack() as st:
            aT = k.sb(st, "aT", [128, KC, gn], BF16)
            s1 = k.sb(st, "s1", [128, KC, gn], F32)
            gb = k.sb(st, "ln1", [128, 64], F32)
            k.dma("sp", gb[:], I["ln1"][:], W=[gb])
            k.dma("sp", aT[:], S["aT"][:].rearrange("(kc p) t -> p kc t", p=128)[:, :, g0:g0 + gn], R=[S["aT"]], W=[aT])
            ws = WTiles(k, st, nslot=3)
            xr = [k.sb(st, "xr", [128, gn], F32) for _ in range(3)]
            for m in range(KC):
                wb = ws.get(S["b_w_o"], m // 2)
                sub = (m % 2) * 128
                x_ = xr[m % 3]
                k.dma("sp", x_[:], S["xnT"][m * 128:(m + 1) * 128, g0:g0 + gn], R=[S["xnT"]], W=[x_])
                b = bank()
                for kc in range(KC):
                    k.op("pe", lambda kc=kc: nc.tensor.matmul(b[:, 0:gn], lhsT=wb[:, kc, sub:sub + 128], rhs=aT[:, kc, :], start=(kc == 0), stop=(kc == KC - 1)),
                         R=[wb, aT], W=[b], sig=(kc == KC - 1))
                k.op("dve", lambda: nc.vector.scalar_tensor_tensor(out=s1[:, m, :], in0=x_[:], scalar=ALPHA, in1=b[:, 0:gn], op0=ALU.mult, op1=ALU.add),
                     R=[x_, b], Wp=[s1])
            rs, nm = ln_stats(k, G, st, s1, gn)
            of = [k.sb(st, "of", [128, gn], F32) for _ in range(2)]
            ob = [k.sb(st, "ob", [128, gn], BF16) for _ in range(2)]
            for m in range(KC):
                o_, b_ = of[m % 2], ob[m % 2]
                k.op("pool", lambda: nc.gpsimd.tensor_tensor(out=o_[:], in0=s1[:, m, :], in1=rs[:], op=ALU.mult), R=[s1, rs], W=[o_])
                k.op("dve", lambda: nc.vector.tensor_tensor(out=o_[:], in0=o_[:], in1=nm[:], op=ALU.add), R=[o_, nm], W=[o_])
                k.op("act", lambda: nc.scalar.activation(out=o_[:], in_=o_[:], func=AF.Identity, scale=gb[:, m:m + 1], bias=gb[:, 32 + m:33 + m]),
                     R=[o_, gb], W=[o_])
                k.op("pool", lambda: nc.gpsimd.tensor_copy(out=b_[:], in_=o_[:]), R=[o_], W=[b_])
                k.dma("pool", S["x1T"][m * 128:(m + 1) * 128, g0:g0 + gn], o_[:], R=[o_], Wp=[S["x1T"]])
                k.dma("pool", S["x1b"][m * 128:(m + 1) * 128, g0:g0 + gn], b_[:], R=[b_], Wp=[S["x1b"]])
        k.barrier()


def seg_pieces(cfg, g0, gn):
    out = []
    for qi, sq in enumerate(cfg.seqs):
        a = max(sq["t0"], g0)
        b = min(sq["t0"] + sq["T"], g0 + gn)
        if a < b:
            out.append((qi, a - g0, b - a, a == sq["t0"], b == sq["t0"] + sq["T"]))
    return out


def phase_E(k, cfg, G):
    nc = k.nc
    I, S, C = G["I"], G["S"], G["C"]
    bank = G["bank"]
    FC, NS, NSEQ, DFF = cfg.FC, cfg.NS, cfg.NSEQ, cfg.DFF
    with ExitStack() as pst:
        fw = k.sb(pst, "ffnw", [128, 2 * FC, 4], F32)
        k.dma("sp", fw[:].rearrange("p a b -> p (a b)"), I["ffnw"][:], W=[fw])
        hsave = k.sb(pst, "hsave", [128, 2 * FC, 2], F32)
        k.op("pool", lambda: nc.gpsimd.memset(hsave[:], 0.0), W=[hsave])
        fst = k.sb(pst, "fst", [128, 2 * FC, NSEQ * 2], F32)
        fH = k.sb(pst, "fH", [128, 2 * FC, max(NS, 1) * 2], F32)
        if NS > 0:
            with ExitStack() as s0:
                srow = [k.sb(s0, "srow", [NS * 2, 512], F32) for _ in range(2)]
                n = 0
                for c4 in range(0, 2 * FC, 4):
                    b = bank()
                    n4 = min(4, 2 * FC - c4)
                    sr = srow[n % 2]
                    n += 1
                    k.dma("sp", sr[:, 0:n4 * 128], I["sffn"][:, c4 * 128:(c4 + n4) * 128], W=[sr])
                    for j in range(n4):
                        k.op("pe", lambda j=j: nc.tensor.transpose(b[:, j * 128:j * 128 + NS * 2], sr[:, j * 128:(j + 1) * 128],
                                                                   C["c_ident"][0:NS * 2, 0:NS * 2]),
                             R=[sr, C["c_ident"]], W=[b], sig=(j == n4 - 1))
                    k.op("dve", lambda: nc.vector.tensor_copy(out=fH[:, c4:c4 + n4, :],
                                                              in_=b[:, 0:n4 * 128].rearrange("p (a b) -> p a b", b=128)[:, :, 0:NS * 2]),
                         R=[b], Wp=[fH])
            k.barrier()
        for (g0, gn) in cfg.groups(768):
            pcs = seg_pieces(cfg, g0, gn)
            offs = []
            o = 0
            for p in pcs:
                offs.append(o)
                o += p[2] + 2
            RW = o
            with ExitStack() as st:
                x1 = k.sb(st, "x1b", [128, KC, gn], BF16)
                k.dma("sp", x1[:], S["x1b"][:].rearrange("(kc p) t -> p kc t", p=128)[:, :, g0:g0 + gn], R=[S["x1b"]], W=[x1])
                ws = WTiles(k, st, nslot=4)
                raw = [[k.sb(st, "raw", [128, RW], F32) for _ in range(2)] for _ in range(2)]
                cv = [[k.sb(st, "cv", [128, gn], F32) for _ in range(2)] for _ in range(2)]
                ao = [k.sb(st, "ao", [128, gn], BF16) for _ in range(2)]
                for c in range(FC):
                    par = c % 2
                    for half in range(2):
                        ch = half * FC + c
                        r_ = raw[half][par]
                        wb = ws.get(S["b_w_up"], ch // 2)
                        sub = (ch % 2) * 128
                        for pi, (qi, a, ln, s_st, s_en) in enumerate(pcs):
                            o_ = offs[pi]
                            if not s_st:
                                k.op("pool", lambda o_=o_: nc.gpsimd.tensor_copy(out=r_[:, o_:o_ + 2], in_=hsave[:, ch, :]), R=[hsave], Wp=[r_])
                            elif qi == 0:
                                k.op("pool", lambda o_=o_: nc.gpsimd.memset(r_[:, o_:o_ + 2], 0.0), Wp=[r_])
                            else:
                                k.op("pool", lambda o_=o_, qi=qi: nc.gpsimd.tensor_copy(out=r_[:, o_:o_ + 2], in_=fH[:, ch, (qi - 1) * 2:qi * 2]), R=[fH], Wp=[r_])
                        for (b0, bn) in blocks(gn):
                            b = bank()
                            for kc in range(KC):
                                k.op("pe", lambda kc=kc: nc.tensor.matmul(b[:, 0:bn], lhsT=wb[:, kc, sub:sub + 128], rhs=x1[:, kc, b0:b0 + bn], start=(kc == 0), stop=(kc == KC - 1)),
                                     R=[wb, x1], W=[b], sig=(kc == KC - 1))
                            for pi, (qi, a, ln, s_st, s_en) in enumerate(pcs):
                                lo, hi = max(a, b0), min(a + ln, b0 + bn)
                                if lo < hi:
                                    d0 = offs[pi] + 2 + (lo - a)
                                    k.op("act", lambda lo=lo, hi=hi, d0=d0: nc.scalar.copy(out=r_[:, d0:d0 + hi - lo], in_=b[:, lo - b0:hi - b0]), R=[b], Wp=[r_])
                        c_ = cv[half][par]
                        for pi, (qi, a, ln, s_st, s_en) in enumerate(pcs):
                            o_ = offs[pi]
                            k.op("dve", lambda o_=o_, a=a, ln=ln: nc.vector.tensor_scalar(out=c_[:, a:a + ln], in0=r_[:, o_:o_ + ln], scalar1=fw[:, ch, 0:1], scalar2=fw[:, ch, 3:4],
                                                                                       op0=ALU.mult, op1=ALU.add), R=[r_, fw], Wp=[c_])
                            for j in (1, 2):
                                k.op("dve", lambda o_=o_, a=a, ln=ln, j=j: nc.vector.scalar_tensor_tensor(out=c_[:, a:a + ln], in0=r_[:, o_ + j:o_ + j + ln], scalar=fw[:, ch, j:j + 1],
                                                                                                         in1=c_[:, a:a + ln], op0=ALU.mult, op1=ALU.add), R=[r_, fw, c_], Wp=[c_])
                            if s_en:
                                k.op("pool", lambda o_=o_, ln=ln, qi=qi: nc.gpsimd.tensor_copy(out=fst[:, ch, qi * 2:qi * 2 + 2], in_=r_[:, o_ + ln:o_ + ln + 2]), R=[r_], Wp=[fst])
                            else:
                                k.op("pool", lambda o_=o_, ln=ln: nc.gpsimd.tensor_copy(out=hsave[:, ch, :], in_=r_[:, o_ + ln:o_ + ln + 2]), R=[r_], Wp=[hsave])
                    gt, vl, a_ = cv[0][par], cv[1][par], ao[par]
                    k.op("act", lambda: nc.scalar.activation(out=gt[:], in_=gt[:], func=AF.Silu), R=[gt], W=[gt])
                    k.op("pool", lambda: nc.gpsimd.tensor_tensor(out=a_[:], in0=gt[:], in1=vl[:], op=ALU.mult), R=[gt, vl], W=[a_])
                    k.dma("pool", S["actT"][c * 128:(c + 1) * 128, g0:g0 + gn], a_[:], R=[a_], Wp=[S["actT"]])
            k.barrier()
        with ExitStack() as st:
            orow = [k.sb(st, "orow", [NSEQ * 2, 512], F32) for _ in range(2)]
            n = 0
            for c4 in range(0, 2 * FC, 4):
                n4 = min(4, 2 * FC - c4)
                b = bank()
                for j in range(n4):
                    k.op("pe", lambda j=j: nc.tensor.transpose(b[0:NSEQ * 2, j * 128:(j + 1) * 128], fst[:, c4 + j, :], C["c_ident"][:]),
                         R=[fst, C["c_ident"]], W=[b], sig=(j == n4 - 1))
                o_ = orow[n % 2]
                n += 1
                k.op("dve", lambda: nc.vector.tensor_copy(out=o_[:, 0:n4 * 128], in_=b[0:NSEQ * 2, 0:n4 * 128]), R=[b], W=[o_])
                k.dma("pool", I["ffno"][:, c4 * 128:(c4 + n4) * 128], o_[:, 0:n4 * 128], R=[o_], Wp=[I["ffno"]])
        k.barrier()


def phase_F(k, cfg, G):
    nc = k.nc
    I, S, C = G["I"], G["S"], G["C"]
    bank = G["bank"]
    FC = cfg.FC
    kgs = [(i, min(32, FC - i)) for i in range(0, FC, 32)]
    for (g0, gn) in cfg.groups(256):
        with ExitStack() as st:
            aT = k.sb(st, "actT", [128, FC, gn], BF16)
            s1 = k.sb(st, "s2", [128, KC, gn], F32)
            gb = k.sb(st, "ln2", [128, 64], F32)
            k.dma("sp", gb[:], I["ln2"][:], W=[gb])
            k.dma("sp", aT[:], S["actT"][:].rearrange("(kc p) t -> p kc t", p=128)[:, :, g0:g0 + gn], R=[S["actT"]], W=[aT])
            ws = WTiles(k, st, nslot=4)
            xr = [k.sb(st, "xr", [128, gn], F32) for _ in range(3)]
            for m in range(KC):
                x_ = xr[m % 3]
                sub = (m % 2) * 128
                k.dma("sp", x_[:], S["x1T"][m * 128:(m + 1) * 128, g0:g0 + gn], R=[S["x1T"]], W=[x_])
                b = bank()
                for gi, (k0, kn) in enumerate(kgs):
                    wb = ws.get(S["b_w_down"], m // 2, kg=gi, kcn=kn)
                    for kc in range(kn):
                        first = (gi == 0 and kc == 0)
                        last = (gi == len(kgs) - 1 and kc == kn - 1)
                        k.op("pe", lambda kc=kc: nc.tensor.matmul(b[:, 0:gn], lhsT=wb[:, kc, sub:sub + 128], rhs=aT[:, k0 + kc, :], start=first, stop=last),
                             R=[wb, aT], W=[b], sig=(kc == kn - 1))
                k.op("dve", lambda: nc.vector.scalar_tensor_tensor(out=s1[:, m, :], in0=x_[:], scalar=ALPHA, in1=b[:, 0:gn], op0=ALU.mult, op1=ALU.add),
                     R=[x_, b], Wp=[s1])
            rs, nm = ln_stats(k, G, st, s1, gn)
            for m in range(KC):
                k.op("pool", lambda: nc.gpsimd.tensor_tensor(out=s1[:, m, :], in0=s1[:, m, :], in1=rs[:], op=ALU.mult), R=[s1, rs], Wp=[s1])
                k.op("dve", lambda: nc.vector.tensor_tensor(out=s1[:, m, :], in0=s1[:, m, :], in1=nm[:], op=ALU.add), R=[s1, nm], Wp=[s1])
                k.op("act", lambda: nc.scalar.activation(out=s1[:, m, :], in_=s1[:, m, :], func=AF.Identity, scale=gb[:, m:m + 1], bias=gb[:, 32 + m:33 + m]),
                     R=[s1, gb], Wp=[s1])
            yt = [k.sb(st, "yt", [128, 2048], F32) for _ in range(2)]
            yn = 0
            for ti in range(gn // 128):
                for hf in range(2):
                    y_ = yt[yn % 2]
                    yn += 1
                    for q in range(4):
                        b = bank()
                        for j in range(4):
                            m = hf * 16 + q * 4 + j
                            k.op("pe", lambda m=m, j=j: nc.tensor.transpose(b[:, j * 128:(j + 1) * 128], s1[:, m, ti * 128:(ti + 1) * 128], C["c_ident"][:]),
                                 R=[s1, C["c_ident"]], W=[b], sig=(j == 3))
                        if q % 2:
                            k.op("act", lambda: nc.scalar.copy(out=y_[:, q * 512:(q + 1) * 512], in_=b[:, :]), R=[b], Wp=[y_])
                        else:
                            k.op("dve", lambda: nc.vector.tensor_copy(out=y_[:, q * 512:(q + 1) * 512], in_=b[:, :]), R=[b], Wp=[y_])
                    k.dma("pool", I["y"][g0 + ti * 128:g0 + (ti + 1) * 128, hf * 2048:(hf + 1) * 2048], y_[:], R=[y_], Wp=[I["y"]])
        k.barrier()


_CACHE = {}


def _pp(v):
    return np.ascontiguousarray(np.asarray(v, np.float32).reshape(32, 128).T)


def make_in_maps(cfg, n_cores, inp):
    f = lambda a: np.ascontiguousarray(np.asarray(a, dtype=np.float32))
    NS, DFF, FC = cfg.NS, cfg.DFF, cfg.FC
    shared = {}
    shared["lnin"] = np.concatenate([_pp(inp["ln_in_g"]), _pp(inp["ln_in_b"])], axis=1)
    shared["ln1"] = np.concatenate([_pp(inp["ln1_g"][0]), _pp(inp["ln1_b"][0])], axis=1)
    shared["ln2"] = np.concatenate([_pp(inp["ln2_g"][0]), _pp(inp["ln2_b"][0])], axis=1)
    shared["w_in"] = f(inp["w_in"][0]); shared["w_o"] = f(inp["w_o"][0])
    shared["w_up"] = f(inp["w_ffn_up"][0]); shared["w_down"] = f(inp["w_ffn_down"][0])
    cw = np.concatenate([f(inp["conv_qkv_w"][0]), f(inp["conv_qkv_b"])], axis=0)
    shared["convw"] = np.ascontiguousarray(cw.reshape(5, 48, 128).transpose(2, 1, 0).reshape(128, 240))
    fw = np.concatenate([f(inp["ffn_conv_w"][0]), f(inp["ffn_conv_b"])], axis=0)
    shared["ffnw"] = np.ascontiguousarray(fw.reshape(4, 2 * FC, 128).transpose(2, 1, 0).reshape(128, 2 * FC * 4))
    shared["alog"] = np.ascontiguousarray(np.broadcast_to(f(inp["a_log"][0])[None, :], (128, 16)))
    shared["dtb"] = np.ascontiguousarray(np.broadcast_to(f(inp["dt_bias"][0])[None, :], (128, 16)))
    shared["dng"] = f(inp["delta_norm_g"][0]).reshape(128, 1)
    shared["relb"] = f(inp["rel_bias"])
    shared.update(host_consts())
    maps = []
    for c in range(n_cores):
        m = dict(shared)
        sl = slice(c * NS, (c + 1) * NS)
        m["x"] = np.concatenate([f(inp["x_prompt"][c]), f(inp["x_sample"][sl]).reshape(NS * DEC, D)], axis=0)
        m["ck"] = f(inp["cache_attn_k"][0, sl]).reshape(NS * PAST, 512)
        m["cv"] = f(inp["cache_attn_v"][0, sl]).reshape(NS * PAST, 512)
        m["cik"] = f(inp["cache_idx_k"][0, sl]).reshape(NS * PAST, 64)
        m["sdel"] = f(inp["state_delta"][0, sl]).reshape(NS * 16 * 128, 128)
        m["sconv"] = f(inp["state_conv_qkv"][0, sl]).reshape(NS * 3, 6144)
        m["sffn"] = f(inp["state_ffn_conv"][0, sl]).reshape(NS * 2, 2 * DFF)
        maps.append(m)
    return maps


def kernel(**inp):
    B, SEQ = inp["x_prompt"].shape[0], inp["x_prompt"].shape[1]
    DB = inp["x_sample"].shape[0]
    DFF = inp["w_ffn_down"].shape[1]
    n_cores = B
    NS = DB // n_cores
    cfg = Cfg(SEQ, NS, DFF)
    key = (SEQ, NS, DFF)
    if key not in _CACHE:
        _CACHE[key] = build(cfg)
    nc = _CACHE[key]
    maps = make_in_maps(cfg, n_cores, inp)
    res = run_bass_kernel_spmd(nc, maps, core_ids=list(range(n_cores)))
    R = res.results
    return assemble(cfg, n_cores, R)


def assemble(cfg, n_cores, R):
    NS, SEQ, DFF, NSEQ = cfg.NS, cfg.SEQ, cfg.DFF, cfg.NSEQ
    g = lambda n: [np.asarray(R[c][n], dtype=np.float32) for c in range(n_cores)]
    y, ko, vo, iko, so, co, fo = g("y"), g("ko"), g("vo"), g("iko"), g("so"), g("convo"), g("ffno")
    yp = np.stack([a[:SEQ] for a in y])
    ys = np.concatenate([a[SEQ:].reshape(NS, DEC, D) for a in y])
    pk = np.stack([a[:SEQ].reshape(SEQ, 4, 128) for a in ko])[None]
    pv = np.stack([a[:SEQ].reshape(SEQ, 4, 128) for a in vo])[None]
    pik = np.stack([a[:SEQ] for a in iko])[None]
    sk = np.concatenate([a[SEQ:].reshape(NS, DEC, 4, 128) for a in ko])[None]
    sv = np.concatenate([a[SEQ:].reshape(NS, DEC, 4, 128) for a in vo])[None]
    sik = np.concatenate([a[SEQ:].reshape(NS, DEC, 64) for a in iko])[None]
    pd = np.stack([a.reshape(NSEQ, 16, 128, 128)[0] for a in so])[None]
    sd = np.concatenate([a.reshape(NSEQ, 16, 128, 128)[1:] for a in so])[None]
    pc = np.stack([a.reshape(NSEQ, 3, 6144)[0] for a in co])[None]
    sc = np.concatenate([a.reshape(NSEQ, 3, 6144)[1:] for a in co])[None]
    pf = np.stack([a.reshape(NSEQ, 2, 2 * DFF)[0] for a in fo])[None]
    sf = np.concatenate([a.reshape(NSEQ, 2, 2 * DFF)[1:] for a in fo])[None]
    return (yp, ys, pk, pv, pik, pd, pc, pf, sk, sv, sik, sd, sc, sf)
```

```python
import math
from contextlib import ExitStack
import numpy as np
import ml_dtypes
import concourse.bass as bass
import concourse.mybir as mybir
from concourse.bass_utils import run_bass_kernel_spmd

F32 = mybir.dt.float32
BF16 = mybir.dt.bfloat16
AF = mybir.ActivationFunctionType
ALU = mybir.AluOpType

D = 4096
KC = 32
NIN = 12400
HD = 128
PAST = 1024
DEC = 64
EPS = 1e-5
ALPHA = 2.0 ** 0.25
IDX_SCALE = (16 ** -0.5) * (64 ** -0.5)
O_QA, O_KA, O_VA, O_IQ, O_IK, O_IW, O_QKV, O_Z, O_BETA, O_A = 0, 2048, 2560, 3072, 4096, 4160, 4176, 10320, 12368, 12384
NEG = -1.0e30
WIN_PIECES = [(0, 0, 4096), (4096, 4096, 64), (4160, 4096, 64), (4224, 4176, 8192), (12416, 4096, 80), (12496, 12368, 32)]
WIN_PACKED = 12544
P_QA, P_KA, P_VA, P_IQ, P_IK2, P_QKV, P_Z, P_SM1, P_SM2 = 0, 2048, 2560, 3072, 4096, 4224, 10368, 12416, 12496
DBG = False
STOP_C = 99


MUTE = [False]


def stop_at(n):
    if STOP_C <= n:
        MUTE[0] = True


class Reg:
    __slots__ = ("w", "r", "n")

    def __init__(s, n=""):
        s.w = {}
        s.r = {}
        s.n = n


class Tile:
    def __init__(s, t, n):
        s.t = t
        s.reg = Reg(n)

    def __getitem__(s, i):
        return s.t[i]


class Eng:
    def __init__(s, name, h):
        s.name = name
        s.h = h
        s.semidx = None
        s.cnt = 0
        s.known = {}
        s.pending = False
        s.dsems = []
        s.dnext = 0


class K:
    SEM_LIMIT = 30000

    def __init__(s, nc, es):
        s.nc = nc
        s.es = es
        s.sems = []
        s.semmax = []
        s.E = {}
        for n, h in (("pe", nc.tensor), ("act", nc.scalar), ("dve", nc.vector), ("pool", nc.gpsimd), ("sp", nc.sync)):
            e = Eng(n, h)
            s.E[n] = e
            if n != "sp":
                e.semidx = s.newsem()
        for n, cnt in (("sp", 12), ("act", 4), ("pool", 8)):
            s.E[n].dsems = [s.newsem() for _ in range(cnt)]
        s.uid = 0

    def newsem(s):
        h = s.es.enter_context(s.nc.semaphore("sem%d" % len(s.sems)))
        s.sems.append(h)
        s.semmax.append(0)
        return len(s.sems) - 1

    def sb(s, st, name, shape, dt):
        s.uid += 1
        nm = "%s_%d" % (name, s.uid)
        return Tile(st.enter_context(s.nc.sbuf_tensor(nm, list(shape), dt)), nm)

    def ps(s, st, name, shape, dt=F32):
        s.uid += 1
        nm = "%s_%d" % (name, s.uid)
        return Tile(st.enter_context(s.nc.psum_tensor(nm, list(shape), dt)), nm)

    def dram(s, name, shape, dt, kind="Internal"):
        if DBG and kind == "Internal":
            kind = "ExternalOutput"
        t = s.nc.dram_tensor(name, list(shape), dt, kind=kind)
        tl = Tile(t.ap(), name)
        return tl

    def _deps(s, R, W, Wp):
        deps = {}
        for r in R:
            for k, v in r.w.items():
                if deps.get(k, 0) < v:
                    deps[k] = v
        for w in list(W) + list(Wp):
            for k, v in w.w.items():
                if deps.get(k, 0) < v:
                    deps[k] = v
            for k, v in w.r.items():
                if deps.get(k, 0) < v:
                    deps[k] = v
        return deps

    def _waits(s, eng, deps, ename):
        for k, v in deps.items():
            if k == eng.semidx and ename == "pe":
                continue
            if eng.known.get(k, 0) >= v:
                continue
            eng.h.wait_ge(s.sems[k], v)
            eng.known[k] = v

    def _mark(s, t, R, W, Wp):
        k, v = t
        for r in R:
            if r.r.get(k, 0) < v:
                r.r[k] = v
        for w in W:
            w.w = {k: v}
            w.r = {}
        for w in Wp:
            if w.w.get(k, 0) < v:
                w.w[k] = v

    def op(s, e, fn, R=(), W=(), Wp=(), sig=True, Wa=()):
        if MUTE[0]:
            return None
        R = [x.reg if isinstance(x, Tile) else x for x in R]
        Wa = [x.reg if isinstance(x, Tile) else x for x in Wa]
        W = [x.reg if isinstance(x, Tile) else x for x in W]
        Wp = [x.reg if isinstance(x, Tile) else x for x in Wp]
        eng = s.E[e]
        if eng.cnt >= s.SEM_LIMIT and not eng.pending:
            eng.semidx = s.newsem()
            eng.cnt = 0
        s._waits(eng, s._deps(R, W, list(Wp) + list(Wa)), e)
        ins = fn()
        if sig:
            eng.cnt += 1
            ins.then_inc(s.sems[eng.semidx], 1)
            s.semmax[eng.semidx] = eng.cnt
            eng.pending = False
            t = (eng.semidx, eng.cnt)
        else:
            eng.pending = True
            t = (eng.semidx, eng.cnt + 1)
        s._mark(t, R, W, Wp)
        return ins

    def dma(s, q, out, in_, R=(), W=(), Wp=(), **kw):
        if MUTE[0]:
            return None
        R = [x.reg if isinstance(x, Tile) else x for x in R]
        W = [x.reg if isinstance(x, Tile) else x for x in W]
        Wp = [x.reg if isinstance(x, Tile) else x for x in Wp]
        eng = s.E[q]
        si = eng.dsems[eng.dnext % len(eng.dsems)]
        eng.dnext += 1
        deps = s._deps(R, W, Wp)
        cur = s.semmax[si]
        if cur > 0 and deps.get(si, 0) < cur:
            deps[si] = cur
        s._waits(eng, deps, q)
        ins = eng.h.dma_start(out=out, in_=in_, **kw)
        ins.then_inc(s.sems[si], 16)
        s.semmax[si] = cur + 16
        s._mark((si, cur + 16), R, W, Wp)
        return ins

    def barrier(s, engines=("pe", "act", "dve", "pool", "sp")):
        for n in engines:
            eng = s.E[n]
            assert not eng.pending
            for k, v in enumerate(s.semmax):
                if v > 0 and eng.known.get(k, 0) < v and k != eng.semidx:
                    eng.h.wait_ge(s.sems[k], v)
                    eng.known[k] = v


class Cfg:
    def __init__(s, SEQ, NS, DFF):
        s.SEQ, s.NS, s.DFF = SEQ, NS, DFF
        s.FC = DFF // 128
        s.NT = SEQ + NS * DEC
        assert s.NT % 128 == 0 and SEQ % 128 == 0 and DFF % 128 == 0
        s.NSEQ = 1 + NS
        s.seqs = [dict(t0=0, T=SEQ, past=0, si=-1)] + [dict(t0=SEQ + DEC * i, T=DEC, past=PAST, si=i) for i in range(NS)]
        s.TOPK_P = min(256, SEQ // 4)
        s.TOPK_S = min(256, (PAST + DEC) // 4)

    def groups(s, gmax):
        n = -(-s.NT // gmax)
        per = -(-(s.NT // 128) // n) * 128
        out = []
        t = 0
        while t < s.NT:
            g = min(per, s.NT - t)
            out.append((t, g))
            t += g
        return out


def blocks(n, b=512):
    return [(i, min(b, n - i)) for i in range(0, n, b)]


def t5_bucket_np(rel):
    rel = np.asarray(rel, np.int64)
    half, max_exact = 16, 8
    side = np.where(rel > 0, half, 0)
    n = np.abs(rel)
    nf = np.maximum(n, 1).astype(np.float32)
    large = max_exact + (np.log(nf / np.float32(max_exact)) / np.float32(math.log(128 / max_exact))
                         * np.float32(half - max_exact)).astype(np.int32)
    large = np.minimum(large, half - 1)
    return side + np.where(n < max_exact, n, large)


def host_consts():
    c = {}
    c["c_ident"] = np.eye(128, dtype=np.float32)
    c["c_anti"] = np.eye(128, dtype=np.float32)[::-1].copy()
    i = np.arange(128)
    same = (i[:, None] // 64) == (i[None, :] // 64)
    c["c_cum"] = (same & (i[:, None] <= i[None, :])).astype(np.float32)
    c["c_blk"] = same.astype(np.float32)
    c["c_nmL"] = np.where(same & (i[:, None] > i[None, :]), 0.0, -1e4).astype(np.float32)
    c["c_nmT"] = np.where(same & (i[None, :] >= i[:, None]), 0.0, -1e4).astype(np.float32)
    c["c_strict"] = (same & (i[:, None] > i[None, :])).astype(np.float32)
    rel = np.arange(384) - 255
    bk = t5_bucket_np(rel)
    oh = np.zeros((32, 384), np.float32)
    oh[bk, np.arange(384)] = 1.0
    oh[15, :] -= 1.0
    c["c_oh"] = oh
    return c


CONST_SHAPES = {"c_ident": [128, 128], "c_anti": [128, 128], "c_cum": [128, 128], "c_blk": [128, 128],
                "c_nmL": [128, 128], "c_nmT": [128, 128], "c_strict": [128, 128], "c_oh": [32, 384]}


def build(cfg, phases="ABCDEF"):
    nc = bass.Bass("TRN2", target_bir_lowering=False)
    es = ExitStack()
    with es:
        k = K(nc, es)
        _program(k, cfg, phases)
    return nc


def _program(k, cfg, phases):
    nc = k.nc
    NT, NS, DFF, FC, NSEQ = cfg.NT, cfg.NS, cfg.DFF, cfg.FC, cfg.NSEQ
    I = {}

    def din(name, shape):
        I[name] = k.dram(name, shape, F32, kind="ExternalInput")
        return I[name]

    def dout(name, shape):
        I[name] = k.dram(name, shape, F32, kind="ExternalOutput")
        return I[name]

    din("x", [NT, D])
    din("ck", [NS * PAST, 512]); din("cv", [NS * PAST, 512]); din("cik", [NS * PAST, 64])
    din("sdel", [NS * 16 * 128, 128]); din("sconv", [NS * 3, 6144]); din("sffn", [NS * 2, 2 * DFF])
    din("lnin", [128, 64]); din("ln1", [128, 64]); din("ln2", [128, 64])
    din("w_in", [D, NIN]); din("w_o", [D, D]); din("w_up", [D, 2 * DFF]); din("w_down", [DFF, D])
    din("convw", [128, 48 * 5]); din("ffnw", [128, 2 * FC * 4])
    din("alog", [128, 16]); din("dtb", [128, 16]); din("dng", [128, 1]); din("relb", [32, 16])
    for n, sh in CONST_SHAPES.items():
        din(n, sh)
    dout("y", [NT, D]); dout("ko", [NT, 512]); dout("vo", [NT, 512]); dout("iko", [NT, 64])
    dout("so", [NSEQ * 16 * 128, 128]); dout("convo", [NSEQ * 3, 6144]); dout("ffno", [NSEQ * 2, 2 * DFF])
    S = {}
    S["xnT"] = k.dram("s_xnT", [D, NT], F32)
    S["qaT"] = k.dram("s_qaT", [2048, NT], BF16)
    S["kaT"] = k.dram("s_kaT", [512, NT], BF16)
    S["vbf"] = k.dram("s_vbf", [NT, 512], BF16)
    S["iqT"] = k.dram("s_iqT", [1024, NT], BF16)
    S["ikT2"] = k.dram("s_ikT2", [128, NT], BF16)
    S["iw"] = k.dram("s_iw", [NT, 16], F32)
    S["ba"] = k.dram("s_ba", [NT, 32], F32)
    S["qkvT"] = k.dram("s_qkvT", [6144, NT], F32)
    S["zT"] = k.dram("s_zT", [2048, NT], F32)
    S["aT"] = k.dram("s_aT", [D, NT], BF16)
    S["x1T"] = k.dram("s_x1T", [D, NT], F32)
    S["x1b"] = k.dram("s_x1b", [D, NT], BF16)
    S["actT"] = k.dram("s_actT", [DFF, NT], BF16)
    S["Fd"] = k.dram("s_Fd", [16, 384 + 128], F32)
    S["b_w_in"] = k.dram("s_bwin", [WIN_PACKED // 256, 1, 128, 32, 256], BF16)
    S["b_w_o"] = k.dram("s_bwo", [D // 256, 1, 128, 32, 256], BF16)
    S["b_w_up"] = k.dram("s_bwup", [2 * DFF // 256, 1, 128, 32, 256], BF16)
    S["b_w_down"] = k.dram("s_bwdn", [D // 256, len(kgroups(FC)), 128, 32, 256], BF16)

    with ExitStack() as gs:
        C = {}
        for n, sh in CONST_SHAPES.items():
            C[n] = k.sb(gs, n, sh, F32)
            k.dma("sp", C[n][:], I[n][:], W=[C[n]])
        ident_b = k.sb(gs, "identb", [128, 128], BF16)
        k.op("pool", lambda: nc.gpsimd.tensor_copy(out=ident_b[:], in_=C["c_ident"][:]), R=[C["c_ident"]], W=[ident_b])
        ones_f = k.sb(gs, "onesf", [128, 128], F32)
        k.op("pool", lambda: nc.gpsimd.memset(ones_f[:], 1.0), W=[ones_f])
        ones_b = k.sb(gs, "onesb", [128, 128], BF16)
        k.op("pool", lambda: nc.gpsimd.memset(ones_b[:], 1.0), W=[ones_b])
        G = dict(C=C, ident_b=ident_b, ones_f=ones_f, ones_b=ones_b, I=I, S=S)
        psum = [k.ps(gs, "bank%d" % i, [128, 512]) for i in range(8)]
        G["psum"] = psum
        G["pn"] = 0

        def bank():
            b = psum[G["pn"] % 8]
            G["pn"] += 1
            return b
        G["bank"] = bank
        G["alt"] = 0

        phase_W(k, cfg, G)
        k.barrier()
        if "A" in phases:
            phase_A(k, cfg, G)
            k.barrier()
        if "B" in phases:
            phase_B(k, cfg, G)
            k.barrier()
        if "C" in phases:
            phase_C(k, cfg, G)
            MUTE[0] = False
            k.barrier()
        if "D" in phases:
            phase_D(k, cfg, G)
            k.barrier()
        if "E" in phases:
            phase_E(k, cfg, G)
            k.barrier()
        if "F" in phases:
            phase_F(k, cfg, G)
        k.barrier()


def evac_engine(G):
    G["alt"] += 1
    return "act" if G["alt"] % 2 else "dve"


class WTiles:
    def __init__(s, k, st, nslot=3):
        s.k = k
        s.slots = [k.sb(st, "wt", [128, 32, 256], BF16) for _ in range(nslot)]
        s.tags = [None] * nslot
        s.n = 0

    def get(s, scr, tile, kg=0, kcn=32):
        tag = (scr.reg.n, tile, kg)
        for i, t in enumerate(s.tags):
            if t == tag:
                return s.slots[i]
        i = s.n % len(s.slots)
        s.n += 1
        s.tags[i] = tag
        wb = s.slots[i]
        s.k.dma("sp", wb[:, 0:kcn, :], scr[tile, kg, :, 0:kcn, :], R=[scr], W=[wb])
        return wb


def kgroups(kctot):
    return [(i, min(32, kctot - i)) for i in range(0, kctot, 32)]


def w_units(k, cfg, G, names, st, kstep=4, gcols=512):
    nc = k.nc
    I, S = G["I"], G["S"]
    allspecs = {"w_in": (I["w_in"], WIN_PIECES, WIN_PACKED, KC), "w_o": (I["w_o"], [(0, 0, D)], D, KC),
                "w_up": (I["w_up"], [(0, 0, 2 * cfg.DFF)], 2 * cfg.DFF, KC), "w_down": (I["w_down"], [(0, 0, D)], D, cfg.FC)}
    nt = gcols // 256
    stg = [k.sb(st, "wstg", [128, kstep, gcols], F32) for _ in range(2)]
    sbf = [k.sb(st, "wsbf", [128, nt, kstep, 256], BF16) for _ in range(2)]
    units = []
    for name in names:
        src, pieces, ncol, kctot = allspecs[name]
        dst = S["b_" + name]
        for g0 in range(0, ncol, gcols):
            gw = min(gcols, ncol - g0)
            for kgi, (kg0, kgn) in enumerate(kgroups(kctot)):
                for k8 in range(0, kgn, kstep):
                    units.append((src, pieces, dst, g0, gw, kgi, kg0, k8, min(kstep, kgn - k8)))

    def load(n):
        src, pieces, dst, g0, gw, kgi, kg0, k8, kn = units[n]
        r0 = (kg0 + k8) * 128
        st_ = stg[n % 2]
        first = True
        for (d0, s0, pn) in pieces:
            lo, hi = max(d0, g0), min(d0 + pn, g0 + gw)
            if lo < hi:
                srcap = src[r0:r0 + kn * 128, s0 + lo - d0:s0 + hi - d0].rearrange("(kc p) n -> p kc n", p=128)
                if first:
                    k.dma("sp", st_[:, 0:kn, lo - g0:hi - g0], srcap, W=[st_])
                else:
                    k.dma("sp", st_[:, 0:kn, lo - g0:hi - g0], srcap, Wp=[st_])
                first = False

    def finish(n):
        src, pieces, dst, g0, gw, kgi, kg0, k8, kn = units[n]
        st_ = stg[n % 2]
        sb_ = sbf[n % 2]
        nt4 = gw // 256
        iv = st_[:, 0:kn, 0:gw].rearrange("p k (t c) -> p t k c", c=256)
        ov = sb_[:, 0:nt4, 0:kn, :]
        e = G.get("wcast", ("act", "dve", "pool"))
        e = e[n % len(e)]
        if e == "act":
            k.op("act", lambda: nc.scalar.copy(out=ov, in_=iv), R=[st_], W=[sb_])
        elif e == "dve":
            k.op("dve", lambda: nc.vector.tensor_copy(out=ov, in_=iv), R=[st_], W=[sb_])
        else:
            k.op("pool", lambda: nc.gpsimd.tensor_copy(out=ov, in_=iv), R=[st_], W=[sb_])
        t0 = g0 // 256
        k.dma("pool", dst[t0:t0 + nt4, kgi, :, k8:k8 + kn, :].rearrange("t p k c -> p t k c"), ov, R=[sb_], Wp=[dst])

    for n in range(len(units)):
        load(n)
        if n >= 1:
            finish(n - 1)
        yield n
    finish(len(units) - 1)
    yield len(units)


def phase_W(k, cfg, G):
    with ExitStack() as st:
        G["wcast"] = ("act", "dve")
        for _ in w_units(k, cfg, G, ["w_in"], st, kstep=8, gcols=1024):
            pass


def phase_A(k, cfg, G):
    nc = k.nc
    I, S, C = G["I"], G["S"], G["C"]
    NT = cfg.NT
    bank = G["bank"]
    for (g0, gn) in cfg.groups(1152):
        with ExitStack() as st:
            xnT = k.sb(st, "xnT", [128, KC, gn], BF16)
            gb = k.sb(st, "lnin", [128, 64], F32)
            k.dma("sp", gb[:], I["lnin"][:], W=[gb])
            with ExitStack() as s1:
                xs = [k.sb(s1, "xs", [128, D], F32) for _ in range(2)]
                xf = [k.sb(s1, "xf", [128, KC, 128], F32) for _ in range(2)]
                stt = [k.sb(s1, "stt", [128, 8, 6], F32) for _ in range(2)]
                mv = [k.sb(s1, "mv", [128, 4], F32) for _ in range(2)]
                for ti in range(gn // 128):
                    t0 = g0 + ti * 128
                    x_, f_, st_, mv_ = xs[ti % 2], xf[ti % 2], stt[ti % 2], mv[ti % 2]
                    k.dma("sp", x_[:], I["x"][t0:t0 + 128, :], W=[x_])
                    for j in range(8):
                        k.op("dve", lambda j=j: nc.vector.bn_stats(out=st_[:, j, :], in_=x_[:, j * 512:(j + 1) * 512]),
                             R=[x_], Wp=[st_] if j else (), W=() if j else [st_])
                    k.op("dve", lambda: nc.vector.bn_aggr(out=mv_[:, 0:2], in_=st_[:].rearrange("p a b -> p (a b)")), R=[st_], W=[mv_])
                    k.op("act", lambda: nc.scalar.activation(out=mv_[:, 2:3], in_=mv_[:, 1:2], func=AF.Sqrt, bias=EPS, scale=1.0), R=[mv_], Wp=[mv_])
                    k.op("dve", lambda: nc.vector.reciprocal(out=mv_[:, 3:4], in_=mv_[:, 2:3]), R=[mv_], Wp=[mv_])
                    k.op("dve", lambda: nc.vector.tensor_scalar(out=x_[:], in0=x_[:], scalar1=mv_[:, 0:1], scalar2=mv_[:, 3:4],
                                                                op0=ALU.subtract, op1=ALU.mult), R=[mv_, x_], W=[x_])
                    for q in range(8):
                        b = bank()
                        for j in range(4):
                            kc = q * 4 + j
                            k.op("pe", lambda kc=kc, j=j: nc.tensor.transpose(b[:, j * 128:(j + 1) * 128], x_[:, kc * 128:(kc + 1) * 128], C["c_ident"][:]),
                                 R=[x_, C["c_ident"]], W=[b], sig=(j == 3))
                        for j in range(4):
                            kc = q * 4 + j
                            if (kc % 2) == 0:
                                k.op("act", lambda kc=kc, j=j: nc.scalar.activation(out=f_[:, kc, :], in_=b[:, j * 128:(j + 1) * 128], func=AF.Identity,
                                                                                   scale=gb[:, kc:kc + 1], bias=gb[:, 32 + kc:33 + kc]),
                                     R=[b, gb], Wp=[f_])
                            else:
                                k.op("dve", lambda kc=kc, j=j: nc.vector.tensor_scalar(out=f_[:, kc, :], in0=b[:, j * 128:(j + 1) * 128],
                                                                                      scalar1=gb[:, kc:kc + 1], scalar2=gb[:, 32 + kc:33 + kc],
                                                                                      op0=ALU.mult, op1=ALU.add),
                                     R=[b, gb], Wp=[f_])
                    k.op("pool", lambda: nc.gpsimd.tensor_copy(out=xnT[:, :, ti * 128:(ti + 1) * 128], in_=f_[:]), R=[f_], Wp=[xnT])
                    k.dma("pool", S["xnT"][:].rearrange("(kc p) t -> p kc t", p=128)[:, :, t0:t0 + 128], f_[:], R=[f_], Wp=[S["xnT"]])
            k.barrier()
            with ExitStack() as s2:
                ws = WTiles(k, s2, nslot=3)
                osf = [k.sb(s2, "osf", [128, gn], F32) for _ in range(2)]
                osb = [k.sb(s2, "osb", [128, gn], BF16) for _ in range(2)]
                otk = [k.sb(s2, "otk", [128, gn // 128, 128], F32) for _ in range(2)]
                otb = [k.sb(s2, "otb", [128, gn // 128, 128], BF16) for _ in range(2)]
                cnt = {"f": 0, "b": 0, "t": 0}

                def fm_job(pc, m, dst, drow, mode):
                    wb = ws.get(S["b_w_in"], pc // 256)
                    sub = pc % 256
                    if mode == "f32" or mode == "silu":
                        o = osf[cnt["f"] % 2]; cnt["f"] += 1
                    else:
                        o = osb[cnt["b"] % 2]; cnt["b"] += 1
                    for (b0, bn) in blocks(gn):
                        b = bank()
                        for kc in range(KC):
                            k.op("pe", lambda kc=kc: nc.tensor.matmul(b[0:m, 0:bn], lhsT=wb[:, kc, sub:sub + m], rhs=xnT[:, kc, b0:b0 + bn],
                                                                       start=(kc == 0), stop=(kc == KC - 1)),
                                 R=[wb, xnT], W=[b], sig=(kc == KC - 1))
                        e = evac_engine(G)
                        if mode == "silu":
                            k.op("act", lambda: nc.scalar.activation(out=o[0:m, b0:b0 + bn], in_=b[0:m, 0:bn], func=AF.Silu), R=[b], Wp=[o])
                        elif mode == "qs":
                            k.op("act", lambda: nc.scalar.mul(o[0:m, b0:b0 + bn], b[0:m, 0:bn], HD ** -0.5), R=[b], Wp=[o])
                        elif e == "act":
                            k.op("act", lambda: nc.scalar.copy(out=o[0:m, b0:b0 + bn], in_=b[0:m, 0:bn]), R=[b], Wp=[o])
                        else:
                            k.op("dve", lambda: nc.vector.tensor_copy(out=o[0:m, b0:b0 + bn], in_=b[0:m, 0:bn]), R=[b], Wp=[o])
                    k.dma("pool", dst[drow:drow + m, g0:g0 + gn], o[0:m, :], R=[o], Wp=[dst])

                def tm_job(pc, ncols, outs):
                    wb = ws.get(S["b_w_in"], pc // 256)
                    sub = pc % 256
                    o = otk[cnt["t"] % 2]
                    ob = otb[cnt["t"] % 2]
                    cnt["t"] += 1
                    for ti in range(gn // 128):
                        b = bank()
                        for kc in range(KC):
                            k.op("pe", lambda kc=kc: nc.tensor.matmul(b[:, 0:ncols], lhsT=xnT[:, kc, ti * 128:(ti + 1) * 128], rhs=wb[:, kc, sub:sub + ncols],
                                                                       start=(kc == 0), stop=(kc == KC - 1)),
                                 R=[wb, xnT], W=[b], sig=(kc == KC - 1))
                        k.op("dve", lambda: nc.vector.tensor_copy(out=o[:, ti, 0:ncols], in_=b[:, 0:ncols]), R=[b], Wp=[o])
                    for (dst, dc0, sc0, n, dt) in outs:
                        dview = dst[g0:g0 + gn, dc0:dc0 + n].rearrange("(ti p) n -> p ti n", p=128)
                        if dt == "bf16":
                            k.op("pool", lambda: nc.gpsimd.tensor_copy(out=ob[:, :, sc0:sc0 + n], in_=o[:, :, sc0:sc0 + n]), R=[o], W=[ob])
                            k.dma("pool", dview, ob[:, :, sc0:sc0 + n], R=[ob], Wp=[dst])
                        else:
                            k.dma("pool", dview, o[:, :, sc0:sc0 + n], R=[o], Wp=[dst])

                for c in range(16):
                    fm_job(P_QA + c * 128, 128, S["qaT"], c * 128, "qs")
                for c in range(4):
                    fm_job(P_KA + c * 128, 128, S["kaT"], c * 128, "bf16")
                for c in range(4):
                    tm_job(P_KA + c * 128, 128, [(I["ko"], c * 128, 0, 128, "f32")])
                for c in range(4):
                    tm_job(P_VA + c * 128, 128, [(I["vo"], c * 128, 0, 128, "f32"), (S["vbf"], c * 128, 0, 128, "bf16")])
                for c in range(8):
                    fm_job(P_IQ + c * 128, 128, S["iqT"], c * 128, "bf16")
                fm_job(P_IK2, 128, S["ikT2"], 0, "bf16")
                for c in range(48):
                    fm_job(P_QKV + c * 128, 128, S["qkvT"], c * 128, "f32")
                for c in range(16):
                    fm_job(P_Z + c * 128, 128, S["zT"], c * 128, "silu")
                tm_job(P_SM1, 80, [(I["iko"], 0, 0, 64, "f32"), (S["iw"], 0, 64, 16, "f32")])
                tm_job(P_SM2, 32, [(S["ba"], 0, 0, 32, "f32")])
            k.barrier()


def phase_B(k, cfg, G):
    nc = k.nc
    I, S, C = G["I"], G["S"], G["C"]
    bank = G["bank"]
    ident, ident_b, ones_b = C["c_ident"], G["ident_b"], G["ones_b"]
    NS = cfg.NS
    SKMAX = max(cfg.SEQ, PAST + 128)
    KTMAX = SKMAX // 128
    with ExitStack() as pst:
        sb = lambda n, sh, dt=F32: k.sb(pst, n, sh, dt)
        biasT = sb("biasT", [128, 2, 16, 128], BF16)
        with ExitStack() as s0:
            relb = k.sb(s0, "relb", [32, 16], F32)
            Fs = k.sb(s0, "Fs", [16, 512], F32)
            XT = k.sb(s0, "XT", [128, 2, 16, 128], F32)
            k.dma("sp", relb[:], I["relb"][:], W=[relb])
            k.op("pool", lambda: nc.gpsimd.memset(Fs[:], 0.0), W=[Fs])
            b = bank()
            k.op("pe", lambda: nc.tensor.matmul(b[0:16, 0:384], lhsT=relb[:, :], rhs=C["c_oh"][:, :], start=True, stop=True), R=[relb, C["c_oh"]], W=[b])
            k.op("dve", lambda: nc.vector.tensor_copy(out=Fs[:, 0:384], in_=b[0:16, 0:384]), R=[b], Wp=[Fs])
            k.dma("sp", S["Fd"][:], Fs[:], R=[Fs], W=[S["Fd"]])
            fd_t = S["Fd"][:].tensor
            for w, off in ((0, 128), (1, 0)):
                src = bass.AP(tensor=fd_t, offset=off, ap=[[1, 128], [512, 16], [1, 128]])
                k.dma("sp", XT[:, w, :, :], src, R=[S["Fd"]], Wp=[XT])
            for w in range(2):
                for h4 in range(0, 16, 4):
                    b = bank()
                    for j in range(4):
                        k.op("pe", lambda: nc.tensor.matmul(b[:, j * 128:(j + 1) * 128], lhsT=XT[:, w, h4 + j, :], rhs=C["c_anti"][:], start=True, stop=True),
                             R=[XT, C["c_anti"]], W=[b], sig=(j == 3))
                    k.op("dve", lambda: nc.vector.tensor_copy(out=biasT[:, w, h4:h4 + 4, :], in_=b[:, :].rearrange("p (a b) -> p a b", b=128)), R=[b], Wp=[biasT])
        k.barrier()
        kT = sb("kT", [128, 4, SKMAX], BF16)
        vv = sb("vv", [128, KTMAX, 4, 128], BF16)
        ik2 = sb("ik2", [128, SKMAX], BF16)
        cst = sb("cst", [128, 8, 512], F32)
        cikst = sb("cikst", [128, 8, 128], F32)
        qT = [sb("qT", [128, 16, 128], BF16) for _ in range(2)]
        iq = [sb("iq", [128, 8, 128], BF16) for _ in range(2)]
        iw = [sb("iw", [128, 16], F32) for _ in range(2)]
        index = sb("index", [128, SKMAX]); work = sb("work", [128, SKMAX]); mask01 = sb("mask01", [128, SKMAX])
        rr_ = [sb("relu", [128, 512]) for _ in range(2)]
        m8 = sb("m8", [128, 8]); thr = sb("thr", [128, 1])
        maskT = sb("maskT", [128, KTMAX, 128], BF16)
        pt = [sb("pt", [128, 512], BF16) for _ in range(3)]
        rcp = sb("rcp", [128, 512])
        oa = [sb("oa", [128, 16, 128], BF16) for _ in range(2)]
        G["wcast"] = ("act",)
        wgen = w_units(k, cfg, G, ["w_o", "w_up", "w_down"], pst, kstep=4, gcols=512)
        n_units = 0
        for nm_, kct_ in (("w_o", KC), ("w_up", KC), ("w_down", cfg.FC)):
            ncol_ = {"w_o": D, "w_up": 2 * cfg.DFF, "w_down": D}[nm_]
            n_units += (-(-ncol_ // 512)) * sum(-(-kn_ // 4) for (_, kn_) in kgroups(kct_))
        n_iter = sum(4 * (-(-sq_["T"] // 128)) for sq_ in cfg.seqs)
        per_iter = -(-n_units // n_iter)

        def wstep(cnt):
            for _ in range(cnt):
                try:
                    next(wgen)
                except StopIteration:
                    return
        qn = 0
        for qi, sq in enumerate(cfg.seqs):
            t0, T, si, past = sq["t0"], sq["T"], sq["si"], sq["past"]
            SK = past + T
            if si < 0:
                k.dma("sp", kT[:, :, 0:T], S["kaT"][:, t0:t0 + T].rearrange("(g p) t -> p g t", p=128), R=[S["kaT"]], W=[kT])
                k.dma("sp", vv[:, 0:T // 128, :, :], S["vbf"][t0:t0 + T, :].rearrange("(kt p) (g d) -> p kt g d", p=128, d=128), R=[S["vbf"]], W=[vv])
                k.dma("sp", ik2[:, 0:T], S["ikT2"][:, t0:t0 + T], R=[S["ikT2"]], W=[ik2])
            else:
                k.dma("sp", cst[:], I["ck"][si * PAST:(si + 1) * PAST, :].rearrange("(kt p) n -> p kt n", p=128), W=[cst])
                for kt in range(8):
                    b = bank()
                    for g in range(4):
                        k.op("pe", lambda: nc.tensor.transpose(b[:, g * 128:(g + 1) * 128], cst[:, kt, g * 128:(g + 1) * 128], ident[:]), R=[cst, ident], W=[b], sig=(g == 3))
                    k.op("act", lambda: nc.scalar.copy(out=kT[:, :, kt * 128:(kt + 1) * 128], in_=b[:, :].rearrange("p (a b) -> p a b", b=128)), R=[b], Wp=[kT])
                k.dma("sp", cst[:], I["cv"][si * PAST:(si + 1) * PAST, :].rearrange("(kt p) n -> p kt n", p=128), W=[cst])
                k.op("pool", lambda: nc.gpsimd.tensor_copy(out=vv[:, 0:8, :, :].rearrange("p a g d -> p a (g d)"), in_=cst[:]), R=[cst], Wp=[vv])
                ciksrc = I["cik"][si * PAST:(si + 1) * PAST, :].rearrange("(kt p) n -> p kt n", p=128)
                k.dma("sp", cikst[:, :, 0:64], ciksrc, W=[cikst])
                k.dma("sp", cikst[:, :, 64:128], ciksrc, Wp=[cikst])
                for k4 in range(0, 8, 4):
                    b = bank()
                    for j in range(4):
                        k.op("pe", lambda: nc.tensor.transpose(b[:, j * 128:(j + 1) * 128], cikst[:, k4 + j, :], ident[:]), R=[cikst, ident], W=[b], sig=(j == 3))
                    k.op("dve", lambda: nc.vector.tensor_copy(out=ik2[:, k4 * 128:(k4 + 4) * 128], in_=b[:, :]), R=[b], Wp=[ik2])
                k.dma("sp", kT[:, :, PAST:PAST + T], S["kaT"][:, t0:t0 + T].rearrange("(g p) t -> p g t", p=128), R=[S["kaT"]], Wp=[kT])
                k.dma("sp", vv[0:T, 8, :, :], S["vbf"][t0:t0 + T, :].rearrange("p (g d) -> p g d", d=128), R=[S["vbf"]], Wp=[vv])
                k.dma("sp", ik2[:, PAST:PAST + T], S["ikT2"][:, t0:t0 + T], R=[S["ikT2"]], Wp=[ik2])
            topk = cfg.TOPK_P if si < 0 else cfg.TOPK_S
            for qt in range(-(-T // 128)):
                nq = min(128, T - qt * 128)
                ta = t0 + qt * 128
                SKq = past + qt * 128 + nq if si < 0 else SK
                KTq = -(-SKq // 128)
                q_, iq_, iw_, oa_ = qT[qn % 2], iq[qn % 2], iw[qn % 2], oa[qn % 2]
                qn += 1
                k.dma("sp", q_[:, :, 0:nq], S["qaT"][:, ta:ta + nq].rearrange("(h p) t -> p h t", p=128), R=[S["qaT"]], W=[q_])
                k.dma("sp", iq_[:, :, 0:nq], S["iqT"][:, ta:ta + nq].rearrange("(h p) t -> p h t", p=128), R=[S["iqT"]], W=[iq_])
                k.dma("sp", iw_[0:nq, :], S["iw"][ta:ta + nq, :], R=[S["iw"]], W=[iw_])
                k.op("pool", lambda: nc.gpsimd.tensor_scalar(out=iw_[0:nq, :], in0=iw_[0:nq, :], scalar1=IDX_SCALE, scalar2=None, op0=ALU.mult), R=[iw_], W=[iw_])
                rn = 0
                for (c0, cn) in blocks(SKq):
                    for hp in range(8):
                        for half in range(2):
                            h = hp * 2 + half
                            pr = slice(half * 64, half * 64 + 64)
                            b = bank()
                            k.op("pe", lambda: nc.tensor.matmul(b[0:nq, 0:cn], lhsT=iq_[pr, hp, 0:nq], rhs=ik2[pr, c0:c0 + cn], start=True, stop=True), R=[iq_, ik2], W=[b])
                            r_ = rr_[rn % 2]
                            rn += 1
                            k.op("act", lambda: nc.scalar.activation(out=r_[0:nq, 0:cn], in_=b[0:nq, 0:cn], func=AF.Relu), R=[b], W=[r_])
                            if h == 0:
                                k.op("dve", lambda: nc.vector.tensor_scalar(out=index[0:nq, c0:c0 + cn], in0=r_[0:nq, 0:cn], scalar1=iw_[0:nq, 0:1], scalar2=None, op0=ALU.mult),
                                     R=[r_, iw_], Wp=[index])
                            else:
                                k.op("dve", lambda: nc.vector.scalar_tensor_tensor(out=index[0:nq, c0:c0 + cn], in0=r_[0:nq, 0:cn], scalar=iw_[0:nq, h:h + 1],
                                                                                   in1=index[0:nq, c0:c0 + cn], op0=ALU.mult, op1=ALU.add), R=[r_, iw_, index], Wp=[index])
                if si < 0:
                    k.op("dve", lambda: nc.vector.memset(index[0:64, SKq - 64:SKq], NEG), R=[index], Wp=[index])
                if SKq > topk:
                    nr = topk // 8
                    for rd in range(nr):
                        srcw = index if rd == 0 else work
                        k.op("dve", lambda: nc.vector.max(out=m8[0:nq, :], in_=srcw[0:nq, 0:SKq]), R=[srcw], W=[m8])
                        if rd < nr - 1:
                            k.op("dve", lambda: nc.vector.match_replace(out=work[0:nq, 0:SKq], in_to_replace=m8[0:nq, :], in_values=srcw[0:nq, 0:SKq], imm_value=NEG),
                                 R=[srcw, m8], W=[work])
                    k.op("dve", lambda: nc.vector.tensor_scalar(out=thr[0:nq, :], in0=m8[0:nq, 7:8], scalar1=-1.0e29, scalar2=None, op0=ALU.max), R=[m8], W=[thr])
                    k.op("dve", lambda: nc.vector.tensor_scalar(out=mask01[0:nq, 0:SKq], in0=index[0:nq, 0:SKq], scalar1=thr[0:nq, 0:1], scalar2=None, op0=ALU.is_ge),
                         R=[index, thr], W=[mask01])
                else:
                    k.op("dve", lambda: nc.vector.tensor_scalar(out=mask01[0:nq, 0:SKq], in0=index[0:nq, 0:SKq], scalar1=-1.0e29, scalar2=None, op0=ALU.is_ge),
                         R=[index], W=[mask01])
                for k4 in range(0, KTq, 4):
                    b = bank()
                    n4 = min(4, KTq - k4)
                    for j in range(n4):
                        kt = k4 + j
                        ks = min(128, SKq - kt * 128)
                        k.op("pe", lambda: nc.tensor.transpose(b[0:ks, j * 128:j * 128 + nq], mask01[0:nq, kt * 128:kt * 128 + ks], ident[0:nq, 0:nq]), R=[mask01, ident], W=[b], sig=(j == n4 - 1))
                    for j in range(n4):
                        kt = k4 + j
                        ks = min(128, SKq - kt * 128)
                        k.op("act", lambda: nc.scalar.copy(out=maskT[0:ks, kt, 0:nq], in_=b[0:ks, j * 128:j * 128 + nq]), R=[b], Wp=[maskT])
                pn = 0
                for g in range(4):
                    bO, bR = (G["psum"][4], G["psum"][5]) if g % 2 == 0 else (G["psum"][6], G["psum"][7])
                    for kt in range(KTq):
                        ks = min(128, SKq - kt * 128)
                        near = kt >= KTq - 2
                        w = 0 if kt == KTq - 1 else 1
                        wstep(1)
                        bl = G["psum"][pn % 4]
                        k.op("pe", lambda: nc.tensor.matmul(bl[0:ks, 0:4 * nq], lhsT=kT[:, g, kt * 128:kt * 128 + ks], rhs=q_[:, 4 * g:4 * g + 4, 0:nq], start=True, stop=not near),
                             R=[kT, q_], W=[bl], sig=not near)
                        if near:
                            k.op("pe", lambda: nc.tensor.matmul(bl[0:ks, 0:4 * nq], lhsT=ident_b[:, 0:ks], rhs=biasT[:, w, 4 * g:4 * g + 4, 0:nq], start=False, stop=True),
                                 R=[ident_b, biasT], W=[bl])
                        p_ = pt[pn % 3]
                        pn += 1
                        k.op("act", lambda: nc.scalar.activation(out=p_[0:ks, 0:4 * nq], in_=bl[0:ks, 0:4 * nq], func=AF.Exp), R=[bl], W=[p_])
                        pv = p_[0:ks, 0:4 * nq].rearrange("p (a b) -> p a b", b=nq)
                        k.op("pool", lambda: nc.gpsimd.tensor_tensor(out=pv, in0=pv, in1=maskT[0:ks, kt, 0:nq].unsqueeze(1).to_broadcast([ks, 4, nq]), op=ALU.mult),
                             R=[p_, maskT], W=[p_])
                        k.op("pe", lambda: nc.tensor.matmul(bO[:, 0:4 * nq], lhsT=vv[0:ks, kt, g, :], rhs=p_[0:ks, 0:4 * nq], start=(kt == 0), stop=(kt == KTq - 1)),
                             R=[vv, p_], W=[bO], sig=False)
                        k.op("pe", lambda: nc.tensor.matmul(bR[:, 0:4 * nq], lhsT=ones_b[0:ks, :], rhs=p_[0:ks, 0:4 * nq], start=(kt == 0), stop=(kt == KTq - 1)),
                             R=[ones_b, p_], W=[bR], sig=True)
                    k.op("dve", lambda: nc.vector.reciprocal(out=rcp[:, 0:4 * nq], in_=bR[:, 0:4 * nq]), R=[bR], W=[rcp])
                    k.op("dve", lambda: nc.vector.tensor_tensor(out=oa_[:, 4 * g:4 * g + 4, 0:nq], in0=bO[:, 0:4 * nq].rearrange("p (a b) -> p a b", b=nq),
                                                                in1=rcp[:, 0:4 * nq].rearrange("p (a b) -> p a b", b=nq), op=ALU.mult), R=[bO, rcp], Wp=[oa_])
                k.dma("pool", S["aT"][0:2048, ta:ta + nq].rearrange("(h p) t -> p h t", p=128), oa_[:, :, 0:nq], R=[oa_], Wp=[S["aT"]])
        wstep(10 ** 9)


def phase_C(k, cfg, G):
    nc = k.nc
    I, S, C = G["I"], G["S"], G["C"]
    bank = G["bank"]
    ones_f = G["ones_f"]
    ident = C["c_ident"]
    NS, NSEQ = cfg.NS, cfg.NSEQ
    HG = 4
    with ExitStack() as pst:
        sb = lambda n, sh, dt=F32: k.sb(pst, n, sh, dt)
        cw = sb("convw", [128, 48, 5])
        k.dma("sp", cw[:].rearrange("p a b -> p (a b)"), I["convw"][:], W=[cw])
        nea = sb("nea", [128, 16]); dtb = sb("dtb", [128, 16]); dng = sb("dng", [128, 1])
        k.dma("sp", nea[:], I["alog"][:], W=[nea])
        k.dma("sp", dtb[:], I["dtb"][:], W=[dtb])
        k.dma("sp", dng[:], I["dng"][:], W=[dng])
        k.op("act", lambda: nc.scalar.activation(out=nea[:], in_=nea[:], func=AF.Exp), R=[nea], W=[nea])
        k.op("pool", lambda: nc.gpsimd.tensor_scalar(out=nea[:], in0=nea[:], scalar1=-1.0, scalar2=None, op0=ALU.mult), R=[nea], W=[nea])
        cH = sb("cH", [128, 48, max(NS, 1) * 3])
        lst = sb("lst", [128, 48, NSEQ * 3])
        s0 = ExitStack()
        orow = k.sb(s0, "orow", [NSEQ * 3, 6144], F32)
        if NS > 0:
            srow = k.sb(s0, "srowc", [NS * 3, 6144], F32)
            k.dma("sp", srow[:], I["sconv"][:], W=[srow])
            for c4 in range(0, 48, 4):
                b = bank()
                for j in range(4):
                    k.op("pe", lambda j=j: nc.tensor.transpose(b[:, j * 128:j * 128 + NS * 3], srow[:, (c4 + j) * 128:(c4 + j + 1) * 128],
                                                               ident[0:NS * 3, 0:NS * 3]), R=[srow, ident], W=[b], sig=(j == 3))
                k.op("dve", lambda: nc.vector.tensor_copy(out=cH[:, c4:c4 + 4, :], in_=b[:, :].rearrange("p (a b) -> p a b", b=128)[:, :, 0:NS * 3]),
                     R=[b], Wp=[cH])
        qv = S["qkvT"][:].rearrange("(c p) t -> p c t", p=128)
        for qi, sq in enumerate(cfg.seqs):
            te = sq["t0"] + sq["T"]
            k.dma("sp", lst[:, :, qi * 3:(qi + 1) * 3], qv[:, :, te - 3:te], R=[S["qkvT"]], Wp=[lst])
        for c4 in range(0, 48, 4):
            b = bank()
            for j in range(4):
                k.op("pe", lambda j=j: nc.tensor.transpose(b[0:NSEQ * 3, j * 128:(j + 1) * 128], lst[:, c4 + j, :], ident[:]),
                     R=[lst, ident], W=[b], sig=(j == 3))
            k.op("dve", lambda: nc.vector.tensor_copy(out=orow[:, c4 * 128:(c4 + 4) * 128], in_=b[0:NSEQ * 3, :]), R=[b], Wp=[orow])
        k.dma("pool", I["convo"][:], orow[:], R=[orow], W=[I["convo"]])

        k.barrier()
        s0.close()
        stop_at(1)
        S_g = [sb("S", [128, HG, 128]) for _ in range(16 // HG)]
        ba = sb("ba", [128, 32]); beta = sb("beta", [128, 16]); xx = sb("xx", [128, 16]); t16 = sb("t16", [128, 16])
        g_ = sb("g", [128, 16]); Gs = sb("Gs", [128, 16]); bg = sb("bg", [128, 16]); edec = sb("edec", [128, 16])
        Dm = sb("Dm", [128, 16, 128]); eGbc = sb("eGbc", [128, 16, 128]); dmS = sb("dmS", [128, 16, 128]); dmT = sb("dmT", [128, 16, 128])

        def make_set():
            Bf = {}
            Bf['raw'] = sb("raw", [128, HG, 3, 131]); Bf['cv'] = sb("cv", [128, HG, 3, 128]); Bf['sqb'] = sb("sqb", [128, HG, 2, 128])
            Bf['ctmp'] = sb("ctmp", [128, HG, 3, 128])
            Bf['cvR'] = [[Reg("cvR") for _ in range(3)] for _ in range(HG)]
            Bf['ctR'] = [[Reg("ctR") for _ in range(3)] for _ in range(HG)]
            Bf['rst'] = sb("rst", [128, HG, 2, 128]); Bf['qd'] = sb("qd", [128, HG, 128])
            Bf['kbg'] = sb("kbg", [128, HG, 128]); Bf['kdec'] = sb("kdec", [128, HG, 128]); Bf['vb'] = sb("vb", [128, HG, 128])
            Bf['L'] = [sb("L", [128, HG, 128]) for _ in range(2)]; Bf['U'] = [sb("U", [128, HG, 128]) for _ in range(2)]
            Bf['P'] = sb("P", [128, HG, 128]); Bf['qkm'] = sb("qkm", [128, HG, 128]); Bf['wT'] = sb("wT", [128, HG, 128]); Bf['u'] = sb("u", [128, HG, 128])
            Bf['vnew'] = sb("vnew", [128, HG, 128]); Bf['oT'] = sb("oT", [128, HG, 128]); Bf['zs'] = sb("zs", [128, HG, 128]); Bf['ob'] = sb("ob", [128, HG, 128], BF16)
            k.op("pool", lambda: nc.gpsimd.memset(Bf['vnew'][:], 0.0), W=[Bf['vnew']])
            return Bf
        BS = [make_set() for _ in range(2)]

        for qi, sq in enumerate(cfg.seqs):
            t0, T, si = sq["t0"], sq["T"], sq["si"]
            for gi_ in range(16 // HG):
                Sg_ = S_g[gi_]
                if si < 0:
                    k.op("pool", lambda: nc.gpsimd.memset(Sg_[:], 0.0), W=[Sg_])
                else:
                    r0_ = si * 2048 + gi_ * HG * 128
                    k.dma("sp", Sg_[:], I["sdel"][r0_:r0_ + HG * 128, :].rearrange("(h d) e -> d h e", d=128), W=[Sg_])
            for tt in range(-(-T // 128)):
                nt = min(128, T - tt * 128)
                ta = t0 + tt * 128
                nch = nt // 64
                k.dma("sp", ba[0:nt, :], S["ba"][ta:ta + nt, :], R=[S["ba"]], W=[ba])
                k.op("act", lambda: nc.scalar.activation(out=beta[0:nt, :], in_=ba[0:nt, 0:16], func=AF.Sigmoid), R=[ba], W=[beta])
                k.op("dve", lambda: nc.vector.tensor_tensor(out=xx[0:nt, :], in0=ba[0:nt, 16:32], in1=dtb[0:nt, :], op=ALU.add), R=[ba, dtb], W=[xx])
                k.op("act", lambda: nc.scalar.activation(out=t16[0:nt, :], in_=xx[0:nt, :], func=AF.Abs), R=[xx], W=[t16])
                k.op("act", lambda: nc.scalar.activation(out=t16[0:nt, :], in_=t16[0:nt, :], func=AF.Exp, scale=-1.0), R=[t16], W=[t16])
                k.op("act", lambda: nc.scalar.activation(out=t16[0:nt, :], in_=t16[0:nt, :], func=AF.Ln, bias=1.0, scale=1.0), R=[t16], W=[t16])
                k.op("dve", lambda: nc.vector.scalar_tensor_tensor(out=g_[0:nt, :], in0=xx[0:nt, :], scalar=0.0, in1=t16[0:nt, :], op0=ALU.max, op1=ALU.add),
                     R=[xx, t16], W=[g_])
                k.op("dve", lambda: nc.vector.tensor_tensor(out=g_[0:nt, :], in0=g_[0:nt, :], in1=nea[0:nt, :], op=ALU.mult), R=[g_, nea], W=[g_])
                stop_at(1.2)
                bG = bank()
                k.op("pe", lambda: nc.tensor.matmul(bG[0:nt, 0:16], lhsT=C["c_cum"][0:nt, 0:nt], rhs=g_[0:nt, :], start=True, stop=True),
                     R=[C["c_cum"], g_], W=[bG], sig=False)
                k.op("pe", lambda: nc.tensor.matmul(bG[0:nt, 16:32], lhsT=C["c_blk"][0:nt, 0:nt], rhs=g_[0:nt, :], start=True, stop=True),
                     R=[C["c_blk"], g_], W=[bG])
                k.op("dve", lambda: nc.vector.tensor_copy(out=Gs[0:nt, :], in_=bG[0:nt, 0:16]), R=[bG], W=[Gs])
                k.op("act", lambda: nc.scalar.activation(out=bg[0:nt, :], in_=Gs[0:nt, :], func=AF.Exp), R=[Gs], W=[bg])
                k.op("dve", lambda: nc.vector.tensor_tensor(out=bg[0:nt, :], in0=bg[0:nt, :], in1=beta[0:nt, :], op=ALU.mult), R=[bg, beta], W=[bg])
                k.op("dve", lambda: nc.vector.tensor_tensor(out=edec[0:nt, :], in0=bG[0:nt, 16:32], in1=Gs[0:nt, :], op=ALU.subtract), R=[bG, Gs], W=[edec])
                k.op("act", lambda: nc.scalar.activation(out=edec[0:nt, :], in_=edec[0:nt, :], func=AF.Exp), R=[edec], W=[edec])
                stop_at(1.4)
                for h in range(16):
                    e = "pool" if h % 2 else "dve"
                    eh = nc.gpsimd if h % 2 else nc.vector
                    k.op(e, lambda: eh.tensor_scalar(out=Dm[0:nt, h, 0:nt], in0=ident[0:nt, 0:nt], scalar1=Gs[0:nt, h:h + 1], scalar2=0.0, op0=ALU.mult, op1=ALU.add),
                         R=[ident, Gs], Wp=[Dm])
                stop_at(1.6)
                for q4 in range(4):
                    b = bank()
                    k.op("pe", lambda: nc.tensor.matmul(b[:, 0:4 * nt], lhsT=ones_f[0:nt, :], rhs=Dm[0:nt, 4 * q4:4 * q4 + 4, 0:nt], start=True, stop=True),
                         R=[ones_f, Dm], W=[b])
                    stop_at(1.65)
                    bv = b[:, 0:4 * nt].rearrange("p (a b) -> p a b", b=nt)
                    k.op("act", lambda: nc.scalar.activation(out=eGbc[:, 4 * q4:4 * q4 + 4, 0:nt], in_=bv, func=AF.Exp), R=[b], Wp=[eGbc, b])
                    stop_at(1.7)
                    for j in range(4):
                        h = 4 * q4 + j
                        k.op("dve", lambda: nc.vector.scalar_tensor_tensor(out=dmS[0:nt, h, 0:nt], in0=b[0:nt, j * nt:(j + 1) * nt], scalar=Gs[0:nt, h:h + 1],
                                                                           in1=C["c_nmL"][0:nt, 0:nt], op0=ALU.subtract, op1=ALU.subtract),
                             R=[b, Gs, C["c_nmL"]], Wp=[dmS])
                        k.op("dve", lambda: nc.vector.scalar_tensor_tensor(out=dmT[0:nt, h, 0:nt], in0=b[0:nt, j * nt:(j + 1) * nt], scalar=Gs[0:nt, h:h + 1],
                                                                           in1=C["c_nmT"][0:nt, 0:nt], op0=ALU.subtract, op1=ALU.add),
                             R=[b, Gs, C["c_nmT"]], Wp=[dmT])
                stop_at(1.8)
                k.op("act", lambda: nc.scalar.activation(out=dmS[0:nt, :, 0:nt], in_=dmS[0:nt, :, 0:nt], func=AF.Exp, scale=-1.0), R=[dmS], W=[dmS])
                k.op("act", lambda: nc.scalar.activation(out=dmT[0:nt, :, 0:nt], in_=dmT[0:nt, :, 0:nt], func=AF.Exp), R=[dmT], W=[dmT])
                k.op("dve", lambda: nc.vector.tensor_tensor(out=dmS[0:nt, :, 0:nt], in0=dmS[0:nt, :, 0:nt], in1=beta[0:nt, :].unsqueeze(2).to_broadcast([nt, 16, nt]),
                                                            op=ALU.mult), R=[dmS, beta], W=[dmS])
                stop_at(2)
                def hg_gen(hg, Bf, Sg):
                    raw, cv, sqb, rst, qd, kbg, kdec, vb = Bf['raw'], Bf['cv'], Bf['sqb'], Bf['rst'], Bf['qd'], Bf['kbg'], Bf['kdec'], Bf['vb']
                    L, U, P, qkm, wT, u_, vnew, oT, zs, ob = Bf['L'], Bf['U'], Bf['P'], Bf['qkm'], Bf['wT'], Bf['u'], Bf['vnew'], Bf['oT'], Bf['zs'], Bf['ob']
                    for hh in range(HG):
                        h = hg + hh
                        src = S["qkvT"][:].rearrange("(c h p) t -> p c h t", c=3, h=16)[:, :, h, :]
                        if tt > 0:
                            k.dma("sp", raw[:, hh, :, 0:3 + nt], src[:, :, ta - 3:ta + nt], R=[S["qkvT"]], Wp=[raw])
                        else:
                            k.dma("sp", raw[:, hh, :, 3:3 + nt], src[:, :, ta:ta + nt], R=[S["qkvT"]], Wp=[raw])
                            for comp in range(3):
                                if si < 0:
                                    k.op("pool", lambda: nc.gpsimd.memset(raw[:, hh, comp, 0:3], 0.0), Wp=[raw])
                                else:
                                    k.op("pool", lambda: nc.gpsimd.tensor_copy(out=raw[:, hh, comp, 0:3], in_=cH[:, comp * 16 + h, si * 3:(si + 1) * 3]), R=[cH], Wp=[raw])
                    k.dma("sp", zs[:, :, 0:nt], S["zT"][hg * 128:(hg + HG) * 128, ta:ta + nt].rearrange("(h p) t -> p h t", p=128), R=[S["zT"]], W=[zs])
                    cvR, ctmp = Bf['cvR'], Bf['ctmp']
                    for j in range(4):
                        for hh in range(HG):
                            h = hg + hh
                            for comp in range(3):
                                ch = comp * 16 + h
                                rg = cvR[hh][comp]
                                on_pool = comp == 2
                                o_ = cv[:, hh, comp, 0:nt]
                                i_ = raw[:, hh, comp, j:j + nt]
                                if j == 0:
                                    if on_pool:
                                        k.op("pool", lambda: nc.gpsimd.tensor_scalar(out=o_, in0=i_, scalar1=cw[:, ch, 0:1], scalar2=cw[:, ch, 4:5], op0=ALU.mult, op1=ALU.add),
                                             R=[raw, cw], Wp=[rg], Wa=[cv])
                                    else:
                                        k.op("dve", lambda: nc.vector.tensor_scalar(out=o_, in0=i_, scalar1=cw[:, ch, 0:1], scalar2=cw[:, ch, 4:5], op0=ALU.mult, op1=ALU.add),
                                             R=[raw, cw], Wp=[rg], Wa=[cv])
                                elif on_pool:
                                    t_ = ctmp[:, hh, comp, 0:nt]
                                    tr = Bf['ctR'][hh][comp]
                                    k.op("pool", lambda: nc.gpsimd.tensor_scalar(out=t_, in0=i_, scalar1=cw[:, ch, j:j + 1], scalar2=0.0, op0=ALU.mult, op1=ALU.add), R=[raw, cw], W=[tr])
                                    k.op("pool", lambda: nc.gpsimd.tensor_tensor(out=o_, in0=o_, in1=t_, op=ALU.add), R=[tr, rg], Wp=[rg])
                                else:
                                    k.op("dve", lambda: nc.vector.scalar_tensor_tensor(out=o_, in0=i_, scalar=cw[:, ch, j:j + 1], in1=o_, op0=ALU.mult, op1=ALU.add),
                                         R=[raw, cw, rg], Wp=[rg])
                    allcv = [cvR[a][b_] for a in range(HG) for b_ in range(3)]
                    yield
                    k.op("act", lambda: nc.scalar.activation(out=cv[:, :, :, 0:nt], in_=cv[:, :, :, 0:nt], func=AF.Silu), R=allcv, W=[cv] + allcv)
                    k.op("pool", lambda: nc.gpsimd.tensor_tensor(out=sqb[:, :, :, 0:nt], in0=cv[:, :, 0:2, 0:nt], in1=cv[:, :, 0:2, 0:nt], op=ALU.mult), R=[cv], W=[sqb])
                    for b4 in range(0, HG, 2):
                        b = bank()
                        for j in range(2):
                            k.op("pe", lambda: nc.tensor.matmul(b[:, j * 2 * nt:(j + 1) * 2 * nt], lhsT=ones_f[:], rhs=sqb[:, b4 + j, :, 0:nt], start=True, stop=True),
                                 R=[ones_f, sqb], W=[b], sig=(j == 1))
                        bv = b[:, 0:4 * nt].rearrange("p (a c b) -> p a c b", a=2, c=2)
                        k.op("act", lambda: nc.scalar.activation(out=rst[:, b4:b4 + 2, :, 0:nt], in_=bv, func=AF.Sqrt, bias=1e-6, scale=1.0), R=[b], Wp=[rst])
                    k.op("dve", lambda: nc.vector.reciprocal(out=rst[:, :, :, 0:nt], in_=rst[:, :, :, 0:nt]), R=[rst], W=[rst])
                    k.op("dve", lambda: nc.vector.scalar_tensor_tensor(out=cv[:, :, 0, 0:nt], in0=cv[:, :, 0, 0:nt], scalar=HD ** -0.5, in1=rst[:, :, 0, 0:nt],
                                                                       op0=ALU.mult, op1=ALU.mult), R=[cv, rst], Wp=[cv])
                    k.op("pool", lambda: nc.gpsimd.tensor_tensor(out=cv[:, :, 1, 0:nt], in0=cv[:, :, 1, 0:nt], in1=rst[:, :, 1, 0:nt], op=ALU.mult), R=[cv, rst], Wp=[cv])
                    k.op("dve", lambda: nc.vector.tensor_tensor(out=qd[:, :, 0:nt], in0=cv[:, :, 0, 0:nt], in1=eGbc[:, hg:hg + HG, 0:nt], op=ALU.mult), R=[cv, eGbc], W=[qd])
                    yield
                    for b4 in range(0, HG, 4):
                        bk_, bv_ = bank(), bank()
                        for j in range(4):
                            k.op("pe", lambda: nc.tensor.transpose(bk_[0:nt, j * 128:(j + 1) * 128], cv[:, b4 + j, 1, 0:nt], ident[:]), R=[cv, ident], W=[bk_], sig=(j == 3))
                        for j in range(4):
                            k.op("pe", lambda: nc.tensor.transpose(bv_[0:nt, j * 128:(j + 1) * 128], cv[:, b4 + j, 2, 0:nt], ident[:]), R=[cv, ident], W=[bv_], sig=(j == 3))
                        hs = slice(hg + b4, hg + b4 + 4)
                        kv3 = bk_[0:nt, :].rearrange("p (a b) -> p a b", b=128)
                        vv3 = bv_[0:nt, :].rearrange("p (a b) -> p a b", b=128)
                        k.op("dve", lambda: nc.vector.tensor_tensor(out=kbg[0:nt, b4:b4 + 4, :], in0=kv3, in1=bg[0:nt, hs].unsqueeze(2).to_broadcast([nt, 4, 128]), op=ALU.mult),
                             R=[bk_, bg], Wp=[kbg])
                        k.op("dve", lambda: nc.vector.tensor_tensor(out=kdec[0:nt, b4:b4 + 4, :], in0=kv3, in1=edec[0:nt, hs].unsqueeze(2).to_broadcast([nt, 4, 128]), op=ALU.mult),
                             R=[bk_, edec], Wp=[kdec])
                        k.op("dve", lambda: nc.vector.tensor_tensor(out=vb[0:nt, b4:b4 + 4, :], in0=vv3, in1=beta[0:nt, hs].unsqueeze(2).to_broadcast([nt, 4, 128]), op=ALU.mult),
                             R=[bv_, beta], Wp=[vb])
                    yield
                    for b4 in range(0, HG, 4):
                        b1, b2 = bank(), bank()
                        for j in range(4):
                            k.op("pe", lambda: nc.tensor.matmul(b1[0:nt, j * nt:(j + 1) * nt], lhsT=cv[:, b4 + j, 1, 0:nt], rhs=cv[:, b4 + j, 1, 0:nt], start=True, stop=True),
                                 R=[cv], W=[b1], sig=(j == 3))
                        for j in range(4):
                            k.op("pe", lambda: nc.tensor.matmul(b2[0:nt, j * nt:(j + 1) * nt], lhsT=cv[:, b4 + j, 1, 0:nt], rhs=cv[:, b4 + j, 0, 0:nt], start=True, stop=True),
                                 R=[cv], W=[b2], sig=(j == 3))
                        hs = slice(hg + b4, hg + b4 + 4)
                        k.op("dve", lambda: nc.vector.tensor_tensor(out=L[0][0:nt, b4:b4 + 4, 0:nt], in0=b1[0:nt, 0:4 * nt].rearrange("p (a b) -> p a b", b=nt),
                                                                    in1=dmS[0:nt, hs, 0:nt], op=ALU.mult), R=[b1, dmS], Wp=[L[0]])
                        k.op("dve", lambda: nc.vector.tensor_tensor(out=qkm[0:nt, b4:b4 + 4, 0:nt], in0=b2[0:nt, 0:4 * nt].rearrange("p (a b) -> p a b", b=nt),
                                                                    in1=dmT[0:nt, hs, 0:nt], op=ALU.mult), R=[b2, dmT], Wp=[qkm])
                    yield
                    for b4 in range(0, HG, 4):
                        b = bank()
                        for j in range(4):
                            k.op("pe", lambda: nc.tensor.transpose(b[0:nt, j * nt:(j + 1) * nt], L[0][0:nt, b4 + j, 0:nt], ident[0:nt, 0:nt]), R=[L[0], ident], W=[b], sig=(j == 3))
                        bv = b[0:nt, 0:4 * nt].rearrange("p (a b) -> p a b", b=nt)
                        k.op("act", lambda: nc.scalar.copy(out=U[0][0:nt, b4:b4 + 4, 0:nt], in_=bv), R=[b], Wp=[U[0]])
                        k.op("pool", lambda: nc.gpsimd.tensor_tensor(out=P[0:nt, b4:b4 + 4, 0:nt], in0=ident[0:nt, 0:nt].unsqueeze(1).to_broadcast([nt, 4, nt]),
                                                                     in1=U[0][0:nt, b4:b4 + 4, 0:nt], op=ALU.subtract), R=[U[0], ident], Wp=[P])
                    cur = 0
                    for step in range(5):
                        nx = 1 - cur
                        for b4 in range(0, HG, 4):
                            b1 = bank()
                            for j in range(4):
                                k.op("pe", lambda: nc.tensor.matmul(b1[0:nt, j * nt:(j + 1) * nt], lhsT=U[cur][0:nt, b4 + j, 0:nt], rhs=L[cur][0:nt, b4 + j, 0:nt], start=True, stop=True),
                                     R=[U[cur], L[cur]], W=[b1], sig=(j == 3))
                            k.op("act", lambda: nc.scalar.copy(out=L[nx][0:nt, b4:b4 + 4, 0:nt], in_=b1[0:nt, 0:4 * nt].rearrange("p (a b) -> p a b", b=nt)), R=[b1], Wp=[L[nx]])
                            if step < 4:
                                b2 = bank()
                                for j in range(4):
                                    k.op("pe", lambda: nc.tensor.matmul(b2[0:nt, j * nt:(j + 1) * nt], lhsT=L[cur][0:nt, b4 + j, 0:nt], rhs=U[cur][0:nt, b4 + j, 0:nt], start=True, stop=True),
                                         R=[U[cur], L[cur]], W=[b2], sig=(j == 3))
                                k.op("dve", lambda: nc.vector.tensor_copy(out=U[nx][0:nt, b4:b4 + 4, 0:nt], in_=b2[0:nt, 0:4 * nt].rearrange("p (a b) -> p a b", b=nt)), R=[b2], Wp=[U[nx]])
                        for b4 in range(0, HG, 4):
                            b3 = bank()
                            for j in range(4):
                                k.op("pe", lambda: nc.tensor.matmul(b3[0:nt, j * nt:(j + 1) * nt], lhsT=L[nx][0:nt, b4 + j, 0:nt], rhs=P[0:nt, b4 + j, 0:nt], start=True, stop=True),
                                     R=[L[nx], P], W=[b3], sig=(j == 3))
                            k.op("dve", lambda: nc.vector.tensor_tensor(out=P[0:nt, b4:b4 + 4, 0:nt], in0=P[0:nt, b4:b4 + 4, 0:nt],
                                                                        in1=b3[0:nt, 0:4 * nt].rearrange("p (a b) -> p a b", b=nt), op=ALU.add), R=[b3, P], Wp=[P])
                        cur = nx
                        yield
                    yield
                    for b4 in range(0, HG, 4):
                        b1, b2 = bank(), bank()
                        for j in range(4):
                            k.op("pe", lambda: nc.tensor.matmul(b1[:, j * nt:(j + 1) * nt], lhsT=kbg[0:nt, b4 + j, :], rhs=P[0:nt, b4 + j, 0:nt], start=True, stop=True),
                                 R=[kbg, P], W=[b1], sig=(j == 3))
                        for j in range(4):
                            k.op("pe", lambda: nc.tensor.matmul(b2[0:nt, j * 128:(j + 1) * 128], lhsT=P[0:nt, b4 + j, 0:nt], rhs=vb[0:nt, b4 + j, :], start=True, stop=True),
                                 R=[vb, P], W=[b2], sig=(j == 3))
                        k.op("act", lambda: nc.scalar.copy(out=wT[:, b4:b4 + 4, 0:nt], in_=b1[:, 0:4 * nt].rearrange("p (a b) -> p a b", b=nt)), R=[b1], Wp=[wT])
                        k.op("dve", lambda: nc.vector.tensor_copy(out=u_[0:nt, b4:b4 + 4, :], in_=b2[0:nt, :].rearrange("p (a b) -> p a b", b=128)), R=[b2], Wp=[u_])
                    yield
                    for ci in range(nch):
                        r = slice(ci * 64, ci * 64 + 64)
                        for b4 in range(0, HG, 4):
                            b1 = bank()
                            for j in range(4):
                                k.op("pe", lambda: nc.tensor.matmul(b1[r, j * 128:(j + 1) * 128], lhsT=wT[:, b4 + j, r], rhs=Sg[:, b4 + j, :], start=True, stop=True),
                                     R=[wT, Sg], W=[b1], sig=(j == 3))
                            k.op("dve", lambda: nc.vector.tensor_tensor(out=vnew[r, b4:b4 + 4, :], in0=u_[r, b4:b4 + 4, :], in1=b1[r, :].rearrange("p (a b) -> p a b", b=128),
                                                                        op=ALU.subtract), R=[u_, b1], Wp=[vnew])
                        bo = bank()
                        for hh in range(HG):
                            k.op("pe", lambda: nc.tensor.matmul(bo[:, hh * 64:(hh + 1) * 64], lhsT=Sg[:, hh, :], rhs=qd[:, hh, r], start=True, stop=False),
                                 R=[Sg, qd], W=[bo], sig=False)
                            k.op("pe", lambda: nc.tensor.matmul(bo[:, hh * 64:(hh + 1) * 64], lhsT=vnew[0:nt, hh, :], rhs=qkm[0:nt, hh, r], start=False, stop=True),
                                 R=[vnew, qkm], W=[bo], sig=(hh == HG - 1))
                        k.op("act", lambda: nc.scalar.copy(out=oT[:, :, r], in_=bo[:, 0:HG * 64].rearrange("p (a b) -> p a b", b=64)), R=[bo], Wp=[oT])
                        for b4 in range(0, HG, 4):
                            b2 = bank()
                            for j in range(4):
                                k.op("pe", lambda: nc.tensor.matmul(b2[:, j * 128:(j + 1) * 128], lhsT=kdec[r, b4 + j, :], rhs=vnew[r, b4 + j, :], start=True, stop=True),
                                     R=[kdec, vnew], W=[b2], sig=(j == 3))
                            hs = slice(hg + b4, hg + b4 + 4)
                            col = ci * 64 + 63
                            k.op("pool", lambda: nc.gpsimd.tensor_tensor(out=Sg[:, b4:b4 + 4, :], in0=Sg[:, b4:b4 + 4, :], in1=eGbc[:, hs, col:col + 1].to_broadcast([128, 4, 128]), op=ALU.mult),
                                 R=[Sg, eGbc], Wp=[Sg])
                            k.op("dve", lambda: nc.vector.tensor_tensor(out=Sg[:, b4:b4 + 4, :], in0=Sg[:, b4:b4 + 4, :], in1=b2[:, :].rearrange("p (a b) -> p a b", b=128), op=ALU.add),
                                 R=[Sg, b2], Wp=[Sg])
                        yield
                    yield
                    k.op("pool", lambda: nc.gpsimd.tensor_tensor(out=sqb[:, :, 0, 0:nt], in0=oT[:, :, 0:nt], in1=oT[:, :, 0:nt], op=ALU.mult), R=[oT], Wp=[sqb])
                    for b4 in range(0, HG, 4):
                        b = bank()
                        for j in range(4):
                            k.op("pe", lambda: nc.tensor.matmul(b[:, j * nt:(j + 1) * nt], lhsT=ones_f[:], rhs=sqb[:, b4 + j, 0, 0:nt], start=True, stop=True),
                                 R=[ones_f, sqb], W=[b], sig=(j == 3))
                        k.op("act", lambda: nc.scalar.activation(out=rst[:, b4:b4 + 4, 0, 0:nt], in_=b[:, 0:4 * nt].rearrange("p (a b) -> p a b", b=nt), func=AF.Sqrt,
                                                                 bias=EPS, scale=1.0 / 128), R=[b], Wp=[rst])
                    k.op("dve", lambda: nc.vector.reciprocal(out=rst[:, :, 0, 0:nt], in_=rst[:, :, 0, 0:nt]), R=[rst], Wp=[rst])
                    k.op("dve", lambda: nc.vector.tensor_tensor(out=oT[:, :, 0:nt], in0=oT[:, :, 0:nt], in1=rst[:, :, 0, 0:nt], op=ALU.mult), R=[oT, rst], W=[oT])
                    k.op("dve", lambda: nc.vector.scalar_tensor_tensor(out=ob[:, :, 0:nt], in0=oT[:, :, 0:nt], scalar=dng[:, 0:1], in1=zs[:, :, 0:nt], op0=ALU.mult, op1=ALU.mult),
                         R=[oT, dng, zs], W=[ob])
                    k.dma("pool", S["aT"][2048 + hg * 128:2048 + (hg + HG) * 128, ta:ta + nt].rearrange("(h p) t -> p h t", p=128), ob[:, :, 0:nt], R=[ob], Wp=[S["aT"]])

                gens = [hg_gen(hg, BS[i % 2], S_g[hg // HG]) for i, hg in enumerate(range(0, 16, HG))]
                active = []
                while gens or active:
                    while len(active) < 2 and gens:
                        active.append(gens.pop(0))
                    for gen_ in list(active):
                        try:
                            next(gen_)
                        except StopIteration:
                            active.remove(gen_)
            for gi_ in range(16 // HG):
                r0_ = qi * 2048 + gi_ * HG * 128
                k.dma("pool", I["so"][r0_:r0_ + HG * 128, :].rearrange("(h d) e -> d h e", d=128), S_g[gi_][:], R=[S_g[gi_]], Wp=[I["so"]])


def ln_alloc(k, st, gmax):
    return dict(sq=[k.sb(st, "lnsq", [128, gmax], F32) for _ in range(2)], mt=k.sb(st, "lnm", [128, gmax], F32),
                t1=k.sb(st, "lnt", [128, gmax], F32), rs=k.sb(st, "lnrs", [128, gmax], F32), nm=k.sb(st, "lnnm", [128, gmax], F32))


def ln_stats(k, G, T, s1, s1R, gn):
    nc = k.nc
    bank = G["bank"]
    ones_f = G["ones_f"]
    sq, mt, t1, rs, nm = T["sq"], T["mt"], T["t1"], T["rs"], T["nm"]
    bs, bq = bank(), bank()
    for m in range(KC):
        q_ = sq[m % 2]
        k.op("act", lambda: nc.scalar.activation(out=q_[:, 0:gn], in_=s1[:, m, 0:gn], func=AF.Square), R=[s1R[m]], W=[q_])
        k.op("pe", lambda: nc.tensor.matmul(bs[:, 0:gn], lhsT=ones_f[:], rhs=s1[:, m, 0:gn], start=(m == 0), stop=(m == KC - 1)),
             R=[s1R[m], ones_f], W=[bs], sig=(m == KC - 1))
        k.op("pe", lambda: nc.tensor.matmul(bq[:, 0:gn], lhsT=ones_f[:], rhs=q_[:, 0:gn], start=(m == 0), stop=(m == KC - 1)),
             R=[q_, ones_f], W=[bq], sig=True)
    g = slice(0, gn)
    k.op("dve", lambda: nc.vector.tensor_scalar(out=mt[:, g], in0=bs[:, g], scalar1=1.0 / D, scalar2=None, op0=ALU.mult), R=[bs], W=[mt])
    k.op("dve", lambda: nc.vector.tensor_tensor(out=t1[:, g], in0=mt[:, g], in1=mt[:, g], op=ALU.mult), R=[mt], W=[t1])
    k.op("dve", lambda: nc.vector.scalar_tensor_tensor(out=t1[:, g], in0=bq[:, g], scalar=1.0 / D, in1=t1[:, g], op0=ALU.mult, op1=ALU.subtract),
         R=[bq, t1], W=[t1])
    k.op("act", lambda: nc.scalar.activation(out=t1[:, g], in_=t1[:, g], func=AF.Sqrt, bias=EPS, scale=1.0), R=[t1], W=[t1])
    k.op("dve", lambda: nc.vector.reciprocal(out=rs[:, g], in_=t1[:, g]), R=[t1], W=[rs])
    k.op("dve", lambda: nc.vector.scalar_tensor_tensor(out=nm[:, g], in0=mt[:, g], scalar=-1.0, in1=rs[:, g], op0=ALU.mult, op1=ALU.mult),
         R=[mt, rs], W=[nm])
    return rs, nm


def phase_D(k, cfg, G):
    nc = k.nc
    I, S = G["I"], G["S"]
    bank = G["bank"]
    groups = cfg.groups(384)
    gmax = max(g[1] for g in groups)
    with ExitStack() as st:
        aT = k.sb(st, "aT", [128, KC, gmax], BF16)
        s1 = k.sb(st, "s1", [128, KC, gmax], F32)
        s1R = [Reg("s1R") for _ in range(KC)]
        gb = k.sb(st, "ln1", [128, 64], F32)
        k.dma("sp", gb[:], I["ln1"][:], W=[gb])
        ws = WTiles(k, st, nslot=3)
        xr = [k.sb(st, "xr", [128, gmax], F32) for _ in range(3)]
        T = ln_alloc(k, st, gmax)
        of = [k.sb(st, "of", [128, gmax], F32) for _ in range(2)]
        ob = [k.sb(st, "ob", [128, gmax], BF16) for _ in range(2)]
        for (g0, gn) in groups:
            g = slice(0, gn)
            k.dma("sp", aT[:, :, g], S["aT"][:].rearrange("(kc p) t -> p kc t", p=128)[:, :, g0:g0 + gn], R=[S["aT"]], W=[aT])
            for m in range(KC):
                wb = ws.get(S["b_w_o"], m // 2)
                sub = (m % 2) * 128
                x_ = xr[m % 3]
                k.dma("sp", x_[:, g], S["xnT"][m * 128:(m + 1) * 128, g0:g0 + gn], R=[S["xnT"]], W=[x_])
                b = bank()
                for kc in range(KC):
                    k.op("pe", lambda kc=kc: nc.tensor.matmul(b[:, 0:gn], lhsT=wb[:, kc, sub:sub + 128], rhs=aT[:, kc, g], start=(kc == 0), stop=(kc == KC - 1)),
                         R=[wb, aT], W=[b], sig=(kc == KC - 1))
                k.op("dve", lambda: nc.vector.scalar_tensor_tensor(out=s1[:, m, g], in0=x_[:, g], scalar=ALPHA, in1=b[:, 0:gn], op0=ALU.mult, op1=ALU.add),
                     R=[x_, b], W=[s1R[m]])
            rs, nm = ln_stats(k, G, T, s1, s1R, gn)
            for m in range(KC):
                o_, b_ = of[m % 2], ob[m % 2]
                k.op("pool", lambda: nc.gpsimd.tensor_tensor(out=o_[:, g], in0=s1[:, m, g], in1=rs[:, g], op=ALU.mult), R=[s1R[m], rs], W=[o_])
                k.op("dve", lambda: nc.vector.tensor_tensor(out=o_[:, g], in0=o_[:, g], in1=nm[:, g], op=ALU.add), R=[o_, nm], W=[o_])
                k.op("act", lambda: nc.scalar.activation(out=o_[:, g], in_=o_[:, g], func=AF.Identity, scale=gb[:, m:m + 1], bias=gb[:, 32 + m:33 + m]),
                     R=[o_, gb], W=[o_])
                k.op("pool", lambda: nc.gpsimd.tensor_copy(out=b_[:, g], in_=o_[:, g]), R=[o_], W=[b_])
                k.dma("pool", S["x1T"][m * 128:(m + 1) * 128, g0:g0 + gn], o_[:, g], R=[o_], Wp=[S["x1T"]])
                k.dma("pool", S["x1b"][m * 128:(m + 1) * 128, g0:g0 + gn], b_[:, g], R=[b_], Wp=[S["x1b"]])
    k.barrier()


def seg_pieces(cfg, g0, gn):
    out = []
    for qi, sq in enumerate(cfg.seqs):
        a = max(sq["t0"], g0)
        b = min(sq["t0"] + sq["T"], g0 + gn)
        if a < b:
            out.append((qi, a - g0, b - a, a == sq["t0"], b == sq["t0"] + sq["T"]))
    return out


def phase_E(k, cfg, G):
    nc = k.nc
    I, S, C = G["I"], G["S"], G["C"]
    bank = G["bank"]
    FC, NS, NSEQ, DFF = cfg.FC, cfg.NS, cfg.NSEQ, cfg.DFF
    with ExitStack() as pst:
        fw = k.sb(pst, "ffnw", [128, 2 * FC, 4], F32)
        k.dma("sp", fw[:].rearrange("p a b -> p (a b)"), I["ffnw"][:], W=[fw])
        hsave = k.sb(pst, "hsave", [128, 2 * FC, 2], F32)
        k.op("pool", lambda: nc.gpsimd.memset(hsave[:], 0.0), W=[hsave])
        fst = k.sb(pst, "fst", [128, 2 * FC, NSEQ * 2], F32)
        fH = k.sb(pst, "fH", [128, 2 * FC, max(NS, 1) * 2], F32)
        if NS > 0:
            with ExitStack() as s0:
                srow = [k.sb(s0, "srow", [NS * 2, 512], F32) for _ in range(2)]
                n = 0
                for c4 in range(0, 2 * FC, 4):
                    b = bank()
                    n4 = min(4, 2 * FC - c4)
                    sr = srow[n % 2]
                    n += 1
                    k.dma("sp", sr[:, 0:n4 * 128], I["sffn"][:, c4 * 128:(c4 + n4) * 128], W=[sr])
                    for j in range(n4):
                        k.op("pe", lambda j=j: nc.tensor.transpose(b[:, j * 128:j * 128 + NS * 2], sr[:, j * 128:(j + 1) * 128],
                                                                   C["c_ident"][0:NS * 2, 0:NS * 2]),
                             R=[sr, C["c_ident"]], W=[b], sig=(j == n4 - 1))
                    k.op("dve", lambda: nc.vector.tensor_copy(out=fH[:, c4:c4 + n4, :],
                                                              in_=b[:, 0:n4 * 128].rearrange("p (a b) -> p a b", b=128)[:, :, 0:NS * 2]),
                         R=[b], Wp=[fH])
            k.barrier()
        for (g0, gn) in cfg.groups(768):
            pcs = seg_pieces(cfg, g0, gn)
            offs = []
            o = 0
            for p in pcs:
                offs.append(o)
                o += p[2] + 2
            RW = o
            with ExitStack() as st:
                x1 = k.sb(st, "x1b", [128, KC, gn], BF16)
                k.dma("sp", x1[:], S["x1b"][:].rearrange("(kc p) t -> p kc t", p=128)[:, :, g0:g0 + gn], R=[S["x1b"]], W=[x1])
                ws = WTiles(k, st, nslot=4)
                raw = [[k.sb(st, "raw", [128, RW], F32) for _ in range(2)] for _ in range(2)]
                cv = [[k.sb(st, "cv", [128, gn], F32) for _ in range(2)] for _ in range(2)]
                ao = [k.sb(st, "ao", [128, gn], BF16) for _ in range(2)]
                for c in range(FC):
                    par = c % 2
                    for half in range(2):
                        ch = half * FC + c
                        r_ = raw[half][par]
                        wb = ws.get(S["b_w_up"], ch // 2)
                        sub = (ch % 2) * 128
                        for pi, (qi, a, ln, s_st, s_en) in enumerate(pcs):
                            o_ = offs[pi]
                            if not s_st:
                                k.op("pool", lambda o_=o_: nc.gpsimd.tensor_copy(out=r_[:, o_:o_ + 2], in_=hsave[:, ch, :]), R=[hsave], Wp=[r_])
                            elif qi == 0:
                                k.op("pool", lambda o_=o_: nc.gpsimd.memset(r_[:, o_:o_ + 2], 0.0), Wp=[r_])
                            else:
                                k.op("pool", lambda o_=o_, qi=qi: nc.gpsimd.tensor_copy(out=r_[:, o_:o_ + 2], in_=fH[:, ch, (qi - 1) * 2:qi * 2]), R=[fH], Wp=[r_])
                        for (b0, bn) in blocks(gn):
                            b = bank()
                            for kc in range(KC):
                                k.op("pe", lambda kc=kc: nc.tensor.matmul(b[:, 0:bn], lhsT=wb[:, kc, sub:sub + 128], rhs=x1[:, kc, b0:b0 + bn], start=(kc == 0), stop=(kc == KC - 1)),
                                     R=[wb, x1], W=[b], sig=(kc == KC - 1))
                            for pi, (qi, a, ln, s_st, s_en) in enumerate(pcs):
                                lo, hi = max(a, b0), min(a + ln, b0 + bn)
                                if lo < hi:
                                    d0 = offs[pi] + 2 + (lo - a)
                                    k.op("act", lambda lo=lo, hi=hi, d0=d0: nc.scalar.copy(out=r_[:, d0:d0 + hi - lo], in_=b[:, lo - b0:hi - b0]), R=[b], Wp=[r_])
                        c_ = cv[half][par]
                        for pi, (qi, a, ln, s_st, s_en) in enumerate(pcs):
                            o_ = offs[pi]
                            k.op("dve", lambda o_=o_, a=a, ln=ln: nc.vector.tensor_scalar(out=c_[:, a:a + ln], in0=r_[:, o_:o_ + ln], scalar1=fw[:, ch, 0:1], scalar2=fw[:, ch, 3:4],
                                                                                       op0=ALU.mult, op1=ALU.add), R=[r_, fw], Wp=[c_])
                            for j in (1, 2):
                                k.op("dve", lambda o_=o_, a=a, ln=ln, j=j: nc.vector.scalar_tensor_tensor(out=c_[:, a:a + ln], in0=r_[:, o_ + j:o_ + j + ln], scalar=fw[:, ch, j:j + 1],
                                                                                                         in1=c_[:, a:a + ln], op0=ALU.mult, op1=ALU.add), R=[r_, fw, c_], Wp=[c_])
                            if s_en:
                                k.op("pool", lambda o_=o_, ln=ln, qi=qi: nc.gpsimd.tensor_copy(out=fst[:, ch, qi * 2:qi * 2 + 2], in_=r_[:, o_ + ln:o_ + ln + 2]), R=[r_], Wp=[fst])
                            else:
                                k.op("pool", lambda o_=o_, ln=ln: nc.gpsimd.tensor_copy(out=hsave[:, ch, :], in_=r_[:, o_ + ln:o_ + ln + 2]), R=[r_], Wp=[hsave])
                    gt, vl, a_ = cv[0][par], cv[1][par], ao[par]
                    k.op("act", lambda: nc.scalar.activation(out=gt[:], in_=gt[:], func=AF.Silu), R=[gt], W=[gt])
                    k.op("pool", lambda: nc.gpsimd.tensor_tensor(out=a_[:], in0=gt[:], in1=vl[:], op=ALU.mult), R=[gt, vl], W=[a_])
                    k.dma("pool", S["actT"][c * 128:(c + 1) * 128, g0:g0 + gn], a_[:], R=[a_], Wp=[S["actT"]])
            k.barrier()
        with ExitStack() as st:
            orow = [k.sb(st, "orow", [NSEQ * 2, 512], F32) for _ in range(2)]
            n = 0
            for c4 in range(0, 2 * FC, 4):
                n4 = min(4, 2 * FC - c4)
                b = bank()
                for j in range(n4):
                    k.op("pe", lambda j=j: nc.tensor.transpose(b[0:NSEQ * 2, j * 128:(j + 1) * 128], fst[:, c4 + j, :], C["c_ident"][:]),
                         R=[fst, C["c_ident"]], W=[b], sig=(j == n4 - 1))
                o_ = orow[n % 2]
                n += 1
                k.op("dve", lambda: nc.vector.tensor_copy(out=o_[:, 0:n4 * 128], in_=b[0:NSEQ * 2, 0:n4 * 128]), R=[b], W=[o_])
                k.dma("pool", I["ffno"][:, c4 * 128:(c4 + n4) * 128], o_[:, 0:n4 * 128], R=[o_], Wp=[I["ffno"]])
        k.barrier()


def phase_F(k, cfg, G):
    nc = k.nc
    I, S, C = G["I"], G["S"], G["C"]
    bank = G["bank"]
    FC = cfg.FC
    kgs = [(i, min(32, FC - i)) for i in range(0, FC, 32)]
    groups = cfg.groups(256)
    gmax = max(g[1] for g in groups)
    with ExitStack() as st:
        aT = k.sb(st, "actT", [128, FC, gmax], BF16)
        s1 = k.sb(st, "s2", [128, KC, gmax], F32)
        s1R = [Reg("s2R") for _ in range(KC)]
        gb = k.sb(st, "ln2", [128, 64], F32)
        k.dma("sp", gb[:], I["ln2"][:], W=[gb])
        ws = WTiles(k, st, nslot=4)
        xr = [k.sb(st, "xr", [128, gmax], F32) for _ in range(3)]
        T = ln_alloc(k, st, gmax)
        yt = [k.sb(st, "yt", [128, 2048], F32) for _ in range(2)]
        yn = 0
        for (g0, gn) in groups:
            g = slice(0, gn)
            k.dma("sp", aT[:, :, g], S["actT"][:].rearrange("(kc p) t -> p kc t", p=128)[:, :, g0:g0 + gn], R=[S["actT"]], W=[aT])
            for m in range(KC):
                x_ = xr[m % 3]
                sub = (m % 2) * 128
                k.dma("sp", x_[:, g], S["x1T"][m * 128:(m + 1) * 128, g0:g0 + gn], R=[S["x1T"]], W=[x_])
                b = bank()
                for gi, (k0, kn) in enumerate(kgs):
                    wb = ws.get(S["b_w_down"], m // 2, kg=gi, kcn=kn)
                    for kc in range(kn):
                        first = (gi == 0 and kc == 0)
                        last = (gi == len(kgs) - 1 and kc == kn - 1)
                        k.op("pe", lambda kc=kc: nc.tensor.matmul(b[:, 0:gn], lhsT=wb[:, kc, sub:sub + 128], rhs=aT[:, k0 + kc, g], start=first, stop=last),
                             R=[wb, aT], W=[b], sig=(kc == kn - 1))
                k.op("dve", lambda: nc.vector.scalar_tensor_tensor(out=s1[:, m, g], in0=x_[:, g], scalar=ALPHA, in1=b[:, 0:gn], op0=ALU.mult, op1=ALU.add),
                     R=[x_, b], W=[s1R[m]])
            rs, nm = ln_stats(k, G, T, s1, s1R, gn)
            for m in range(KC):
                k.op("pool", lambda: nc.gpsimd.tensor_tensor(out=s1[:, m, g], in0=s1[:, m, g], in1=rs[:, g], op=ALU.mult), R=[s1R[m], rs], W=[s1R[m]])
                k.op("dve", lambda: nc.vector.tensor_tensor(out=s1[:, m, g], in0=s1[:, m, g], in1=nm[:, g], op=ALU.add), R=[s1R[m], nm], W=[s1R[m]])
                k.op("act", lambda: nc.scalar.activation(out=s1[:, m, g], in_=s1[:, m, g], func=AF.Identity, scale=gb[:, m:m + 1], bias=gb[:, 32 + m:33 + m]),
                     R=[s1R[m], gb], W=[s1R[m]])
            for ti in range(gn // 128):
                for hf in range(2):
                    y_ = yt[yn % 2]
                    yn += 1
                    for q in range(4):
                        b = bank()
                        for j in range(4):
                            m = hf * 16 + q * 4 + j
                            k.op("pe", lambda m=m, j=j: nc.tensor.transpose(b[:, j * 128:(j + 1) * 128], s1[:, m, ti * 128:(ti + 1) * 128], C["c_ident"][:]),
                                 R=[s1R[m], C["c_ident"]], W=[b], sig=(j == 3))
                        if q % 2:
                            k.op("act", lambda: nc.scalar.copy(out=y_[:, q * 512:(q + 1) * 512], in_=b[:, :]), R=[b], Wp=[y_])
                        else:
                            k.op("dve", lambda: nc.vector.tensor_copy(out=y_[:, q * 512:(q + 1) * 512], in_=b[:, :]), R=[b], Wp=[y_])
                    k.dma("pool", I["y"][g0 + ti * 128:g0 + (ti + 1) * 128, hf * 2048:(hf + 1) * 2048], y_[:], R=[y_], Wp=[I["y"]])
    k.barrier()


_CACHE = {}


def _pp(v):
    return np.ascontiguousarray(np.asarray(v, np.float32).reshape(32, 128).T)


def make_in_maps(cfg, n_cores, inp):
    f = lambda a: np.ascontiguousarray(np.asarray(a, dtype=np.float32))
    NS, DFF, FC = cfg.NS, cfg.DFF, cfg.FC
    shared = {}
    shared["lnin"] = np.concatenate([_pp(inp["ln_in_g"]), _pp(inp["ln_in_b"])], axis=1)
    shared["ln1"] = np.concatenate([_pp(inp["ln1_g"][0]), _pp(inp["ln1_b"][0])], axis=1)
    shared["ln2"] = np.concatenate([_pp(inp["ln2_g"][0]), _pp(inp["ln2_b"][0])], axis=1)
    shared["w_in"] = f(inp["w_in"][0]); shared["w_o"] = f(inp["w_o"][0])
    shared["w_up"] = f(inp["w_ffn_up"][0]); shared["w_down"] = f(inp["w_ffn_down"][0])
    cw = np.concatenate([f(inp["conv_qkv_w"][0]), f(inp["conv_qkv_b"])], axis=0)
    shared["convw"] = np.ascontiguousarray(cw.reshape(5, 48, 128).transpose(2, 1, 0).reshape(128, 240))
    fw = np.concatenate([f(inp["ffn_conv_w"][0]), f(inp["ffn_conv_b"])], axis=0)
    shared["ffnw"] = np.ascontiguousarray(fw.reshape(4, 2 * FC, 128).transpose(2, 1, 0).reshape(128, 2 * FC * 4))
    shared["alog"] = np.ascontiguousarray(np.broadcast_to(f(inp["a_log"][0])[None, :], (128, 16)))
    shared["dtb"] = np.ascontiguousarray(np.broadcast_to(f(inp["dt_bias"][0])[None, :], (128, 16)))
    shared["dng"] = f(inp["delta_norm_g"][0]).reshape(128, 1)
    shared["relb"] = f(inp["rel_bias"])
    shared.update(host_consts())
    maps = []
    for c in range(n_cores):
        m = dict(shared)
        sl = slice(c * NS, (c + 1) * NS)
        m["x"] = np.concatenate([f(inp["x_prompt"][c]), f(inp["x_sample"][sl]).reshape(NS * DEC, D)], axis=0)
        m["ck"] = f(inp["cache_attn_k"][0, sl]).reshape(NS * PAST, 512)
        m["cv"] = f(inp["cache_attn_v"][0, sl]).reshape(NS * PAST, 512)
        m["cik"] = f(inp["cache_idx_k"][0, sl]).reshape(NS * PAST, 64)
        m["sdel"] = f(inp["state_delta"][0, sl]).reshape(NS * 16 * 128, 128)
        m["sconv"] = f(inp["state_conv_qkv"][0, sl]).reshape(NS * 3, 6144)
        m["sffn"] = f(inp["state_ffn_conv"][0, sl]).reshape(NS * 2, 2 * DFF)
        maps.append(m)
    return maps


def kernel(**inp):
    B, SEQ = inp["x_prompt"].shape[0], inp["x_prompt"].shape[1]
    DB = inp["x_sample"].shape[0]
    DFF = inp["w_ffn_down"].shape[1]
    n_cores = B
    NS = DB // n_cores
    cfg = Cfg(SEQ, NS, DFF)
    key = (SEQ, NS, DFF)
    if key not in _CACHE:
        _CACHE[key] = build(cfg)
    nc = _CACHE[key]
    maps = make_in_maps(cfg, n_cores, inp)
    res = run_bass_kernel_spmd(nc, maps, core_ids=list(range(n_cores)))
    R = res.results
    return assemble(cfg, n_cores, R)


def assemble(cfg, n_cores, R):
    NS, SEQ, DFF, NSEQ = cfg.NS, cfg.SEQ, cfg.DFF, cfg.NSEQ
    g = lambda n: [np.asarray(R[c][n], dtype=np.float32) for c in range(n_cores)]
    y, ko, vo, iko, so, co, fo = g("y"), g("ko"), g("vo"), g("iko"), g("so"), g("convo"), g("ffno")
    yp = np.stack([a[:SEQ] for a in y])
    ys = np.concatenate([a[SEQ:].reshape(NS, DEC, D) for a in y])
    pk = np.stack([a[:SEQ].reshape(SEQ, 4, 128) for a in ko])[None]
    pv = np.stack([a[:SEQ].reshape(SEQ, 4, 128) for a in vo])[None]
    pik = np.stack([a[:SEQ] for a in iko])[None]
    sk = np.concatenate([a[SEQ:].reshape(NS, DEC, 4, 128) for a in ko])[None]
    sv = np.concatenate([a[SEQ:].reshape(NS, DEC, 4, 128) for a in vo])[None]
    sik = np.concatenate([a[SEQ:].reshape(NS, DEC, 64) for a in iko])[None]
    pd = np.stack([a.reshape(NSEQ, 16, 128, 128)[0] for a in so])[None]
    sd = np.concatenate([a.reshape(NSEQ, 16, 128, 128)[1:] for a in so])[None]
    pc = np.stack([a.reshape(NSEQ, 3, 6144)[0] for a in co])[None]
    sc = np.concatenate([a.reshape(NSEQ, 3, 6144)[1:] for a in co])[None]
    pf = np.stack([a.reshape(NSEQ, 2, 2 * DFF)[0] for a in fo])[None]
    sf = np.concatenate([a.reshape(NSEQ, 2, 2 * DFF)[1:] for a in fo])[None]
    return (yp, ys, pk, pv, pik, pd, pc, pf, sk, sv, sik, sd, sc, sf)
```

```python
import math
from contextlib import ExitStack
import numpy as np
import ml_dtypes
import concourse.bass as bass
import concourse.mybir as mybir
from concourse.bass_utils import run_bass_kernel_spmd

F32 = mybir.dt.float32
BF16 = mybir.dt.bfloat16
AF = mybir.ActivationFunctionType
ALU = mybir.AluOpType

D = 4096
KC = 32
NIN = 12400
HD = 128
PAST = 1024
DEC = 64
EPS = 1e-5
ALPHA = 2.0 ** 0.25
IDX_SCALE = (16 ** -0.5) * (64 ** -0.5)
O_QA, O_KA, O_VA, O_IQ, O_IK, O_IW, O_QKV, O_Z, O_BETA, O_A = 0, 2048, 2560, 3072, 4096, 4160, 4176, 10320, 12368, 12384
NEG = -1.0e30
WIN_PIECES = [(0, 0, 4096), (4096, 4096, 64), (4160, 4096, 64), (4224, 4176, 8192), (12416, 4096, 80), (12496, 12368, 32)]
WIN_PACKED = 12544
P_QA, P_KA, P_VA, P_IQ, P_IK2, P_QKV, P_Z, P_SM1, P_SM2 = 0, 2048, 2560, 3072, 4096, 4224, 10368, 12416, 12496
DBG = False
STOP_C = 99


MUTE = [False]


def stop_at(n):
    if STOP_C <= n:
        MUTE[0] = True


class Reg:
    __slots__ = ("w", "r", "n")

    def __init__(s, n=""):
        s.w = {}
        s.r = {}
        s.n = n


class Tile:
    def __init__(s, t, n):
        s.t = t
        s.reg = Reg(n)

    def __getitem__(s, i):
        return s.t[i]


class Eng:
    def __init__(s, name, h):
        s.name = name
        s.h = h
        s.semidx = None
        s.cnt = 0
        s.known = {}
        s.pending = False
        s.dsems = []
        s.dnext = 0


class K:
    SEM_LIMIT = 30000

    def __init__(s, nc, es):
        s.nc = nc
        s.es = es
        s.sems = []
        s.semmax = []
        s.E = {}
        for n, h in (("pe", nc.tensor), ("act", nc.scalar), ("dve", nc.vector), ("pool", nc.gpsimd), ("sp", nc.sync)):
            e = Eng(n, h)
            s.E[n] = e
            if n != "sp":
                e.semidx = s.newsem()
        for n, cnt in (("sp", 12), ("act", 8), ("pool", 8)):
            s.E[n].dsems = [s.newsem() for _ in range(cnt)]
        s.uid = 0

    def newsem(s):
        h = s.es.enter_context(s.nc.semaphore("sem%d" % len(s.sems)))
        s.sems.append(h)
        s.semmax.append(0)
        return len(s.sems) - 1

    def sb(s, st, name, shape, dt):
        s.uid += 1
        nm = "%s_%d" % (name, s.uid)
        return Tile(st.enter_context(s.nc.sbuf_tensor(nm, list(shape), dt)), nm)

    def ps(s, st, name, shape, dt=F32):
        s.uid += 1
        nm = "%s_%d" % (name, s.uid)
        return Tile(st.enter_context(s.nc.psum_tensor(nm, list(shape), dt)), nm)

    def dram(s, name, shape, dt, kind="Internal"):
        if DBG and kind == "Internal":
            kind = "ExternalOutput"
        t = s.nc.dram_tensor(name, list(shape), dt, kind=kind)
        tl = Tile(t.ap(), name)
        return tl

    def _deps(s, R, W, Wp):
        deps = {}
        for r in R:
            for k, v in r.w.items():
                if deps.get(k, 0) < v:
                    deps[k] = v
        for w in list(W) + list(Wp):
            for k, v in w.w.items():
                if deps.get(k, 0) < v:
                    deps[k] = v
            for k, v in w.r.items():
                if deps.get(k, 0) < v:
                    deps[k] = v
        return deps

    def _waits(s, eng, deps, ename):
        for k, v in deps.items():
            if k == eng.semidx and ename == "pe":
                continue
            if eng.known.get(k, 0) >= v:
                continue
            eng.h.wait_ge(s.sems[k], v)
            eng.known[k] = v

    def _mark(s, t, R, W, Wp):
        k, v = t
        for r in R:
            if r.r.get(k, 0) < v:
                r.r[k] = v
        for w in W:
            w.w = {k: v}
            w.r = {}
        for w in Wp:
            if w.w.get(k, 0) < v:
                w.w[k] = v

    def op(s, e, fn, R=(), W=(), Wp=(), sig=True, Wa=()):
        if MUTE[0]:
            return None
        R = [x.reg if isinstance(x, Tile) else x for x in R]
        Wa = [x.reg if isinstance(x, Tile) else x for x in Wa]
        W = [x.reg if isinstance(x, Tile) else x for x in W]
        Wp = [x.reg if isinstance(x, Tile) else x for x in Wp]
        eng = s.E[e]
        if eng.cnt >= s.SEM_LIMIT and not eng.pending:
            eng.semidx = s.newsem()
            eng.cnt = 0
        s._waits(eng, s._deps(R, W, list(Wp) + list(Wa)), e)
        ins = fn()
        if sig:
            eng.cnt += 1
            ins.then_inc(s.sems[eng.semidx], 1)
            s.semmax[eng.semidx] = eng.cnt
            eng.pending = False
            t = (eng.semidx, eng.cnt)
        else:
            eng.pending = True
            t = (eng.semidx, eng.cnt + 1)
        s._mark(t, R, W, Wp)
        return ins

    def dma(s, q, out, in_, R=(), W=(), Wp=(), **kw):
        if MUTE[0]:
            return None
        R = [x.reg if isinstance(x, Tile) else x for x in R]
        W = [x.reg if isinstance(x, Tile) else x for x in W]
        Wp = [x.reg if isinstance(x, Tile) else x for x in Wp]
        eng = s.E[q]
        si = eng.dsems[eng.dnext % len(eng.dsems)]
        eng.dnext += 1
        deps = s._deps(R, W, Wp)
        cur = s.semmax[si]
        if cur > 0 and deps.get(si, 0) < cur:
            deps[si] = cur
        s._waits(eng, deps, q)
        ins = eng.h.dma_start(out=out, in_=in_, **kw)
        ins.then_inc(s.sems[si], 16)
        s.semmax[si] = cur + 16
        s._mark((si, cur + 16), R, W, Wp)
        return ins

    def barrier(s, engines=("pe", "act", "dve", "pool", "sp")):
        for n in engines:
            eng = s.E[n]
            assert not eng.pending
            for k, v in enumerate(s.semmax):
                if v > 0 and eng.known.get(k, 0) < v and k != eng.semidx:
                    eng.h.wait_ge(s.sems[k], v)
                    eng.known[k] = v


class Cfg:
    def __init__(s, SEQ, NS, DFF):
        s.SEQ, s.NS, s.DFF = SEQ, NS, DFF
        s.FC = DFF // 128
        s.NT = SEQ + NS * DEC
        assert s.NT % 128 == 0 and SEQ % 128 == 0 and DFF % 128 == 0
        s.NSEQ = 1 + NS
        s.seqs = [dict(t0=0, T=SEQ, past=0, si=-1)] + [dict(t0=SEQ + DEC * i, T=DEC, past=PAST, si=i) for i in range(NS)]
        s.TOPK_P = min(256, SEQ // 4)
        s.TOPK_S = min(256, (PAST + DEC) // 4)

    def groups(s, gmax):
        n = -(-s.NT // gmax)
        per = -(-(s.NT // 128) // n) * 128
        out = []
        t = 0
        while t < s.NT:
            g = min(per, s.NT - t)
            out.append((t, g))
            t += g
        return out


def blocks(n, b=512):
    return [(i, min(b, n - i)) for i in range(0, n, b)]


def t5_bucket_np(rel):
    rel = np.asarray(rel, np.int64)
    half, max_exact = 16, 8
    side = np.where(rel > 0, half, 0)
    n = np.abs(rel)
    nf = np.maximum(n, 1).astype(np.float32)
    large = max_exact + (np.log(nf / np.float32(max_exact)) / np.float32(math.log(128 / max_exact))
                         * np.float32(half - max_exact)).astype(np.int32)
    large = np.minimum(large, half - 1)
    return side + np.where(n < max_exact, n, large)


def host_consts():
    c = {}
    c["c_ident"] = np.eye(128, dtype=np.float32)
    c["c_anti"] = np.eye(128, dtype=np.float32)[::-1].copy()
    i = np.arange(128)
    same = (i[:, None] // 64) == (i[None, :] // 64)
    c["c_cum"] = (same & (i[:, None] <= i[None, :])).astype(np.float32)
    c["c_blk"] = same.astype(np.float32)
    c["c_nmL"] = np.where(same & (i[:, None] > i[None, :]), 0.0, -1e4).astype(np.float32)
    c["c_nmT"] = np.where(same & (i[None, :] >= i[:, None]), 0.0, -1e4).astype(np.float32)
    c["c_strict"] = (same & (i[:, None] > i[None, :])).astype(np.float32)
    rel = np.arange(384) - 255
    bk = t5_bucket_np(rel)
    oh = np.zeros((32, 384), np.float32)
    oh[bk, np.arange(384)] = 1.0
    oh[15, :] -= 1.0
    c["c_oh"] = oh
    return c


CONST_SHAPES = {"c_ident": [128, 128], "c_anti": [128, 128], "c_cum": [128, 128], "c_blk": [128, 128],
                "c_nmL": [128, 128], "c_nmT": [128, 128], "c_strict": [128, 128], "c_oh": [32, 384]}


def build(cfg, phases="ABCDEF"):
    nc = bass.Bass("TRN2", target_bir_lowering=False)
    es = ExitStack()
    with es:
        k = K(nc, es)
        _program(k, cfg, phases)
    return nc


def _program(k, cfg, phases):
    nc = k.nc
    NT, NS, DFF, FC, NSEQ = cfg.NT, cfg.NS, cfg.DFF, cfg.FC, cfg.NSEQ
    I = {}

    def din(name, shape):
        I[name] = k.dram(name, shape, F32, kind="ExternalInput")
        return I[name]

    def dout(name, shape):
        I[name] = k.dram(name, shape, F32, kind="ExternalOutput")
        return I[name]

    din("x", [NT, D])
    din("ck", [NS * PAST, 512]); din("cv", [NS * PAST, 512]); din("cik", [NS * PAST, 64])
    din("sdel", [NS * 16 * 128, 128]); din("sconv", [NS * 3, 6144]); din("sffn", [NS * 2, 2 * DFF])
    din("lnin", [128, 64]); din("ln1", [128, 64]); din("ln2", [128, 64])
    din("w_in", [D, NIN]); din("w_o", [D, D]); din("w_up", [D, 2 * DFF]); din("w_down", [DFF, D])
    din("convw", [128, 48 * 5]); din("ffnw", [128, 2 * FC * 4])
    din("alog", [128, 16]); din("dtb", [128, 16]); din("dng", [128, 1]); din("relb", [32, 16])
    for n, sh in CONST_SHAPES.items():
        din(n, sh)
    dout("y", [NT, D]); dout("ko", [NT, 512]); dout("vo", [NT, 512]); dout("iko", [NT, 64])
    dout("so", [NSEQ * 16 * 128, 128]); dout("convo", [NSEQ * 3, 6144]); dout("ffno", [NSEQ * 2, 2 * DFF])
    S = {}
    S["xnT"] = k.dram("s_xnT", [D, NT], F32)
    S["qaT"] = k.dram("s_qaT", [2048, NT], BF16)
    S["kaT"] = k.dram("s_kaT", [512, NT], BF16)
    S["vbf"] = k.dram("s_vbf", [NT, 512], BF16)
    S["iqT"] = k.dram("s_iqT", [1024, NT], BF16)
    S["ikT2"] = k.dram("s_ikT2", [128, NT], BF16)
    S["iw"] = k.dram("s_iw", [NT, 16], F32)
    S["ba"] = k.dram("s_ba", [NT, 32], F32)
    S["qkvT"] = k.dram("s_qkvT", [6144, NT], F32)
    S["zT"] = k.dram("s_zT", [2048, NT], F32)
    S["aT"] = k.dram("s_aT", [D, NT], BF16)
    S["x1T"] = k.dram("s_x1T", [D, NT], F32)
    S["x1b"] = k.dram("s_x1b", [D, NT], BF16)
    S["actT"] = k.dram("s_actT", [DFF, NT], BF16)
    S["Fd"] = k.dram("s_Fd", [16, 384 + 128], F32)
    S["b_w_in"] = k.dram("s_bwin", [WIN_PACKED // 256, 1, 128, 32, 256], BF16)
    S["b_w_o"] = k.dram("s_bwo", [D // 256, 1, 128, 32, 256], BF16)
    S["b_w_up"] = k.dram("s_bwup", [2 * DFF // 256, 1, 128, 32, 256], BF16)
    S["b_w_down"] = k.dram("s_bwdn", [D // 256, len(kgroups(FC)), 128, 32, 256], BF16)

    with ExitStack() as gs:
        C = {}
        for n, sh in CONST_SHAPES.items():
            C[n] = k.sb(gs, n, sh, F32)
            k.dma("sp", C[n][:], I[n][:], W=[C[n]])
        ident_b = k.sb(gs, "identb", [128, 128], BF16)
        k.op("pool", lambda: nc.gpsimd.tensor_copy(out=ident_b[:], in_=C["c_ident"][:]), R=[C["c_ident"]], W=[ident_b])
        ones_f = k.sb(gs, "onesf", [128, 128], F32)
        k.op("pool", lambda: nc.gpsimd.memset(ones_f[:], 1.0), W=[ones_f])
        ones_b = k.sb(gs, "onesb", [128, 128], BF16)
        k.op("pool", lambda: nc.gpsimd.memset(ones_b[:], 1.0), W=[ones_b])
        G = dict(C=C, ident_b=ident_b, ones_f=ones_f, ones_b=ones_b, I=I, S=S)
        psum = [k.ps(gs, "bank%d" % i, [128, 512]) for i in range(8)]
        G["psum"] = psum
        G["pn"] = 0

        def bank():
            b = psum[G["pn"] % 8]
            G["pn"] += 1
            return b
        G["bank"] = bank
        G["alt"] = 0

        phase_W(k, cfg, G)
        k.barrier()
        if "A" in phases:
            phase_A(k, cfg, G)
            k.barrier()
        if "B" in phases:
            phase_B(k, cfg, G)
            k.barrier()
        if "C" in phases:
            phase_C(k, cfg, G)
            MUTE[0] = False
            k.barrier()
        if "D" in phases:
            phase_D(k, cfg, G)
            k.barrier()
        if "E" in phases:
            phase_E(k, cfg, G)
            k.barrier()
        if "F" in phases:
            phase_F(k, cfg, G)
        k.barrier()


def evac_engine(G):
    G["alt"] += 1
    return "act" if G["alt"] % 2 else "dve"


class WTiles:
    def __init__(s, k, st, nslot=3):
        s.k = k
        s.slots = [k.sb(st, "wt", [128, 32, 256], BF16) for _ in range(nslot)]
        s.tags = [None] * nslot
        s.n = 0

    def get(s, scr, tile, kg=0, kcn=32):
        tag = (scr.reg.n, tile, kg)
        for i, t in enumerate(s.tags):
            if t == tag:
                return s.slots[i]
        i = s.n % len(s.slots)
        s.n += 1
        s.tags[i] = tag
        wb = s.slots[i]
        s.k.dma("sp", wb[:, 0:kcn, :], scr[tile, kg, :, 0:kcn, :], R=[scr], W=[wb])
        return wb


def kgroups(kctot):
    return [(i, min(32, kctot - i)) for i in range(0, kctot, 32)]


def w_units(k, cfg, G, names, st, kstep=4, gcols=512):
    nc = k.nc
    I, S = G["I"], G["S"]
    allspecs = {"w_in": (I["w_in"], WIN_PIECES, WIN_PACKED, KC), "w_o": (I["w_o"], [(0, 0, D)], D, KC),
                "w_up": (I["w_up"], [(0, 0, 2 * cfg.DFF)], 2 * cfg.DFF, KC), "w_down": (I["w_down"], [(0, 0, D)], D, cfg.FC)}
    nt = gcols // 256
    stg = [k.sb(st, "wstg", [128, kstep, gcols], F32) for _ in range(2)]
    sbf = [k.sb(st, "wsbf", [128, nt, kstep, 256], BF16) for _ in range(2)]
    units = []
    for name in names:
        src, pieces, ncol, kctot = allspecs[name]
        dst = S["b_" + name]
        for g0 in range(0, ncol, gcols):
            gw = min(gcols, ncol - g0)
            for kgi, (kg0, kgn) in enumerate(kgroups(kctot)):
                for k8 in range(0, kgn, kstep):
                    units.append((src, pieces, dst, g0, gw, kgi, kg0, k8, min(kstep, kgn - k8)))

    def load(n):
        src, pieces, dst, g0, gw, kgi, kg0, k8, kn = units[n]
        r0 = (kg0 + k8) * 128
        st_ = stg[n % 2]
        first = True
        for (d0, s0, pn) in pieces:
            lo, hi = max(d0, g0), min(d0 + pn, g0 + gw)
            if lo < hi:
                srcap = src[r0:r0 + kn * 128, s0 + lo - d0:s0 + hi - d0].rearrange("(kc p) n -> p kc n", p=128)
                if first:
                    k.dma("sp", st_[:, 0:kn, lo - g0:hi - g0], srcap, W=[st_])
                else:
                    k.dma("sp", st_[:, 0:kn, lo - g0:hi - g0], srcap, Wp=[st_])
                first = False

    def finish(n):
        src, pieces, dst, g0, gw, kgi, kg0, k8, kn = units[n]
        st_ = stg[n % 2]
        sb_ = sbf[n % 2]
        nt4 = gw // 256
        iv = st_[:, 0:kn, 0:gw].rearrange("p k (t c) -> p t k c", c=256)
        ov = sb_[:, 0:nt4, 0:kn, :]
        e = G.get("wcast", ("act", "dve", "pool"))
        e = e[n % len(e)]
        if e == "act":
            k.op("act", lambda: nc.scalar.copy(out=ov, in_=iv), R=[st_], W=[sb_])
        elif e == "dve":
            k.op("dve", lambda: nc.vector.tensor_copy(out=ov, in_=iv), R=[st_], W=[sb_])
        else:
            k.op("pool", lambda: nc.gpsimd.tensor_copy(out=ov, in_=iv), R=[st_], W=[sb_])
        t0 = g0 // 256
        k.dma(G.get("wstoreq", "pool"), dst[t0:t0 + nt4, kgi, :, k8:k8 + kn, :].rearrange("t p k c -> p t k c"), ov, R=[sb_], Wp=[dst])

    for n in range(len(units)):
        load(n)
        if n >= 1:
            finish(n - 1)
        yield n
    finish(len(units) - 1)
    yield len(units)


def phase_W(k, cfg, G):
    with ExitStack() as st:
        G["wcast"] = ("act", "dve")
        for _ in w_units(k, cfg, G, ["w_in"], st, kstep=8, gcols=1024):
            pass


def phase_A(k, cfg, G):
    nc = k.nc
    I, S, C = G["I"], G["S"], G["C"]
    NT = cfg.NT
    bank = G["bank"]
    for (g0, gn) in cfg.groups(1152):
        with ExitStack() as st:
            xnT = k.sb(st, "xnT", [128, KC, gn], BF16)
            gb = k.sb(st, "lnin", [128, 64], F32)
            k.dma("sp", gb[:], I["lnin"][:], W=[gb])
            with ExitStack() as s1:
                xs = [k.sb(s1, "xs", [128, D], F32) for _ in range(2)]
                xf = [k.sb(s1, "xf", [128, KC, 128], F32) for _ in range(2)]
                stt = [k.sb(s1, "stt", [128, 8, 6], F32) for _ in range(2)]
                mv = [k.sb(s1, "mv", [128, 4], F32) for _ in range(2)]
                for ti in range(gn // 128):
                    t0 = g0 + ti * 128
                    x_, f_, st_, mv_ = xs[ti % 2], xf[ti % 2], stt[ti % 2], mv[ti % 2]
                    k.dma("sp", x_[:], I["x"][t0:t0 + 128, :], W=[x_])
                    for j in range(8):
                        k.op("dve", lambda j=j: nc.vector.bn_stats(out=st_[:, j, :], in_=x_[:, j * 512:(j + 1) * 512]),
                             R=[x_], Wp=[st_] if j else (), W=() if j else [st_])
                    k.op("dve", lambda: nc.vector.bn_aggr(out=mv_[:, 0:2], in_=st_[:].rearrange("p a b -> p (a b)")), R=[st_], W=[mv_])
                    k.op("act", lambda: nc.scalar.activation(out=mv_[:, 2:3], in_=mv_[:, 1:2], func=AF.Sqrt, bias=EPS, scale=1.0), R=[mv_], Wp=[mv_])
                    k.op("dve", lambda: nc.vector.reciprocal(out=mv_[:, 3:4], in_=mv_[:, 2:3]), R=[mv_], Wp=[mv_])
                    k.op("dve", lambda: nc.vector.tensor_scalar(out=x_[:], in0=x_[:], scalar1=mv_[:, 0:1], scalar2=mv_[:, 3:4],
                                                                op0=ALU.subtract, op1=ALU.mult), R=[mv_, x_], W=[x_])
                    for q in range(8):
                        b = bank()
                        for j in range(4):
                            kc = q * 4 + j
                            k.op("pe", lambda kc=kc, j=j: nc.tensor.transpose(b[:, j * 128:(j + 1) * 128], x_[:, kc * 128:(kc + 1) * 128], C["c_ident"][:]),
                                 R=[x_, C["c_ident"]], W=[b], sig=(j == 3))
                        for j in range(4):
                            kc = q * 4 + j
                            if (kc % 2) == 0:
                                k.op("act", lambda kc=kc, j=j: nc.scalar.activation(out=f_[:, kc, :], in_=b[:, j * 128:(j + 1) * 128], func=AF.Identity,
                                                                                   scale=gb[:, kc:kc + 1], bias=gb[:, 32 + kc:33 + kc]),
                                     R=[b, gb], Wp=[f_])
                            else:
                                k.op("dve", lambda kc=kc, j=j: nc.vector.tensor_scalar(out=f_[:, kc, :], in0=b[:, j * 128:(j + 1) * 128],
                                                                                      scalar1=gb[:, kc:kc + 1], scalar2=gb[:, 32 + kc:33 + kc],
                                                                                      op0=ALU.mult, op1=ALU.add),
                                     R=[b, gb], Wp=[f_])
                    k.op("pool", lambda: nc.gpsimd.tensor_copy(out=xnT[:, :, ti * 128:(ti + 1) * 128], in_=f_[:]), R=[f_], Wp=[xnT])
                    k.dma("pool", S["xnT"][:].rearrange("(kc p) t -> p kc t", p=128)[:, :, t0:t0 + 128], f_[:], R=[f_], Wp=[S["xnT"]])
            k.barrier()
            with ExitStack() as s2:
                ws = WTiles(k, s2, nslot=3)
                osf = [k.sb(s2, "osf", [128, gn], F32) for _ in range(2)]
                osb = [k.sb(s2, "osb", [128, gn], BF16) for _ in range(2)]
                otk = [k.sb(s2, "otk", [128, gn // 128, 128], F32) for _ in range(2)]
                otb = [k.sb(s2, "otb", [128, gn // 128, 128], BF16) for _ in range(2)]
                cnt = {"f": 0, "b": 0, "t": 0}

                def fm_job(pc, m, dst, drow, mode):
                    wb = ws.get(S["b_w_in"], pc // 256)
                    sub = pc % 256
                    if mode == "f32" or mode == "silu":
                        o = osf[cnt["f"] % 2]; cnt["f"] += 1
                    else:
                        o = osb[cnt["b"] % 2]; cnt["b"] += 1
                    for (b0, bn) in blocks(gn):
                        b = bank()
                        for kc in range(KC):
                            k.op("pe", lambda kc=kc: nc.tensor.matmul(b[0:m, 0:bn], lhsT=wb[:, kc, sub:sub + m], rhs=xnT[:, kc, b0:b0 + bn],
                                                                       start=(kc == 0), stop=(kc == KC - 1)),
                                 R=[wb, xnT], W=[b], sig=(kc == KC - 1))
                        e = evac_engine(G)
                        if mode == "silu":
                            k.op("act", lambda: nc.scalar.activation(out=o[0:m, b0:b0 + bn], in_=b[0:m, 0:bn], func=AF.Silu), R=[b], Wp=[o])
                        elif mode == "qs":
                            k.op("act", lambda: nc.scalar.mul(o[0:m, b0:b0 + bn], b[0:m, 0:bn], HD ** -0.5), R=[b], Wp=[o])
                        elif e == "act":
                            k.op("act", lambda: nc.scalar.copy(out=o[0:m, b0:b0 + bn], in_=b[0:m, 0:bn]), R=[b], Wp=[o])
                        else:
                            k.op("dve", lambda: nc.vector.tensor_copy(out=o[0:m, b0:b0 + bn], in_=b[0:m, 0:bn]), R=[b], Wp=[o])
                    k.dma("pool", dst[drow:drow + m, g0:g0 + gn], o[0:m, :], R=[o], Wp=[dst])

                def tm_job(pc, ncols, outs):
                    wb = ws.get(S["b_w_in"], pc // 256)
                    sub = pc % 256
                    o = otk[cnt["t"] % 2]
                    ob = otb[cnt["t"] % 2]
                    cnt["t"] += 1
                    for ti in range(gn // 128):
                        b = bank()
                        for kc in range(KC):
                            k.op("pe", lambda kc=kc: nc.tensor.matmul(b[:, 0:ncols], lhsT=xnT[:, kc, ti * 128:(ti + 1) * 128], rhs=wb[:, kc, sub:sub + ncols],
                                                                       start=(kc == 0), stop=(kc == KC - 1)),
                                 R=[wb, xnT], W=[b], sig=(kc == KC - 1))
                        k.op("dve", lambda: nc.vector.tensor_copy(out=o[:, ti, 0:ncols], in_=b[:, 0:ncols]), R=[b], Wp=[o])
                    for (dst, dc0, sc0, n, dt) in outs:
                        dview = dst[g0:g0 + gn, dc0:dc0 + n].rearrange("(ti p) n -> p ti n", p=128)
                        if dt == "bf16":
                            k.op("pool", lambda: nc.gpsimd.tensor_copy(out=ob[:, :, sc0:sc0 + n], in_=o[:, :, sc0:sc0 + n]), R=[o], W=[ob])
                            k.dma("pool", dview, ob[:, :, sc0:sc0 + n], R=[ob], Wp=[dst])
                        else:
                            k.dma("pool", dview, o[:, :, sc0:sc0 + n], R=[o], Wp=[dst])

                for c in range(16):
                    fm_job(P_QA + c * 128, 128, S["qaT"], c * 128, "qs")
                for c in range(4):
                    fm_job(P_KA + c * 128, 128, S["kaT"], c * 128, "bf16")
                for c in range(4):
                    tm_job(P_KA + c * 128, 128, [(I["ko"], c * 128, 0, 128, "f32")])
                for c in range(4):
                    tm_job(P_VA + c * 128, 128, [(I["vo"], c * 128, 0, 128, "f32"), (S["vbf"], c * 128, 0, 128, "bf16")])
                for c in range(8):
                    fm_job(P_IQ + c * 128, 128, S["iqT"], c * 128, "bf16")
                fm_job(P_IK2, 128, S["ikT2"], 0, "bf16")
                for c in range(48):
                    fm_job(P_QKV + c * 128, 128, S["qkvT"], c * 128, "f32")
                for c in range(16):
                    fm_job(P_Z + c * 128, 128, S["zT"], c * 128, "silu")
                tm_job(P_SM1, 80, [(I["iko"], 0, 0, 64, "f32"), (S["iw"], 0, 64, 16, "f32")])
                tm_job(P_SM2, 32, [(S["ba"], 0, 0, 32, "f32")])
            k.barrier()


def phase_B(k, cfg, G):
    nc = k.nc
    I, S, C = G["I"], G["S"], G["C"]
    bank = G["bank"]
    ident, ident_b, ones_b = C["c_ident"], G["ident_b"], G["ones_b"]
    NS = cfg.NS
    SKMAX = max(cfg.SEQ, PAST + 128)
    KTMAX = SKMAX // 128
    with ExitStack() as pst:
        sb = lambda n, sh, dt=F32: k.sb(pst, n, sh, dt)
        biasT = sb("biasT", [128, 2, 16, 128], BF16)
        with ExitStack() as s0:
            relb = k.sb(s0, "relb", [32, 16], F32)
            Fs = k.sb(s0, "Fs", [16, 512], F32)
            XT = k.sb(s0, "XT", [128, 2, 16, 128], F32)
            k.dma("sp", relb[:], I["relb"][:], W=[relb])
            k.op("pool", lambda: nc.gpsimd.memset(Fs[:], 0.0), W=[Fs])
            b = bank()
            k.op("pe", lambda: nc.tensor.matmul(b[0:16, 0:384], lhsT=relb[:, :], rhs=C["c_oh"][:, :], start=True, stop=True), R=[relb, C["c_oh"]], W=[b])
            k.op("dve", lambda: nc.vector.tensor_copy(out=Fs[:, 0:384], in_=b[0:16, 0:384]), R=[b], Wp=[Fs])
            k.dma("sp", S["Fd"][:], Fs[:], R=[Fs], W=[S["Fd"]])
            fd_t = S["Fd"][:].tensor
            for w, off in ((0, 128), (1, 0)):
                src = bass.AP(tensor=fd_t, offset=off, ap=[[1, 128], [512, 16], [1, 128]])
                k.dma("sp", XT[:, w, :, :], src, R=[S["Fd"]], Wp=[XT])
            for w in range(2):
                for h4 in range(0, 16, 4):
                    b = bank()
                    for j in range(4):
                        k.op("pe", lambda: nc.tensor.matmul(b[:, j * 128:(j + 1) * 128], lhsT=XT[:, w, h4 + j, :], rhs=C["c_anti"][:], start=True, stop=True),
                             R=[XT, C["c_anti"]], W=[b], sig=(j == 3))
                    k.op("dve", lambda: nc.vector.tensor_copy(out=biasT[:, w, h4:h4 + 4, :], in_=b[:, :].rearrange("p (a b) -> p a b", b=128)), R=[b], Wp=[biasT])
        k.barrier()
        kT = sb("kT", [128, 4, SKMAX], BF16)
        vv = sb("vv", [128, KTMAX, 4, 128], BF16)
        ik2 = sb("ik2", [128, SKMAX], BF16)
        cst = sb("cst", [128, 8, 512], F32)
        cikst = sb("cikst", [128, 8, 128], F32)
        qT = [sb("qT", [128, 16, 128], BF16) for _ in range(2)]
        iq = [sb("iq", [128, 8, 128], BF16) for _ in range(2)]
        iw = [sb("iw", [128, 16], F32) for _ in range(2)]
        index = sb("index", [128, SKMAX]); work = sb("work", [128, SKMAX]); mask01 = sb("mask01", [128, SKMAX])
        rr_ = [sb("relu", [128, 512]) for _ in range(2)]
        m8 = sb("m8", [128, 8]); thr = sb("thr", [128, 1])
        maskT = sb("maskT", [128, KTMAX, 128], BF16)
        pt = [sb("pt", [128, 512], BF16) for _ in range(3)]
        rcp = sb("rcp", [128, 512])
        oa = [sb("oa", [128, 16, 128], BF16) for _ in range(2)]
        G["wcast"] = ("act",)
        G["wstoreq"] = "act"
        wgen = w_units(k, cfg, G, ["w_o", "w_up", "w_down"], pst, kstep=4, gcols=512)
        n_units = 0
        for nm_, kct_ in (("w_o", KC), ("w_up", KC), ("w_down", cfg.FC)):
            ncol_ = {"w_o": D, "w_up": 2 * cfg.DFF, "w_down": D}[nm_]
            n_units += (-(-ncol_ // 512)) * sum(-(-kn_ // 4) for (_, kn_) in kgroups(kct_))
        n_iter = sum(4 * (-(-sq_["T"] // 128)) for sq_ in cfg.seqs)
        per_iter = -(-n_units // n_iter)

        def wstep(cnt):
            for _ in range(cnt):
                try:
                    next(wgen)
                except StopIteration:
                    return
        qn = 0
        for qi, sq in enumerate(cfg.seqs):
            t0, T, si, past = sq["t0"], sq["T"], sq["si"], sq["past"]
            SK = past + T
            if si < 0:
                k.dma("sp", kT[:, :, 0:T], S["kaT"][:, t0:t0 + T].rearrange("(g p) t -> p g t", p=128), R=[S["kaT"]], W=[kT])
                k.dma("sp", vv[:, 0:T // 128, :, :], S["vbf"][t0:t0 + T, :].rearrange("(kt p) (g d) -> p kt g d", p=128, d=128), R=[S["vbf"]], W=[vv])
                k.dma("sp", ik2[:, 0:T], S["ikT2"][:, t0:t0 + T], R=[S["ikT2"]], W=[ik2])
            else:
                k.dma("sp", cst[:], I["ck"][si * PAST:(si + 1) * PAST, :].rearrange("(kt p) n -> p kt n", p=128), W=[cst])
                for kt in range(8):
                    b = bank()
                    for g in range(4):
                        k.op("pe", lambda: nc.tensor.transpose(b[:, g * 128:(g + 1) * 128], cst[:, kt, g * 128:(g + 1) * 128], ident[:]), R=[cst, ident], W=[b], sig=(g == 3))
                    k.op("act", lambda: nc.scalar.copy(out=kT[:, :, kt * 128:(kt + 1) * 128], in_=b[:, :].rearrange("p (a b) -> p a b", b=128)), R=[b], Wp=[kT])
                k.dma("sp", cst[:], I["cv"][si * PAST:(si + 1) * PAST, :].rearrange("(kt p) n -> p kt n", p=128), W=[cst])
                k.op("pool", lambda: nc.gpsimd.tensor_copy(out=vv[:, 0:8, :, :].rearrange("p a g d -> p a (g d)"), in_=cst[:]), R=[cst], Wp=[vv])
                ciksrc = I["cik"][si * PAST:(si + 1) * PAST, :].rearrange("(kt p) n -> p kt n", p=128)
                k.dma("sp", cikst[:, :, 0:64], ciksrc, W=[cikst])
                k.dma("sp", cikst[:, :, 64:128], ciksrc, Wp=[cikst])
                for k4 in range(0, 8, 4):
                    b = bank()
                    for j in range(4):
                        k.op("pe", lambda: nc.tensor.transpose(b[:, j * 128:(j + 1) * 128], cikst[:, k4 + j, :], ident[:]), R=[cikst, ident], W=[b], sig=(j == 3))
                    k.op("dve", lambda: nc.vector.tensor_copy(out=ik2[:, k4 * 128:(k4 + 4) * 128], in_=b[:, :]), R=[b], Wp=[ik2])
                k.dma("sp", kT[:, :, PAST:PAST + T], S["kaT"][:, t0:t0 + T].rearrange("(g p) t -> p g t", p=128), R=[S["kaT"]], Wp=[kT])
                k.dma("sp", vv[0:T, 8, :, :], S["vbf"][t0:t0 + T, :].rearrange("p (g d) -> p g d", d=128), R=[S["vbf"]], Wp=[vv])
                k.dma("sp", ik2[:, PAST:PAST + T], S["ikT2"][:, t0:t0 + T], R=[S["ikT2"]], Wp=[ik2])
            topk = cfg.TOPK_P if si < 0 else cfg.TOPK_S
            for qt in range(-(-T // 128)):
                nq = min(128, T - qt * 128)
                ta = t0 + qt * 128
                SKq = past + qt * 128 + nq if si < 0 else SK
                KTq = -(-SKq // 128)
                q_, iq_, iw_, oa_ = qT[qn % 2], iq[qn % 2], iw[qn % 2], oa[qn % 2]
                qn += 1
                k.dma("sp", q_[:, :, 0:nq], S["qaT"][:, ta:ta + nq].rearrange("(h p) t -> p h t", p=128), R=[S["qaT"]], W=[q_])
                k.dma("sp", iq_[:, :, 0:nq], S["iqT"][:, ta:ta + nq].rearrange("(h p) t -> p h t", p=128), R=[S["iqT"]], W=[iq_])
                k.dma("sp", iw_[0:nq, :], S["iw"][ta:ta + nq, :], R=[S["iw"]], W=[iw_])
                k.op("pool", lambda: nc.gpsimd.tensor_scalar(out=iw_[0:nq, :], in0=iw_[0:nq, :], scalar1=IDX_SCALE, scalar2=None, op0=ALU.mult), R=[iw_], W=[iw_])
                rn = 0
                for (c0, cn) in blocks(SKq):
                    for hp in range(8):
                        for half in range(2):
                            h = hp * 2 + half
                            pr = slice(half * 64, half * 64 + 64)
                            b = bank()
                            k.op("pe", lambda: nc.tensor.matmul(b[0:nq, 0:cn], lhsT=iq_[pr, hp, 0:nq], rhs=ik2[pr, c0:c0 + cn], start=True, stop=True), R=[iq_, ik2], W=[b])
                            r_ = rr_[rn % 2]
                            rn += 1
                            k.op("act", lambda: nc.scalar.activation(out=r_[0:nq, 0:cn], in_=b[0:nq, 0:cn], func=AF.Relu), R=[b], W=[r_])
                            if h == 0:
                                k.op("dve", lambda: nc.vector.tensor_scalar(out=index[0:nq, c0:c0 + cn], in0=r_[0:nq, 0:cn], scalar1=iw_[0:nq, 0:1], scalar2=None, op0=ALU.mult),
                                     R=[r_, iw_], Wp=[index])
                            else:
                                k.op("dve", lambda: nc.vector.scalar_tensor_tensor(out=index[0:nq, c0:c0 + cn], in0=r_[0:nq, 0:cn], scalar=iw_[0:nq, h:h + 1],
                                                                                   in1=index[0:nq, c0:c0 + cn], op0=ALU.mult, op1=ALU.add), R=[r_, iw_, index], Wp=[index])
                if si < 0:
                    k.op("dve", lambda: nc.vector.memset(index[0:64, SKq - 64:SKq], NEG), R=[index], Wp=[index])
                if SKq > topk:
                    nr = topk // 8
                    for rd in range(nr):
                        srcw = index if rd == 0 else work
                        k.op("dve", lambda: nc.vector.max(out=m8[0:nq, :], in_=srcw[0:nq, 0:SKq]), R=[srcw], W=[m8])
                        if rd < nr - 1:
                            k.op("dve", lambda: nc.vector.match_replace(out=work[0:nq, 0:SKq], in_to_replace=m8[0:nq, :], in_values=srcw[0:nq, 0:SKq], imm_value=NEG),
                                 R=[srcw, m8], W=[work])
                    k.op("dve", lambda: nc.vector.tensor_scalar(out=thr[0:nq, :], in0=m8[0:nq, 7:8], scalar1=-1.0e29, scalar2=None, op0=ALU.max), R=[m8], W=[thr])
                    k.op("dve", lambda: nc.vector.tensor_scalar(out=mask01[0:nq, 0:SKq], in0=index[0:nq, 0:SKq], scalar1=thr[0:nq, 0:1], scalar2=None, op0=ALU.is_ge),
                         R=[index, thr], W=[mask01])
                else:
                    k.op("dve", lambda: nc.vector.tensor_scalar(out=mask01[0:nq, 0:SKq], in0=index[0:nq, 0:SKq], scalar1=-1.0e29, scalar2=None, op0=ALU.is_ge),
                         R=[index], W=[mask01])
                for k4 in range(0, KTq, 4):
                    b = bank()
                    n4 = min(4, KTq - k4)
                    for j in range(n4):
                        kt = k4 + j
                        ks = min(128, SKq - kt * 128)
                        k.op("pe", lambda: nc.tensor.transpose(b[0:ks, j * 128:j * 128 + nq], mask01[0:nq, kt * 128:kt * 128 + ks], ident[0:nq, 0:nq]), R=[mask01, ident], W=[b], sig=(j == n4 - 1))
                    for j in range(n4):
                        kt = k4 + j
                        ks = min(128, SKq - kt * 128)
                        k.op("act", lambda: nc.scalar.copy(out=maskT[0:ks, kt, 0:nq], in_=b[0:ks, j * 128:j * 128 + nq]), R=[b], Wp=[maskT])
                pn = 0
                for g in range(4):
                    bO, bR = (G["psum"][4], G["psum"][5]) if g % 2 == 0 else (G["psum"][6], G["psum"][7])
                    for kt in range(KTq):
                        ks = min(128, SKq - kt * 128)
                        near = kt >= KTq - 2
                        w = 0 if kt == KTq - 1 else 1
                        wstep(1)
                        bl = G["psum"][pn % 4]
                        k.op("pe", lambda: nc.tensor.matmul(bl[0:ks, 0:4 * nq], lhsT=kT[:, g, kt * 128:kt * 128 + ks], rhs=q_[:, 4 * g:4 * g + 4, 0:nq], start=True, stop=not near),
                             R=[kT, q_], W=[bl], sig=not near)
                        if near:
                            k.op("pe", lambda: nc.tensor.matmul(bl[0:ks, 0:4 * nq], lhsT=ident_b[:, 0:ks], rhs=biasT[:, w, 4 * g:4 * g + 4, 0:nq], start=False, stop=True),
                                 R=[ident_b, biasT], W=[bl])
                        p_ = pt[pn % 3]
                        pn += 1
                        k.op("act", lambda: nc.scalar.activation(out=p_[0:ks, 0:4 * nq], in_=bl[0:ks, 0:4 * nq], func=AF.Exp), R=[bl], W=[p_])
                        pv = p_[0:ks, 0:4 * nq].rearrange("p (a b) -> p a b", b=nq)
                        k.op("pool", lambda: nc.gpsimd.tensor_tensor(out=pv, in0=pv, in1=maskT[0:ks, kt, 0:nq].unsqueeze(1).to_broadcast([ks, 4, nq]), op=ALU.mult),
                             R=[p_, maskT], W=[p_])
                        k.op("pe", lambda: nc.tensor.matmul(bO[:, 0:4 * nq], lhsT=vv[0:ks, kt, g, :], rhs=p_[0:ks, 0:4 * nq], start=(kt == 0), stop=(kt == KTq - 1)),
                             R=[vv, p_], W=[bO], sig=False)
                        k.op("pe", lambda: nc.tensor.matmul(bR[:, 0:4 * nq], lhsT=ones_b[0:ks, :], rhs=p_[0:ks, 0:4 * nq], start=(kt == 0), stop=(kt == KTq - 1)),
                             R=[ones_b, p_], W=[bR], sig=True)
                    k.op("dve", lambda: nc.vector.reciprocal(out=rcp[:, 0:4 * nq], in_=bR[:, 0:4 * nq]), R=[bR], W=[rcp])
                    k.op("dve", lambda: nc.vector.tensor_tensor(out=oa_[:, 4 * g:4 * g + 4, 0:nq], in0=bO[:, 0:4 * nq].rearrange("p (a b) -> p a b", b=nq),
                                                                in1=rcp[:, 0:4 * nq].rearrange("p (a b) -> p a b", b=nq), op=ALU.mult), R=[bO, rcp], Wp=[oa_])
                k.dma("pool", S["aT"][0:2048, ta:ta + nq].rearrange("(h p) t -> p h t", p=128), oa_[:, :, 0:nq], R=[oa_], Wp=[S["aT"]])
        wstep(10 ** 9)


def phase_C(k, cfg, G):
    nc = k.nc
    I, S, C = G["I"], G["S"], G["C"]
    bank = G["bank"]
    ones_f = G["ones_f"]
    ident = C["c_ident"]
    NS, NSEQ = cfg.NS, cfg.NSEQ
    HG = 4
    with ExitStack() as pst:
        sb = lambda n, sh, dt=F32: k.sb(pst, n, sh, dt)
        cw = sb("convw", [128, 48, 5])
        k.dma("sp", cw[:].rearrange("p a b -> p (a b)"), I["convw"][:], W=[cw])
        nea = sb("nea", [128, 16]); dtb = sb("dtb", [128, 16]); dng = sb("dng", [128, 1])
        k.dma("sp", nea[:], I["alog"][:], W=[nea])
        k.dma("sp", dtb[:], I["dtb"][:], W=[dtb])
        k.dma("sp", dng[:], I["dng"][:], W=[dng])
        k.op("act", lambda: nc.scalar.activation(out=nea[:], in_=nea[:], func=AF.Exp), R=[nea], W=[nea])
        k.op("pool", lambda: nc.gpsimd.tensor_scalar(out=nea[:], in0=nea[:], scalar1=-1.0, scalar2=None, op0=ALU.mult), R=[nea], W=[nea])
        cH = sb("cH", [128, 48, max(NS, 1) * 3])
        lst = sb("lst", [128, 48, NSEQ * 3])
        s0 = ExitStack()
        orow = k.sb(s0, "orow", [NSEQ * 3, 6144], F32)
        if NS > 0:
            srow = k.sb(s0, "srowc", [NS * 3, 6144], F32)
            k.dma("sp", srow[:], I["sconv"][:], W=[srow])
            for c4 in range(0, 48, 4):
                b = bank()
                for j in range(4):
                    k.op("pe", lambda j=j: nc.tensor.transpose(b[:, j * 128:j * 128 + NS * 3], srow[:, (c4 + j) * 128:(c4 + j + 1) * 128],
                                                               ident[0:NS * 3, 0:NS * 3]), R=[srow, ident], W=[b], sig=(j == 3))
                k.op("dve", lambda: nc.vector.tensor_copy(out=cH[:, c4:c4 + 4, :], in_=b[:, :].rearrange("p (a b) -> p a b", b=128)[:, :, 0:NS * 3]),
                     R=[b], Wp=[cH])
        qv = S["qkvT"][:].rearrange("(c p) t -> p c t", p=128)
        for qi, sq in enumerate(cfg.seqs):
            te = sq["t0"] + sq["T"]
            k.dma("sp", lst[:, :, qi * 3:(qi + 1) * 3], qv[:, :, te - 3:te], R=[S["qkvT"]], Wp=[lst])
        for c4 in range(0, 48, 4):
            b = bank()
            for j in range(4):
                k.op("pe", lambda j=j: nc.tensor.transpose(b[0:NSEQ * 3, j * 128:(j + 1) * 128], lst[:, c4 + j, :], ident[:]),
                     R=[lst, ident], W=[b], sig=(j == 3))
            k.op("dve", lambda: nc.vector.tensor_copy(out=orow[:, c4 * 128:(c4 + 4) * 128], in_=b[0:NSEQ * 3, :]), R=[b], Wp=[orow])
        k.dma("pool", I["convo"][:], orow[:], R=[orow], W=[I["convo"]])

        k.barrier()
        s0.close()
        stop_at(1)
        S_g = [sb("S", [128, HG, 128]) for _ in range(16 // HG)]
        ba = sb("ba", [128, 32]); beta = sb("beta", [128, 16]); xx = sb("xx", [128, 16]); t16 = sb("t16", [128, 16])
        g_ = sb("g", [128, 16]); Gs = sb("Gs", [128, 16]); bg = sb("bg", [128, 16]); edec = sb("edec", [128, 16])
        Dm = sb("Dm", [128, 16, 128]); eGbc = sb("eGbc", [128, 16, 128]); dmS = sb("dmS", [128, 16, 128]); dmT = sb("dmT", [128, 16, 128])

        def make_set():
            Bf = {}
            Bf['raw'] = sb("raw", [128, HG, 3, 131]); Bf['cv'] = sb("cv", [128, HG, 3, 128]); Bf['sqb'] = sb("sqb", [128, HG, 2, 128])
            Bf['ctmp'] = sb("ctmp", [128, HG, 3, 128])
            Bf['cvR'] = [[Reg("cvR") for _ in range(3)] for _ in range(HG)]
            Bf['ctR'] = [[Reg("ctR") for _ in range(3)] for _ in range(HG)]
            Bf['rst'] = sb("rst", [128, HG, 2, 128]); Bf['qd'] = sb("qd", [128, HG, 128])
            Bf['kbg'] = sb("kbg", [128, HG, 128]); Bf['kdec'] = sb("kdec", [128, HG, 128]); Bf['vb'] = sb("vb", [128, HG, 128])
            Bf['L'] = [sb("L", [128, HG, 128]) for _ in range(2)]; Bf['U'] = [sb("U", [128, HG, 128]) for _ in range(2)]
            Bf['P'] = sb("P", [128, HG, 128]); Bf['qkm'] = sb("qkm", [128, HG, 128]); Bf['wT'] = sb("wT", [128, HG, 128]); Bf['u'] = sb("u", [128, HG, 128])
            Bf['vnew'] = sb("vnew", [128, HG, 128]); Bf['oT'] = sb("oT", [128, HG, 128]); Bf['zs'] = sb("zs", [128, HG, 128]); Bf['ob'] = sb("ob", [128, HG, 128], BF16)
            k.op("pool", lambda: nc.gpsimd.memset(Bf['vnew'][:], 0.0), W=[Bf['vnew']])
            return Bf
        BS = [make_set() for _ in range(2)]

        for qi, sq in enumerate(cfg.seqs):
            t0, T, si = sq["t0"], sq["T"], sq["si"]
            for gi_ in range(16 // HG):
                Sg_ = S_g[gi_]
                if si < 0:
                    k.op("pool", lambda: nc.gpsimd.memset(Sg_[:], 0.0), W=[Sg_])
                else:
                    r0_ = si * 2048 + gi_ * HG * 128
                    k.dma("sp", Sg_[:], I["sdel"][r0_:r0_ + HG * 128, :].rearrange("(h d) e -> d h e", d=128), W=[Sg_])
            for tt in range(-(-T // 128)):
                nt = min(128, T - tt * 128)
                ta = t0 + tt * 128
                nch = nt // 64
                k.dma("sp", ba[0:nt, :], S["ba"][ta:ta + nt, :], R=[S["ba"]], W=[ba])
                k.op("act", lambda: nc.scalar.activation(out=beta[0:nt, :], in_=ba[0:nt, 0:16], func=AF.Sigmoid), R=[ba], W=[beta])
                k.op("dve", lambda: nc.vector.tensor_tensor(out=xx[0:nt, :], in0=ba[0:nt, 16:32], in1=dtb[0:nt, :], op=ALU.add), R=[ba, dtb], W=[xx])
                k.op("act", lambda: nc.scalar.activation(out=t16[0:nt, :], in_=xx[0:nt, :], func=AF.Abs), R=[xx], W=[t16])
                k.op("act", lambda: nc.scalar.activation(out=t16[0:nt, :], in_=t16[0:nt, :], func=AF.Exp, scale=-1.0), R=[t16], W=[t16])
                k.op("act", lambda: nc.scalar.activation(out=t16[0:nt, :], in_=t16[0:nt, :], func=AF.Ln, bias=1.0, scale=1.0), R=[t16], W=[t16])
                k.op("dve", lambda: nc.vector.scalar_tensor_tensor(out=g_[0:nt, :], in0=xx[0:nt, :], scalar=0.0, in1=t16[0:nt, :], op0=ALU.max, op1=ALU.add),
                     R=[xx, t16], W=[g_])
                k.op("dve", lambda: nc.vector.tensor_tensor(out=g_[0:nt, :], in0=g_[0:nt, :], in1=nea[0:nt, :], op=ALU.mult), R=[g_, nea], W=[g_])
                stop_at(1.2)
                bG = bank()
                k.op("pe", lambda: nc.tensor.matmul(bG[0:nt, 0:16], lhsT=C["c_cum"][0:nt, 0:nt], rhs=g_[0:nt, :], start=True, stop=True),
                     R=[C["c_cum"], g_], W=[bG], sig=False)
                k.op("pe", lambda: nc.tensor.matmul(bG[0:nt, 16:32], lhsT=C["c_blk"][0:nt, 0:nt], rhs=g_[0:nt, :], start=True, stop=True),
                     R=[C["c_blk"], g_], W=[bG])
                k.op("dve", lambda: nc.vector.tensor_copy(out=Gs[0:nt, :], in_=bG[0:nt, 0:16]), R=[bG], W=[Gs])
                k.op("act", lambda: nc.scalar.activation(out=bg[0:nt, :], in_=Gs[0:nt, :], func=AF.Exp), R=[Gs], W=[bg])
                k.op("dve", lambda: nc.vector.tensor_tensor(out=bg[0:nt, :], in0=bg[0:nt, :], in1=beta[0:nt, :], op=ALU.mult), R=[bg, beta], W=[bg])
                k.op("dve", lambda: nc.vector.tensor_tensor(out=edec[0:nt, :], in0=bG[0:nt, 16:32], in1=Gs[0:nt, :], op=ALU.subtract), R=[bG, Gs], W=[edec])
                k.op("act", lambda: nc.scalar.activation(out=edec[0:nt, :], in_=edec[0:nt, :], func=AF.Exp), R=[edec], W=[edec])
                stop_at(1.4)
                for h in range(16):
                    e = "pool" if h % 2 else "dve"
                    eh = nc.gpsimd if h % 2 else nc.vector
                    k.op(e, lambda: eh.tensor_scalar(out=Dm[0:nt, h, 0:nt], in0=ident[0:nt, 0:nt], scalar1=Gs[0:nt, h:h + 1], scalar2=0.0, op0=ALU.mult, op1=ALU.add),
                         R=[ident, Gs], Wp=[Dm])
                stop_at(1.6)
                for q4 in range(4):
                    b = bank()
                    k.op("pe", lambda: nc.tensor.matmul(b[:, 0:4 * nt], lhsT=ones_f[0:nt, :], rhs=Dm[0:nt, 4 * q4:4 * q4 + 4, 0:nt], start=True, stop=True),
                         R=[ones_f, Dm], W=[b])
                    stop_at(1.65)
                    bv = b[:, 0:4 * nt].rearrange("p (a b) -> p a b", b=nt)
                    k.op("act", lambda: nc.scalar.activation(out=eGbc[:, 4 * q4:4 * q4 + 4, 0:nt], in_=bv, func=AF.Exp), R=[b], Wp=[eGbc, b])
                    stop_at(1.7)
                    for j in range(4):
                        h = 4 * q4 + j
                        k.op("dve", lambda: nc.vector.scalar_tensor_tensor(out=dmS[0:nt, h, 0:nt], in0=b[0:nt, j * nt:(j + 1) * nt], scalar=Gs[0:nt, h:h + 1],
                                                                           in1=C["c_nmL"][0:nt, 0:nt], op0=ALU.subtract, op1=ALU.subtract),
                             R=[b, Gs, C["c_nmL"]], Wp=[dmS])
                        k.op("dve", lambda: nc.vector.scalar_tensor_tensor(out=dmT[0:nt, h, 0:nt], in0=b[0:nt, j * nt:(j + 1) * nt], scalar=Gs[0:nt, h:h + 1],
                                                                           in1=C["c_nmT"][0:nt, 0:nt], op0=ALU.subtract, op1=ALU.add),
                             R=[b, Gs, C["c_nmT"]], Wp=[dmT])
                stop_at(1.8)
                k.op("act", lambda: nc.scalar.activation(out=dmS[0:nt, :, 0:nt], in_=dmS[0:nt, :, 0:nt], func=AF.Exp, scale=-1.0), R=[dmS], W=[dmS])
                k.op("act", lambda: nc.scalar.activation(out=dmT[0:nt, :, 0:nt], in_=dmT[0:nt, :, 0:nt], func=AF.Exp), R=[dmT], W=[dmT])
                k.op("dve", lambda: nc.vector.tensor_tensor(out=dmS[0:nt, :, 0:nt], in0=dmS[0:nt, :, 0:nt], in1=beta[0:nt, :].unsqueeze(2).to_broadcast([nt, 16, nt]),
                                                            op=ALU.mult), R=[dmS, beta], W=[dmS])
                stop_at(2)
                def hg_gen(hg, Bf, Sg):
                    raw, cv, sqb, rst, qd, kbg, kdec, vb = Bf['raw'], Bf['cv'], Bf['sqb'], Bf['rst'], Bf['qd'], Bf['kbg'], Bf['kdec'], Bf['vb']
                    L, U, P, qkm, wT, u_, vnew, oT, zs, ob = Bf['L'], Bf['U'], Bf['P'], Bf['qkm'], Bf['wT'], Bf['u'], Bf['vnew'], Bf['oT'], Bf['zs'], Bf['ob']
                    for hh in range(HG):
                        h = hg + hh
                        src = S["qkvT"][:].rearrange("(c h p) t -> p c h t", c=3, h=16)[:, :, h, :]
                        if tt > 0:
                            k.dma("sp", raw[:, hh, :, 0:3 + nt], src[:, :, ta - 3:ta + nt], R=[S["qkvT"]], Wp=[raw])
                        else:
                            k.dma("sp", raw[:, hh, :, 3:3 + nt], src[:, :, ta:ta + nt], R=[S["qkvT"]], Wp=[raw])
                            for comp in range(3):
                                if si < 0:
                                    k.op("pool", lambda: nc.gpsimd.memset(raw[:, hh, comp, 0:3], 0.0), Wp=[raw])
                                else:
                                    k.op("pool", lambda: nc.gpsimd.tensor_copy(out=raw[:, hh, comp, 0:3], in_=cH[:, comp * 16 + h, si * 3:(si + 1) * 3]), R=[cH], Wp=[raw])
                    k.dma("sp", zs[:, :, 0:nt], S["zT"][hg * 128:(hg + HG) * 128, ta:ta + nt].rearrange("(h p) t -> p h t", p=128), R=[S["zT"]], W=[zs])
                    cvR, ctmp = Bf['cvR'], Bf['ctmp']
                    for j in range(4):
                        for hh in range(HG):
                            h = hg + hh
                            for comp in range(3):
                                ch = comp * 16 + h
                                rg = cvR[hh][comp]
                                on_pool = comp == 2
                                o_ = cv[:, hh, comp, 0:nt]
                                i_ = raw[:, hh, comp, j:j + nt]
                                if j == 0:
                                    if on_pool:
                                        k.op("pool", lambda: nc.gpsimd.tensor_scalar(out=o_, in0=i_, scalar1=cw[:, ch, 0:1], scalar2=cw[:, ch, 4:5], op0=ALU.mult, op1=ALU.add),
                                             R=[raw, cw], Wp=[rg], Wa=[cv])
                                    else:
                                        k.op("dve", lambda: nc.vector.tensor_scalar(out=o_, in0=i_, scalar1=cw[:, ch, 0:1], scalar2=cw[:, ch, 4:5], op0=ALU.mult, op1=ALU.add),
                                             R=[raw, cw], Wp=[rg], Wa=[cv])
                                elif on_pool:
                                    t_ = ctmp[:, hh, comp, 0:nt]
                                    tr = Bf['ctR'][hh][comp]
                                    k.op("pool", lambda: nc.gpsimd.tensor_scalar(out=t_, in0=i_, scalar1=cw[:, ch, j:j + 1], scalar2=0.0, op0=ALU.mult, op1=ALU.add), R=[raw, cw], W=[tr])
                                    k.op("pool", lambda: nc.gpsimd.tensor_tensor(out=o_, in0=o_, in1=t_, op=ALU.add), R=[tr, rg], Wp=[rg])
                                else:
                                    k.op("dve", lambda: nc.vector.scalar_tensor_tensor(out=o_, in0=i_, scalar=cw[:, ch, j:j + 1], in1=o_, op0=ALU.mult, op1=ALU.add),
                                         R=[raw, cw, rg], Wp=[rg])
                    allcv = [cvR[a][b_] for a in range(HG) for b_ in range(3)]
                    yield
                    k.op("act", lambda: nc.scalar.activation(out=cv[:, :, :, 0:nt], in_=cv[:, :, :, 0:nt], func=AF.Silu), R=allcv, W=[cv] + allcv)
                    k.op("pool", lambda: nc.gpsimd.tensor_tensor(out=sqb[:, :, :, 0:nt], in0=cv[:, :, 0:2, 0:nt], in1=cv[:, :, 0:2, 0:nt], op=ALU.mult), R=[cv], W=[sqb])
                    for b4 in range(0, HG, 2):
                        b = bank()
                        for j in range(2):
                            k.op("pe", lambda: nc.tensor.matmul(b[:, j * 2 * nt:(j + 1) * 2 * nt], lhsT=ones_f[:], rhs=sqb[:, b4 + j, :, 0:nt], start=True, stop=True),
                                 R=[ones_f, sqb], W=[b], sig=(j == 1))
                        bv = b[:, 0:4 * nt].rearrange("p (a c b) -> p a c b", a=2, c=2)
                        k.op("act", lambda: nc.scalar.activation(out=rst[:, b4:b4 + 2, :, 0:nt], in_=bv, func=AF.Sqrt, bias=1e-6, scale=1.0), R=[b], Wp=[rst])
                    k.op("dve", lambda: nc.vector.reciprocal(out=rst[:, :, :, 0:nt], in_=rst[:, :, :, 0:nt]), R=[rst], W=[rst])
                    k.op("dve", lambda: nc.vector.scalar_tensor_tensor(out=cv[:, :, 0, 0:nt], in0=cv[:, :, 0, 0:nt], scalar=HD ** -0.5, in1=rst[:, :, 0, 0:nt],
                                                                       op0=ALU.mult, op1=ALU.mult), R=[cv, rst], Wp=[cv])
                    k.op("pool", lambda: nc.gpsimd.tensor_tensor(out=cv[:, :, 1, 0:nt], in0=cv[:, :, 1, 0:nt], in1=rst[:, :, 1, 0:nt], op=ALU.mult), R=[cv, rst], Wp=[cv])
                    k.op("dve", lambda: nc.vector.tensor_tensor(out=qd[:, :, 0:nt], in0=cv[:, :, 0, 0:nt], in1=eGbc[:, hg:hg + HG, 0:nt], op=ALU.mult), R=[cv, eGbc], W=[qd])
                    yield
                    for b4 in range(0, HG, 4):
                        bk_, bv_ = bank(), bank()
                        for j in range(4):
                            k.op("pe", lambda: nc.tensor.transpose(bk_[0:nt, j * 128:(j + 1) * 128], cv[:, b4 + j, 1, 0:nt], ident[:]), R=[cv, ident], W=[bk_], sig=(j == 3))
                        for j in range(4):
                            k.op("pe", lambda: nc.tensor.transpose(bv_[0:nt, j * 128:(j + 1) * 128], cv[:, b4 + j, 2, 0:nt], ident[:]), R=[cv, ident], W=[bv_], sig=(j == 3))
                        hs = slice(hg + b4, hg + b4 + 4)
                        kv3 = bk_[0:nt, :].rearrange("p (a b) -> p a b", b=128)
                        vv3 = bv_[0:nt, :].rearrange("p (a b) -> p a b", b=128)
                        k.op("dve", lambda: nc.vector.tensor_tensor(out=kbg[0:nt, b4:b4 + 4, :], in0=kv3, in1=bg[0:nt, hs].unsqueeze(2).to_broadcast([nt, 4, 128]), op=ALU.mult),
                             R=[bk_, bg], Wp=[kbg])
                        k.op("dve", lambda: nc.vector.tensor_tensor(out=kdec[0:nt, b4:b4 + 4, :], in0=kv3, in1=edec[0:nt, hs].unsqueeze(2).to_broadcast([nt, 4, 128]), op=ALU.mult),
                             R=[bk_, edec], Wp=[kdec])
                        k.op("dve", lambda: nc.vector.tensor_tensor(out=vb[0:nt, b4:b4 + 4, :], in0=vv3, in1=beta[0:nt, hs].unsqueeze(2).to_broadcast([nt, 4, 128]), op=ALU.mult),
                             R=[bv_, beta], Wp=[vb])
                    yield
                    for b4 in range(0, HG, 4):
                        b1, b2 = bank(), bank()
                        for j in range(4):
                            k.op("pe", lambda: nc.tensor.matmul(b1[0:nt, j * nt:(j + 1) * nt], lhsT=cv[:, b4 + j, 1, 0:nt], rhs=cv[:, b4 + j, 1, 0:nt], start=True, stop=True),
                                 R=[cv], W=[b1], sig=(j == 3))
                        for j in range(4):
                            k.op("pe", lambda: nc.tensor.matmul(b2[0:nt, j * nt:(j + 1) * nt], lhsT=cv[:, b4 + j, 1, 0:nt], rhs=cv[:, b4 + j, 0, 0:nt], start=True, stop=True),
                                 R=[cv], W=[b2], sig=(j == 3))
                        hs = slice(hg + b4, hg + b4 + 4)
                        k.op("dve", lambda: nc.vector.tensor_tensor(out=L[0][0:nt, b4:b4 + 4, 0:nt], in0=b1[0:nt, 0:4 * nt].rearrange("p (a b) -> p a b", b=nt),
                                                                    in1=dmS[0:nt, hs, 0:nt], op=ALU.mult), R=[b1, dmS], Wp=[L[0]])
                        k.op("dve", lambda: nc.vector.tensor_tensor(out=qkm[0:nt, b4:b4 + 4, 0:nt], in0=b2[0:nt, 0:4 * nt].rearrange("p (a b) -> p a b", b=nt),
                                                                    in1=dmT[0:nt, hs, 0:nt], op=ALU.mult), R=[b2, dmT], Wp=[qkm])
                    yield
                    for b4 in range(0, HG, 4):
                        b = bank()
                        for j in range(4):
                            k.op("pe", lambda: nc.tensor.transpose(b[0:nt, j * nt:(j + 1) * nt], L[0][0:nt, b4 + j, 0:nt], ident[0:nt, 0:nt]), R=[L[0], ident], W=[b], sig=(j == 3))
                        bv = b[0:nt, 0:4 * nt].rearrange("p (a b) -> p a b", b=nt)
                        k.op("act", lambda: nc.scalar.copy(out=U[0][0:nt, b4:b4 + 4, 0:nt], in_=bv), R=[b], Wp=[U[0]])
                        k.op("pool", lambda: nc.gpsimd.tensor_tensor(out=P[0:nt, b4:b4 + 4, 0:nt], in0=ident[0:nt, 0:nt].unsqueeze(1).to_broadcast([nt, 4, nt]),
                                                                     in1=U[0][0:nt, b4:b4 + 4, 0:nt], op=ALU.subtract), R=[U[0], ident], Wp=[P])
                    cur = 0
                    for step in range(5):
                        nx = 1 - cur
                        for b4 in range(0, HG, 4):
                            b1 = bank()
                            for j in range(4):
                                k.op("pe", lambda: nc.tensor.matmul(b1[0:nt, j * nt:(j + 1) * nt], lhsT=U[cur][0:nt, b4 + j, 0:nt], rhs=L[cur][0:nt, b4 + j, 0:nt], start=True, stop=True),
                                     R=[U[cur], L[cur]], W=[b1], sig=(j == 3))
                            k.op("act", lambda: nc.scalar.copy(out=L[nx][0:nt, b4:b4 + 4, 0:nt], in_=b1[0:nt, 0:4 * nt].rearrange("p (a b) -> p a b", b=nt)), R=[b1], Wp=[L[nx]])
                            if step < 4:
                                b2 = bank()
                                for j in range(4):
                                    k.op("pe", lambda: nc.tensor.matmul(b2[0:nt, j * nt:(j + 1) * nt], lhsT=L[cur][0:nt, b4 + j, 0:nt], rhs=U[cur][0:nt, b4 + j, 0:nt], start=True, stop=True),
                                         R=[U[cur], L[cur]], W=[b2], sig=(j == 3))
                                k.op("dve", lambda: nc.vector.tensor_copy(out=U[nx][0:nt, b4:b4 + 4, 0:nt], in_=b2[0:nt, 0:4 * nt].rearrange("p (a b) -> p a b", b=nt)), R=[b2], Wp=[U[nx]])
                        for b4 in range(0, HG, 4):
                            b3 = bank()
                            for j in range(4):
                                k.op("pe", lambda: nc.tensor.matmul(b3[0:nt, j * nt:(j + 1) * nt], lhsT=L[nx][0:nt, b4 + j, 0:nt], rhs=P[0:nt, b4 + j, 0:nt], start=True, stop=True),
                                     R=[L[nx], P], W=[b3], sig=(j == 3))
                            k.op("dve", lambda: nc.vector.tensor_tensor(out=P[0:nt, b4:b4 + 4, 0:nt], in0=P[0:nt, b4:b4 + 4, 0:nt],
                                                                        in1=b3[0:nt, 0:4 * nt].rearrange("p (a b) -> p a b", b=nt), op=ALU.add), R=[b3, P], Wp=[P])
                        cur = nx
                        yield
                    yield
                    for b4 in range(0, HG, 4):
                        b1, b2 = bank(), bank()
                        for j in range(4):
                            k.op("pe", lambda: nc.tensor.matmul(b1[:, j * nt:(j + 1) * nt], lhsT=kbg[0:nt, b4 + j, :], rhs=P[0:nt, b4 + j, 0:nt], start=True, stop=True),
                                 R=[kbg, P], W=[b1], sig=(j == 3))
                        for j in range(4):
                            k.op("pe", lambda: nc.tensor.matmul(b2[0:nt, j * 128:(j + 1) * 128], lhsT=P[0:nt, b4 + j, 0:nt], rhs=vb[0:nt, b4 + j, :], start=True, stop=True),
                                 R=[vb, P], W=[b2], sig=(j == 3))
                        k.op("act", lambda: nc.scalar.copy(out=wT[:, b4:b4 + 4, 0:nt], in_=b1[:, 0:4 * nt].rearrange("p (a b) -> p a b", b=nt)), R=[b1], Wp=[wT])
                        k.op("dve", lambda: nc.vector.tensor_copy(out=u_[0:nt, b4:b4 + 4, :], in_=b2[0:nt, :].rearrange("p (a b) -> p a b", b=128)), R=[b2], Wp=[u_])
                    yield
                    for ci in range(nch):
                        r = slice(ci * 64, ci * 64 + 64)
                        for b4 in range(0, HG, 4):
                            b1 = bank()
                            for j in range(4):
                                k.op("pe", lambda: nc.tensor.matmul(b1[r, j * 128:(j + 1) * 128], lhsT=wT[:, b4 + j, r], rhs=Sg[:, b4 + j, :], start=True, stop=True),
                                     R=[wT, Sg], W=[b1], sig=(j == 3))
                            k.op("dve", lambda: nc.vector.tensor_tensor(out=vnew[r, b4:b4 + 4, :], in0=u_[r, b4:b4 + 4, :], in1=b1[r, :].rearrange("p (a b) -> p a b", b=128),
                                                                        op=ALU.subtract), R=[u_, b1], Wp=[vnew])
                        bo = bank()
                        for hh in range(HG):
                            k.op("pe", lambda: nc.tensor.matmul(bo[:, hh * 64:(hh + 1) * 64], lhsT=Sg[:, hh, :], rhs=qd[:, hh, r], start=True, stop=False),
                                 R=[Sg, qd], W=[bo], sig=False)
                            k.op("pe", lambda: nc.tensor.matmul(bo[:, hh * 64:(hh + 1) * 64], lhsT=vnew[0:nt, hh, :], rhs=qkm[0:nt, hh, r], start=False, stop=True),
                                 R=[vnew, qkm], W=[bo], sig=(hh == HG - 1))
                        k.op("act", lambda: nc.scalar.copy(out=oT[:, :, r], in_=bo[:, 0:HG * 64].rearrange("p (a b) -> p a b", b=64)), R=[bo], Wp=[oT])
                        for b4 in range(0, HG, 4):
                            b2 = bank()
                            for j in range(4):
                                k.op("pe", lambda: nc.tensor.matmul(b2[:, j * 128:(j + 1) * 128], lhsT=kdec[r, b4 + j, :], rhs=vnew[r, b4 + j, :], start=True, stop=True),
                                     R=[kdec, vnew], W=[b2], sig=(j == 3))
                            hs = slice(hg + b4, hg + b4 + 4)
                            col = ci * 64 + 63
                            k.op("pool", lambda: nc.gpsimd.tensor_tensor(out=Sg[:, b4:b4 + 4, :], in0=Sg[:, b4:b4 + 4, :], in1=eGbc[:, hs, col:col + 1].to_broadcast([128, 4, 128]), op=ALU.mult),
                                 R=[Sg, eGbc], Wp=[Sg])
                            k.op("dve", lambda: nc.vector.tensor_tensor(out=Sg[:, b4:b4 + 4, :], in0=Sg[:, b4:b4 + 4, :], in1=b2[:, :].rearrange("p (a b) -> p a b", b=128), op=ALU.add),
                                 R=[Sg, b2], Wp=[Sg])
                        yield
                    yield
                    k.op("pool", lambda: nc.gpsimd.tensor_tensor(out=sqb[:, :, 0, 0:nt], in0=oT[:, :, 0:nt], in1=oT[:, :, 0:nt], op=ALU.mult), R=[oT], Wp=[sqb])
                    for b4 in range(0, HG, 4):
                        b = bank()
                        for j in range(4):
                            k.op("pe", lambda: nc.tensor.matmul(b[:, j * nt:(j + 1) * nt], lhsT=ones_f[:], rhs=sqb[:, b4 + j, 0, 0:nt], start=True, stop=True),
                                 R=[ones_f, sqb], W=[b], sig=(j == 3))
                        k.op("act", lambda: nc.scalar.activation(out=rst[:, b4:b4 + 4, 0, 0:nt], in_=b[:, 0:4 * nt].rearrange("p (a b) -> p a b", b=nt), func=AF.Sqrt,
                                                                 bias=EPS, scale=1.0 / 128), R=[b], Wp=[rst])
                    k.op("dve", lambda: nc.vector.reciprocal(out=rst[:, :, 0, 0:nt], in_=rst[:, :, 0, 0:nt]), R=[rst], Wp=[rst])
                    k.op("dve", lambda: nc.vector.tensor_tensor(out=oT[:, :, 0:nt], in0=oT[:, :, 0:nt], in1=rst[:, :, 0, 0:nt], op=ALU.mult), R=[oT, rst], W=[oT])
                    k.op("dve", lambda: nc.vector.scalar_tensor_tensor(out=ob[:, :, 0:nt], in0=oT[:, :, 0:nt], scalar=dng[:, 0:1], in1=zs[:, :, 0:nt], op0=ALU.mult, op1=ALU.mult),
                         R=[oT, dng, zs], W=[ob])
                    k.dma("pool", S["aT"][2048 + hg * 128:2048 + (hg + HG) * 128, ta:ta + nt].rearrange("(h p) t -> p h t", p=128), ob[:, :, 0:nt], R=[ob], Wp=[S["aT"]])

                gens = [hg_gen(hg, BS[i % 2], S_g[hg // HG]) for i, hg in enumerate(range(0, 16, HG))]
                active = []
                while gens or active:
                    while len(active) < 2 and gens:
                        active.append(gens.pop(0))
                    for gen_ in list(active):
                        try:
                            next(gen_)
                        except StopIteration:
                            active.remove(gen_)
            for gi_ in range(16 // HG):
                r0_ = qi * 2048 + gi_ * HG * 128
                k.dma("pool", I["so"][r0_:r0_ + HG * 128, :].rearrange("(h d) e -> d h e", d=128), S_g[gi_][:], R=[S_g[gi_]], Wp=[I["so"]])


def ln_alloc(k, st, gmax):
    return dict(sq=[k.sb(st, "lnsq", [128, gmax], F32) for _ in range(2)], mt=k.sb(st, "lnm", [128, gmax], F32),
                t1=k.sb(st, "lnt", [128, gmax], F32), rs=k.sb(st, "lnrs", [128, gmax], F32), nm=k.sb(st, "lnnm", [128, gmax], F32))


def ln_stats(k, G, T, s1, s1R, gn):
    nc = k.nc
    bank = G["bank"]
    ones_f = G["ones_f"]
    sq, mt, t1, rs, nm = T["sq"], T["mt"], T["t1"], T["rs"], T["nm"]
    bs, bq = bank(), bank()
    for m in range(KC):
        q_ = sq[m % 2]
        k.op("act", lambda: nc.scalar.activation(out=q_[:, 0:gn], in_=s1[:, m, 0:gn], func=AF.Square), R=[s1R[m]], W=[q_])
        k.op("pe", lambda: nc.tensor.matmul(bs[:, 0:gn], lhsT=ones_f[:], rhs=s1[:, m, 0:gn], start=(m == 0), stop=(m == KC - 1)),
             R=[s1R[m], ones_f], W=[bs], sig=(m == KC - 1))
        k.op("pe", lambda: nc.tensor.matmul(bq[:, 0:gn], lhsT=ones_f[:], rhs=q_[:, 0:gn], start=(m == 0), stop=(m == KC - 1)),
             R=[q_, ones_f], W=[bq], sig=True)
    g = slice(0, gn)
    k.op("dve", lambda: nc.vector.tensor_scalar(out=mt[:, g], in0=bs[:, g], scalar1=1.0 / D, scalar2=None, op0=ALU.mult), R=[bs], W=[mt])
    k.op("dve", lambda: nc.vector.tensor_tensor(out=t1[:, g], in0=mt[:, g], in1=mt[:, g], op=ALU.mult), R=[mt], W=[t1])
    k.op("dve", lambda: nc.vector.scalar_tensor_tensor(out=t1[:, g], in0=bq[:, g], scalar=1.0 / D, in1=t1[:, g], op0=ALU.mult, op1=ALU.subtract),
         R=[bq, t1], W=[t1])
    k.op("act", lambda: nc.scalar.activation(out=t1[:, g], in_=t1[:, g], func=AF.Sqrt, bias=EPS, scale=1.0), R=[t1], W=[t1])
    k.op("dve", lambda: nc.vector.reciprocal(out=rs[:, g], in_=t1[:, g]), R=[t1], W=[rs])
    k.op("dve", lambda: nc.vector.scalar_tensor_tensor(out=nm[:, g], in0=mt[:, g], scalar=-1.0, in1=rs[:, g], op0=ALU.mult, op1=ALU.mult),
         R=[mt, rs], W=[nm])
    return rs, nm


def phase_D(k, cfg, G):
    nc = k.nc
    I, S = G["I"], G["S"]
    bank = G["bank"]
    groups = cfg.groups(384)
    gmax = max(g[1] for g in groups)
    with ExitStack() as st:
        aT = k.sb(st, "aT", [128, KC, gmax], BF16)
        s1 = k.sb(st, "s1", [128, KC, gmax], F32)
        s1R = [Reg("s1R") for _ in range(KC)]
        gb = k.sb(st, "ln1", [128, 64], F32)
        k.dma("sp", gb[:], I["ln1"][:], W=[gb])
        ws = WTiles(k, st, nslot=3)
        xr = [k.sb(st, "xr", [128, gmax], F32) for _ in range(3)]
        T = ln_alloc(k, st, gmax)
        of = [k.sb(st, "of", [128, gmax], F32) for _ in range(2)]
        ob = [k.sb(st, "ob", [128, gmax], BF16) for _ in range(2)]
        for (g0, gn) in groups:
            g = slice(0, gn)
            k.dma("sp", aT[:, :, g], S["aT"][:].rearrange("(kc p) t -> p kc t", p=128)[:, :, g0:g0 + gn], R=[S["aT"]], W=[aT])
            for m in range(KC):
                wb = ws.get(S["b_w_o"], m // 2)
                sub = (m % 2) * 128
                x_ = xr[m % 3]
                k.dma("sp", x_[:, g], S["xnT"][m * 128:(m + 1) * 128, g0:g0 + gn], R=[S["xnT"]], W=[x_])
                b = bank()
                for kc in range(KC):
                    k.op("pe", lambda kc=kc: nc.tensor.matmul(b[:, 0:gn], lhsT=wb[:, kc, sub:sub + 128], rhs=aT[:, kc, g], start=(kc == 0), stop=(kc == KC - 1)),
                         R=[wb, aT], W=[b], sig=(kc == KC - 1))
                k.op("dve", lambda: nc.vector.scalar_tensor_tensor(out=s1[:, m, g], in0=x_[:, g], scalar=ALPHA, in1=b[:, 0:gn], op0=ALU.mult, op1=ALU.add),
                     R=[x_, b], W=[s1R[m]])
            rs, nm = ln_stats(k, G, T, s1, s1R, gn)
            for m in range(KC):
                o_, b_ = of[m % 2], ob[m % 2]
                k.op("pool", lambda: nc.gpsimd.tensor_tensor(out=o_[:, g], in0=s1[:, m, g], in1=rs[:, g], op=ALU.mult), R=[s1R[m], rs], W=[o_])
                k.op("dve", lambda: nc.vector.tensor_tensor(out=o_[:, g], in0=o_[:, g], in1=nm[:, g], op=ALU.add), R=[o_, nm], W=[o_])
                k.op("act", lambda: nc.scalar.activation(out=o_[:, g], in_=o_[:, g], func=AF.Identity, scale=gb[:, m:m + 1], bias=gb[:, 32 + m:33 + m]),
                     R=[o_, gb], W=[o_])
                k.op("pool", lambda: nc.gpsimd.tensor_copy(out=b_[:, g], in_=o_[:, g]), R=[o_], W=[b_])
                k.dma("act", S["x1T"][m * 128:(m + 1) * 128, g0:g0 + gn], o_[:, g], R=[o_], Wp=[S["x1T"]])
                k.dma("act", S["x1b"][m * 128:(m + 1) * 128, g0:g0 + gn], b_[:, g], R=[b_], Wp=[S["x1b"]])
    k.barrier()


def seg_pieces(cfg, g0, gn):
    out = []
    for qi, sq in enumerate(cfg.seqs):
        a = max(sq["t0"], g0)
        b = min(sq["t0"] + sq["T"], g0 + gn)
        if a < b:
            out.append((qi, a - g0, b - a, a == sq["t0"], b == sq["t0"] + sq["T"]))
    return out


def phase_E(k, cfg, G):
    nc = k.nc
    I, S, C = G["I"], G["S"], G["C"]
    bank = G["bank"]
    FC, NS, NSEQ, DFF = cfg.FC, cfg.NS, cfg.NSEQ, cfg.DFF
    with ExitStack() as pst:
        fw = k.sb(pst, "ffnw", [128, 2 * FC, 4], F32)
        k.dma("sp", fw[:].rearrange("p a b -> p (a b)"), I["ffnw"][:], W=[fw])
        hsave = k.sb(pst, "hsave", [128, 2 * FC, 2], F32)
        k.op("pool", lambda: nc.gpsimd.memset(hsave[:], 0.0), W=[hsave])
        fst = k.sb(pst, "fst", [128, 2 * FC, NSEQ * 2], F32)
        fH = k.sb(pst, "fH", [128, 2 * FC, max(NS, 1) * 2], F32)
        if NS > 0:
            with ExitStack() as s0:
                srow = [k.sb(s0, "srow", [NS * 2, 512], F32) for _ in range(2)]
                n = 0
                for c4 in range(0, 2 * FC, 4):
                    b = bank()
                    n4 = min(4, 2 * FC - c4)
                    sr = srow[n % 2]
                    n += 1
                    k.dma("sp", sr[:, 0:n4 * 128], I["sffn"][:, c4 * 128:(c4 + n4) * 128], W=[sr])
                    for j in range(n4):
                        k.op("pe", lambda j=j: nc.tensor.transpose(b[:, j * 128:j * 128 + NS * 2], sr[:, j * 128:(j + 1) * 128],
                                                                   C["c_ident"][0:NS * 2, 0:NS * 2]),
                             R=[sr, C["c_ident"]], W=[b], sig=(j == n4 - 1))
                    k.op("dve", lambda: nc.vector.tensor_copy(out=fH[:, c4:c4 + n4, :],
                                                              in_=b[:, 0:n4 * 128].rearrange("p (a b) -> p a b", b=128)[:, :, 0:NS * 2]),
                         R=[b], Wp=[fH])
            k.barrier()
        for (g0, gn) in cfg.groups(768):
            pcs = seg_pieces(cfg, g0, gn)
            offs = []
            o = 0
            for p in pcs:
                offs.append(o)
                o += p[2] + 2
            RW = o
            with ExitStack() as st:
                x1 = k.sb(st, "x1b", [128, KC, gn], BF16)
                k.dma("sp", x1[:], S["x1b"][:].rearrange("(kc p) t -> p kc t", p=128)[:, :, g0:g0 + gn], R=[S["x1b"]], W=[x1])
                ws = WTiles(k, st, nslot=4)
                raw = [[k.sb(st, "raw", [128, RW], F32) for _ in range(2)] for _ in range(2)]
                cv = [[k.sb(st, "cv", [128, gn], F32) for _ in range(2)] for _ in range(2)]
                ao = [k.sb(st, "ao", [128, gn], BF16) for _ in range(2)]
                for c in range(FC):
                    par = c % 2
                    for half in range(2):
                        ch = half * FC + c
                        r_ = raw[half][par]
                        wb = ws.get(S["b_w_up"], ch // 2)
                        sub = (ch % 2) * 128
                        for pi, (qi, a, ln, s_st, s_en) in enumerate(pcs):
                            o_ = offs[pi]
                            if not s_st:
                                k.op("pool", lambda o_=o_: nc.gpsimd.tensor_copy(out=r_[:, o_:o_ + 2], in_=hsave[:, ch, :]), R=[hsave], Wp=[r_])
                            elif qi == 0:
                                k.op("pool", lambda o_=o_: nc.gpsimd.memset(r_[:, o_:o_ + 2], 0.0), Wp=[r_])
                            else:
                                k.op("pool", lambda o_=o_, qi=qi: nc.gpsimd.tensor_copy(out=r_[:, o_:o_ + 2], in_=fH[:, ch, (qi - 1) * 2:qi * 2]), R=[fH], Wp=[r_])
                        for (b0, bn) in blocks(gn):
                            b = bank()
                            for kc in range(KC):
                                k.op("pe", lambda kc=kc: nc.tensor.matmul(b[:, 0:bn], lhsT=wb[:, kc, sub:sub + 128], rhs=x1[:, kc, b0:b0 + bn], start=(kc == 0), stop=(kc == KC - 1)),
                                     R=[wb, x1], W=[b], sig=(kc == KC - 1))
                            for pi, (qi, a, ln, s_st, s_en) in enumerate(pcs):
                                lo, hi = max(a, b0), min(a + ln, b0 + bn)
                                if lo < hi:
                                    d0 = offs[pi] + 2 + (lo - a)
                                    k.op("act", lambda lo=lo, hi=hi, d0=d0: nc.scalar.copy(out=r_[:, d0:d0 + hi - lo], in_=b[:, lo - b0:hi - b0]), R=[b], Wp=[r_])
                        c_ = cv[half][par]
                        for pi, (qi, a, ln, s_st, s_en) in enumerate(pcs):
                            o_ = offs[pi]
                            k.op("dve", lambda o_=o_, a=a, ln=ln: nc.vector.tensor_scalar(out=c_[:, a:a + ln], in0=r_[:, o_:o_ + ln], scalar1=fw[:, ch, 0:1], scalar2=fw[:, ch, 3:4],
                                                                                       op0=ALU.mult, op1=ALU.add), R=[r_, fw], Wp=[c_])
                            for j in (1, 2):
                                k.op("dve", lambda o_=o_, a=a, ln=ln, j=j: nc.vector.scalar_tensor_tensor(out=c_[:, a:a + ln], in0=r_[:, o_ + j:o_ + j + ln], scalar=fw[:, ch, j:j + 1],
                                                                                                         in1=c_[:, a:a + ln], op0=ALU.mult, op1=ALU.add), R=[r_, fw, c_], Wp=[c_])
                            if s_en:
                                k.op("pool", lambda o_=o_, ln=ln, qi=qi: nc.gpsimd.tensor_copy(out=fst[:, ch, qi * 2:qi * 2 + 2], in_=r_[:, o_ + ln:o_ + ln + 2]), R=[r_], Wp=[fst])
                            else:
                                k.op("pool", lambda o_=o_, ln=ln: nc.gpsimd.tensor_copy(out=hsave[:, ch, :], in_=r_[:, o_ + ln:o_ + ln + 2]), R=[r_], Wp=[hsave])
                    gt, vl, a_ = cv[0][par], cv[1][par], ao[par]
                    k.op("act", lambda: nc.scalar.activation(out=gt[:], in_=gt[:], func=AF.Silu), R=[gt], W=[gt])
                    k.op("pool", lambda: nc.gpsimd.tensor_tensor(out=a_[:], in0=gt[:], in1=vl[:], op=ALU.mult), R=[gt, vl], W=[a_])
                    k.dma("act", S["actT"][c * 128:(c + 1) * 128, g0:g0 + gn], a_[:], R=[a_], Wp=[S["actT"]])
            k.barrier()
        with ExitStack() as st:
            orow = [k.sb(st, "orow", [NSEQ * 2, 512], F32) for _ in range(2)]
            n = 0
            for c4 in range(0, 2 * FC, 4):
                n4 = min(4, 2 * FC - c4)
                b = bank()
                for j in range(n4):
                    k.op("pe", lambda j=j: nc.tensor.transpose(b[0:NSEQ * 2, j * 128:(j + 1) * 128], fst[:, c4 + j, :], C["c_ident"][:]),
                         R=[fst, C["c_ident"]], W=[b], sig=(j == n4 - 1))
                o_ = orow[n % 2]
                n += 1
                k.op("dve", lambda: nc.vector.tensor_copy(out=o_[:, 0:n4 * 128], in_=b[0:NSEQ * 2, 0:n4 * 128]), R=[b], W=[o_])
                k.dma("pool", I["ffno"][:, c4 * 128:(c4 + n4) * 128], o_[:, 0:n4 * 128], R=[o_], Wp=[I["ffno"]])
        k.barrier()


def phase_F(k, cfg, G):
    nc = k.nc
    I, S, C = G["I"], G["S"], G["C"]
    bank = G["bank"]
    FC = cfg.FC
    kgs = [(i, min(32, FC - i)) for i in range(0, FC, 32)]
    groups = cfg.groups(256)
    gmax = max(g[1] for g in groups)
    with ExitStack() as st:
        aT = k.sb(st, "actT", [128, FC, gmax], BF16)
        s1 = k.sb(st, "s2", [128, KC, gmax], F32)
        s1R = [Reg("s2R") for _ in range(KC)]
        gb = k.sb(st, "ln2", [128, 64], F32)
        k.dma("sp", gb[:], I["ln2"][:], W=[gb])
        ws = WTiles(k, st, nslot=4)
        xr = [k.sb(st, "xr", [128, gmax], F32) for _ in range(3)]
        T = ln_alloc(k, st, gmax)
        yt = [k.sb(st, "yt", [128, 2048], F32) for _ in range(2)]
        yn = 0
        for (g0, gn) in groups:
            g = slice(0, gn)
            k.dma("sp", aT[:, :, g], S["actT"][:].rearrange("(kc p) t -> p kc t", p=128)[:, :, g0:g0 + gn], R=[S["actT"]], W=[aT])
            for m in range(KC):
                x_ = xr[m % 3]
                sub = (m % 2) * 128
                k.dma("sp", x_[:, g], S["x1T"][m * 128:(m + 1) * 128, g0:g0 + gn], R=[S["x1T"]], W=[x_])
                b = bank()
                for gi, (k0, kn) in enumerate(kgs):
                    wb = ws.get(S["b_w_down"], m // 2, kg=gi, kcn=kn)
                    for kc in range(kn):
                        first = (gi == 0 and kc == 0)
                        last = (gi == len(kgs) - 1 and kc == kn - 1)
                        k.op("pe", lambda kc=kc: nc.tensor.matmul(b[:, 0:gn], lhsT=wb[:, kc, sub:sub + 128], rhs=aT[:, k0 + kc, g], start=first, stop=last),
                             R=[wb, aT], W=[b], sig=(kc == kn - 1))
                k.op("dve", lambda: nc.vector.scalar_tensor_tensor(out=s1[:, m, g], in0=x_[:, g], scalar=ALPHA, in1=b[:, 0:gn], op0=ALU.mult, op1=ALU.add),
                     R=[x_, b], W=[s1R[m]])
            rs, nm = ln_stats(k, G, T, s1, s1R, gn)
            for m in range(KC):
                k.op("pool", lambda: nc.gpsimd.tensor_tensor(out=s1[:, m, g], in0=s1[:, m, g], in1=rs[:, g], op=ALU.mult), R=[s1R[m], rs], W=[s1R[m]])
                k.op("dve", lambda: nc.vector.tensor_tensor(out=s1[:, m, g], in0=s1[:, m, g], in1=nm[:, g], op=ALU.add), R=[s1R[m], nm], W=[s1R[m]])
                k.op("act", lambda: nc.scalar.activation(out=s1[:, m, g], in_=s1[:, m, g], func=AF.Identity, scale=gb[:, m:m + 1], bias=gb[:, 32 + m:33 + m]),
                     R=[s1R[m], gb], W=[s1R[m]])
            for ti in range(gn // 128):
                for hf in range(2):
                    y_ = yt[yn % 2]
                    yn += 1
                    for q in range(4):
                        b = bank()
                        for j in range(4):
                            m = hf * 16 + q * 4 + j
                            k.op("pe", lambda m=m, j=j: nc.tensor.transpose(b[:, j * 128:(j + 1) * 128], s1[:, m, ti * 128:(ti + 1) * 128], C["c_ident"][:]),
                                 R=[s1R[m], C["c_ident"]], W=[b], sig=(j == 3))
                        if q % 2:
                            k.op("act", lambda: nc.scalar.copy(out=y_[:, q * 512:(q + 1) * 512], in_=b[:, :]), R=[b], Wp=[y_])
                        else:
                            k.op("dve", lambda: nc.vector.tensor_copy(out=y_[:, q * 512:(q + 1) * 512], in_=b[:, :]), R=[b], Wp=[y_])
                    k.dma("pool", I["y"][g0 + ti * 128:g0 + (ti + 1) * 128, hf * 2048:(hf + 1) * 2048], y_[:], R=[y_], Wp=[I["y"]])
    k.barrier()


_CACHE = {}


def _pp(v):
    return np.ascontiguousarray(np.asarray(v, np.float32).reshape(32, 128).T)


def make_in_maps(cfg, n_cores, inp):
    f = lambda a: np.ascontiguousarray(np.asarray(a, dtype=np.float32))
    NS, DFF, FC = cfg.NS, cfg.DFF, cfg.FC
    shared = {}
    shared["lnin"] = np.concatenate([_pp(inp["ln_in_g"]), _pp(inp["ln_in_b"])], axis=1)
    shared["ln1"] = np.concatenate([_pp(inp["ln1_g"][0]), _pp(inp["ln1_b"][0])], axis=1)
    shared["ln2"] = np.concatenate([_pp(inp["ln2_g"][0]), _pp(inp["ln2_b"][0])], axis=1)
    shared["w_in"] = f(inp["w_in"][0]); shared["w_o"] = f(inp["w_o"][0])
    shared["w_up"] = f(inp["w_ffn_up"][0]); shared["w_down"] = f(inp["w_ffn_down"][0])
    cw = np.concatenate([f(inp["conv_qkv_w"][0]), f(inp["conv_qkv_b"])], axis=0)
    shared["convw"] = np.ascontiguousarray(cw.reshape(5, 48, 128).transpose(2, 1, 0).reshape(128, 240))
    fw = np.concatenate([f(inp["ffn_conv_w"][0]), f(inp["ffn_conv_b"])], axis=0)
    shared["ffnw"] = np.ascontiguousarray(fw.reshape(4, 2 * FC, 128).transpose(2, 1, 0).reshape(128, 2 * FC * 4))
    shared["alog"] = np.ascontiguousarray(np.broadcast_to(f(inp["a_log"][0])[None, :], (128, 16)))
    shared["dtb"] = np.ascontiguousarray(np.broadcast_to(f(inp["dt_bias"][0])[None, :], (128, 16)))
    shared["dng"] = f(inp["delta_norm_g"][0]).reshape(128, 1)
    shared["relb"] = f(inp["rel_bias"])
    shared.update(host_consts())
    maps = []
    for c in range(n_cores):
        m = dict(shared)
        sl = slice(c * NS, (c + 1) * NS)
        m["x"] = np.concatenate([f(inp["x_prompt"][c]), f(inp["x_sample"][sl]).reshape(NS * DEC, D)], axis=0)
        m["ck"] = f(inp["cache_attn_k"][0, sl]).reshape(NS * PAST, 512)
        m["cv"] = f(inp["cache_attn_v"][0, sl]).reshape(NS * PAST, 512)
        m["cik"] = f(inp["cache_idx_k"][0, sl]).reshape(NS * PAST, 64)
        m["sdel"] = f(inp["state_delta"][0, sl]).reshape(NS * 16 * 128, 128)
        m["sconv"] = f(inp["state_conv_qkv"][0, sl]).reshape(NS * 3, 6144)
        m["sffn"] = f(inp["state_ffn_conv"][0, sl]).reshape(NS * 2, 2 * DFF)
        maps.append(m)
    return maps


def kernel(**inp):
    B, SEQ = inp["x_prompt"].shape[0], inp["x_prompt"].shape[1]
    DB = inp["x_sample"].shape[0]
    DFF = inp["w_ffn_down"].shape[1]
    n_cores = B
    NS = DB // n_cores
    cfg = Cfg(SEQ, NS, DFF)
    key = (SEQ, NS, DFF)
    if key not in _CACHE:
        _CACHE[key] = build(cfg)
    nc = _CACHE[key]
    maps = make_in_maps(cfg, n_cores, inp)
    res = run_bass_kernel_spmd(nc, maps, core_ids=list(range(n_cores)))
    R = res.results
    return assemble(cfg, n_cores, R)


def assemble(cfg, n_cores, R):
    NS, SEQ, DFF, NSEQ = cfg.NS, cfg.SEQ, cfg.DFF, cfg.NSEQ
    g = lambda n: [np.asarray(R[c][n], dtype=np.float32) for c in range(n_cores)]
    y, ko, vo, iko, so, co, fo = g("y"), g("ko"), g("vo"), g("iko"), g("so"), g("convo"), g("ffno")
    yp = np.stack([a[:SEQ] for a in y])
    ys = np.concatenate([a[SEQ:].reshape(NS, DEC, D) for a in y])
    pk = np.stack([a[:SEQ].reshape(SEQ, 4, 128) for a in ko])[None]
    pv = np.stack([a[:SEQ].reshape(SEQ, 4, 128) for a in vo])[None]
    pik = np.stack([a[:SEQ] for a in iko])[None]
    sk = np.concatenate([a[SEQ:].reshape(NS, DEC, 4, 128) for a in ko])[None]
    sv = np.concatenate([a[SEQ:].reshape(NS, DEC, 4, 128) for a in vo])[None]
    sik = np.concatenate([a[SEQ:].reshape(NS, DEC, 64) for a in iko])[None]
    pd = np.stack([a.reshape(NSEQ, 16, 128, 128)[0] for a in so])[None]
    sd = np.concatenate([a.reshape(NSEQ, 16, 128, 128)[1:] for a in so])[None]
    pc = np.stack([a.reshape(NSEQ, 3, 6144)[0] for a in co])[None]
    sc = np.concatenate([a.reshape(NSEQ, 3, 6144)[1:] for a in co])[None]
    pf = np.stack([a.reshape(NSEQ, 2, 2 * DFF)[0] for a in fo])[None]
    sf = np.concatenate([a.reshape(NSEQ, 2, 2 * DFF)[1:] for a in fo])[None]
    return (yp, ys, pk, pv, pik, pd, pc, pf, sk, sv, sik, sd, sc, sf)
```

```python
import math
from contextlib import ExitStack
import numpy as np
import ml_dtypes
import concourse.bass as bass
import concourse.mybir as mybir
from concourse.bass_utils import run_bass_kernel_spmd

F32 = mybir.dt.float32
BF16 = mybir.dt.bfloat16
AF = mybir.ActivationFunctionType
ALU = mybir.AluOpType

D = 4096
KC = 32
NIN = 12400
HD = 128
PAST = 1024
DEC = 64
EPS = 1e-5
ALPHA = 2.0 ** 0.25
IDX_SCALE = (16 ** -0.5) * (64 ** -0.5)
O_QA, O_KA, O_VA, O_IQ, O_IK, O_IW, O_QKV, O_Z, O_BETA, O_A = 0, 2048, 2560, 3072, 4096, 4160, 4176, 10320, 12368, 12384
NEG = -1.0e30
WIN_PIECES = [(0, 0, 4096), (4096, 4096, 64), (4160, 4096, 64), (4224, 4176, 8192), (12416, 4096, 80), (12496, 12368, 32)]
WIN_PACKED = 12544
P_QA, P_KA, P_VA, P_IQ, P_IK2, P_QKV, P_Z, P_SM1, P_SM2 = 0, 2048, 2560, 3072, 4096, 4224, 10368, 12416, 12496
DBG = False
STOP_C = 99


MUTE = [False]


def stop_at(n):
    if STOP_C <= n:
        MUTE[0] = True


class Reg:
    __slots__ = ("w", "r", "n")

    def __init__(s, n=""):
        s.w = {}
        s.r = {}
        s.n = n


class Tile:
    def __init__(s, t, n):
        s.t = t
        s.reg = Reg(n)

    def __getitem__(s, i):
        return s.t[i]


class Eng:
    def __init__(s, name, h):
        s.name = name
        s.h = h
        s.semidx = None
        s.cnt = 0
        s.known = {}
        s.pending = False
        s.dsems = []
        s.dnext = 0


class K:
    SEM_LIMIT = 30000

    def __init__(s, nc, es):
        s.nc = nc
        s.es = es
        s.sems = []
        s.semmax = []
        s.E = {}
        for n, h in (("pe", nc.tensor), ("act", nc.scalar), ("dve", nc.vector), ("pool", nc.gpsimd), ("sp", nc.sync)):
            e = Eng(n, h)
            s.E[n] = e
            if n != "sp":
                e.semidx = s.newsem()
        for n, cnt in (("sp", 12), ("act", 8), ("pool", 8)):
            s.E[n].dsems = [s.newsem() for _ in range(cnt)]
        s.uid = 0

    def newsem(s):
        h = s.es.enter_context(s.nc.semaphore("sem%d" % len(s.sems)))
        s.sems.append(h)
        s.semmax.append(0)
        return len(s.sems) - 1

    def sb(s, st, name, shape, dt):
        s.uid += 1
        nm = "%s_%d" % (name, s.uid)
        return Tile(st.enter_context(s.nc.sbuf_tensor(nm, list(shape), dt)), nm)

    def ps(s, st, name, shape, dt=F32):
        s.uid += 1
        nm = "%s_%d" % (name, s.uid)
        return Tile(st.enter_context(s.nc.psum_tensor(nm, list(shape), dt)), nm)

    def dram(s, name, shape, dt, kind="Internal"):
        if DBG and kind == "Internal":
            kind = "ExternalOutput"
        t = s.nc.dram_tensor(name, list(shape), dt, kind=kind)
        tl = Tile(t.ap(), name)
        return tl

    def _deps(s, R, W, Wp):
        deps = {}
        for r in R:
            for k, v in r.w.items():
                if deps.get(k, 0) < v:
                    deps[k] = v
        for w in list(W) + list(Wp):
            for k, v in w.w.items():
                if deps.get(k, 0) < v:
                    deps[k] = v
            for k, v in w.r.items():
                if deps.get(k, 0) < v:
                    deps[k] = v
        return deps

    def _waits(s, eng, deps, ename):
        for k, v in deps.items():
            if k == eng.semidx and ename == "pe":
                continue
            if eng.known.get(k, 0) >= v:
                continue
            eng.h.wait_ge(s.sems[k], v)
            eng.known[k] = v

    def _mark(s, t, R, W, Wp):
        k, v = t
        for r in R:
            if r.r.get(k, 0) < v:
                r.r[k] = v
        for w in W:
            w.w = {k: v}
            w.r = {}
        for w in Wp:
            if w.w.get(k, 0) < v:
                w.w[k] = v

    def op(s, e, fn, R=(), W=(), Wp=(), sig=True, Wa=()):
        if MUTE[0]:
            return None
        R = [x.reg if isinstance(x, Tile) else x for x in R]
        Wa = [x.reg if isinstance(x, Tile) else x for x in Wa]
        W = [x.reg if isinstance(x, Tile) else x for x in W]
        Wp = [x.reg if isinstance(x, Tile) else x for x in Wp]
        eng = s.E[e]
        if eng.cnt >= s.SEM_LIMIT and not eng.pending:
            eng.semidx = s.newsem()
            eng.cnt = 0
        s._waits(eng, s._deps(R, W, list(Wp) + list(Wa)), e)
        ins = fn()
        if sig:
            eng.cnt += 1
            ins.then_inc(s.sems[eng.semidx], 1)
            s.semmax[eng.semidx] = eng.cnt
            eng.pending = False
            t = (eng.semidx, eng.cnt)
        else:
            eng.pending = True
            t = (eng.semidx, eng.cnt + 1)
        s._mark(t, R, W, Wp)
        return ins

    def dma(s, q, out, in_, R=(), W=(), Wp=(), **kw):
        if MUTE[0]:
            return None
        R = [x.reg if isinstance(x, Tile) else x for x in R]
        W = [x.reg if isinstance(x, Tile) else x for x in W]
        Wp = [x.reg if isinstance(x, Tile) else x for x in Wp]
        eng = s.E[q]
        si = eng.dsems[eng.dnext % len(eng.dsems)]
        eng.dnext += 1
        deps = s._deps(R, W, Wp)
        cur = s.semmax[si]
        if cur > 0 and deps.get(si, 0) < cur:
            deps[si] = cur
        s._waits(eng, deps, q)
        ins = eng.h.dma_start(out=out, in_=in_, **kw)
        ins.then_inc(s.sems[si], 16)
        s.semmax[si] = cur + 16
        s._mark((si, cur + 16), R, W, Wp)
        return ins

    def barrier(s, engines=("pe", "act", "dve", "pool", "sp")):
        for n in engines:
            eng = s.E[n]
            assert not eng.pending
            for k, v in enumerate(s.semmax):
                if v > 0 and eng.known.get(k, 0) < v and k != eng.semidx:
                    eng.h.wait_ge(s.sems[k], v)
                    eng.known[k] = v


class Cfg:
    def __init__(s, SEQ, NS, DFF):
        s.SEQ, s.NS, s.DFF = SEQ, NS, DFF
        s.FC = DFF // 128
        s.NT = SEQ + NS * DEC
        assert s.NT % 128 == 0 and SEQ % 128 == 0 and DFF % 128 == 0
        s.NSEQ = 1 + NS
        s.seqs = [dict(t0=0, T=SEQ, past=0, si=-1)] + [dict(t0=SEQ + DEC * i, T=DEC, past=PAST, si=i) for i in range(NS)]
        s.TOPK_P = min(256, SEQ // 4)
        s.TOPK_S = min(256, (PAST + DEC) // 4)

    def groups(s, gmax):
        n = -(-s.NT // gmax)
        per = -(-(s.NT // 128) // n) * 128
        out = []
        t = 0
        while t < s.NT:
            g = min(per, s.NT - t)
            out.append((t, g))
            t += g
        return out


def blocks(n, b=512):
    return [(i, min(b, n - i)) for i in range(0, n, b)]


def t5_bucket_np(rel):
    rel = np.asarray(rel, np.int64)
    half, max_exact = 16, 8
    side = np.where(rel > 0, half, 0)
    n = np.abs(rel)
    nf = np.maximum(n, 1).astype(np.float32)
    large = max_exact + (np.log(nf / np.float32(max_exact)) / np.float32(math.log(128 / max_exact))
                         * np.float32(half - max_exact)).astype(np.int32)
    large = np.minimum(large, half - 1)
    return side + np.where(n < max_exact, n, large)


def host_consts():
    c = {}
    c["c_ident"] = np.eye(128, dtype=np.float32)
    c["c_anti"] = np.eye(128, dtype=np.float32)[::-1].copy()
    i = np.arange(128)
    same = (i[:, None] // 64) == (i[None, :] // 64)
    c["c_cum"] = (same & (i[:, None] <= i[None, :])).astype(np.float32)
    c["c_blk"] = same.astype(np.float32)
    c["c_nmL"] = np.where(same & (i[:, None] > i[None, :]), 0.0, -1e4).astype(np.float32)
    c["c_nmT"] = np.where(same & (i[None, :] >= i[:, None]), 0.0, -1e4).astype(np.float32)
    c["c_strict"] = (same & (i[:, None] > i[None, :])).astype(np.float32)
    rel = np.arange(384) - 255
    bk = t5_bucket_np(rel)
    oh = np.zeros((32, 384), np.float32)
    oh[bk, np.arange(384)] = 1.0
    oh[15, :] -= 1.0
    c["c_oh"] = oh
    return c


CONST_SHAPES = {"c_ident": [128, 128], "c_anti": [128, 128], "c_cum": [128, 128], "c_blk": [128, 128],
                "c_nmL": [128, 128], "c_nmT": [128, 128], "c_strict": [128, 128], "c_oh": [32, 384]}


def build(cfg, phases="ABCDEF"):
    nc = bass.Bass("TRN2", target_bir_lowering=False)
    es = ExitStack()
    with es:
        k = K(nc, es)
        _program(k, cfg, phases)
    return nc


def _program(k, cfg, phases):
    nc = k.nc
    NT, NS, DFF, FC, NSEQ = cfg.NT, cfg.NS, cfg.DFF, cfg.FC, cfg.NSEQ
    I = {}

    def din(name, shape):
        I[name] = k.dram(name, shape, F32, kind="ExternalInput")
        return I[name]

    def dout(name, shape):
        I[name] = k.dram(name, shape, F32, kind="ExternalOutput")
        return I[name]

    din("x", [NT, D])
    din("ck", [NS * PAST, 512]); din("cv", [NS * PAST, 512]); din("cik", [NS * PAST, 64])
    din("sdel", [NS * 16 * 128, 128]); din("sconv", [NS * 3, 6144]); din("sffn", [NS * 2, 2 * DFF])
    din("lnin", [128, 64]); din("ln1", [128, 64]); din("ln2", [128, 64])
    din("w_in", [D, NIN]); din("w_o", [D, D]); din("w_up", [D, 2 * DFF]); din("w_down", [DFF, D])
    din("convw", [128, 48 * 5]); din("ffnw", [128, 2 * FC * 4])
    din("alog", [128, 16]); din("dtb", [128, 16]); din("dng", [128, 1]); din("relb", [32, 16])
    for n, sh in CONST_SHAPES.items():
        din(n, sh)
    dout("y", [NT, D]); dout("ko", [NT, 512]); dout("vo", [NT, 512]); dout("iko", [NT, 64])
    dout("so", [NSEQ * 16 * 128, 128]); dout("convo", [NSEQ * 3, 6144]); dout("ffno", [NSEQ * 2, 2 * DFF])
    S = {}
    S["xnT"] = k.dram("s_xnT", [D, NT], F32)
    S["qaT"] = k.dram("s_qaT", [2048, NT], BF16)
    S["kaT"] = k.dram("s_kaT", [512, NT], BF16)
    S["vbf"] = k.dram("s_vbf", [NT, 512], BF16)
    S["iqT"] = k.dram("s_iqT", [1024, NT], BF16)
    S["ikT2"] = k.dram("s_ikT2", [128, NT], BF16)
    S["iw"] = k.dram("s_iw", [NT, 16], F32)
    S["ba"] = k.dram("s_ba", [NT, 32], F32)
    S["qkvT"] = k.dram("s_qkvT", [6144, NT], F32)
    S["zT"] = k.dram("s_zT", [2048, NT], F32)
    S["aT"] = k.dram("s_aT", [D, NT], BF16)
    S["x1T"] = k.dram("s_x1T", [D, NT], F32)
    S["x1b"] = k.dram("s_x1b", [D, NT], BF16)
    S["actT"] = k.dram("s_actT", [DFF, NT], BF16)
    S["Fd"] = k.dram("s_Fd", [16, 384 + 128], F32)
    S["b_w_in"] = k.dram("s_bwin", [WIN_PACKED // 256, 1, 128, 32, 256], BF16)
    S["b_w_o"] = k.dram("s_bwo", [D // 256, 1, 128, 32, 256], BF16)
    S["b_w_up"] = k.dram("s_bwup", [2 * DFF // 256, 1, 128, 32, 256], BF16)
    S["b_w_down"] = k.dram("s_bwdn", [D // 256, len(kgroups(FC)), 128, 32, 256], BF16)

    with ExitStack() as gs:
        C = {}
        for n, sh in CONST_SHAPES.items():
            C[n] = k.sb(gs, n, sh, F32)
            k.dma("sp", C[n][:], I[n][:], W=[C[n]])
        ident_b = k.sb(gs, "identb", [128, 128], BF16)
        k.op("pool", lambda: nc.gpsimd.tensor_copy(out=ident_b[:], in_=C["c_ident"][:]), R=[C["c_ident"]], W=[ident_b])
        ones_f = k.sb(gs, "onesf", [128, 128], F32)
        k.op("pool", lambda: nc.gpsimd.memset(ones_f[:], 1.0), W=[ones_f])
        ones_b = k.sb(gs, "onesb", [128, 128], BF16)
        k.op("pool", lambda: nc.gpsimd.memset(ones_b[:], 1.0), W=[ones_b])
        G = dict(C=C, ident_b=ident_b, ones_f=ones_f, ones_b=ones_b, I=I, S=S)
        psum = [k.ps(gs, "bank%d" % i, [128, 512]) for i in range(8)]
        G["psum"] = psum
        G["pn"] = 0

        def bank():
            b = psum[G["pn"] % 8]
            G["pn"] += 1
            return b
        G["bank"] = bank
        G["alt"] = 0

        phase_W(k, cfg, G)
        k.barrier()
        if "A" in phases:
            phase_A(k, cfg, G)
            k.barrier()
        if "B" in phases:
            phase_B(k, cfg, G)
            k.barrier()
        if "C" in phases:
            phase_C(k, cfg, G)
            MUTE[0] = False
            k.barrier()
        if "D" in phases:
            phase_D(k, cfg, G)
            k.barrier()
        if "E" in phases:
            phase_E(k, cfg, G)
            k.barrier()
        if "F" in phases:
            phase_F(k, cfg, G)
        k.barrier()


def evac_engine(G):
    G["alt"] += 1
    return "act" if G["alt"] % 2 else "dve"


class WTiles:
    def __init__(s, k, st, nslot=3):
        s.k = k
        s.slots = [k.sb(st, "wt", [128, 32, 256], BF16) for _ in range(nslot)]
        s.tags = [None] * nslot
        s.n = 0

    def get(s, scr, tile, kg=0, kcn=32):
        tag = (scr.reg.n, tile, kg)
        for i, t in enumerate(s.tags):
            if t == tag:
                return s.slots[i]
        i = s.n % len(s.slots)
        s.n += 1
        s.tags[i] = tag
        wb = s.slots[i]
        s.k.dma("sp", wb[:, 0:kcn, :], scr[tile, kg, :, 0:kcn, :], R=[scr], W=[wb])
        return wb


def kgroups(kctot):
    return [(i, min(32, kctot - i)) for i in range(0, kctot, 32)]


def w_units(k, cfg, G, names, st, kstep=4, gcols=512):
    nc = k.nc
    I, S = G["I"], G["S"]
    allspecs = {"w_in": (I["w_in"], WIN_PIECES, WIN_PACKED, KC), "w_o": (I["w_o"], [(0, 0, D)], D, KC),
                "w_up": (I["w_up"], [(0, 0, 2 * cfg.DFF)], 2 * cfg.DFF, KC), "w_down": (I["w_down"], [(0, 0, D)], D, cfg.FC)}
    nt = gcols // 256
    stg = [k.sb(st, "wstg", [128, kstep, gcols], F32) for _ in range(2)]
    sbf = [k.sb(st, "wsbf", [128, nt, kstep, 256], BF16) for _ in range(2)]
    units = []
    for name in names:
        src, pieces, ncol, kctot = allspecs[name]
        dst = S["b_" + name]
        for g0 in range(0, ncol, gcols):
            gw = min(gcols, ncol - g0)
            for kgi, (kg0, kgn) in enumerate(kgroups(kctot)):
                for k8 in range(0, kgn, kstep):
                    units.append((src, pieces, dst, g0, gw, kgi, kg0, k8, min(kstep, kgn - k8)))

    def load(n):
        src, pieces, dst, g0, gw, kgi, kg0, k8, kn = units[n]
        r0 = (kg0 + k8) * 128
        st_ = stg[n % 2]
        first = True
        for (d0, s0, pn) in pieces:
            lo, hi = max(d0, g0), min(d0 + pn, g0 + gw)
            if lo < hi:
                srcap = src[r0:r0 + kn * 128, s0 + lo - d0:s0 + hi - d0].rearrange("(kc p) n -> p kc n", p=128)
                if first:
                    k.dma("sp", st_[:, 0:kn, lo - g0:hi - g0], srcap, W=[st_])
                else:
                    k.dma("sp", st_[:, 0:kn, lo - g0:hi - g0], srcap, Wp=[st_])
                first = False

    def finish(n):
        src, pieces, dst, g0, gw, kgi, kg0, k8, kn = units[n]
        st_ = stg[n % 2]
        sb_ = sbf[n % 2]
        nt4 = gw // 256
        iv = st_[:, 0:kn, 0:gw].rearrange("p k (t c) -> p t k c", c=256)
        ov = sb_[:, 0:nt4, 0:kn, :]
        e = G.get("wcast", ("act", "dve", "pool"))
        e = e[n % len(e)]
        if e == "act":
            k.op("act", lambda: nc.scalar.copy(out=ov, in_=iv), R=[st_], W=[sb_])
        elif e == "dve":
            k.op("dve", lambda: nc.vector.tensor_copy(out=ov, in_=iv), R=[st_], W=[sb_])
        else:
            k.op("pool", lambda: nc.gpsimd.tensor_copy(out=ov, in_=iv), R=[st_], W=[sb_])
        t0 = g0 // 256
        k.dma(G.get("wstoreq", "pool"), dst[t0:t0 + nt4, kgi, :, k8:k8 + kn, :].rearrange("t p k c -> p t k c"), ov, R=[sb_], Wp=[dst])

    for n in range(len(units)):
        load(n)
        if n >= 1:
            finish(n - 1)
        yield n
    finish(len(units) - 1)
    yield len(units)


def phase_W(k, cfg, G):
    with ExitStack() as st:
        G["wcast"] = ("act", "dve")
        for _ in w_units(k, cfg, G, ["w_in"], st, kstep=8, gcols=1024):
            pass


def phase_A(k, cfg, G):
    nc = k.nc
    I, S, C = G["I"], G["S"], G["C"]
    NT = cfg.NT
    bank = G["bank"]
    for (g0, gn) in cfg.groups(1152):
        with ExitStack() as st:
            xnT = k.sb(st, "xnT", [128, KC, gn], BF16)
            gb = k.sb(st, "lnin", [128, 64], F32)
            k.dma("sp", gb[:], I["lnin"][:], W=[gb])
            with ExitStack() as s1:
                xs = [k.sb(s1, "xs", [128, D], F32) for _ in range(2)]
                xf = [k.sb(s1, "xf", [128, KC, 128], F32) for _ in range(2)]
                stt = [k.sb(s1, "stt", [128, 8, 6], F32) for _ in range(2)]
                mv = [k.sb(s1, "mv", [128, 4], F32) for _ in range(2)]
                for ti in range(gn // 128):
                    t0 = g0 + ti * 128
                    x_, f_, st_, mv_ = xs[ti % 2], xf[ti % 2], stt[ti % 2], mv[ti % 2]
                    k.dma("sp", x_[:], I["x"][t0:t0 + 128, :], W=[x_])
                    for j in range(8):
                        k.op("dve", lambda j=j: nc.vector.bn_stats(out=st_[:, j, :], in_=x_[:, j * 512:(j + 1) * 512]),
                             R=[x_], Wp=[st_] if j else (), W=() if j else [st_])
                    k.op("dve", lambda: nc.vector.bn_aggr(out=mv_[:, 0:2], in_=st_[:].rearrange("p a b -> p (a b)")), R=[st_], W=[mv_])
                    k.op("act", lambda: nc.scalar.activation(out=mv_[:, 2:3], in_=mv_[:, 1:2], func=AF.Sqrt, bias=EPS, scale=1.0), R=[mv_], Wp=[mv_])
                    k.op("dve", lambda: nc.vector.reciprocal(out=mv_[:, 3:4], in_=mv_[:, 2:3]), R=[mv_], Wp=[mv_])
                    k.op("dve", lambda: nc.vector.tensor_scalar(out=x_[:], in0=x_[:], scalar1=mv_[:, 0:1], scalar2=mv_[:, 3:4],
                                                                op0=ALU.subtract, op1=ALU.mult), R=[mv_, x_], W=[x_])
                    for q in range(8):
                        b = bank()
                        for j in range(4):
                            kc = q * 4 + j
                            k.op("pe", lambda kc=kc, j=j: nc.tensor.transpose(b[:, j * 128:(j + 1) * 128], x_[:, kc * 128:(kc + 1) * 128], C["c_ident"][:]),
                                 R=[x_, C["c_ident"]], W=[b], sig=(j == 3))
                        for j in range(4):
                            kc = q * 4 + j
                            if (kc % 2) == 0:
                                k.op("act", lambda kc=kc, j=j: nc.scalar.activation(out=f_[:, kc, :], in_=b[:, j * 128:(j + 1) * 128], func=AF.Identity,
                                                                                   scale=gb[:, kc:kc + 1], bias=gb[:, 32 + kc:33 + kc]),
                                     R=[b, gb], Wp=[f_])
                            else:
                                k.op("dve", lambda kc=kc, j=j: nc.vector.tensor_scalar(out=f_[:, kc, :], in0=b[:, j * 128:(j + 1) * 128],
                                                                                      scalar1=gb[:, kc:kc + 1], scalar2=gb[:, 32 + kc:33 + kc],
                                                                                      op0=ALU.mult, op1=ALU.add),
                                     R=[b, gb], Wp=[f_])
                    k.op("pool", lambda: nc.gpsimd.tensor_copy(out=xnT[:, :, ti * 128:(ti + 1) * 128], in_=f_[:]), R=[f_], Wp=[xnT])
                    k.dma("pool", S["xnT"][:].rearrange("(kc p) t -> p kc t", p=128)[:, :, t0:t0 + 128], f_[:], R=[f_], Wp=[S["xnT"]])
            k.barrier()
            with ExitStack() as s2:
                ws = WTiles(k, s2, nslot=3)
                osf = [k.sb(s2, "osf", [128, gn], F32) for _ in range(2)]
                osb = [k.sb(s2, "osb", [128, gn], BF16) for _ in range(2)]
                otk = [k.sb(s2, "otk", [128, gn // 128, 128], F32) for _ in range(2)]
                otb = [k.sb(s2, "otb", [128, gn // 128, 128], BF16) for _ in range(2)]
                cnt = {"f": 0, "b": 0, "t": 0}

                def fm_job(pc, m, dst, drow, mode):
                    wb = ws.get(S["b_w_in"], pc // 256)
                    sub = pc % 256
                    if mode == "f32" or mode == "silu":
                        o = osf[cnt["f"] % 2]; cnt["f"] += 1
                    else:
                        o = osb[cnt["b"] % 2]; cnt["b"] += 1
                    for (b0, bn) in blocks(gn):
                        b = bank()
                        for kc in range(KC):
                            k.op("pe", lambda kc=kc: nc.tensor.matmul(b[0:m, 0:bn], lhsT=wb[:, kc, sub:sub + m], rhs=xnT[:, kc, b0:b0 + bn],
                                                                       start=(kc == 0), stop=(kc == KC - 1)),
                                 R=[wb, xnT], W=[b], sig=(kc == KC - 1))
                        e = evac_engine(G)
                        if mode == "silu":
                            k.op("act", lambda: nc.scalar.activation(out=o[0:m, b0:b0 + bn], in_=b[0:m, 0:bn], func=AF.Silu), R=[b], Wp=[o])
                        elif mode == "qs":
                            k.op("act", lambda: nc.scalar.mul(o[0:m, b0:b0 + bn], b[0:m, 0:bn], HD ** -0.5), R=[b], Wp=[o])
                        elif e == "act":
                            k.op("act", lambda: nc.scalar.copy(out=o[0:m, b0:b0 + bn], in_=b[0:m, 0:bn]), R=[b], Wp=[o])
                        else:
                            k.op("dve", lambda: nc.vector.tensor_copy(out=o[0:m, b0:b0 + bn], in_=b[0:m, 0:bn]), R=[b], Wp=[o])
                    k.dma("pool", dst[drow:drow + m, g0:g0 + gn], o[0:m, :], R=[o], Wp=[dst])

                def tm_job(pc, ncols, outs):
                    wb = ws.get(S["b_w_in"], pc // 256)
                    sub = pc % 256
                    o = otk[cnt["t"] % 2]
                    ob = otb[cnt["t"] % 2]
                    cnt["t"] += 1
                    for ti in range(gn // 128):
                        b = bank()
                        for kc in range(KC):
                            k.op("pe", lambda kc=kc: nc.tensor.matmul(b[:, 0:ncols], lhsT=xnT[:, kc, ti * 128:(ti + 1) * 128], rhs=wb[:, kc, sub:sub + ncols],
                                                                       start=(kc == 0), stop=(kc == KC - 1)),
                                 R=[wb, xnT], W=[b], sig=(kc == KC - 1))
                        k.op("dve", lambda: nc.vector.tensor_copy(out=o[:, ti, 0:ncols], in_=b[:, 0:ncols]), R=[b], Wp=[o])
                    for (dst, dc0, sc0, n, dt) in outs:
                        dview = dst[g0:g0 + gn, dc0:dc0 + n].rearrange("(ti p) n -> p ti n", p=128)
                        if dt == "bf16":
                            k.op("pool", lambda: nc.gpsimd.tensor_copy(out=ob[:, :, sc0:sc0 + n], in_=o[:, :, sc0:sc0 + n]), R=[o], W=[ob])
                            k.dma("pool", dview, ob[:, :, sc0:sc0 + n], R=[ob], Wp=[dst])
                        else:
                            k.dma("pool", dview, o[:, :, sc0:sc0 + n], R=[o], Wp=[dst])

                for c in range(16):
                    fm_job(P_QA + c * 128, 128, S["qaT"], c * 128, "qs")
                for c in range(4):
                    fm_job(P_KA + c * 128, 128, S["kaT"], c * 128, "bf16")
                for c in range(4):
                    tm_job(P_KA + c * 128, 128, [(I["ko"], c * 128, 0, 128, "f32")])
                for c in range(4):
                    tm_job(P_VA + c * 128, 128, [(I["vo"], c * 128, 0, 128, "f32"), (S["vbf"], c * 128, 0, 128, "bf16")])
                for c in range(8):
                    fm_job(P_IQ + c * 128, 128, S["iqT"], c * 128, "bf16")
                fm_job(P_IK2, 128, S["ikT2"], 0, "bf16")
                for c in range(48):
                    fm_job(P_QKV + c * 128, 128, S["qkvT"], c * 128, "f32")
                for c in range(16):
                    fm_job(P_Z + c * 128, 128, S["zT"], c * 128, "silu")
                tm_job(P_SM1, 80, [(I["iko"], 0, 0, 64, "f32"), (S["iw"], 0, 64, 16, "f32")])
                tm_job(P_SM2, 32, [(S["ba"], 0, 0, 32, "f32")])
            k.barrier()


def phase_B(k, cfg, G):
    nc = k.nc
    I, S, C = G["I"], G["S"], G["C"]
    bank = G["bank"]
    ident, ident_b, ones_b = C["c_ident"], G["ident_b"], G["ones_b"]
    NS = cfg.NS
    SKMAX = max(cfg.SEQ, PAST + 128)
    KTMAX = SKMAX // 128
    with ExitStack() as pst:
        sb = lambda n, sh, dt=F32: k.sb(pst, n, sh, dt)
        biasT = sb("biasT", [128, 2, 16, 128], BF16)
        with ExitStack() as s0:
            relb = k.sb(s0, "relb", [32, 16], F32)
            Fs = k.sb(s0, "Fs", [16, 512], F32)
            XT = k.sb(s0, "XT", [128, 2, 16, 128], F32)
            k.dma("sp", relb[:], I["relb"][:], W=[relb])
            k.op("pool", lambda: nc.gpsimd.memset(Fs[:], 0.0), W=[Fs])
            b = bank()
            k.op("pe", lambda: nc.tensor.matmul(b[0:16, 0:384], lhsT=relb[:, :], rhs=C["c_oh"][:, :], start=True, stop=True), R=[relb, C["c_oh"]], W=[b])
            k.op("dve", lambda: nc.vector.tensor_copy(out=Fs[:, 0:384], in_=b[0:16, 0:384]), R=[b], Wp=[Fs])
            k.dma("sp", S["Fd"][:], Fs[:], R=[Fs], W=[S["Fd"]])
            fd_t = S["Fd"][:].tensor
            for w, off in ((0, 128), (1, 0)):
                src = bass.AP(tensor=fd_t, offset=off, ap=[[1, 128], [512, 16], [1, 128]])
                k.dma("sp", XT[:, w, :, :], src, R=[S["Fd"]], Wp=[XT])
            for w in range(2):
                for h4 in range(0, 16, 4):
                    b = bank()
                    for j in range(4):
                        k.op("pe", lambda: nc.tensor.matmul(b[:, j * 128:(j + 1) * 128], lhsT=XT[:, w, h4 + j, :], rhs=C["c_anti"][:], start=True, stop=True),
                             R=[XT, C["c_anti"]], W=[b], sig=(j == 3))
                    k.op("dve", lambda: nc.vector.tensor_copy(out=biasT[:, w, h4:h4 + 4, :], in_=b[:, :].rearrange("p (a b) -> p a b", b=128)), R=[b], Wp=[biasT])
        k.barrier()
        kT = sb("kT", [128, 4, SKMAX], BF16)
        vv = sb("vv", [128, KTMAX, 4, 128], BF16)
        ik2 = sb("ik2", [128, SKMAX], BF16)
        cst = sb("cst", [128, 8, 512], F32)
        cikst = sb("cikst", [128, 8, 128], F32)
        qT = [sb("qT", [128, 16, 128], BF16) for _ in range(2)]
        iq = [sb("iq", [128, 8, 128], BF16) for _ in range(2)]
        iw = [sb("iw", [128, 16], F32) for _ in range(2)]
        index = sb("index", [128, SKMAX]); work = sb("work", [128, SKMAX]); mask01 = sb("mask01", [128, SKMAX])
        rr_ = [sb("relu", [128, 512]) for _ in range(2)]
        m8 = sb("m8", [128, 8]); thr = sb("thr", [128, 1])
        maskT = sb("maskT", [128, KTMAX, 128], BF16)
        pt = [sb("pt", [128, 512], BF16) for _ in range(3)]
        rcp = sb("rcp", [128, 512])
        oa = [sb("oa", [128, 16, 128], BF16) for _ in range(2)]
        G["wcast"] = ("act",)
        G["wstoreq"] = "act"
        wgen = w_units(k, cfg, G, ["w_o", "w_up", "w_down"], pst, kstep=4, gcols=512)
        n_units = 0
        for nm_, kct_ in (("w_o", KC), ("w_up", KC), ("w_down", cfg.FC)):
            ncol_ = {"w_o": D, "w_up": 2 * cfg.DFF, "w_down": D}[nm_]
            n_units += (-(-ncol_ // 512)) * sum(-(-kn_ // 4) for (_, kn_) in kgroups(kct_))
        n_iter = sum(4 * (-(-sq_["T"] // 128)) for sq_ in cfg.seqs)
        per_iter = -(-n_units // n_iter)

        def wstep(cnt):
            for _ in range(cnt):
                try:
                    next(wgen)
                except StopIteration:
                    return
        qn = 0
        for qi, sq in enumerate(cfg.seqs):
            t0, T, si, past = sq["t0"], sq["T"], sq["si"], sq["past"]
            SK = past + T
            if si < 0:
                k.dma("sp", kT[:, :, 0:T], S["kaT"][:, t0:t0 + T].rearrange("(g p) t -> p g t", p=128), R=[S["kaT"]], W=[kT])
                k.dma("sp", vv[:, 0:T // 128, :, :], S["vbf"][t0:t0 + T, :].rearrange("(kt p) (g d) -> p kt g d", p=128, d=128), R=[S["vbf"]], W=[vv])
                k.dma("sp", ik2[:, 0:T], S["ikT2"][:, t0:t0 + T], R=[S["ikT2"]], W=[ik2])
            else:
                k.dma("sp", cst[:], I["ck"][si * PAST:(si + 1) * PAST, :].rearrange("(kt p) n -> p kt n", p=128), W=[cst])
                for kt in range(8):
                    b = bank()
                    for g in range(4):
                        k.op("pe", lambda: nc.tensor.transpose(b[:, g * 128:(g + 1) * 128], cst[:, kt, g * 128:(g + 1) * 128], ident[:]), R=[cst, ident], W=[b], sig=(g == 3))
                    k.op("act", lambda: nc.scalar.copy(out=kT[:, :, kt * 128:(kt + 1) * 128], in_=b[:, :].rearrange("p (a b) -> p a b", b=128)), R=[b], Wp=[kT])
                k.dma("sp", cst[:], I["cv"][si * PAST:(si + 1) * PAST, :].rearrange("(kt p) n -> p kt n", p=128), W=[cst])
                k.op("pool", lambda: nc.gpsimd.tensor_copy(out=vv[:, 0:8, :, :].rearrange("p a g d -> p a (g d)"), in_=cst[:]), R=[cst], Wp=[vv])
                ciksrc = I["cik"][si * PAST:(si + 1) * PAST, :].rearrange("(kt p) n -> p kt n", p=128)
                k.dma("sp", cikst[:, :, 0:64], ciksrc, W=[cikst])
                k.dma("sp", cikst[:, :, 64:128], ciksrc, Wp=[cikst])
                for k4 in range(0, 8, 4):
                    b = bank()
                    for j in range(4):
                        k.op("pe", lambda: nc.tensor.transpose(b[:, j * 128:(j + 1) * 128], cikst[:, k4 + j, :], ident[:]), R=[cikst, ident], W=[b], sig=(j == 3))
                    k.op("dve", lambda: nc.vector.tensor_copy(out=ik2[:, k4 * 128:(k4 + 4) * 128], in_=b[:, :]), R=[b], Wp=[ik2])
                k.dma("sp", kT[:, :, PAST:PAST + T], S["kaT"][:, t0:t0 + T].rearrange("(g p) t -> p g t", p=128), R=[S["kaT"]], Wp=[kT])
                k.dma("sp", vv[0:T, 8, :, :], S["vbf"][t0:t0 + T, :].rearrange("p (g d) -> p g d", d=128), R=[S["vbf"]], Wp=[vv])
                k.dma("sp", ik2[:, PAST:PAST + T], S["ikT2"][:, t0:t0 + T], R=[S["ikT2"]], Wp=[ik2])
            topk = cfg.TOPK_P if si < 0 else cfg.TOPK_S
            for qt in range(-(-T // 128)):
                nq = min(128, T - qt * 128)
                ta = t0 + qt * 128
                SKq = past + qt * 128 + nq if si < 0 else SK
                KTq = -(-SKq // 128)
                q_, iq_, iw_, oa_ = qT[qn % 2], iq[qn % 2], iw[qn % 2], oa[qn % 2]
                qn += 1
                k.dma("sp", q_[:, :, 0:nq], S["qaT"][:, ta:ta + nq].rearrange("(h p) t -> p h t", p=128), R=[S["qaT"]], W=[q_])
                k.dma("sp", iq_[:, :, 0:nq], S["iqT"][:, ta:ta + nq].rearrange("(h p) t -> p h t", p=128), R=[S["iqT"]], W=[iq_])
                k.dma("sp", iw_[0:nq, :], S["iw"][ta:ta + nq, :], R=[S["iw"]], W=[iw_])
                k.op("pool", lambda: nc.gpsimd.tensor_scalar(out=iw_[0:nq, :], in0=iw_[0:nq, :], scalar1=IDX_SCALE, scalar2=None, op0=ALU.mult), R=[iw_], W=[iw_])
                rn = 0
                for (c0, cn) in blocks(SKq):
                    for hp in range(8):
                        for half in range(2):
                            h = hp * 2 + half
                            pr = slice(half * 64, half * 64 + 64)
                            b = bank()
                            k.op("pe", lambda: nc.tensor.matmul(b[0:nq, 0:cn], lhsT=iq_[pr, hp, 0:nq], rhs=ik2[pr, c0:c0 + cn], start=True, stop=True), R=[iq_, ik2], W=[b])
                            r_ = rr_[rn % 2]
                            rn += 1
                            k.op("act", lambda: nc.scalar.activation(out=r_[0:nq, 0:cn], in_=b[0:nq, 0:cn], func=AF.Relu), R=[b], W=[r_])
                            if h == 0:
                                k.op("dve", lambda: nc.vector.tensor_scalar(out=index[0:nq, c0:c0 + cn], in0=r_[0:nq, 0:cn], scalar1=iw_[0:nq, 0:1], scalar2=None, op0=ALU.mult),
                                     R=[r_, iw_], Wp=[index])
                            else:
                                k.op("dve", lambda: nc.vector.scalar_tensor_tensor(out=index[0:nq, c0:c0 + cn], in0=r_[0:nq, 0:cn], scalar=iw_[0:nq, h:h + 1],
                                                                                   in1=index[0:nq, c0:c0 + cn], op0=ALU.mult, op1=ALU.add), R=[r_, iw_, index], Wp=[index])
                if si < 0:
                    k.op("dve", lambda: nc.vector.memset(index[0:64, SKq - 64:SKq], NEG), R=[index], Wp=[index])
                if SKq > topk:
                    nr = topk // 8
                    for rd in range(nr):
                        srcw = index if rd == 0 else work
                        k.op("dve", lambda: nc.vector.max(out=m8[0:nq, :], in_=srcw[0:nq, 0:SKq]), R=[srcw], W=[m8])
                        if rd < nr - 1:
                            k.op("dve", lambda: nc.vector.match_replace(out=work[0:nq, 0:SKq], in_to_replace=m8[0:nq, :], in_values=srcw[0:nq, 0:SKq], imm_value=NEG),
                                 R=[srcw, m8], W=[work])
                    k.op("dve", lambda: nc.vector.tensor_scalar(out=thr[0:nq, :], in0=m8[0:nq, 7:8], scalar1=-1.0e29, scalar2=None, op0=ALU.max), R=[m8], W=[thr])
                    k.op("dve", lambda: nc.vector.tensor_scalar(out=mask01[0:nq, 0:SKq], in0=index[0:nq, 0:SKq], scalar1=thr[0:nq, 0:1], scalar2=None, op0=ALU.is_ge),
                         R=[index, thr], W=[mask01])
                else:
                    k.op("dve", lambda: nc.vector.tensor_scalar(out=mask01[0:nq, 0:SKq], in0=index[0:nq, 0:SKq], scalar1=-1.0e29, scalar2=None, op0=ALU.is_ge),
                         R=[index], W=[mask01])
                for k4 in range(0, KTq, 4):
                    b = bank()
                    n4 = min(4, KTq - k4)
                    for j in range(n4):
                        kt = k4 + j
                        ks = min(128, SKq - kt * 128)
                        k.op("pe", lambda: nc.tensor.transpose(b[0:ks, j * 128:j * 128 + nq], mask01[0:nq, kt * 128:kt * 128 + ks], ident[0:nq, 0:nq]), R=[mask01, ident], W=[b], sig=(j == n4 - 1))
                    for j in range(n4):
                        kt = k4 + j
                        ks = min(128, SKq - kt * 128)
                        k.op("act", lambda: nc.scalar.copy(out=maskT[0:ks, kt, 0:nq], in_=b[0:ks, j * 128:j * 128 + nq]), R=[b], Wp=[maskT])
                pn = 0
                for g in range(4):
                    bO, bR = (G["psum"][4], G["psum"][5]) if g % 2 == 0 else (G["psum"][6], G["psum"][7])
                    for kt in range(KTq):
                        ks = min(128, SKq - kt * 128)
                        near = kt >= KTq - 2
                        w = 0 if kt == KTq - 1 else 1
                        wstep(1)
                        bl = G["psum"][pn % 4]
                        k.op("pe", lambda: nc.tensor.matmul(bl[0:ks, 0:4 * nq], lhsT=kT[:, g, kt * 128:kt * 128 + ks], rhs=q_[:, 4 * g:4 * g + 4, 0:nq], start=True, stop=not near),
                             R=[kT, q_], W=[bl], sig=not near)
                        if near:
                            k.op("pe", lambda: nc.tensor.matmul(bl[0:ks, 0:4 * nq], lhsT=ident_b[:, 0:ks], rhs=biasT[:, w, 4 * g:4 * g + 4, 0:nq], start=False, stop=True),
                                 R=[ident_b, biasT], W=[bl])
                        p_ = pt[pn % 3]
                        pn += 1
                        k.op("act", lambda: nc.scalar.activation(out=p_[0:ks, 0:4 * nq], in_=bl[0:ks, 0:4 * nq], func=AF.Exp), R=[bl], W=[p_])
                        pv = p_[0:ks, 0:4 * nq].rearrange("p (a b) -> p a b", b=nq)
                        k.op("pool", lambda: nc.gpsimd.tensor_tensor(out=pv, in0=pv, in1=maskT[0:ks, kt, 0:nq].unsqueeze(1).to_broadcast([ks, 4, nq]), op=ALU.mult),
                             R=[p_, maskT], W=[p_])
                        k.op("pe", lambda: nc.tensor.matmul(bO[:, 0:4 * nq], lhsT=vv[0:ks, kt, g, :], rhs=p_[0:ks, 0:4 * nq], start=(kt == 0), stop=(kt == KTq - 1)),
                             R=[vv, p_], W=[bO], sig=False)
                        k.op("pe", lambda: nc.tensor.matmul(bR[:, 0:4 * nq], lhsT=ones_b[0:ks, :], rhs=p_[0:ks, 0:4 * nq], start=(kt == 0), stop=(kt == KTq - 1)),
                             R=[ones_b, p_], W=[bR], sig=True)
                    k.op("dve", lambda: nc.vector.reciprocal(out=rcp[:, 0:4 * nq], in_=bR[:, 0:4 * nq]), R=[bR], W=[rcp])
                    k.op("dve", lambda: nc.vector.tensor_tensor(out=oa_[:, 4 * g:4 * g + 4, 0:nq], in0=bO[:, 0:4 * nq].rearrange("p (a b) -> p a b", b=nq),
                                                                in1=rcp[:, 0:4 * nq].rearrange("p (a b) -> p a b", b=nq), op=ALU.mult), R=[bO, rcp], Wp=[oa_])
                k.dma("pool", S["aT"][0:2048, ta:ta + nq].rearrange("(h p) t -> p h t", p=128), oa_[:, :, 0:nq], R=[oa_], Wp=[S["aT"]])
        wstep(10 ** 9)


def phase_C(k, cfg, G):
    nc = k.nc
    I, S, C = G["I"], G["S"], G["C"]
    bank = G["bank"]
    ones_f = G["ones_f"]
    ident = C["c_ident"]
    NS, NSEQ = cfg.NS, cfg.NSEQ
    HG = 4
    with ExitStack() as pst:
        sb = lambda n, sh, dt=F32: k.sb(pst, n, sh, dt)
        cw = sb("convw", [128, 48, 5])
        k.dma("sp", cw[:].rearrange("p a b -> p (a b)"), I["convw"][:], W=[cw])
        nea = sb("nea", [128, 16]); dtb = sb("dtb", [128, 16]); dng = sb("dng", [128, 1])
        k.dma("sp", nea[:], I["alog"][:], W=[nea])
        k.dma("sp", dtb[:], I["dtb"][:], W=[dtb])
        k.dma("sp", dng[:], I["dng"][:], W=[dng])
        k.op("act", lambda: nc.scalar.activation(out=nea[:], in_=nea[:], func=AF.Exp), R=[nea], W=[nea])
        k.op("pool", lambda: nc.gpsimd.tensor_scalar(out=nea[:], in0=nea[:], scalar1=-1.0, scalar2=None, op0=ALU.mult), R=[nea], W=[nea])
        cH = sb("cH", [128, 48, max(NS, 1) * 3])
        lst = sb("lst", [128, 48, NSEQ * 3])
        s0 = ExitStack()
        orow = k.sb(s0, "orow", [NSEQ * 3, 6144], F32)
        if NS > 0:
            srow = k.sb(s0, "srowc", [NS * 3, 6144], F32)
            k.dma("sp", srow[:], I["sconv"][:], W=[srow])
            for c4 in range(0, 48, 4):
                b = bank()
                for j in range(4):
                    k.op("pe", lambda j=j: nc.tensor.transpose(b[:, j * 128:j * 128 + NS * 3], srow[:, (c4 + j) * 128:(c4 + j + 1) * 128],
                                                               ident[0:NS * 3, 0:NS * 3]), R=[srow, ident], W=[b], sig=(j == 3))
                k.op("dve", lambda: nc.vector.tensor_copy(out=cH[:, c4:c4 + 4, :], in_=b[:, :].rearrange("p (a b) -> p a b", b=128)[:, :, 0:NS * 3]),
                     R=[b], Wp=[cH])
        qv = S["qkvT"][:].rearrange("(c p) t -> p c t", p=128)
        for qi, sq in enumerate(cfg.seqs):
            te = sq["t0"] + sq["T"]
            k.dma("sp", lst[:, :, qi * 3:(qi + 1) * 3], qv[:, :, te - 3:te], R=[S["qkvT"]], Wp=[lst])
        for c4 in range(0, 48, 4):
            b = bank()
            for j in range(4):
                k.op("pe", lambda j=j: nc.tensor.transpose(b[0:NSEQ * 3, j * 128:(j + 1) * 128], lst[:, c4 + j, :], ident[:]),
                     R=[lst, ident], W=[b], sig=(j == 3))
            k.op("dve", lambda: nc.vector.tensor_copy(out=orow[:, c4 * 128:(c4 + 4) * 128], in_=b[0:NSEQ * 3, :]), R=[b], Wp=[orow])
        k.dma("pool", I["convo"][:], orow[:], R=[orow], W=[I["convo"]])

        k.barrier()
        s0.close()
        stop_at(1)
        S_g = [sb("S", [128, HG, 128]) for _ in range(16 // HG)]
        ba = sb("ba", [128, 32]); beta = sb("beta", [128, 16]); xx = sb("xx", [128, 16]); t16 = sb("t16", [128, 16])
        g_ = sb("g", [128, 16]); Gs = sb("Gs", [128, 16]); bg = sb("bg", [128, 16]); edec = sb("edec", [128, 16])
        Dm = sb("Dm", [128, 16, 128]); eGbc = sb("eGbc", [128, 16, 128]); dmS = sb("dmS", [128, 16, 128]); dmT = sb("dmT", [128, 16, 128])

        def make_set():
            Bf = {}
            Bf['raw'] = sb("raw", [128, HG, 3, 131]); Bf['cv'] = sb("cv", [128, HG, 3, 128]); Bf['sqb'] = sb("sqb", [128, HG, 2, 128])
            Bf['ctmp'] = sb("ctmp", [128, HG, 3, 128])
            Bf['cvR'] = [[Reg("cvR") for _ in range(3)] for _ in range(HG)]
            Bf['ctR'] = [[Reg("ctR") for _ in range(3)] for _ in range(HG)]
            Bf['rst'] = sb("rst", [128, HG, 2, 128]); Bf['qd'] = sb("qd", [128, HG, 128])
            Bf['kbg'] = sb("kbg", [128, HG, 128]); Bf['kdec'] = sb("kdec", [128, HG, 128]); Bf['vb'] = sb("vb", [128, HG, 128])
            Bf['L'] = [sb("L", [128, HG, 128]) for _ in range(2)]; Bf['U'] = [sb("U", [128, HG, 128]) for _ in range(2)]
            Bf['P'] = sb("P", [128, HG, 128]); Bf['qkm'] = sb("qkm", [128, HG, 128]); Bf['wT'] = sb("wT", [128, HG, 128]); Bf['u'] = sb("u", [128, HG, 128])
            Bf['vnew'] = sb("vnew", [128, HG, 128]); Bf['oT'] = sb("oT", [128, HG, 128]); Bf['zs'] = sb("zs", [128, HG, 128]); Bf['ob'] = sb("ob", [128, HG, 128], BF16)
            k.op("pool", lambda: nc.gpsimd.memset(Bf['vnew'][:], 0.0), W=[Bf['vnew']])
            return Bf
        BS = [make_set() for _ in range(2)]

        for qi, sq in enumerate(cfg.seqs):
            t0, T, si = sq["t0"], sq["T"], sq["si"]
            for gi_ in range(16 // HG):
                Sg_ = S_g[gi_]
                if si < 0:
                    k.op("pool", lambda: nc.gpsimd.memset(Sg_[:], 0.0), W=[Sg_])
                else:
                    r0_ = si * 2048 + gi_ * HG * 128
                    k.dma("sp", Sg_[:], I["sdel"][r0_:r0_ + HG * 128, :].rearrange("(h d) e -> d h e", d=128), W=[Sg_])
            for tt in range(-(-T // 128)):
                nt = min(128, T - tt * 128)
                ta = t0 + tt * 128
                nch = nt // 64
                k.dma("sp", ba[0:nt, :], S["ba"][ta:ta + nt, :], R=[S["ba"]], W=[ba])
                k.op("act", lambda: nc.scalar.activation(out=beta[0:nt, :], in_=ba[0:nt, 0:16], func=AF.Sigmoid), R=[ba], W=[beta])
                k.op("dve", lambda: nc.vector.tensor_tensor(out=xx[0:nt, :], in0=ba[0:nt, 16:32], in1=dtb[0:nt, :], op=ALU.add), R=[ba, dtb], W=[xx])
                k.op("act", lambda: nc.scalar.activation(out=t16[0:nt, :], in_=xx[0:nt, :], func=AF.Abs), R=[xx], W=[t16])
                k.op("act", lambda: nc.scalar.activation(out=t16[0:nt, :], in_=t16[0:nt, :], func=AF.Exp, scale=-1.0), R=[t16], W=[t16])
                k.op("act", lambda: nc.scalar.activation(out=t16[0:nt, :], in_=t16[0:nt, :], func=AF.Ln, bias=1.0, scale=1.0), R=[t16], W=[t16])
                k.op("dve", lambda: nc.vector.scalar_tensor_tensor(out=g_[0:nt, :], in0=xx[0:nt, :], scalar=0.0, in1=t16[0:nt, :], op0=ALU.max, op1=ALU.add),
                     R=[xx, t16], W=[g_])
                k.op("dve", lambda: nc.vector.tensor_tensor(out=g_[0:nt, :], in0=g_[0:nt, :], in1=nea[0:nt, :], op=ALU.mult), R=[g_, nea], W=[g_])
                stop_at(1.2)
                bG = bank()
                k.op("pe", lambda: nc.tensor.matmul(bG[0:nt, 0:16], lhsT=C["c_cum"][0:nt, 0:nt], rhs=g_[0:nt, :], start=True, stop=True),
                     R=[C["c_cum"], g_], W=[bG], sig=False)
                k.op("pe", lambda: nc.tensor.matmul(bG[0:nt, 16:32], lhsT=C["c_blk"][0:nt, 0:nt], rhs=g_[0:nt, :], start=True, stop=True),
                     R=[C["c_blk"], g_], W=[bG])
                k.op("dve", lambda: nc.vector.tensor_copy(out=Gs[0:nt, :], in_=bG[0:nt, 0:16]), R=[bG], W=[Gs])
                k.op("act", lambda: nc.scalar.activation(out=bg[0:nt, :], in_=Gs[0:nt, :], func=AF.Exp), R=[Gs], W=[bg])
                k.op("dve", lambda: nc.vector.tensor_tensor(out=bg[0:nt, :], in0=bg[0:nt, :], in1=beta[0:nt, :], op=ALU.mult), R=[bg, beta], W=[bg])
                k.op("dve", lambda: nc.vector.tensor_tensor(out=edec[0:nt, :], in0=bG[0:nt, 16:32], in1=Gs[0:nt, :], op=ALU.subtract), R=[bG, Gs], W=[edec])
                k.op("act", lambda: nc.scalar.activation(out=edec[0:nt, :], in_=edec[0:nt, :], func=AF.Exp), R=[edec], W=[edec])
                stop_at(1.4)
                for h in range(16):
                    e = "pool" if h % 2 else "dve"
                    eh = nc.gpsimd if h % 2 else nc.vector
                    k.op(e, lambda: eh.tensor_scalar(out=Dm[0:nt, h, 0:nt], in0=ident[0:nt, 0:nt], scalar1=Gs[0:nt, h:h + 1], scalar2=0.0, op0=ALU.mult, op1=ALU.add),
                         R=[ident, Gs], Wp=[Dm])
                stop_at(1.6)
                for q4 in range(4):
                    b = bank()
                    k.op("pe", lambda: nc.tensor.matmul(b[:, 0:4 * nt], lhsT=ones_f[0:nt, :], rhs=Dm[0:nt, 4 * q4:4 * q4 + 4, 0:nt], start=True, stop=True),
                         R=[ones_f, Dm], W=[b])
                    stop_at(1.65)
                    bv = b[:, 0:4 * nt].rearrange("p (a b) -> p a b", b=nt)
                    k.op("act", lambda: nc.scalar.activation(out=eGbc[:, 4 * q4:4 * q4 + 4, 0:nt], in_=bv, func=AF.Exp), R=[b], Wp=[eGbc, b])
                    stop_at(1.7)
                    for j in range(4):
                        h = 4 * q4 + j
                        k.op("dve", lambda: nc.vector.scalar_tensor_tensor(out=dmS[0:nt, h, 0:nt], in0=b[0:nt, j * nt:(j + 1) * nt], scalar=Gs[0:nt, h:h + 1],
                                                                           in1=C["c_nmL"][0:nt, 0:nt], op0=ALU.subtract, op1=ALU.subtract),
                             R=[b, Gs, C["c_nmL"]], Wp=[dmS])
                        k.op("dve", lambda: nc.vector.scalar_tensor_tensor(out=dmT[0:nt, h, 0:nt], in0=b[0:nt, j * nt:(j + 1) * nt], scalar=Gs[0:nt, h:h + 1],
                                                                           in1=C["c_nmT"][0:nt, 0:nt], op0=ALU.subtract, op1=ALU.add),
                             R=[b, Gs, C["c_nmT"]], Wp=[dmT])
                stop_at(1.8)
                k.op("act", lambda: nc.scalar.activation(out=dmS[0:nt, :, 0:nt], in_=dmS[0:nt, :, 0:nt], func=AF.Exp, scale=-1.0), R=[dmS], W=[dmS])
                k.op("act", lambda: nc.scalar.activation(out=dmT[0:nt, :, 0:nt], in_=dmT[0:nt, :, 0:nt], func=AF.Exp), R=[dmT], W=[dmT])
                k.op("dve", lambda: nc.vector.tensor_tensor(out=dmS[0:nt, :, 0:nt], in0=dmS[0:nt, :, 0:nt], in1=beta[0:nt, :].unsqueeze(2).to_broadcast([nt, 16, nt]),
                                                            op=ALU.mult), R=[dmS, beta], W=[dmS])
                stop_at(2)
                def hg_gen(hg, Bf, Sg):
                    raw, cv, sqb, rst, qd, kbg, kdec, vb = Bf['raw'], Bf['cv'], Bf['sqb'], Bf['rst'], Bf['qd'], Bf['kbg'], Bf['kdec'], Bf['vb']
                    L, U, P, qkm, wT, u_, vnew, oT, zs, ob = Bf['L'], Bf['U'], Bf['P'], Bf['qkm'], Bf['wT'], Bf['u'], Bf['vnew'], Bf['oT'], Bf['zs'], Bf['ob']
                    for hh in range(HG):
                        h = hg + hh
                        src = S["qkvT"][:].rearrange("(c h p) t -> p c h t", c=3, h=16)[:, :, h, :]
                        if tt > 0:
                            k.dma("sp", raw[:, hh, :, 0:3 + nt], src[:, :, ta - 3:ta + nt], R=[S["qkvT"]], Wp=[raw])
                        else:
                            k.dma("sp", raw[:, hh, :, 3:3 + nt], src[:, :, ta:ta + nt], R=[S["qkvT"]], Wp=[raw])
                            for comp in range(3):
                                if si < 0:
                                    k.op("pool", lambda: nc.gpsimd.memset(raw[:, hh, comp, 0:3], 0.0), Wp=[raw])
                                else:
                                    k.op("pool", lambda: nc.gpsimd.tensor_copy(out=raw[:, hh, comp, 0:3], in_=cH[:, comp * 16 + h, si * 3:(si + 1) * 3]), R=[cH], Wp=[raw])
                    k.dma("sp", zs[:, :, 0:nt], S["zT"][hg * 128:(hg + HG) * 128, ta:ta + nt].rearrange("(h p) t -> p h t", p=128), R=[S["zT"]], W=[zs])
                    cvR, ctmp = Bf['cvR'], Bf['ctmp']
                    for j in range(4):
                        for hh in range(HG):
                            h = hg + hh
                            for comp in range(3):
                                ch = comp * 16 + h
                                rg = cvR[hh][comp]
                                on_pool = comp == 2
                                o_ = cv[:, hh, comp, 0:nt]
                                i_ = raw[:, hh, comp, j:j + nt]
                                if j == 0:
                                    if on_pool:
                                        k.op("pool", lambda: nc.gpsimd.tensor_scalar(out=o_, in0=i_, scalar1=cw[:, ch, 0:1], scalar2=cw[:, ch, 4:5], op0=ALU.mult, op1=ALU.add),
                                             R=[raw, cw], Wp=[rg], Wa=[cv])
                                    else:
                                        k.op("dve", lambda: nc.vector.tensor_scalar(out=o_, in0=i_, scalar1=cw[:, ch, 0:1], scalar2=cw[:, ch, 4:5], op0=ALU.mult, op1=ALU.add),
                                             R=[raw, cw], Wp=[rg], Wa=[cv])
                                elif on_pool:
                                    t_ = ctmp[:, hh, comp, 0:nt]
                                    tr = Bf['ctR'][hh][comp]
                                    k.op("pool", lambda: nc.gpsimd.tensor_scalar(out=t_, in0=i_, scalar1=cw[:, ch, j:j + 1], scalar2=0.0, op0=ALU.mult, op1=ALU.add), R=[raw, cw], W=[tr])
                                    k.op("pool", lambda: nc.gpsimd.tensor_tensor(out=o_, in0=o_, in1=t_, op=ALU.add), R=[tr, rg], Wp=[rg])
                                else:
                                    k.op("dve", lambda: nc.vector.scalar_tensor_tensor(out=o_, in0=i_, scalar=cw[:, ch, j:j + 1], in1=o_, op0=ALU.mult, op1=ALU.add),
                                         R=[raw, cw, rg], Wp=[rg])
                    allcv = [cvR[a][b_] for a in range(HG) for b_ in range(3)]
                    yield
                    k.op("act", lambda: nc.scalar.activation(out=cv[:, :, :, 0:nt], in_=cv[:, :, :, 0:nt], func=AF.Silu), R=allcv, W=[cv] + allcv)
                    k.op("pool", lambda: nc.gpsimd.tensor_tensor(out=sqb[:, :, :, 0:nt], in0=cv[:, :, 0:2, 0:nt], in1=cv[:, :, 0:2, 0:nt], op=ALU.mult), R=[cv], W=[sqb])
                    for b4 in range(0, HG, 2):
                        b = bank()
                        for j in range(2):
                            k.op("pe", lambda: nc.tensor.matmul(b[:, j * 2 * nt:(j + 1) * 2 * nt], lhsT=ones_f[:], rhs=sqb[:, b4 + j, :, 0:nt], start=True, stop=True),
                                 R=[ones_f, sqb], W=[b], sig=(j == 1))
                        bv = b[:, 0:4 * nt].rearrange("p (a c b) -> p a c b", a=2, c=2)
                        k.op("act", lambda: nc.scalar.activation(out=rst[:, b4:b4 + 2, :, 0:nt], in_=bv, func=AF.Sqrt, bias=1e-6, scale=1.0), R=[b], Wp=[rst])
                    k.op("dve", lambda: nc.vector.reciprocal(out=rst[:, :, :, 0:nt], in_=rst[:, :, :, 0:nt]), R=[rst], W=[rst])
                    k.op("dve", lambda: nc.vector.scalar_tensor_tensor(out=cv[:, :, 0, 0:nt], in0=cv[:, :, 0, 0:nt], scalar=HD ** -0.5, in1=rst[:, :, 0, 0:nt],
                                                                       op0=ALU.mult, op1=ALU.mult), R=[cv, rst], Wp=[cv])
                    k.op("pool", lambda: nc.gpsimd.tensor_tensor(out=cv[:, :, 1, 0:nt], in0=cv[:, :, 1, 0:nt], in1=rst[:, :, 1, 0:nt], op=ALU.mult), R=[cv, rst], Wp=[cv])
                    k.op("dve", lambda: nc.vector.tensor_tensor(out=qd[:, :, 0:nt], in0=cv[:, :, 0, 0:nt], in1=eGbc[:, hg:hg + HG, 0:nt], op=ALU.mult), R=[cv, eGbc], W=[qd])
                    yield
                    for b4 in range(0, HG, 4):
                        bk_, bv_ = bank(), bank()
                        for j in range(4):
                            k.op("pe", lambda: nc.tensor.transpose(bk_[0:nt, j * 128:(j + 1) * 128], cv[:, b4 + j, 1, 0:nt], ident[:]), R=[cv, ident], W=[bk_], sig=(j == 3))
                        for j in range(4):
                            k.op("pe", lambda: nc.tensor.transpose(bv_[0:nt, j * 128:(j + 1) * 128], cv[:, b4 + j, 2, 0:nt], ident[:]), R=[cv, ident], W=[bv_], sig=(j == 3))
                        hs = slice(hg + b4, hg + b4 + 4)
                        kv3 = bk_[0:nt, :].rearrange("p (a b) -> p a b", b=128)
                        vv3 = bv_[0:nt, :].rearrange("p (a b) -> p a b", b=128)
                        k.op("dve", lambda: nc.vector.tensor_tensor(out=kbg[0:nt, b4:b4 + 4, :], in0=kv3, in1=bg[0:nt, hs].unsqueeze(2).to_broadcast([nt, 4, 128]), op=ALU.mult),
                             R=[bk_, bg], Wp=[kbg])
                        k.op("dve", lambda: nc.vector.tensor_tensor(out=kdec[0:nt, b4:b4 + 4, :], in0=kv3, in1=edec[0:nt, hs].unsqueeze(2).to_broadcast([nt, 4, 128]), op=ALU.mult),
                             R=[bk_, edec], Wp=[kdec])
                        k.op("dve", lambda: nc.vector.tensor_tensor(out=vb[0:nt, b4:b4 + 4, :], in0=vv3, in1=beta[0:nt, hs].unsqueeze(2).to_broadcast([nt, 4, 128]), op=ALU.mult),
                             R=[bv_, beta], Wp=[vb])
                    yield
                    for b4 in range(0, HG, 4):
                        b1, b2 = bank(), bank()
                        for j in range(4):
                            k.op("pe", lambda: nc.tensor.matmul(b1[0:nt, j * nt:(j + 1) * nt], lhsT=cv[:, b4 + j, 1, 0:nt], rhs=cv[:, b4 + j, 1, 0:nt], start=True, stop=True),
                                 R=[cv], W=[b1], sig=(j == 3))
                        for j in range(4):
                            k.op("pe", lambda: nc.tensor.matmul(b2[0:nt, j * nt:(j + 1) * nt], lhsT=cv[:, b4 + j, 1, 0:nt], rhs=cv[:, b4 + j, 0, 0:nt], start=True, stop=True),
                                 R=[cv], W=[b2], sig=(j == 3))
                        hs = slice(hg + b4, hg + b4 + 4)
                        k.op("dve", lambda: nc.vector.tensor_tensor(out=L[0][0:nt, b4:b4 + 4, 0:nt], in0=b1[0:nt, 0:4 * nt].rearrange("p (a b) -> p a b", b=nt),
                                                                    in1=dmS[0:nt, hs, 0:nt], op=ALU.mult), R=[b1, dmS], Wp=[L[0]])
                        k.op("dve", lambda: nc.vector.tensor_tensor(out=qkm[0:nt, b4:b4 + 4, 0:nt], in0=b2[0:nt, 0:4 * nt].rearrange("p (a b) -> p a b", b=nt),
                                                                    in1=dmT[0:nt, hs, 0:nt], op=ALU.mult), R=[b2, dmT], Wp=[qkm])
                    yield
                    for b4 in range(0, HG, 4):
                        b = bank()
                        for j in range(4):
                            k.op("pe", lambda: nc.tensor.transpose(b[0:nt, j * nt:(j + 1) * nt], L[0][0:nt, b4 + j, 0:nt], ident[0:nt, 0:nt]), R=[L[0], ident], W=[b], sig=(j == 3))
                        bv = b[0:nt, 0:4 * nt].rearrange("p (a b) -> p a b", b=nt)
                        k.op("act", lambda: nc.scalar.copy(out=U[0][0:nt, b4:b4 + 4, 0:nt], in_=bv), R=[b], Wp=[U[0]])
                        k.op("pool", lambda: nc.gpsimd.tensor_tensor(out=P[0:nt, b4:b4 + 4, 0:nt], in0=ident[0:nt, 0:nt].unsqueeze(1).to_broadcast([nt, 4, nt]),
                                                                     in1=U[0][0:nt, b4:b4 + 4, 0:nt], op=ALU.subtract), R=[U[0], ident], Wp=[P])
                    cur = 0
                    for step in range(5):
                        nx = 1 - cur
                        for b4 in range(0, HG, 4):
                            b1 = bank()
                            for j in range(4):
                                k.op("pe", lambda: nc.tensor.matmul(b1[0:nt, j * nt:(j + 1) * nt], lhsT=U[cur][0:nt, b4 + j, 0:nt], rhs=L[cur][0:nt, b4 + j, 0:nt], start=True, stop=True),
                                     R=[U[cur], L[cur]], W=[b1], sig=(j == 3))
                            k.op("act", lambda: nc.scalar.copy(out=L[nx][0:nt, b4:b4 + 4, 0:nt], in_=b1[0:nt, 0:4 * nt].rearrange("p (a b) -> p a b", b=nt)), R=[b1], Wp=[L[nx]])
                            if step < 4:
                                b2 = bank()
                                for j in range(4):
                                    k.op("pe", lambda: nc.tensor.matmul(b2[0:nt, j * nt:(j + 1) * nt], lhsT=L[cur][0:nt, b4 + j, 0:nt], rhs=U[cur][0:nt, b4 + j, 0:nt], start=True, stop=True),
                                         R=[U[cur], L[cur]], W=[b2], sig=(j == 3))
                                k.op("dve", lambda: nc.vector.tensor_copy(out=U[nx][0:nt, b4:b4 + 4, 0:nt], in_=b2[0:nt, 0:4 * nt].rearrange("p (a b) -> p a b", b=nt)), R=[b2], Wp=[U[nx]])
                        for b4 in range(0, HG, 4):
                            b3 = bank()
                            for j in range(4):
                                k.op("pe", lambda: nc.tensor.matmul(b3[0:nt, j * nt:(j + 1) * nt], lhsT=L[nx][0:nt, b4 + j, 0:nt], rhs=P[0:nt, b4 + j, 0:nt], start=True, stop=True),
                                     R=[L[nx], P], W=[b3], sig=(j == 3))
                            k.op("dve", lambda: nc.vector.tensor_tensor(out=P[0:nt, b4:b4 + 4, 0:nt], in0=P[0:nt, b4:b4 + 4, 0:nt],
                                                                        in1=b3[0:nt, 0:4 * nt].rearrange("p (a b) -> p a b", b=nt), op=ALU.add), R=[b3, P], Wp=[P])
                        cur = nx
                        yield
                    yield
                    for b4 in range(0, HG, 4):
                        b1, b2 = bank(), bank()
                        for j in range(4):
                            k.op("pe", lambda: nc.tensor.matmul(b1[:, j * nt:(j + 1) * nt], lhsT=kbg[0:nt, b4 + j, :], rhs=P[0:nt, b4 + j, 0:nt], start=True, stop=True),
                                 R=[kbg, P], W=[b1], sig=(j == 3))
                        for j in range(4):
                            k.op("pe", lambda: nc.tensor.matmul(b2[0:nt, j * 128:(j + 1) * 128], lhsT=P[0:nt, b4 + j, 0:nt], rhs=vb[0:nt, b4 + j, :], start=True, stop=True),
                                 R=[vb, P], W=[b2], sig=(j == 3))
                        k.op("act", lambda: nc.scalar.copy(out=wT[:, b4:b4 + 4, 0:nt], in_=b1[:, 0:4 * nt].rearrange("p (a b) -> p a b", b=nt)), R=[b1], Wp=[wT])
                        k.op("dve", lambda: nc.vector.tensor_copy(out=u_[0:nt, b4:b4 + 4, :], in_=b2[0:nt, :].rearrange("p (a b) -> p a b", b=128)), R=[b2], Wp=[u_])
                    yield
                    for ci in range(nch):
                        r = slice(ci * 64, ci * 64 + 64)
                        for b4 in range(0, HG, 4):
                            b1 = bank()
                            for j in range(4):
                                k.op("pe", lambda: nc.tensor.matmul(b1[r, j * 128:(j + 1) * 128], lhsT=wT[:, b4 + j, r], rhs=Sg[:, b4 + j, :], start=True, stop=True),
                                     R=[wT, Sg], W=[b1], sig=(j == 3))
                            k.op("dve", lambda: nc.vector.tensor_tensor(out=vnew[r, b4:b4 + 4, :], in0=u_[r, b4:b4 + 4, :], in1=b1[r, :].rearrange("p (a b) -> p a b", b=128),
                                                                        op=ALU.subtract), R=[u_, b1], Wp=[vnew])
                        bo = bank()
                        for hh in range(HG):
                            k.op("pe", lambda: nc.tensor.matmul(bo[:, hh * 64:(hh + 1) * 64], lhsT=Sg[:, hh, :], rhs=qd[:, hh, r], start=True, stop=False),
                                 R=[Sg, qd], W=[bo], sig=False)
                            k.op("pe", lambda: nc.tensor.matmul(bo[:, hh * 64:(hh + 1) * 64], lhsT=vnew[0:nt, hh, :], rhs=qkm[0:nt, hh, r], start=False, stop=True),
                                 R=[vnew, qkm], W=[bo], sig=(hh == HG - 1))
                        k.op("act", lambda: nc.scalar.copy(out=oT[:, :, r], in_=bo[:, 0:HG * 64].rearrange("p (a b) -> p a b", b=64)), R=[bo], Wp=[oT])
                        for b4 in range(0, HG, 4):
                            b2 = bank()
                            for j in range(4):
                                k.op("pe", lambda: nc.tensor.matmul(b2[:, j * 128:(j + 1) * 128], lhsT=kdec[r, b4 + j, :], rhs=vnew[r, b4 + j, :], start=True, stop=True),
                                     R=[kdec, vnew], W=[b2], sig=(j == 3))
                            hs = slice(hg + b4, hg + b4 + 4)
                            col = ci * 64 + 63
                            k.op("pool", lambda: nc.gpsimd.tensor_tensor(out=Sg[:, b4:b4 + 4, :], in0=Sg[:, b4:b4 + 4, :], in1=eGbc[:, hs, col:col + 1].to_broadcast([128, 4, 128]), op=ALU.mult),
                                 R=[Sg, eGbc], Wp=[Sg])
                            k.op("dve", lambda: nc.vector.tensor_tensor(out=Sg[:, b4:b4 + 4, :], in0=Sg[:, b4:b4 + 4, :], in1=b2[:, :].rearrange("p (a b) -> p a b", b=128), op=ALU.add),
                                 R=[Sg, b2], Wp=[Sg])
                        yield
                    yield
                    k.op("pool", lambda: nc.gpsimd.tensor_tensor(out=sqb[:, :, 0, 0:nt], in0=oT[:, :, 0:nt], in1=oT[:, :, 0:nt], op=ALU.mult), R=[oT], Wp=[sqb])
                    for b4 in range(0, HG, 4):
                        b = bank()
                        for j in range(4):
                            k.op("pe", lambda: nc.tensor.matmul(b[:, j * nt:(j + 1) * nt], lhsT=ones_f[:], rhs=sqb[:, b4 + j, 0, 0:nt], start=True, stop=True),
                                 R=[ones_f, sqb], W=[b], sig=(j == 3))
                        k.op("act", lambda: nc.scalar.activation(out=rst[:, b4:b4 + 4, 0, 0:nt], in_=b[:, 0:4 * nt].rearrange("p (a b) -> p a b", b=nt), func=AF.Sqrt,
                                                                 bias=EPS, scale=1.0 / 128), R=[b], Wp=[rst])
                    k.op("dve", lambda: nc.vector.reciprocal(out=rst[:, :, 0, 0:nt], in_=rst[:, :, 0, 0:nt]), R=[rst], Wp=[rst])
                    k.op("dve", lambda: nc.vector.tensor_tensor(out=oT[:, :, 0:nt], in0=oT[:, :, 0:nt], in1=rst[:, :, 0, 0:nt], op=ALU.mult), R=[oT, rst], W=[oT])
                    k.op("dve", lambda: nc.vector.scalar_tensor_tensor(out=ob[:, :, 0:nt], in0=oT[:, :, 0:nt], scalar=dng[:, 0:1], in1=zs[:, :, 0:nt], op0=ALU.mult, op1=ALU.mult),
                         R=[oT, dng, zs], W=[ob])
                    k.dma("pool", S["aT"][2048 + hg * 128:2048 + (hg + HG) * 128, ta:ta + nt].rearrange("(h p) t -> p h t", p=128), ob[:, :, 0:nt], R=[ob], Wp=[S["aT"]])

                gens = [hg_gen(hg, BS[i % 2], S_g[hg // HG]) for i, hg in enumerate(range(0, 16, HG))]
                active = []
                while gens or active:
                    while len(active) < 2 and gens:
                        active.append(gens.pop(0))
                    for gen_ in list(active):
                        try:
                            next(gen_)
                        except StopIteration:
                            active.remove(gen_)
            for gi_ in range(16 // HG):
                r0_ = qi * 2048 + gi_ * HG * 128
                k.dma("pool", I["so"][r0_:r0_ + HG * 128, :].rearrange("(h d) e -> d h e", d=128), S_g[gi_][:], R=[S_g[gi_]], Wp=[I["so"]])


def ln_alloc(k, st, gmax):
    return dict(sq=[k.sb(st, "lnsq", [128, gmax], F32) for _ in range(2)], mt=k.sb(st, "lnm", [128, gmax], F32),
                t1=k.sb(st, "lnt", [128, gmax], F32), rs=k.sb(st, "lnrs", [128, gmax], F32), nm=k.sb(st, "lnnm", [128, gmax], F32))


def ln_stats(k, G, T, s1, s1R, gn):
    nc = k.nc
    bank = G["bank"]
    ones_f = G["ones_f"]
    sq, mt, t1, rs, nm = T["sq"], T["mt"], T["t1"], T["rs"], T["nm"]
    bs, bq = bank(), bank()
    for m in range(KC):
        q_ = sq[m % 2]
        k.op("act", lambda: nc.scalar.activation(out=q_[:, 0:gn], in_=s1[:, m, 0:gn], func=AF.Square), R=[s1R[m]], W=[q_])
        k.op("pe", lambda: nc.tensor.matmul(bs[:, 0:gn], lhsT=ones_f[:], rhs=s1[:, m, 0:gn], start=(m == 0), stop=(m == KC - 1)),
             R=[s1R[m], ones_f], W=[bs], sig=(m == KC - 1))
        k.op("pe", lambda: nc.tensor.matmul(bq[:, 0:gn], lhsT=ones_f[:], rhs=q_[:, 0:gn], start=(m == 0), stop=(m == KC - 1)),
             R=[q_, ones_f], W=[bq], sig=True)
    g = slice(0, gn)
    k.op("dve", lambda: nc.vector.tensor_scalar(out=mt[:, g], in0=bs[:, g], scalar1=1.0 / D, scalar2=None, op0=ALU.mult), R=[bs], W=[mt])
    k.op("dve", lambda: nc.vector.tensor_tensor(out=t1[:, g], in0=mt[:, g], in1=mt[:, g], op=ALU.mult), R=[mt], W=[t1])
    k.op("dve", lambda: nc.vector.scalar_tensor_tensor(out=t1[:, g], in0=bq[:, g], scalar=1.0 / D, in1=t1[:, g], op0=ALU.mult, op1=ALU.subtract),
         R=[bq, t1], W=[t1])
    k.op("act", lambda: nc.scalar.activation(out=t1[:, g], in_=t1[:, g], func=AF.Sqrt, bias=EPS, scale=1.0), R=[t1], W=[t1])
    k.op("dve", lambda: nc.vector.reciprocal(out=rs[:, g], in_=t1[:, g]), R=[t1], W=[rs])
    k.op("dve", lambda: nc.vector.scalar_tensor_tensor(out=nm[:, g], in0=mt[:, g], scalar=-1.0, in1=rs[:, g], op0=ALU.mult, op1=ALU.mult),
         R=[mt, rs], W=[nm])
    return rs, nm


def phase_D(k, cfg, G):
    nc = k.nc
    I, S = G["I"], G["S"]
    bank = G["bank"]
    groups = cfg.groups(384)
    gmax = max(g[1] for g in groups)
    with ExitStack() as st:
        aT = k.sb(st, "aT", [128, KC, gmax], BF16)
        s1 = k.sb(st, "s1", [128, KC, gmax], F32)
        s1R = [Reg("s1R") for _ in range(KC)]
        gb = k.sb(st, "ln1", [128, 64], F32)
        k.dma("sp", gb[:], I["ln1"][:], W=[gb])
        ws = WTiles(k, st, nslot=3)
        xr = [k.sb(st, "xr", [128, gmax], F32) for _ in range(3)]
        T = ln_alloc(k, st, gmax)
        of = [k.sb(st, "of", [128, gmax], F32) for _ in range(2)]
        ob = [k.sb(st, "ob", [128, gmax], BF16) for _ in range(2)]
        def d_main(g0, gn, m):
            g = slice(0, gn)
            wb = ws.get(S["b_w_o"], m // 2)
            sub = (m % 2) * 128
            x_ = xr[m % 3]
            k.dma("sp", x_[:, g], S["xnT"][m * 128:(m + 1) * 128, g0:g0 + gn], R=[S["xnT"]], W=[x_])
            b = bank()
            for kc in range(KC):
                k.op("pe", lambda kc=kc: nc.tensor.matmul(b[:, 0:gn], lhsT=wb[:, kc, sub:sub + 128], rhs=aT[:, kc, g], start=(kc == 0), stop=(kc == KC - 1)),
                     R=[wb, aT], W=[b], sig=(kc == KC - 1))
            k.op("dve", lambda: nc.vector.scalar_tensor_tensor(out=s1[:, m, g], in0=x_[:, g], scalar=ALPHA, in1=b[:, 0:gn], op0=ALU.mult, op1=ALU.add),
                 R=[x_, b], W=[s1R[m]])

        def d_tail(g0, gn, m, rs, nm):
            g = slice(0, gn)
            o_, b_ = of[m % 2], ob[m % 2]
            k.op("pool", lambda: nc.gpsimd.tensor_tensor(out=o_[:, g], in0=s1[:, m, g], in1=rs[:, g], op=ALU.mult), R=[s1R[m], rs], W=[o_])
            k.op("dve", lambda: nc.vector.tensor_tensor(out=o_[:, g], in0=o_[:, g], in1=nm[:, g], op=ALU.add), R=[o_, nm], W=[o_])
            k.op("act", lambda: nc.scalar.activation(out=o_[:, g], in_=o_[:, g], func=AF.Identity, scale=gb[:, m:m + 1], bias=gb[:, 32 + m:33 + m]),
                 R=[o_, gb], W=[o_])
            k.op("pool", lambda: nc.gpsimd.tensor_copy(out=b_[:, g], in_=o_[:, g]), R=[o_], W=[b_])
            k.dma("act", S["x1T"][m * 128:(m + 1) * 128, g0:g0 + gn], o_[:, g], R=[o_], Wp=[S["x1T"]])
            k.dma("act", S["x1b"][m * 128:(m + 1) * 128, g0:g0 + gn], b_[:, g], R=[b_], Wp=[S["x1b"]])

        prev = None
        for cur in list(groups) + [None]:
            if cur is not None:
                g0, gn = cur
                k.dma("sp", aT[:, :, 0:gn], S["aT"][:].rearrange("(kc p) t -> p kc t", p=128)[:, :, g0:g0 + gn], R=[S["aT"]], W=[aT])
            for m in range(KC):
                if prev is not None:
                    d_tail(prev[0], prev[1], m, prev[2], prev[3])
                if cur is not None:
                    d_main(cur[0], cur[1], m)
            if cur is not None:
                rs, nm = ln_stats(k, G, T, s1, s1R, cur[1])
                prev = (cur[0], cur[1], rs, nm)
            else:
                prev = None
    k.barrier()


def seg_pieces(cfg, g0, gn):
    out = []
    for qi, sq in enumerate(cfg.seqs):
        a = max(sq["t0"], g0)
        b = min(sq["t0"] + sq["T"], g0 + gn)
        if a < b:
            out.append((qi, a - g0, b - a, a == sq["t0"], b == sq["t0"] + sq["T"]))
    return out


def phase_E(k, cfg, G):
    nc = k.nc
    I, S, C = G["I"], G["S"], G["C"]
    bank = G["bank"]
    FC, NS, NSEQ, DFF = cfg.FC, cfg.NS, cfg.NSEQ, cfg.DFF
    with ExitStack() as pst:
        fw = k.sb(pst, "ffnw", [128, 2 * FC, 4], F32)
        k.dma("sp", fw[:].rearrange("p a b -> p (a b)"), I["ffnw"][:], W=[fw])
        hsave = k.sb(pst, "hsave", [128, 2 * FC, 2], F32)
        k.op("pool", lambda: nc.gpsimd.memset(hsave[:], 0.0), W=[hsave])
        fst = k.sb(pst, "fst", [128, 2 * FC, NSEQ * 2], F32)
        fH = k.sb(pst, "fH", [128, 2 * FC, max(NS, 1) * 2], F32)
        if NS > 0:
            with ExitStack() as s0:
                srow = [k.sb(s0, "srow", [NS * 2, 512], F32) for _ in range(2)]
                n = 0
                for c4 in range(0, 2 * FC, 4):
                    b = bank()
                    n4 = min(4, 2 * FC - c4)
                    sr = srow[n % 2]
                    n += 1
                    k.dma("sp", sr[:, 0:n4 * 128], I["sffn"][:, c4 * 128:(c4 + n4) * 128], W=[sr])
                    for j in range(n4):
                        k.op("pe", lambda j=j: nc.tensor.transpose(b[:, j * 128:j * 128 + NS * 2], sr[:, j * 128:(j + 1) * 128],
                                                                   C["c_ident"][0:NS * 2, 0:NS * 2]),
                             R=[sr, C["c_ident"]], W=[b], sig=(j == n4 - 1))
                    k.op("dve", lambda: nc.vector.tensor_copy(out=fH[:, c4:c4 + n4, :],
                                                              in_=b[:, 0:n4 * 128].rearrange("p (a b) -> p a b", b=128)[:, :, 0:NS * 2]),
                         R=[b], Wp=[fH])
            k.barrier()
        for (g0, gn) in cfg.groups(768):
            pcs = seg_pieces(cfg, g0, gn)
            offs = []
            o = 0
            for p in pcs:
                offs.append(o)
                o += p[2] + 2
            RW = o
            with ExitStack() as st:
                x1 = k.sb(st, "x1b", [128, KC, gn], BF16)
                k.dma("sp", x1[:], S["x1b"][:].rearrange("(kc p) t -> p kc t", p=128)[:, :, g0:g0 + gn], R=[S["x1b"]], W=[x1])
                ws = WTiles(k, st, nslot=4)
                raw = [[k.sb(st, "raw", [128, RW], F32) for _ in range(2)] for _ in range(2)]
                cv = [[k.sb(st, "cv", [128, gn], F32) for _ in range(2)] for _ in range(2)]
                ao = [k.sb(st, "ao", [128, gn], BF16) for _ in range(2)]
                for c in range(FC):
                    par = c % 2
                    for half in range(2):
                        ch = half * FC + c
                        r_ = raw[half][par]
                        wb = ws.get(S["b_w_up"], ch // 2)
                        sub = (ch % 2) * 128
                        for pi, (qi, a, ln, s_st, s_en) in enumerate(pcs):
                            o_ = offs[pi]
                            if not s_st:
                                k.op("pool", lambda o_=o_: nc.gpsimd.tensor_copy(out=r_[:, o_:o_ + 2], in_=hsave[:, ch, :]), R=[hsave], Wp=[r_])
                            elif qi == 0:
                                k.op("pool", lambda o_=o_: nc.gpsimd.memset(r_[:, o_:o_ + 2], 0.0), Wp=[r_])
                            else:
                                k.op("pool", lambda o_=o_, qi=qi: nc.gpsimd.tensor_copy(out=r_[:, o_:o_ + 2], in_=fH[:, ch, (qi - 1) * 2:qi * 2]), R=[fH], Wp=[r_])
                        for (b0, bn) in blocks(gn):
                            b = bank()
                            for kc in range(KC):
                                k.op("pe", lambda kc=kc: nc.tensor.matmul(b[:, 0:bn], lhsT=wb[:, kc, sub:sub + 128], rhs=x1[:, kc, b0:b0 + bn], start=(kc == 0), stop=(kc == KC - 1)),
                                     R=[wb, x1], W=[b], sig=(kc == KC - 1))
                            for pi, (qi, a, ln, s_st, s_en) in enumerate(pcs):
                                lo, hi = max(a, b0), min(a + ln, b0 + bn)
                                if lo < hi:
                                    d0 = offs[pi] + 2 + (lo - a)
                                    k.op("act", lambda lo=lo, hi=hi, d0=d0: nc.scalar.copy(out=r_[:, d0:d0 + hi - lo], in_=b[:, lo - b0:hi - b0]), R=[b], Wp=[r_])
                        c_ = cv[half][par]
                        for pi, (qi, a, ln, s_st, s_en) in enumerate(pcs):
                            o_ = offs[pi]
                            k.op("dve", lambda o_=o_, a=a, ln=ln: nc.vector.tensor_scalar(out=c_[:, a:a + ln], in0=r_[:, o_:o_ + ln], scalar1=fw[:, ch, 0:1], scalar2=fw[:, ch, 3:4],
                                                                                       op0=ALU.mult, op1=ALU.add), R=[r_, fw], Wp=[c_])
                            for j in (1, 2):
                                k.op("dve", lambda o_=o_, a=a, ln=ln, j=j: nc.vector.scalar_tensor_tensor(out=c_[:, a:a + ln], in0=r_[:, o_ + j:o_ + j + ln], scalar=fw[:, ch, j:j + 1],
                                                                                                         in1=c_[:, a:a + ln], op0=ALU.mult, op1=ALU.add), R=[r_, fw, c_], Wp=[c_])
                            if s_en:
                                k.op("pool", lambda o_=o_, ln=ln, qi=qi: nc.gpsimd.tensor_copy(out=fst[:, ch, qi * 2:qi * 2 + 2], in_=r_[:, o_ + ln:o_ + ln + 2]), R=[r_], Wp=[fst])
                            else:
                                k.op("pool", lambda o_=o_, ln=ln: nc.gpsimd.tensor_copy(out=hsave[:, ch, :], in_=r_[:, o_ + ln:o_ + ln + 2]), R=[r_], Wp=[hsave])
                    gt, vl, a_ = cv[0][par], cv[1][par], ao[par]
                    k.op("act", lambda: nc.scalar.activation(out=gt[:], in_=gt[:], func=AF.Silu), R=[gt], W=[gt])
                    k.op("pool", lambda: nc.gpsimd.tensor_tensor(out=a_[:], in0=gt[:], in1=vl[:], op=ALU.mult), R=[gt, vl], W=[a_])
                    k.dma("act", S["actT"][c * 128:(c + 1) * 128, g0:g0 + gn], a_[:], R=[a_], Wp=[S["actT"]])
            k.barrier()
        with ExitStack() as st:
            orow = [k.sb(st, "orow", [NSEQ * 2, 512], F32) for _ in range(2)]
            n = 0
            for c4 in range(0, 2 * FC, 4):
                n4 = min(4, 2 * FC - c4)
                b = bank()
                for j in range(n4):
                    k.op("pe", lambda j=j: nc.tensor.transpose(b[0:NSEQ * 2, j * 128:(j + 1) * 128], fst[:, c4 + j, :], C["c_ident"][:]),
                         R=[fst, C["c_ident"]], W=[b], sig=(j == n4 - 1))
                o_ = orow[n % 2]
                n += 1
                k.op("dve", lambda: nc.vector.tensor_copy(out=o_[:, 0:n4 * 128], in_=b[0:NSEQ * 2, 0:n4 * 128]), R=[b], W=[o_])
                k.dma("pool", I["ffno"][:, c4 * 128:(c4 + n4) * 128], o_[:, 0:n4 * 128], R=[o_], Wp=[I["ffno"]])
        k.barrier()


def phase_F(k, cfg, G):
    nc = k.nc
    I, S, C = G["I"], G["S"], G["C"]
    bank = G["bank"]
    FC = cfg.FC
    kgs = [(i, min(32, FC - i)) for i in range(0, FC, 32)]
    groups = cfg.groups(256)
    gmax = max(g[1] for g in groups)
    with ExitStack() as st:
        aT = k.sb(st, "actT", [128, FC, gmax], BF16)
        s1 = k.sb(st, "s2", [128, KC, gmax], F32)
        s1R = [Reg("s2R") for _ in range(KC)]
        gb = k.sb(st, "ln2", [128, 64], F32)
        k.dma("sp", gb[:], I["ln2"][:], W=[gb])
        ws = WTiles(k, st, nslot=4)
        xr = [k.sb(st, "xr", [128, gmax], F32) for _ in range(3)]
        T = ln_alloc(k, st, gmax)
        yt = [k.sb(st, "yt", [128, 2048], F32) for _ in range(2)]
        yn = 0
        for (g0, gn) in groups:
            g = slice(0, gn)
            k.dma("sp", aT[:, :, g], S["actT"][:].rearrange("(kc p) t -> p kc t", p=128)[:, :, g0:g0 + gn], R=[S["actT"]], W=[aT])
            for m in range(KC):
                x_ = xr[m % 3]
                sub = (m % 2) * 128
                k.dma("sp", x_[:, g], S["x1T"][m * 128:(m + 1) * 128, g0:g0 + gn], R=[S["x1T"]], W=[x_])
                b = bank()
                for gi, (k0, kn) in enumerate(kgs):
                    wb = ws.get(S["b_w_down"], m // 2, kg=gi, kcn=kn)
                    for kc in range(kn):
                        first = (gi == 0 and kc == 0)
                        last = (gi == len(kgs) - 1 and kc == kn - 1)
                        k.op("pe", lambda kc=kc: nc.tensor.matmul(b[:, 0:gn], lhsT=wb[:, kc, sub:sub + 128], rhs=aT[:, k0 + kc, g], start=first, stop=last),
                             R=[wb, aT], W=[b], sig=(kc == kn - 1))
                k.op("dve", lambda: nc.vector.scalar_tensor_tensor(out=s1[:, m, g], in0=x_[:, g], scalar=ALPHA, in1=b[:, 0:gn], op0=ALU.mult, op1=ALU.add),
                     R=[x_, b], W=[s1R[m]])
            rs, nm = ln_stats(k, G, T, s1, s1R, gn)
            for m in range(KC):
                k.op("pool", lambda: nc.gpsimd.tensor_tensor(out=s1[:, m, g], in0=s1[:, m, g], in1=rs[:, g], op=ALU.mult), R=[s1R[m], rs], W=[s1R[m]])
                k.op("dve", lambda: nc.vector.tensor_tensor(out=s1[:, m, g], in0=s1[:, m, g], in1=nm[:, g], op=ALU.add), R=[s1R[m], nm], W=[s1R[m]])
                k.op("act", lambda: nc.scalar.activation(out=s1[:, m, g], in_=s1[:, m, g], func=AF.Identity, scale=gb[:, m:m + 1], bias=gb[:, 32 + m:33 + m]),
                     R=[s1R[m], gb], W=[s1R[m]])
            for ti in range(gn // 128):
                for hf in range(2):
                    y_ = yt[yn % 2]
                    yn += 1
                    for q in range(4):
                        b = bank()
                        for j in range(4):
                            m = hf * 16 + q * 4 + j
                            k.op("pe", lambda m=m, j=j: nc.tensor.transpose(b[:, j * 128:(j + 1) * 128], s1[:, m, ti * 128:(ti + 1) * 128], C["c_ident"][:]),
                                 R=[s1R[m], C["c_ident"]], W=[b], sig=(j == 3))
                        if q % 2:
                            k.op("act", lambda: nc.scalar.copy(out=y_[:, q * 512:(q + 1) * 512], in_=b[:, :]), R=[b], Wp=[y_])
                        else:
                            k.op("dve", lambda: nc.vector.tensor_copy(out=y_[:, q * 512:(q + 1) * 512], in_=b[:, :]), R=[b], Wp=[y_])
                    k.dma("pool", I["y"][g0 + ti * 128:g0 + (ti + 1) * 128, hf * 2048:(hf + 1) * 2048], y_[:], R=[y_], Wp=[I["y"]])
    k.barrier()


_CACHE = {}


def _pp(v):
    return np.ascontiguousarray(np.asarray(v, np.float32).reshape(32, 128).T)


def make_in_maps(cfg, n_cores, inp):
    f = lambda a: np.ascontiguousarray(np.asarray(a, dtype=np.float32))
    NS, DFF, FC = cfg.NS, cfg.DFF, cfg.FC
    shared = {}
    shared["lnin"] = np.concatenate([_pp(inp["ln_in_g"]), _pp(inp["ln_in_b"])], axis=1)
    shared["ln1"] = np.concatenate([_pp(inp["ln1_g"][0]), _pp(inp["ln1_b"][0])], axis=1)
    shared["ln2"] = np.concatenate([_pp(inp["ln2_g"][0]), _pp(inp["ln2_b"][0])], axis=1)
    shared["w_in"] = f(inp["w_in"][0]); shared["w_o"] = f(inp["w_o"][0])
    shared["w_up"] = f(inp["w_ffn_up"][0]); shared["w_down"] = f(inp["w_ffn_down"][0])
    cw = np.concatenate([f(inp["conv_qkv_w"][0]), f(inp["conv_qkv_b"])], axis=0)
    shared["convw"] = np.ascontiguousarray(cw.reshape(5, 48, 128).transpose(2, 1, 0).reshape(128, 240))
    fw = np.concatenate([f(inp["ffn_conv_w"][0]), f(inp["ffn_conv_b"])], axis=0)
    shared["ffnw"] = np.ascontiguousarray(fw.reshape(4, 2 * FC, 128).transpose(2, 1, 0).reshape(128, 2 * FC * 4))
    shared["alog"] = np.ascontiguousarray(np.broadcast_to(f(inp["a_log"][0])[None, :], (128, 16)))
    shared["dtb"] = np.ascontiguousarray(np.broadcast_to(f(inp["dt_bias"][0])[None, :], (128, 16)))
    shared["dng"] = f(inp["delta_norm_g"][0]).reshape(128, 1)
    shared["relb"] = f(inp["rel_bias"])
    shared.update(host_consts())
    maps = []
    for c in range(n_cores):
        m = dict(shared)
        sl = slice(c * NS, (c + 1) * NS)
        m["x"] = np.concatenate([f(inp["x_prompt"][c]), f(inp["x_sample"][sl]).reshape(NS * DEC, D)], axis=0)
        m["ck"] = f(inp["cache_attn_k"][0, sl]).reshape(NS * PAST, 512)
        m["cv"] = f(inp["cache_attn_v"][0, sl]).reshape(NS * PAST, 512)
        m["cik"] = f(inp["cache_idx_k"][0, sl]).reshape(NS * PAST, 64)
        m["sdel"] = f(inp["state_delta"][0, sl]).reshape(NS * 16 * 128, 128)
        m["sconv"] = f(inp["state_conv_qkv"][0, sl]).reshape(NS * 3, 6144)
        m["sffn"] = f(inp["state_ffn_conv"][0, sl]).reshape(NS * 2, 2 * DFF)
        maps.append(m)
    return maps


def kernel(**inp):
    B, SEQ = inp["x_prompt"].shape[0], inp["x_prompt"].shape[1]
    DB = inp["x_sample"].shape[0]
    DFF = inp["w_ffn_down"].shape[1]
    n_cores = B
    NS = DB // n_cores
    cfg = Cfg(SEQ, NS, DFF)
    key = (SEQ, NS, DFF)
    if key not in _CACHE:
        _CACHE[key] = build(cfg)
    nc = _CACHE[key]
    maps = make_in_maps(cfg, n_cores, inp)
    res = run_bass_kernel_spmd(nc, maps, core_ids=list(range(n_cores)))
    R = res.results
    return assemble(cfg, n_cores, R)


def assemble(cfg, n_cores, R):
    NS, SEQ, DFF, NSEQ = cfg.NS, cfg.SEQ, cfg.DFF, cfg.NSEQ
    g = lambda n: [np.asarray(R[c][n], dtype=np.float32) for c in range(n_cores)]
    y, ko, vo, iko, so, co, fo = g("y"), g("ko"), g("vo"), g("iko"), g("so"), g("convo"), g("ffno")
    yp = np.stack([a[:SEQ] for a in y])
    ys = np.concatenate([a[SEQ:].reshape(NS, DEC, D) for a in y])
    pk = np.stack([a[:SEQ].reshape(SEQ, 4, 128) for a in ko])[None]
    pv = np.stack([a[:SEQ].reshape(SEQ, 4, 128) for a in vo])[None]
    pik = np.stack([a[:SEQ] for a in iko])[None]
    sk = np.concatenate([a[SEQ:].reshape(NS, DEC, 4, 128) for a in ko])[None]
    sv = np.concatenate([a[SEQ:].reshape(NS, DEC, 4, 128) for a in vo])[None]
    sik = np.concatenate([a[SEQ:].reshape(NS, DEC, 64) for a in iko])[None]
    pd = np.stack([a.reshape(NSEQ, 16, 128, 128)[0] for a in so])[None]
    sd = np.concatenate([a.reshape(NSEQ, 16, 128, 128)[1:] for a in so])[None]
    pc = np.stack([a.reshape(NSEQ, 3, 6144)[0] for a in co])[None]
    sc = np.concatenate([a.reshape(NSEQ, 3, 6144)[1:] for a in co])[None]
    pf = np.stack([a.reshape(NSEQ, 2, 2 * DFF)[0] for a in fo])[None]
    sf = np.concatenate([a.reshape(NSEQ, 2, 2 * DFF)[1:] for a in fo])[None]
    return (yp, ys, pk, pv, pik, pd, pc, pf, sk, sv, sik, sd, sc, sf)
```
